# Optimizing a Trainium2 kernel written in Bass

```python
import math
import jax, jax.numpy as jnp
from jax import lax
import numpy as np

D_MODEL = 1024
BATCH = 8
SEQ = 2048
DEPTH = 1

GDN_HEADS = 8
GDN_HEAD_DIM = 128
GDN_WIDTH = GDN_HEADS * GDN_HEAD_DIM
CONV_K = 5
GLA_HEADS = 4
GLA_KEY_DIM = D_MODEL // 2
GLA_VAL_DIM = D_MODEL
GLA_HEAD_K = GLA_KEY_DIM // GLA_HEADS
GLA_HEAD_V = GLA_VAL_DIM // GLA_HEADS
GLA_GATE_RANK = 16
GLA_GATE_NORMALIZER = 16.0
CHUNK = 64
NORM_EPS = 1e-6

IN_SIZES = [
    3 * GDN_WIDTH,
    GDN_WIDTH,
    GDN_HEADS,
    GDN_HEADS,
    GDN_HEADS,
    GDN_HEADS,
    GLA_KEY_DIM,
    GLA_KEY_DIM,
    GLA_VAL_DIM,
    GLA_VAL_DIM,
    GLA_GATE_RANK,
    GLA_GATE_RANK,
    D_MODEL,
    D_MODEL,
]
N_IN = int(sum(IN_SIZES))
IN_SPLITS = [int(s) for s in np.cumsum(IN_SIZES)[:-1]]

kernel_name = "bidir_gdn_gla_gated_hybrid"


def rmsnorm(x, w):
    xf = x.astype(jnp.float32)
    y = xf * lax.rsqrt(jnp.mean(xf * xf, axis=-1, keepdims=True) + NORM_EPS)
    return (y * w.astype(jnp.float32)).astype(x.dtype)


def l2norm(x):
    return x * lax.rsqrt(jnp.sum(x * x, axis=-1, keepdims=True) + NORM_EPS)


def to_heads(x, n_heads):
    b, t, _ = x.shape
    return x.reshape(b, t, n_heads, -1).transpose(0, 2, 1, 3)


def from_heads(x):
    return x.transpose(0, 2, 1, 3)


def to_chunks(x):
    b, h, t = x.shape[:3]
    return x.reshape(b, h, t // CHUNK, CHUNK, *x.shape[3:])


def centred_depthwise_conv(x, w):
    c = x.shape[-1]
    return lax.conv_general_dilated(
        x, w[:, None, :].astype(x.dtype), window_strides=(1,),
        padding=[(CONV_K // 2, CONV_K // 2)],
        dimension_numbers=("NWC", "WIO", "NWC"), feature_group_count=c)


def gated_delta_rule(q, k, v, g, beta):
    bsz, nh, t, dk = q.shape
    dv = v.shape[-1]
    q = to_chunks(q * (dk ** -0.5))
    k = to_chunks(k)
    v = to_chunks(v)
    beta = to_chunks(beta)
    g = jnp.cumsum(to_chunks(g), axis=-1)
    incl = jnp.tril(jnp.ones((CHUNK, CHUNK), dtype=bool))
    strict = jnp.tril(jnp.ones((CHUNK, CHUNK), dtype=bool), -1)
    diff = g[..., :, None] - g[..., None, :]
    decay = jnp.where(incl, jnp.exp(jnp.where(incl, diff, 0.0)), 0.0)
    kb = k * beta[..., None]
    lower = jnp.where(strict, jnp.einsum("bhnid,bhnjd->bhnij", kb, k) * decay, 0.0)
    rhs = jnp.concatenate([v * beta[..., None], kb * jnp.exp(g)[..., None]], axis=-1)
    sol = lax.linalg.triangular_solve(lower, rhs, left_side=True, lower=True,
                                      unit_diagonal=True)
    u, w = sol[..., :dv], sol[..., dv:]
    attn = jnp.einsum("bhnid,bhnjd->bhnij", q, k) * decay
    q_dec = q * jnp.exp(g)[..., None]
    g_last = g[..., -1]
    k_dec = k * jnp.exp(g_last[..., None] - g)[..., None]

    def step(S, inp):
        u_n, w_n, attn_n, qd_n, kd_n, gl_n = inp
        v_new = u_n - jnp.einsum("bhcd,bhde->bhce", w_n, S)
        o = (jnp.einsum("bhcd,bhde->bhce", qd_n, S)
             + jnp.einsum("bhij,bhje->bhie", attn_n, v_new))
        S = S * jnp.exp(gl_n)[..., None, None] + jnp.einsum("bhcd,bhce->bhde", kd_n, v_new)
        return S, o

    xs = tuple(jnp.moveaxis(a, 2, 0) for a in (u, w, attn, q_dec, k_dec, g_last))
    S0 = jnp.zeros((bsz, nh, dk, dv), q.dtype)
    _, o = lax.scan(step, S0, xs)
    return jnp.moveaxis(o, 0, 2).reshape(bsz, nh, t, dv)


def gla_chunked(q, k, v, gk):
    bsz, nh, t, dk = q.shape
    dv = v.shape[-1]
    q = to_chunks(q * (dk ** -0.5))
    k = to_chunks(k)
    v = to_chunks(v)
    G = jnp.cumsum(to_chunks(gk), axis=3)
    qg = q * jnp.exp(G)
    kg = k * jnp.exp(-G)
    incl = jnp.tril(jnp.ones((CHUNK, CHUNK), dtype=bool))
    attn = jnp.where(incl, jnp.einsum("bhnid,bhnjd->bhnij", qg, kg), 0.0)
    intra = jnp.einsum("bhnij,bhnje->bhnie", attn, v)
    G_last = G[..., -1, :]
    k_dec = k * jnp.exp(G_last[..., None, :] - G)

    def step(S, inp):
        qg_n, kd_n, v_n, gl_n = inp
        o = jnp.einsum("bhcd,bhde->bhce", qg_n, S)
        S = S * jnp.exp(gl_n)[..., :, None] + jnp.einsum("bhcd,bhce->bhde", kd_n, v_n)
        return S, o

    xs = tuple(jnp.moveaxis(a, 2, 0) for a in (qg, k_dec, v, G_last))
    S0 = jnp.zeros((bsz, nh, dk, dv), q.dtype)
    _, inter = lax.scan(step, S0, xs)
    o = jnp.moveaxis(inter, 0, 2) + intra
    return o.reshape(bsz, nh, t, dv)


def flip_t(a):
    return jnp.flip(a, axis=2)


def hybrid_layer(x, ln_pre_w, w_in, conv_w, a_log_fwd, a_log_bwd, dt_bias_fwd, dt_bias_bwd,
                 gdn_norm_w, w_proj_gdn, gk_w2_fwd, gk_b2_fwd, gk_w2_bwd, gk_b2_bwd,
                 gla_norm_w, w_proj_gla, w_out, ln_post_w):
    f32 = jnp.float32
    h = rmsnorm(x, ln_pre_w)
    proj = h @ w_in
    (qkv_a, z_a, a_f, a_b, b_f, b_b, q_b, k_b, v_b, g_b,
     r_f, r_b, gate_a, gate_b) = jnp.split(proj, IN_SPLITS, axis=-1)

    qkv_a = jax.nn.silu(centred_depthwise_conv(qkv_a, conv_w)).astype(f32)
    q_a, k_a, v_a = jnp.split(qkv_a, 3, axis=-1)
    q_a = l2norm(to_heads(q_a, GDN_HEADS))
    k_a = l2norm(to_heads(k_a, GDN_HEADS))
    v_a = to_heads(v_a, GDN_HEADS)
    lg_f = (-jnp.exp(a_log_fwd.astype(f32)) * jax.nn.softplus(a_f.astype(f32) + dt_bias_fwd.astype(f32))).transpose(0, 2, 1)
    lg_b = (-jnp.exp(a_log_bwd.astype(f32)) * jax.nn.softplus(a_b.astype(f32) + dt_bias_bwd.astype(f32))).transpose(0, 2, 1)
    beta_f = jax.nn.sigmoid(b_f.astype(f32)).transpose(0, 2, 1)
    beta_b = jax.nn.sigmoid(b_b.astype(f32)).transpose(0, 2, 1)
    o_a = (gated_delta_rule(q_a, k_a, v_a, lg_f, beta_f)
           + flip_t(gated_delta_rule(flip_t(q_a), flip_t(k_a), flip_t(v_a),
                                     flip_t(lg_b), flip_t(beta_b))))
    o_a = rmsnorm(from_heads(o_a), gdn_norm_w)
    o_a = o_a * jax.nn.silu(z_a.astype(f32)).reshape(o_a.shape)
    y_a = o_a.reshape(x.shape[0], x.shape[1], GDN_WIDTH).astype(x.dtype) @ w_proj_gdn

    q_bh = to_heads(q_b.astype(f32), GLA_HEADS)
    k_bh = to_heads(k_b.astype(f32), GLA_HEADS)
    v_bh = to_heads(v_b.astype(f32), GLA_HEADS)
    gk_f = jax.nn.log_sigmoid((r_f @ gk_w2_fwd + gk_b2_fwd).astype(f32)) / GLA_GATE_NORMALIZER
    gk_b = jax.nn.log_sigmoid((r_b @ gk_w2_bwd + gk_b2_bwd).astype(f32)) / GLA_GATE_NORMALIZER
    gk_f = to_heads(gk_f, GLA_HEADS)
    gk_b = to_heads(gk_b, GLA_HEADS)
    o_b = (gla_chunked(q_bh, k_bh, v_bh, gk_f)
           + flip_t(gla_chunked(flip_t(q_bh), flip_t(k_bh), flip_t(v_bh), flip_t(gk_b))))
    o_b = rmsnorm(from_heads(o_b), gla_norm_w)
    o_b = o_b * jax.nn.silu(g_b.astype(f32)).reshape(o_b.shape)
    y_b = o_b.reshape(x.shape[0], x.shape[1], GLA_VAL_DIM).astype(x.dtype) @ w_proj_gla

    merged = jax.nn.sigmoid(gate_a) * y_a + jax.nn.sigmoid(gate_b) * y_b
    out = merged @ w_out
    return x + rmsnorm(out, ln_post_w)


def setup_inputs(seed: int = 0) -> dict:
    key = jax.random.key(seed)
    ks = jax.random.split(key, 20)
    L, D = DEPTH, D_MODEL

    def nrm(k, shape, scale):
        return jax.random.normal(k, shape, jnp.float32) * scale

    def dt_bias(k):
        u = jax.random.uniform(k, (L, GDN_HEADS), jnp.float32)
        dt = jnp.exp(u * (math.log(0.1) - math.log(0.001)) + math.log(0.001))
        return dt + jnp.log(-jnp.expm1(-dt))

    def a_log(k):
        return jnp.log(jax.random.uniform(k, (L, GDN_HEADS), jnp.float32, 1.0, 16.0))

    return {
        "x": nrm(ks[0], (BATCH, SEQ, D), 1.0),
        "ln_pre_w": 1.0 + nrm(ks[1], (L, D), 0.02),
        "w_in": nrm(ks[2], (L, D, N_IN), D ** -0.5),
        "conv_w": nrm(ks[3], (L, CONV_K, 3 * GDN_WIDTH), CONV_K ** -0.5),
        "a_log_fwd": a_log(ks[4]),
        "a_log_bwd": a_log(ks[5]),
        "dt_bias_fwd": dt_bias(ks[6]),
        "dt_bias_bwd": dt_bias(ks[7]),
        "gdn_norm_w": 1.0 + nrm(ks[8], (L, GDN_HEAD_DIM), 0.02),
        "w_proj_gdn": nrm(ks[9], (L, GDN_WIDTH, D), GDN_WIDTH ** -0.5),
        "gk_w2_fwd": nrm(ks[10], (L, GLA_GATE_RANK, GLA_KEY_DIM), GLA_GATE_RANK ** -0.5),
        "gk_b2_fwd": nrm(ks[11], (L, GLA_KEY_DIM), 0.01),
        "gk_w2_bwd": nrm(ks[12], (L, GLA_GATE_RANK, GLA_KEY_DIM), GLA_GATE_RANK ** -0.5),
        "gk_b2_bwd": nrm(ks[13], (L, GLA_KEY_DIM), 0.01),
        "gla_norm_w": 1.0 + nrm(ks[14], (L, GLA_HEAD_V), 0.02),
        "w_proj_gla": nrm(ks[15], (L, GLA_VAL_DIM, D), GLA_VAL_DIM ** -0.5),
        "w_out": nrm(ks[16], (L, D, D), D ** -0.5),
        "ln_post_w": 1.0 + nrm(ks[17], (L, D), 0.02),
    }


def reference(x, ln_pre_w, w_in, conv_w, a_log_fwd, a_log_bwd, dt_bias_fwd, dt_bias_bwd,
              gdn_norm_w, w_proj_gdn, gk_w2_fwd, gk_b2_fwd, gk_w2_bwd, gk_b2_bwd,
              gla_norm_w, w_proj_gla, w_out, ln_post_w):
    h = x
    for l in range(DEPTH):
        h = hybrid_layer(h, ln_pre_w[l], w_in[l], conv_w[l], a_log_fwd[l], a_log_bwd[l],
                         dt_bias_fwd[l], dt_bias_bwd[l], gdn_norm_w[l], w_proj_gdn[l],
                         gk_w2_fwd[l], gk_b2_fwd[l], gk_w2_bwd[l], gk_b2_bwd[l],
                         gla_norm_w[l], w_proj_gla[l], w_out[l], ln_post_w[l])
    return h
```

```python
import numpy as np
import concourse.bass as bass
import concourse.mybir as mybir
from concourse.bass_utils import run_bass_kernel_spmd

F32 = mybir.dt.float32
BF16 = mybir.dt.bfloat16
AF = mybir.ActivationFunctionType
ALU = mybir.AluOpType

T = 2048
NT = 16
D = 1024
NIN = 9280
EPS = 1e-6
BIG = 30000.0
OFF_Q, OFF_K, OFF_V, OFF_Z = 0, 1024, 2048, 3072
OFF_SM = 4096
OFF_QB, OFF_KB, OFF_VB, OFF_GB = 4128, 4640, 5152, 6176
OFF_R = 7200
OFF_GA, OFF_GBG = 7232, 8256


class Ev:
    __slots__ = ("sem", "val", "key")

    def __init__(self, sem, val, key):
        self.sem, self.val, self.key = sem, val, key


class Buf:
    __slots__ = ("name", "wev", "revs")

    def __init__(self, name):
        self.name, self.wev, self.revs = name, None, {}


class Slot:
    registry = None

    def __init__(self, nc, name):
        if Slot.registry is not None:
            Slot.registry.append(self)
        self.name = name
        self.sem = nc.semaphore("ds_" + name).__enter__()
        self.cnt = 0


class Eng:
    def __init__(self, nc, name, h, selfsync):
        self.name, self.h, self.selfsync = name, h, selfsync
        self.sem = nc.semaphore("es_" + name).__enter__()
        self.cnt = 0
        self.seen = {}

    def wait(self, ev):
        if ev is None:
            return
        if ev.sem is self.sem and not self.selfsync:
            return
        if self.seen.get(ev.key, 0) >= ev.val:
            return
        self.h.wait_ge(ev.sem, ev.val)
        self.seen[ev.key] = ev.val

    def _deps(self, r, w):
        for b in r:
            self.wait(b.wev)
        for b in w:
            self.wait(b.wev)
            for ev in b.revs.values():
                self.wait(ev)

    def _mark(self, ev, r, w):
        for b in r:
            b.revs[ev.key] = ev
        for b in w:
            b.wev = ev
            b.revs = {}

    def op(self, fn, r=(), w=()):
        self._deps(r, w)
        ins = fn(self.h)
        self.cnt += 1
        ins.then_inc(self.sem, 1)
        ev = Ev(self.sem, self.cnt, self.name)
        self._mark(ev, r, w)
        return ev

    def dma(self, out, in_, slot, r=(), w=(), **kw):
        self._deps(r, w)
        ins = self.h.dma_start(out=out, in_=in_, **kw)
        slot.cnt += 16
        ins.then_inc(slot.sem, 16)
        ev = Ev(slot.sem, slot.cnt, slot.name)
        self._mark(ev, r, w)
        return ev


def build_nc(debug=(), gdn_heads=8, gla_heads=4, do_final=True, gla_stop=99):
    nc = bass.Bass("TRN2", target_bir_lowering=False)
    dram_in = lambda n, s: nc.dram_tensor(n, s, F32, kind="ExternalInput").ap()
    x_d = dram_in("x", [T, D])
    ln_pre_d = dram_in("ln_pre_w", [1, D])
    w_in_d = dram_in("w_in", [1, D, NIN])
    conv_d = dram_in("conv_w", [1, 5, 3072])
    alog_f_d = dram_in("a_log_fwd", [1, 8]); alog_b_d = dram_in("a_log_bwd", [1, 8])
    dtb_f_d = dram_in("dt_bias_fwd", [1, 8]); dtb_b_d = dram_in("dt_bias_bwd", [1, 8])
    gdn_nw_d = dram_in("gdn_norm_w", [1, 128])
    wpg_d = dram_in("w_proj_gdn", [1, D, D])
    gkw_f_d = dram_in("gk_w2_fwd", [1, 16, 512]); gkb_f_d = dram_in("gk_b2_fwd", [1, 512])
    gkw_b_d = dram_in("gk_w2_bwd", [1, 16, 512]); gkb_b_d = dram_in("gk_b2_bwd", [1, 512])
    gla_nw_d = dram_in("gla_norm_w", [1, 256])
    wpl_d = dram_in("w_proj_gla", [1, D, D])
    wout_d = dram_in("w_out", [1, D, D])
    ln_post_d = dram_in("ln_post_w", [1, D])
    out_d = nc.dram_tensor("out", [T, D], F32, kind="ExternalOutput").ap()
    dbg = {}
    for name, shape in debug:
        dbg[name] = nc.dram_tensor("dbg_" + name, shape, F32, kind="ExternalOutput").ap()

    pe = Eng(nc, "pe", nc.tensor, False)
    act = Eng(nc, "act", nc.scalar, True)
    dve = Eng(nc, "dve", nc.vector, True)
    pool = Eng(nc, "pool", nc.gpsimd, True)
    sp = Eng(nc, "sp", nc.sync, False)

    allocs = []

    def sb(name, shape, dt=F32):
        cm = nc.sbuf_tensor(name, shape, dt)
        t = cm.__enter__()
        allocs.append(cm)
        return t

    all_slots = []
    Slot.registry = all_slots

    def barrier():
        engs = (pe, act, dve, pool, sp)
        for e in engs:
            for f in engs:
                if f is not e and f.cnt:
                    e.wait(Ev(f.sem, f.cnt, f.name))
            for sl in all_slots:
                if sl.cnt:
                    e.wait(Ev(sl.sem, sl.cnt, sl.name))

    def free_to(mark):
        barrier()
        while len(allocs) > mark:
            allocs.pop().__exit__(None, None, None)

    NPS = 7
    ps = [nc.psum_tensor(f"ps{i}", [128, 512], F32).__enter__() for i in range(NPS)]
    psb = [Buf(f"ps{i}") for i in range(NPS)]
    pst = nc.psum_tensor("pst", [128, 1024], BF16).__enter__()
    pstb = Buf("pst")
    ps_rr = [0]

    def next_ps():
        i = ps_rr[0] % NPS
        ps_rr[0] += 1
        return ps[i], psb[i]

    cst = Buf("const")
    ident32 = sb("ident32", [128, 128]); ident_bf = sb("ident_bf", [128, 128], BF16)
    ones32 = sb("ones32", [128, 128]); ones_bf = sb("ones_bf", [128, 128], BF16)
    zeros32 = sb("zeros32", [128, 128])
    pool.op(lambda h: h.memset(ones32[:], 1.0), w=[cst])
    pool.op(lambda h: h.memset(zeros32[:], 0.0), w=[cst])
    pool.op(lambda h: h.affine_select(out=ident32[:], in_=zeros32[:], pattern=[[-1, 128]], compare_op=ALU.not_equal,
                                      fill=1.0, base=0, channel_multiplier=1), r=[cst], w=[cst])
    pool.op(lambda h: h.tensor_copy(out=ident_bf[:], in_=ident32[:]), r=[cst], w=[cst])
    pool.op(lambda h: h.tensor_copy(out=ones_bf[:], in_=ones32[:]), r=[cst], w=[cst])

    src_cache = {}

    def tri_const(name, n, inval, fillval, offval, step, cm, cmp):
        t = sb(name, [128, 128])
        pool.op(lambda h: h.memset(t[:], offval), w=[cst])
        if inval not in src_cache:
            src_cache[inval] = sb(f"src_{len(src_cache)}", [128, 128])
            pool.op(lambda h: h.memset(src_cache[inval][:], inval), w=[cst])
        src = src_cache[inval]
        for b0 in range(0, 128, n):
            pool.op(lambda h: h.affine_select(out=t[b0:b0 + n, b0:b0 + n], in_=src[b0:b0 + n, b0:b0 + n], pattern=[[step, n]],
                                              compare_op=cmp, fill=fillval, base=0, channel_multiplier=cm), r=[cst], w=[cst])
        return t

    TRI = {
        0: tri_const("tri_f", 64, 1.0, 0.0, 0.0, 1, -1, ALU.is_ge),
        1: tri_const("tri_b", 64, 1.0, 0.0, 0.0, -1, 1, ALU.is_ge),
    }
    NEG_A = {
        0: tri_const("nega_f", 64, 0.0, BIG, BIG, -1, 1, ALU.is_gt),
        1: tri_const("nega_b", 64, 0.0, BIG, BIG, 1, -1, ALU.is_gt),
    }
    NEG_B = {
        0: tri_const("negb_f", 64, 0.0, -BIG, -BIG, 1, -1, ALU.is_gt),
        1: tri_const("negb_b", 64, 0.0, -BIG, -BIG, -1, 1, ALU.is_gt),
    }
    NEG_C = {
        0: tri_const("negc_f", 64, 0.0, -BIG, -BIG, 1, -1, ALU.is_ge),
        1: tri_const("negc_b", 64, 0.0, -BIG, -BIG, -1, 1, ALU.is_ge),
    }
    BDONES = tri_const("bdones", 64, 1.0, 1.0, 0.0, 1, 1, ALU.is_ge)
    SEL0 = sb("sel0", [128, 128]); SEL1 = sb("sel1", [128, 128])
    pool.op(lambda h: h.memset(SEL0[:], 0.0), w=[cst]); pool.op(lambda h: h.memset(SEL1[:], 0.0), w=[cst])
    pool.op(lambda h: h.memset(SEL0[0:64, :], 1.0), w=[cst]); pool.op(lambda h: h.memset(SEL1[64:128, :], 1.0), w=[cst])
    GS = -1.0 / 16.0
    TRIG = {0: tri_const("trig_f", 128, GS, 0.0, 0.0, 1, -1, ALU.is_ge),
            1: tri_const("trig_b", 128, GS, 0.0, 0.0, -1, 1, ALU.is_ge)}
    DIFG = {0: tri_const("difg_f", 128, 0.0, GS, 0.0, 1, -1, ALU.is_ge),
            1: tri_const("difg_b", 128, 0.0, GS, 0.0, -1, 1, ALU.is_ge)}
    MASKG = {0: tri_const("maskg_f", 128, 1.0, 0.0, 0.0, 1, -1, ALU.is_ge),
             1: tri_const("maskg_b", 128, 1.0, 0.0, 0.0, -1, 1, ALU.is_ge)}

    prm = Buf("params")
    sl_prm = Slot(nc, "prm")
    lnw_T = sb("lnw_T", [128, 8])
    sp.dma(lnw_T[:], ln_pre_d[0, :].rearrange("(k p) -> p k", p=128), sl_prm, w=[prm], allow_slow_non_contiguous=True)
    cw = sb("cw", [128, 24, 5])
    for t_ in range(5):
        sp.dma(cw[:, :, t_], conv_d[0, t_, :].rearrange("(b p) -> p b", p=128), sl_prm, w=[prm], allow_slow_non_contiguous=True)
    prm16 = sb("prm16", [128, 2, 16])
    sp.dma(prm16[:, 0, 0:8], alog_f_d[0, :].partition_broadcast(128), sl_prm, w=[prm])
    sp.dma(prm16[:, 0, 8:16], alog_b_d[0, :].partition_broadcast(128), sl_prm, w=[prm])
    sp.dma(prm16[:, 1, 0:8], dtb_f_d[0, :].partition_broadcast(128), sl_prm, w=[prm])
    sp.dma(prm16[:, 1, 8:16], dtb_b_d[0, :].partition_broadcast(128), sl_prm, w=[prm])
    gdn_nw = sb("gdn_nw", [128, 1])
    sp.dma(gdn_nw[:], gdn_nw_d[0, :].rearrange("(p o) -> p o", o=1), sl_prm, w=[prm])
    gla_nw = sb("gla_nw", [128, 2])
    sp.dma(gla_nw[:], gla_nw_d[0, :].rearrange("(k p) -> p k", p=128), sl_prm, w=[prm], allow_slow_non_contiguous=True)
    w2cat = sb("w2cat", [33, 2, 512])
    pool.op(lambda h: h.memset(w2cat[0:32, :, :], 0.0), w=[prm])
    sp.dma(w2cat[0:16, 0, :], gkw_f_d[0, :, :], sl_prm, r=[prm], w=[prm])
    sp.dma(w2cat[16:32, 1, :], gkw_b_d[0, :, :], sl_prm, w=[prm])
    sp.dma(w2cat[32:33, 0, :], gkb_f_d[0:1, :], sl_prm, w=[prm])
    sp.dma(w2cat[32:33, 1, :], gkb_b_d[0:1, :], sl_prm, w=[prm])

    def dbg_out(name, src_ap, rbufs, dst=None):
        if name not in dbg:
            return
        d = dbg[name] if dst is None else dst
        sp.dma(d, src_ap, sl_dbg, r=rbufs)

    sl_dbg = Slot(nc, "dbg")
    sl_out = Slot(nc, "out")

    hT = sb("hT", [128, 8, T], BF16); hTb = Buf("hT")
    oaT = sb("oaT", [128, 8, T], BF16); oaTb = Buf("oaT")
    base_mark = len(allocs)

    xt = [sb(f"xt{i}", [128, D]) for i in range(2)]; xtb = [Buf(f"xt{i}") for i in range(2)]
    sl_x = [Slot(nc, f"x{i}") for i in range(2)]
    junk = sb("junk", [128, D], BF16); junkb = Buf("junk")
    xn = sb("xn", [128, D], BF16); xnb = Buf("xn")
    st0 = sb("st0", [128, 4]); st0b = Buf("st0")
    for tt in range(NT):
        s = tt % 2
        sp.dma(xt[s][:], x_d[tt * 128:(tt + 1) * 128, :], sl_x[s], w=[xtb[s]])
        act.op(lambda h: h.activation(out=junk[:], in_=xt[s][:], func=AF.Square, accum_out=st0[:, 0:1]), r=[xtb[s]], w=[junkb, st0b])
        act.op(lambda h: h.activation(out=st0[:, 1:2], in_=st0[:, 0:1], func=AF.Ln, bias=EPS, scale=1.0 / D), r=[st0b], w=[st0b])
        act.op(lambda h: h.activation(out=st0[:, 2:3], in_=st0[:, 1:2], func=AF.Exp, scale=-0.5), r=[st0b], w=[st0b])
        dve.op(lambda h: h.tensor_scalar(out=xn[:], in0=xt[s][:], scalar1=st0[:, 2:3], scalar2=None, op0=ALU.mult), r=[xtb[s], st0b], w=[xnb])
        for k in range(8):
            pe.op(lambda h: h.transpose(pst[:, k * 128:(k + 1) * 128], xn[:, k * 128:(k + 1) * 128], ident_bf[:]), r=[xnb, cst], w=[pstb])
        dve.op(lambda h: h.tensor_tensor(out=hT[:, :, tt * 128:(tt + 1) * 128], in0=pst[:].rearrange("p (k t) -> p k t", k=8),
                                         in1=lnw_T[:, :].unsqueeze(2).to_broadcast([128, 8, 128]), op=ALU.mult), r=[pstb, prm], w=[hTb])
    if "st0" in dbg:
        sp.dma(dbg["st0"], st0[:], sl_dbg, r=[st0b])
        tmpx = sb("dbg_tmpx", [128, D]); tbx = Buf("dbg_tmpx")
        act.op(lambda h: h.copy(out=tmpx[:], in_=xn[:]), r=[xnb], w=[tbx])
        sp.dma(dbg["xn"], tmpx[:], sl_dbg, r=[tbx])
    if "hT" in dbg:
        tmp = sb("dbg_tmp", [128, T])
        tb = Buf("dbg_tmp")
        for k in range(8):
            act.op(lambda h: h.copy(out=tmp[:], in_=hT[:, k, :]), r=[hTb], w=[tb])
            sp.dma(dbg["hT"][k * 128:(k + 1) * 128, :], tmp[:], sl_dbg, r=[tb])
    free_to(base_mark)

    NWS = 3
    wblk = [sb(f"wblk{i}", [128, 8, 128], BF16) for i in range(NWS)]
    wblkb = [Buf(f"wblk{i}") for i in range(NWS)]
    sl_w = [Slot(nc, f"w{i}") for i in range(NWS)]
    w_rr = [0]

    def load_wblk(src_ap):
        i = w_rr[0] % NWS
        w_rr[0] += 1
        pool.dma(wblk[i][:], src_ap.rearrange("(k p) c -> p k c", p=128), sl_w[i], w=[wblkb[i]])
        return wblk[i], wblkb[i]

    def proj_fm(wt, wb, tg, pst_, pb, rhsT=hT, rb=hTb):
        for k in range(8):
            pe.op(lambda h: h.matmul(pst_[:, :], lhsT=wt[:, k, :], rhs=rhsT[:, k, tg * 512:(tg + 1) * 512], start=(k == 0), stop=(k == 7)),
                  r=[wb, rb], w=[pb])

    mix_mark = len(allocs)

    NCOL = 16
    LG = sb("LG", [128, NT, NCOL]); LNB = sb("LNB", [128, NT, NCOL]); BETA = sb("BETA", [128, NT, NCOL])
    A_tok = sb("A_tok", [128, NT, NCOL]); NG_tok = sb("NG_tok", [128, NT, NCOL]); BG = sb("BG", [128, NT, NCOL])
    KD = sb("KD", [128, NT, NCOL]); ET0 = sb("ET0", [128, NT, NCOL]); ET1 = sb("ET1", [128, NT, NCOL])
    scal = Buf("scal")
    s1_mark = len(allocs)
    wsm = sb("wsm", [128, 8, 32], BF16); wsmb = Buf("wsm")
    sl_wsm = Slot(nc, "wsm")
    pool.dma(wsm[:], w_in_d[0, :, OFF_SM:OFF_SM + 32].rearrange("(k p) c -> p k c", p=128), sl_wsm, w=[wsmb])
    asm = sb("asm", [128, NT, 32]); asmb = Buf("asm")
    for tt in range(NT):
        p_, pb = next_ps()
        for k in range(8):
            pe.op(lambda h: h.matmul(p_[:, 0:32], lhsT=hT[:, k, tt * 128:(tt + 1) * 128], rhs=wsm[:, k, :], start=(k == 0), stop=(k == 7)),
                  r=[hTb, wsmb], w=[pb])
        act.op(lambda h: h.copy(out=asm[:, tt, :], in_=p_[:, 0:32]), r=[pb], w=[asmb])
    t16 = sb("t16", [128, NT, NCOL]); t16b = Buf("t16")
    dve.op(lambda h: h.tensor_tensor(out=t16[:], in0=asm[:, :, 0:16], in1=prm16[:, 1:2, :].to_broadcast([128, NT, NCOL]), op=ALU.add), r=[asmb, prm], w=[t16b])
    act.op(lambda h: h.activation(out=t16[:], in_=t16[:], func=AF.Exp), r=[t16b], w=[t16b])
    act.op(lambda h: h.activation(out=t16[:], in_=t16[:], func=AF.Ln, bias=1.0), r=[t16b], w=[t16b])
    nA = sb("nA", [128, 1, NCOL]); nAb = Buf("nA")
    act.op(lambda h: h.activation(out=nA[:], in_=prm16[:, 0:1, :], func=AF.Exp), r=[prm], w=[nAb])
    dve.op(lambda h: h.scalar_tensor_tensor(out=LG[:], in0=t16[:], scalar=-1.0, in1=nA[:].to_broadcast([128, NT, NCOL]), op0=ALU.mult, op1=ALU.mult),
           r=[t16b, nAb], w=[scal])
    act.op(lambda h: h.activation(out=t16[:], in_=asm[:, :, 16:32], func=AF.Exp, scale=-1.0), r=[asmb, scal], w=[t16b])
    act.op(lambda h: h.activation(out=t16[:], in_=t16[:], func=AF.Ln, bias=1.0), r=[t16b], w=[t16b])
    act.op(lambda h: h.mul(out=LNB[:], in_=t16[:], mul=-1.0), r=[t16b], w=[scal])
    act.op(lambda h: h.activation(out=BETA[:], in_=LNB[:], func=AF.Exp), r=[scal], w=[scal])
    G_tok = sb("G_tok", [128, NT, NCOL]); TOTO = sb("TOTO", [128, NT, NCOL])
    for tt in range(NT):
        p_, pb = next_ps()
        for i, M in enumerate([TRI[0], TRI[1], BDONES, SEL0, SEL1]):
            pe.op(lambda h: h.matmul(p_[:, i * 16:(i + 1) * 16], lhsT=M[:], rhs=LG[:, tt, :], start=True, stop=True), r=[cst, scal], w=[pb])
        act.op(lambda h: h.copy(out=G_tok[:, tt, 0:8], in_=p_[:, 0:8]), r=[pb], w=[scal])
        act.op(lambda h: h.copy(out=G_tok[:, tt, 8:16], in_=p_[:, 24:32]), r=[pb], w=[scal])
        act.op(lambda h: h.copy(out=TOTO[:, tt, :], in_=p_[:, 32:48]), r=[pb], w=[scal])
        act.op(lambda h: h.activation(out=ET0[:, tt, :], in_=p_[:, 48:64], func=AF.Exp), r=[pb], w=[scal])
        act.op(lambda h: h.activation(out=ET1[:, tt, :], in_=p_[:, 64:80], func=AF.Exp), r=[pb], w=[scal])
    dve.op(lambda h: h.tensor_tensor(out=A_tok[:], in0=G_tok[:], in1=LNB[:], op=ALU.add), r=[scal], w=[scal])
    act.op(lambda h: h.mul(out=NG_tok[:], in_=G_tok[:], mul=-1.0), r=[scal], w=[scal])
    act.op(lambda h: h.activation(out=BG[:], in_=A_tok[:], func=AF.Exp), r=[scal], w=[scal])
    dve.op(lambda h: h.tensor_tensor(out=KD[:], in0=TOTO[:], in1=G_tok[:], op=ALU.subtract), r=[scal], w=[scal])
    act.op(lambda h: h.activation(out=KD[:], in_=KD[:], func=AF.Exp), r=[scal], w=[scal])
    if "LG" in dbg:
        sp.dma(dbg["LG"].rearrange("(t p) c -> p t c", p=128), LG[:], sl_dbg, r=[scal])
        sp.dma(dbg["BETA"].rearrange("(t p) c -> p t c", p=128), BETA[:], sl_dbg, r=[scal])
        sp.dma(dbg["G_tok"].rearrange("(t p) c -> p t c", p=128), G_tok[:], sl_dbg, r=[scal])

    gdn_mark = len(allocs)
    if gdn_heads > 0:
        pre = sb("pre", [128, T + 4]); preb = Buf("pre")
        pool.op(lambda h: h.memset(pre[:, 0:2], 0.0), w=[preb]); pool.op(lambda h: h.memset(pre[:, T + 2:T + 4], 0.0), w=[preb])
        acc = sb("acc", [128, T]); accb = Buf("acc")
        sqt = sb("sqt", [128, 512], BF16); sqtb = Buf("sqt")
        rn = sb("rn", [128, 512]); rnb = Buf("rn")
        qT = sb("qT", [128, T], BF16); qTb = Buf("qT")
        kT = sb("kT", [128, T], BF16); kTb = Buf("kT")
        vT = sb("vT", [128, T], BF16); vTb = Buf("vT")
        zs = sb("zs", [128, T], BF16); zsb = Buf("zs")
        k_tok = sb("k_tok", [128, NT, 128], BF16); ktb = Buf("k_tok")
        v_tok = sb("v_tok", [128, NT, 128], BF16); vtb = Buf("v_tok")
        oTa = sb("oTa", [128, T]); oTab = Buf("oTa")
        WK = {}
        for d_ in range(2):
            W_ = {}
            for nm, shp, dt in [("rhs1", [128, 128], F32), ("rhsb", [128, 128], F32), ("D1", [128, 128], F32), ("D1T", [128, 128], F32),
                                ("D2T", [128, 128], F32), ("grep", [128, 128], F32),
                                ("LU", [128, 2, 256], BF16),
                                ("LU2", [128, 2, 256], BF16),
                                ("attnT", [128, 128], BF16), ("qdT", [128, 128], BF16), ("khat", [128, 128], BF16), ("bv", [128, 128], BF16),
                                ("kd", [128, 128], BF16), ("wT", [128, 128], BF16), ("u", [128, 128], F32), ("vnew", [128, 128], BF16),
                                ("S32", [128, 128], F32), ("S16", [128, 128], BF16)]:
                W_[nm] = sb(f"{nm}_{d_}", shp, dt)
                W_[nm + "_b"] = Buf(f"{nm}_{d_}")
            WK[d_] = W_

    for hh in range(gdn_heads):
        for which, off in (("q", OFF_Q), ("k", OFF_K), ("v", OFF_V)):
            wt, wb = load_wblk(w_in_d[0, :, off + hh * 128: off + (hh + 1) * 128])
            for tg in range(4):
                p_, pb = next_ps()
                proj_fm(wt, wb, tg, p_, pb)
                act.op(lambda h: h.copy(out=pre[:, 2 + tg * 512: 2 + (tg + 1) * 512], in_=p_[:, :]), r=[pb], w=[preb])
            blk = off // 128 + hh
            dve.op(lambda h: h.tensor_scalar(out=acc[:], in0=pre[:, 0:T], scalar1=cw[:, blk, 0:1], scalar2=None, op0=ALU.mult), r=[preb, prm], w=[accb])
            for t_ in range(1, 5):
                dve.op(lambda h: h.scalar_tensor_tensor(out=acc[:], in0=pre[:, t_:t_ + T], scalar=cw[:, blk, t_:t_ + 1], in1=acc[:], op0=ALU.mult, op1=ALU.add),
                       r=[preb, prm, accb], w=[accb])
            if which == "v":
                act.op(lambda h: h.activation(out=vT[:], in_=acc[:], func=AF.Silu), r=[accb], w=[vTb])
                continue
            act.op(lambda h: h.activation(out=acc[:], in_=acc[:], func=AF.Silu), r=[accb], w=[accb])
            dstT, dstb = (qT, qTb) if which == "q" else (kT, kTb)
            post = (128.0 ** -0.5) if which == "q" else 1.0
            for tg in range(4):
                sl_ = slice(tg * 512, (tg + 1) * 512)
                act.op(lambda h: h.activation(out=sqt[:], in_=acc[:, sl_], func=AF.Square), r=[accb], w=[sqtb])
                p_, pb = next_ps()
                pe.op(lambda h: h.matmul(p_[:, :], lhsT=ones_bf[:], rhs=sqt[:], start=True, stop=True), r=[cst, sqtb], w=[pb])
                act.op(lambda h: h.activation(out=rn[:], in_=p_[:, :], func=AF.Ln, bias=EPS), r=[pb], w=[rnb])
                act.op(lambda h: h.activation(out=rn[:], in_=rn[:], func=AF.Exp, scale=-0.5), r=[rnb], w=[rnb])
                dve.op(lambda h: h.scalar_tensor_tensor(out=dstT[:, sl_], in0=acc[:, sl_], scalar=post, in1=rn[:], op0=ALU.mult, op1=ALU.mult),
                       r=[accb, rnb], w=[dstb])
        wt, wb = load_wblk(w_in_d[0, :, OFF_Z + hh * 128: OFF_Z + (hh + 1) * 128])
        for tg in range(4):
            p_, pb = next_ps()
            proj_fm(wt, wb, tg, p_, pb)
            act.op(lambda h: h.activation(out=zs[:, tg * 512:(tg + 1) * 512], in_=p_[:, :], func=AF.Silu), r=[pb], w=[zsb])
        for srcT, srcb, dst, dstb in ((kT, kTb, k_tok, ktb), (vT, vTb, v_tok, vtb)):
            for g8 in range(2):
                for j in range(8):
                    tt = g8 * 8 + j
                    pe.op(lambda h: h.transpose(pst[:, j * 128:(j + 1) * 128], srcT[:, tt * 128:(tt + 1) * 128], ident_bf[:]), r=[srcb, cst], w=[pstb])
                act.op(lambda h: h.copy(out=dst[:, g8 * 8:(g8 + 1) * 8, :], in_=pst[:].rearrange("p (j c) -> p j c", j=8)), r=[pstb], w=[dstb])
        if hh == 0 and "qT" in dbg:
            tmp = sb("dbg_tmp2", [128, T]); tb = Buf("dbg_tmp2")
            for nm, src, sbf in (("qT", qT, qTb), ("kT", kT, kTb), ("vT", vT, vTb)):
                act.op(lambda h: h.copy(out=tmp[:], in_=src[:]), r=[sbf], w=[tb])
                sp.dma(dbg[nm], tmp[:], sl_dbg, r=[tb])

        for d_ in range(2):
            W_ = WK[d_]
            col = d_ * 8 + hh
            pool.op(lambda h: h.memset(W_["S32"][:], 0.0), w=[W_["S32_b"]])
            pool.op(lambda h: h.memset(W_["S16"][:], 0.0), w=[W_["S16_b"]])
            tiles = range(NT) if d_ == 0 else range(NT - 1, -1, -1)
            for tt in tiles:
                tsl = slice(tt * 128, (tt + 1) * 128)
                pool.op(lambda h: h.tensor_scalar(out=W_["rhs1"][:], in0=TRI[d_][:], scalar1=LG[:, tt, col:col + 1], scalar2=None, op0=ALU.mult),
                        r=[cst, scal], w=[W_["rhs1_b"]])
                pool.op(lambda h: h.tensor_scalar(out=W_["rhsb"][:], in0=ident32[:], scalar1=LNB[:, tt, col:col + 1], scalar2=None, op0=ALU.mult),
                        r=[cst, scal], w=[W_["rhsb_b"]])
                pkq, pkqb = next_ps()
                pe.op(lambda h: h.matmul(pkq[:, 0:128], lhsT=kT[:, tsl], rhs=kT[:, tsl], start=True, stop=True), r=[kTb], w=[pkqb])
                pe.op(lambda h: h.matmul(pkq[:, 128:256], lhsT=kT[:, tsl], rhs=qT[:, tsl], start=True, stop=True), r=[kTb, qTb], w=[pkqb])
                pa, pab = next_ps()
                pe.op(lambda h: h.matmul(pa[:, 0:128], lhsT=ones32[:], rhs=W_["rhs1"][:], start=True, stop=False), r=[cst, W_["rhs1_b"]], w=[pab])
                pe.op(lambda h: h.matmul(pa[:, 0:128], lhsT=ident32[:], rhs=NEG_A[d_][:], start=False, stop=True), r=[cst], w=[pab])
                pe.op(lambda h: h.matmul(pa[:, 128:256], lhsT=ones32[:], rhs=W_["rhs1"][:], start=True, stop=False), r=[cst, W_["rhs1_b"]], w=[pab])
                pe.op(lambda h: h.matmul(pa[:, 128:256], lhsT=ones32[:], rhs=W_["rhsb"][:], start=False, stop=False), r=[cst, W_["rhsb_b"]], w=[pab])
                pe.op(lambda h: h.matmul(pa[:, 128:256], lhsT=ident32[:], rhs=NEG_B[d_][:], start=False, stop=True), r=[cst], w=[pab])
                pe.op(lambda h: h.matmul(pa[:, 256:384], lhsT=ones32[:], rhs=W_["rhs1"][:], start=True, stop=False), r=[cst, W_["rhs1_b"]], w=[pab])
                pe.op(lambda h: h.matmul(pa[:, 256:384], lhsT=ident32[:], rhs=NEG_C[d_][:], start=False, stop=True), r=[cst], w=[pab])
                pe.op(lambda h: h.matmul(pa[:, 384:512], lhsT=ones32[:], rhs=W_["rhs1"][:], start=True, stop=True), r=[cst, W_["rhs1_b"]], w=[pab])
                act.op(lambda h: h.activation(out=W_["D1"][:], in_=pa[:, 0:128], func=AF.Exp, bias=A_tok[:, tt, col:col + 1], scale=-1.0),
                       r=[pab, scal], w=[W_["D1_b"]])
                act.op(lambda h: h.activation(out=W_["D1T"][:], in_=pa[:, 128:256], func=AF.Exp, bias=NG_tok[:, tt, col:col + 1], scale=1.0),
                       r=[pab, scal], w=[W_["D1T_b"]])
                act.op(lambda h: h.activation(out=W_["D2T"][:], in_=pa[:, 256:384], func=AF.Exp, bias=NG_tok[:, tt, col:col + 1], scale=1.0),
                       r=[pab, scal], w=[W_["D2T_b"]])
                act.op(lambda h: h.activation(out=W_["grep"][:], in_=pa[:, 384:512], func=AF.Exp), r=[pab], w=[W_["grep_b"]])
                LU, LUb, LU2, LU2b = W_["LU"], W_["LU_b"], W_["LU2"], W_["LU2_b"]
                dve.op(lambda h: h.tensor_tensor(out=LU[:, 1, 0:128], in0=pkq[:, 0:128], in1=W_["D1"][:], op=ALU.mult), r=[pkqb, W_["D1_b"]], w=[LUb])
                dve.op(lambda h: h.tensor_tensor(out=LU[:, 0, 0:128], in0=pkq[:, 0:128], in1=W_["D1T"][:], op=ALU.mult), r=[pkqb, W_["D1T_b"]], w=[LUb])
                dve.op(lambda h: h.tensor_tensor(out=W_["attnT"][:], in0=pkq[:, 128:256], in1=W_["D2T"][:], op=ALU.mult), r=[pkqb, W_["D2T_b"]], w=[W_["attnT_b"]])
                pool.op(lambda h: h.tensor_tensor(out=LU[:, 0, 128:256], in0=ident32[:], in1=LU[:, 0, 0:128], op=ALU.subtract), r=[cst, LUb], w=[LUb])
                dve.op(lambda h: h.tensor_tensor(out=W_["qdT"][:], in0=qT[:, tsl], in1=W_["grep"][:], op=ALU.mult), r=[qTb, W_["grep_b"]], w=[W_["qdT_b"]])
                pool.op(lambda h: h.tensor_scalar(out=W_["khat"][:], in0=k_tok[:, tt, :], scalar1=BG[:, tt, col:col + 1], scalar2=None, op0=ALU.mult),
                        r=[ktb, scal], w=[W_["khat_b"]])
                pool.op(lambda h: h.tensor_scalar(out=W_["bv"][:], in0=v_tok[:, tt, :], scalar1=BETA[:, tt, col:col + 1], scalar2=None, op0=ALU.mult),
                        r=[vtb, scal], w=[W_["bv_b"]])
                pool.op(lambda h: h.tensor_scalar(out=W_["kd"][:], in0=k_tok[:, tt, :], scalar1=KD[:, tt, col:col + 1], scalar2=None, op0=ALU.mult),
                        r=[ktb, scal], w=[W_["kd_b"]])
                cur, curb, nxt, nxtb = LU, LUb, LU2, LU2b
                for lev in range(6):
                    pn, pnb = next_ps()
                    if lev == 0:
                        pe.op(lambda h: h.matmul(pn[:, 0:128], lhsT=cur[:, 1, 0:128], rhs=cur[:, 0, 0:128], start=True, stop=True), r=[curb], w=[pnb])
                    elif lev < 5:
                        pe.op(lambda h: h.matmul(pn[:, 0:256], lhsT=cur[:, 1, 0:128], rhs=cur[:, 0, 0:256], start=True, stop=True), r=[curb], w=[pnb])
                    else:
                        pe.op(lambda h: h.matmul(pn[:, 128:256], lhsT=cur[:, 1, 0:128], rhs=cur[:, 0, 128:256], start=True, stop=True), r=[curb], w=[pnb])
                    if lev < 5:
                        pe.op(lambda h: h.matmul(pn[:, 256:384], lhsT=cur[:, 0, 0:128], rhs=cur[:, 1, 0:128], start=True, stop=True), r=[curb], w=[pnb])
                        act.op(lambda h: h.copy(out=nxt[:, 0, 0:128], in_=pn[:, 0:128]), r=[pnb], w=[nxtb])
                        act.op(lambda h: h.copy(out=nxt[:, 1, 0:128], in_=pn[:, 256:384]), r=[pnb], w=[nxtb])
                    if lev == 0:
                        pool.op(lambda h: h.tensor_copy(out=nxt[:, 0, 128:256], in_=cur[:, 0, 128:256]), r=[curb], w=[nxtb])
                    else:
                        dve.op(lambda h: h.tensor_tensor(out=nxt[:, 0, 128:256], in0=pn[:, 128:256], in1=cur[:, 0, 128:256], op=ALU.add), r=[pnb, curb], w=[nxtb])
                    cur, curb, nxt, nxtb = nxt, nxtb, cur, curb
                Wm = cur[:, 0, 128:256]; Wmb = curb
                pw, pwb = next_ps()
                pe.op(lambda h: h.matmul(pw[:, 0:128], lhsT=W_["khat"][:], rhs=Wm, start=True, stop=True), r=[W_["khat_b"], Wmb], w=[pwb])
                pe.op(lambda h: h.matmul(pw[:, 128:256], lhsT=Wm, rhs=W_["bv"][:], start=True, stop=True), r=[W_["bv_b"], Wmb], w=[pwb])
                act.op(lambda h: h.copy(out=W_["wT"][:], in_=pw[:, 0:128]), r=[pwb], w=[W_["wT_b"]])
                act.op(lambda h: h.copy(out=W_["u"][:], in_=pw[:, 128:256]), r=[pwb], w=[W_["u_b"]])
                for c in ((0, 1) if d_ == 0 else (1, 0)):
                    rs = slice(c * 64, (c + 1) * 64)
                    ETc = ET0 if c == 0 else ET1
                    p1, p1b = next_ps()
                    pe.op(lambda h: h.matmul(p1[rs, 0:128], lhsT=W_["wT"][:, rs], rhs=W_["S16"][:], start=True, stop=True), r=[W_["wT_b"], W_["S16_b"]], w=[p1b])
                    dve.op(lambda h: h.tensor_tensor(out=W_["vnew"][rs, :], in0=W_["u"][rs, :], in1=p1[rs, 0:128], op=ALU.subtract),
                           r=[W_["u_b"], p1b], w=[W_["vnew_b"]])
                    p2, p2b = next_ps()
                    pe.op(lambda h: h.matmul(p2[:, 0:64], lhsT=W_["S16"][:], rhs=W_["qdT"][:, rs], start=True, stop=False), r=[W_["S16_b"], W_["qdT_b"]], w=[p2b])
                    pe.op(lambda h: h.matmul(p2[:, 0:64], lhsT=W_["vnew"][rs, :], rhs=W_["attnT"][rs, rs], start=False, stop=True),
                          r=[W_["vnew_b"], W_["attnT_b"]], w=[p2b])
                    pe.op(lambda h: h.matmul(p2[:, 128:256], lhsT=W_["kd"][rs, :], rhs=W_["vnew"][rs, :], start=True, stop=True),
                          r=[W_["kd_b"], W_["vnew_b"]], w=[p2b])
                    osl = slice(tt * 128 + c * 64, tt * 128 + (c + 1) * 64)
                    if d_ == 0:
                        act.op(lambda h: h.copy(out=oTa[:, osl], in_=p2[:, 0:64]), r=[p2b], w=[oTab])
                    else:
                        dve.op(lambda h: h.tensor_tensor(out=oTa[:, osl], in0=p2[:, 0:64], in1=oTa[:, osl], op=ALU.add), r=[p2b, oTab], w=[oTab])
                    dve.op(lambda h: h.scalar_tensor_tensor(out=W_["S32"][:], in0=W_["S32"][:], scalar=ETc[:, tt, col:col + 1], in1=p2[:, 128:256],
                                                            op0=ALU.mult, op1=ALU.add), r=[W_["S32_b"], scal, p2b], w=[W_["S32_b"]])
                    act.op(lambda h: h.copy(out=W_["S16"][:], in_=W_["S32"][:]), r=[W_["S32_b"]], w=[W_["S16_b"]])
        if f"oa{hh}" in dbg:
            sp.dma(dbg[f"oa{hh}"], oTa[:], sl_dbg, r=[oTab])
        for tg in range(4):
            sl_ = slice(tg * 512, (tg + 1) * 512)
            act.op(lambda h: h.activation(out=sqt[:], in_=oTa[:, sl_], func=AF.Square), r=[oTab], w=[sqtb])
            p_, pb = next_ps()
            pe.op(lambda h: h.matmul(p_[:, :], lhsT=ones_bf[:], rhs=sqt[:], start=True, stop=True), r=[cst, sqtb], w=[pb])
            act.op(lambda h: h.activation(out=rn[:], in_=p_[:, :], func=AF.Ln, bias=EPS, scale=1.0 / 128), r=[pb], w=[rnb])
            act.op(lambda h: h.activation(out=rn[:], in_=rn[:], func=AF.Exp, scale=-0.5), r=[rnb], w=[rnb])
            dve.op(lambda h: h.scalar_tensor_tensor(out=rn[:], in0=oTa[:, sl_], scalar=gdn_nw[:, 0:1], in1=rn[:], op0=ALU.mult, op1=ALU.mult),
                   r=[oTab, rnb, prm], w=[rnb])
            dve.op(lambda h: h.tensor_tensor(out=oaT[:, hh, sl_], in0=rn[:], in1=zs[:, sl_], op=ALU.mult), r=[rnb, zsb], w=[oaTb])
    if "oaT" in dbg:
        tmp = sb("dbg_tmp3", [128, T]); tb = Buf("dbg_tmp3")
        for k in range(8):
            act.op(lambda h: h.copy(out=tmp[:], in_=oaT[:, k, :]), r=[oaTb], w=[tb])
            sp.dma(dbg["oaT"][k * 128:(k + 1) * 128, :], tmp[:], sl_dbg, r=[tb])
    free_to(s1_mark)

    obT = sb("obT", [128, 8, T], BF16); obTb = Buf("obT")
    gla_mark = len(allocs)
    if gla_heads > 0:
        GW = {}
        for d_ in range(2):
            W_ = {}
            for nm, shp, dt in [("ekd", [128, 128], F32), ("eg", [128, 128], F32), ("eng", [128, 128], F32), ("etot", [128, 1], F32),
                                ("kdg", [128, 128], BF16), ("qgT", [128, 128], BF16), ("kgT", [128, 128], BF16), ("attnT", [128, 128], BF16),
                                ("S32", [128, 256], F32), ("S16", [128, 256], BF16)]:
                W_[nm] = sb(f"g{nm}_{d_}", shp, dt)
                W_[nm + "_b"] = Buf(f"g{nm}_{d_}")
            GW[d_] = W_

        rT1 = sb("rT1", [33, T]); rT1b = Buf("rT1")
        wr = sb("wr", [128, 8, 32], BF16); wrb = Buf("wr")
        sl_wr = Slot(nc, "wr")
        pool.dma(wr[:], w_in_d[0, :, OFF_R:OFF_R + 32].rearrange("(k p) c -> p k c", p=128), sl_wr, w=[wrb])
        pool.op(lambda h: h.memset(rT1[32:33, :], 1.0), w=[rT1b])
        for tg in range(4):
            p_, pb = next_ps()
            for k in range(8):
                pe.op(lambda h: h.matmul(p_[0:32, :], lhsT=wr[:, k, :], rhs=hT[:, k, tg * 512:(tg + 1) * 512], start=(k == 0), stop=(k == 7)),
                      r=[wrb, hTb], w=[pb])
            act.op(lambda h: h.copy(out=rT1[0:32, tg * 512:(tg + 1) * 512], in_=p_[0:32, :]), r=[pb], w=[rT1b])
        wkv = sb("wkv", [128, 8, 384], BF16); wkvb = Buf("wkv"); sl_wkv = Slot(nc, "wkv")
        qTg = sb("qTg", [128, T], BF16); qTgb = Buf("qTg")
        kTg = sb("kTg", [128, T], BF16); kTgb = Buf("kTg")
        kg_tok = sb("kg_tok", [128, NT, 128], BF16); kgtb = Buf("kg_tok")
        vg_tok = sb("vg_tok", [128, NT, 256], BF16); vgtb = Buf("vg_tok")
        gsT = sb("gsT", [128, 2, T], BF16); gsTb = Buf("gsT")
        GKP1 = sb("GKP", [128, NT, 128]); GKP = [GKP1, GKP1]; GKP1b = Buf("GKP"); GKPb = [GKP1b, GKP1b]
        obTa = sb("obTa", [128, 2, T]); obTab = Buf("obTa")
        sqg = sb("sqg", [128, 512], BF16); sqgb = Buf("sqg")
        rng = sb("rng", [128, 512]); rngb = Buf("rng")
        tg1 = sb("tg1", [128, 512]); tg1b = Buf("tg1")
    for hb in range(gla_heads):
        if gla_stop < 1:
            break
        for off, dst, dstb, scl in ((OFF_QB, qTg, qTgb, 128.0 ** -0.5), (OFF_KB, kTg, kTgb, 1.0)):
            wt, wb = load_wblk(w_in_d[0, :, off + hb * 128: off + (hb + 1) * 128])
            for tg in range(4):
                p_, pb = next_ps()
                proj_fm(wt, wb, tg, p_, pb)
                act.op(lambda h: h.mul(out=dst[:, tg * 512:(tg + 1) * 512], in_=p_[:, :], mul=scl), r=[pb], w=[dstb])
        for eb in range(2):
            wt, wb = load_wblk(w_in_d[0, :, OFF_GB + hb * 256 + eb * 128: OFF_GB + hb * 256 + (eb + 1) * 128])
            for tg in range(4):
                p_, pb = next_ps()
                proj_fm(wt, wb, tg, p_, pb)
                act.op(lambda h: h.activation(out=gsT[:, eb, tg * 512:(tg + 1) * 512], in_=p_[:, :], func=AF.Silu), r=[pb], w=[gsTb])
        if gla_stop < 2:
            break
        pool.dma(wkv[:, :, 0:128], w_in_d[0, :, OFF_KB + hb * 128: OFF_KB + (hb + 1) * 128].rearrange("(k p) c -> p k c", p=128), sl_wkv, w=[wkvb])
        pool.dma(wkv[:, :, 128:384], w_in_d[0, :, OFF_VB + hb * 256: OFF_VB + (hb + 1) * 256].rearrange("(k p) c -> p k c", p=128), sl_wkv, w=[wkvb])
        for tt in range(NT):
            p_, pb = next_ps()
            for k in range(8):
                pe.op(lambda h: h.matmul(p_[:, 0:384], lhsT=hT[:, k, tt * 128:(tt + 1) * 128], rhs=wkv[:, k, :], start=(k == 0), stop=(k == 7)),
                      r=[hTb, wkvb], w=[pb])
            act.op(lambda h: h.copy(out=kg_tok[:, tt, :], in_=p_[:, 0:128]), r=[pb], w=[kgtb])
            act.op(lambda h: h.copy(out=vg_tok[:, tt, :], in_=p_[:, 128:384]), r=[pb], w=[vgtb])
        if gla_stop < 3:
            break
        for d_ in range(2):
            for tt in range(NT):
                p_, pb = next_ps()
                pe.op(lambda h: h.matmul(p_[:, 0:128], lhsT=rT1[:, tt * 128:(tt + 1) * 128], rhs=w2cat[:, d_, hb * 128:(hb + 1) * 128], start=True, stop=True),
                      r=[rT1b, prm], w=[pb])
                act.op(lambda h: h.activation(out=GKP[d_][:, tt, :], in_=p_[:, 0:128], func=AF.Exp, scale=-1.0), r=[pb], w=[GKPb[d_]])
            act.op(lambda h: h.activation(out=GKP[d_][:], in_=GKP[d_][:], func=AF.Ln, bias=1.0), r=[GKPb[d_]], w=[GKPb[d_]])
            W_ = GW[d_]
            pool.op(lambda h: h.memset(W_["S32"][:], 0.0), w=[W_["S32_b"]])
            pool.op(lambda h: h.memset(W_["S16"][:], 0.0), w=[W_["S16_b"]])
            tiles = range(NT) if d_ == 0 else range(NT - 1, -1, -1)
            if gla_stop < 4:
                tiles = []
            for tt in tiles:
                tsl = slice(tt * 128, (tt + 1) * 128)
                pg, pgb = next_ps()
                gk_t = GKP[d_][:, tt, :]
                pe.op(lambda h: h.matmul(pg[:, 0:128], lhsT=DIFG[d_][:], rhs=gk_t, start=True, stop=True), r=[cst, GKPb[d_]], w=[pgb])
                pe.op(lambda h: h.matmul(pg[:, 128:256], lhsT=gk_t, rhs=TRIG[d_][:], start=True, stop=True), r=[cst, GKPb[d_]], w=[pgb])
                act.op(lambda h: h.activation(out=W_["ekd"][:], in_=pg[:, 0:128], func=AF.Exp), r=[pgb], w=[W_["ekd_b"]])
                act.op(lambda h: h.activation(out=W_["eg"][:], in_=pg[:, 128:256], func=AF.Exp), r=[pgb], w=[W_["eg_b"]])
                act.op(lambda h: h.activation(out=W_["eng"][:], in_=pg[:, 128:256], func=AF.Exp, scale=-1.0), r=[pgb], w=[W_["eng_b"]])
                lastc = 128 + (127 if d_ == 0 else 0)
                act.op(lambda h: h.activation(out=W_["etot"][:], in_=pg[:, lastc:lastc + 1], func=AF.Exp), r=[pgb], w=[W_["etot_b"]])
                if gla_stop < 5:
                    continue
                dve.op(lambda h: h.tensor_tensor(out=W_["kdg"][:], in0=kg_tok[:, tt, :], in1=W_["ekd"][:], op=ALU.mult), r=[kgtb, W_["ekd_b"]], w=[W_["kdg_b"]])
                dve.op(lambda h: h.tensor_tensor(out=W_["qgT"][:], in0=qTg[:, tsl], in1=W_["eg"][:], op=ALU.mult), r=[qTgb, W_["eg_b"]], w=[W_["qgT_b"]])
                pool.op(lambda h: h.tensor_tensor(out=W_["kgT"][:], in0=kTg[:, tsl], in1=W_["eng"][:], op=ALU.mult), r=[kTgb, W_["eng_b"]], w=[W_["kgT_b"]])
                if gla_stop < 6:
                    continue
                pa_, pab_ = next_ps()
                pe.op(lambda h: h.matmul(pa_[:, 0:128], lhsT=W_["kgT"][:], rhs=W_["qgT"][:], start=True, stop=True), r=[W_["kgT_b"], W_["qgT_b"]], w=[pab_])
                dve.op(lambda h: h.tensor_tensor(out=W_["attnT"][:], in0=pa_[:, 0:128], in1=MASKG[d_][:], op=ALU.mult), r=[pab_, cst], w=[W_["attnT_b"]])
                if gla_stop < 7:
                    continue
                po, pob = next_ps()
                for eb in range(2):
                    pe.op(lambda h: h.matmul(po[:, eb * 128:(eb + 1) * 128], lhsT=W_["S16"][:, eb * 128:(eb + 1) * 128], rhs=W_["qgT"][:], start=True, stop=False),
                          r=[W_["S16_b"], W_["qgT_b"]], w=[pob])
                    pe.op(lambda h: h.matmul(po[:, eb * 128:(eb + 1) * 128], lhsT=vg_tok[:, tt, eb * 128:(eb + 1) * 128], rhs=W_["attnT"][:], start=False, stop=True),
                          r=[vgtb, W_["attnT_b"]], w=[pob])
                pS, pSb = next_ps()
                pe.op(lambda h: h.matmul(pS[:, 0:256], lhsT=W_["kdg"][:], rhs=vg_tok[:, tt, :], start=True, stop=True), r=[W_["kdg_b"], vgtb], w=[pSb])
                if gla_stop < 8:
                    continue
                if d_ == 0:
                    act.op(lambda h: h.copy(out=obTa[:, :, tsl], in_=po[:, 0:256].rearrange("p (e t) -> p e t", e=2)), r=[pob], w=[obTab])
                else:
                    dve.op(lambda h: h.tensor_tensor(out=obTa[:, :, tsl], in0=po[:, 0:256].rearrange("p (e t) -> p e t", e=2), in1=obTa[:, :, tsl], op=ALU.add),
                           r=[pob, obTab], w=[obTab])
                if gla_stop < 9:
                    continue
                dve.op(lambda h: h.scalar_tensor_tensor(out=W_["S32"][:], in0=W_["S32"][:], scalar=W_["etot"][:, 0:1], in1=pS[:, 0:256],
                                                        op0=ALU.mult, op1=ALU.add), r=[W_["S32_b"], W_["etot_b"], pSb], w=[W_["S32_b"]])
                if gla_stop < 10:
                    continue
                act.op(lambda h: h.copy(out=W_["S16"][:], in_=W_["S32"][:]), r=[W_["S32_b"]], w=[W_["S16_b"]])
        if f"ob{hb}" in dbg:
            for eb in range(2):
                sp.dma(dbg[f"ob{hb}"][eb * 128:(eb + 1) * 128, :], obTa[:, eb, :], sl_dbg, r=[obTab])
        for tg in range(4):
            sl_ = slice(tg * 512, (tg + 1) * 512)
            p_, pb = next_ps()
            for eb in range(2):
                act.op(lambda h: h.activation(out=sqg[:], in_=obTa[:, eb, sl_], func=AF.Square), r=[obTab], w=[sqgb])
                pe.op(lambda h: h.matmul(p_[:, :], lhsT=ones_bf[:], rhs=sqg[:], start=(eb == 0), stop=(eb == 1)), r=[cst, sqgb], w=[pb])
            act.op(lambda h: h.activation(out=rng[:], in_=p_[:, :], func=AF.Ln, bias=EPS, scale=1.0 / 256), r=[pb], w=[rngb])
            act.op(lambda h: h.activation(out=rng[:], in_=rng[:], func=AF.Exp, scale=-0.5), r=[rngb], w=[rngb])
            for eb in range(2):
                dve.op(lambda h: h.scalar_tensor_tensor(out=tg1[:], in0=obTa[:, eb, sl_], scalar=gla_nw[:, eb:eb + 1], in1=rng[:], op0=ALU.mult, op1=ALU.mult),
                       r=[obTab, rngb, prm], w=[tg1b])
                dve.op(lambda h: h.tensor_tensor(out=obT[:, hb * 2 + eb, sl_], in0=tg1[:], in1=gsT[:, eb, sl_], op=ALU.mult), r=[tg1b, gsTb], w=[obTb])
    if "obT" in dbg:
        tmp = sb("dbg_tmp4", [128, T]); tb = Buf("dbg_tmp4")
        for k in range(8):
            act.op(lambda h: h.copy(out=tmp[:], in_=obT[:, k, :]), r=[obTb], w=[tb])
            sp.dma(dbg["obT"][k * 128:(k + 1) * 128, :], tmp[:], sl_dbg, r=[tb])
    free_to(gla_mark)

    if do_final:
        mT = sb("mT", [128, 8, T], BF16); mTb = Buf("mT")
        sga = sb("sga", [128, 512]); sgab = Buf("sga")
        sgb = sb("sgb", [128, 512]); sgbb = Buf("sgb")
        t1 = sb("t1", [128, 512]); t1b = Buf("t1")
        t2 = sb("t2", [128, 512]); t2b = Buf("t2")
        NW2 = 8
        wb2 = [sb(f"wb2_{i}", [128, 8, 128], BF16) for i in range(NW2)]; wb2b = [Buf(f"wb2_{i}") for i in range(NW2)]
        sl_w2 = [Slot(nc, f"w2_{i}") for i in range(NW2)]
        rr2 = [0]

        def load2(src_ap):
            i = rr2[0] % NW2
            rr2[0] += 1
            pool.dma(wb2[i][:], src_ap.rearrange("(k p) c -> p k c", p=128), sl_w2[i], w=[wb2b[i]])
            return wb2[i], wb2b[i]

        for m in range(8):
            msl = slice(m * 128, (m + 1) * 128)
            wg, wgb_ = load2(wpg_d[0, :, msl])
            wl, wlb_ = load2(wpl_d[0, :, msl])
            wa, wab_ = load2(w_in_d[0, :, OFF_GA + m * 128: OFF_GA + (m + 1) * 128])
            wbb, wbbb_ = load2(w_in_d[0, :, OFF_GBG + m * 128: OFF_GBG + (m + 1) * 128])
            for tg in range(4):
                sl_ = slice(tg * 512, (tg + 1) * 512)
                pga, pgab = next_ps(); proj_fm(wa, wab_, tg, pga, pgab)
                act.op(lambda h: h.activation(out=sga[:], in_=pga[:, :], func=AF.Sigmoid), r=[pgab], w=[sgab])
                pgb_, pgbb = next_ps(); proj_fm(wbb, wbbb_, tg, pgb_, pgbb)
                act.op(lambda h: h.activation(out=sgb[:], in_=pgb_[:, :], func=AF.Sigmoid), r=[pgbb], w=[sgbb])
                pya, pyab = next_ps(); proj_fm(wg, wgb_, tg, pya, pyab, rhsT=oaT, rb=oaTb)
                dve.op(lambda h: h.tensor_tensor(out=t1[:], in0=pya[:, :], in1=sga[:], op=ALU.mult), r=[pyab, sgab], w=[t1b])
                pyb, pybb = next_ps(); proj_fm(wl, wlb_, tg, pyb, pybb, rhsT=obT, rb=obTb)
                dve.op(lambda h: h.tensor_tensor(out=t2[:], in0=pyb[:, :], in1=sgb[:], op=ALU.mult), r=[pybb, sgbb], w=[t2b])
                pool.op(lambda h: h.tensor_tensor(out=mT[:, m, sl_], in0=t1[:], in1=t2[:], op=ALU.add), r=[t1b, t2b], w=[mTb])
        if "mT" in dbg:
            tmp = sb("dbg_tmp5", [128, T]); tb = Buf("dbg_tmp5")
            for k in range(8):
                act.op(lambda h: h.copy(out=tmp[:], in_=mT[:, k, :]), r=[mTb], w=[tb])
                sp.dma(dbg["mT"][k * 128:(k + 1) * 128, :], tmp[:], sl_dbg, r=[tb])
        wob = hTb; sl_wo = Slot(nc, "wo")
        for k in range(8):
            pool.dma(hT[:, k, 0:D], wout_d[0, k * 128:(k + 1) * 128, :], sl_wo, w=[wob])
        lnp = sb("lnp", [128, D]); sl_lnp = Slot(nc, "lnp"); lnpb = Buf("lnp")
        sp.dma(lnp[:], ln_post_d[0, :].partition_broadcast(128), sl_lnp, w=[lnpb])
        xr = [sb(f"xr{i}", [128, D]) for i in range(2)]; xrb = [Buf(f"xr{i}") for i in range(2)]
        sl_xr = [Slot(nc, f"xr{i}") for i in range(2)]
        ot = [sb(f"ot{i}", [128, D]) for i in range(2)]; otb = [Buf(f"ot{i}") for i in range(2)]
        st1 = sb("st1", [128, 8]); st1b = Buf("st1")
        junk2 = sb("junk2", [128, 512], BF16); junk2b = Buf("junk2")
        for tt in range(NT):
            s = tt % 2
            tsl = slice(tt * 128, (tt + 1) * 128)
            sp.dma(xr[s][:], x_d[tsl, :], sl_xr[s], w=[xrb[s]])
            pp = []
            for half in range(2):
                p_, pb = next_ps()
                for m in range(8):
                    pe.op(lambda h: h.matmul(p_[:, :], lhsT=mT[:, m, tsl], rhs=hT[:, m, half * 512:(half + 1) * 512], start=(m == 0), stop=(m == 7)),
                          r=[mTb, wob], w=[pb])
                act.op(lambda h: h.activation(out=junk2[:], in_=p_[:, :], func=AF.Square, accum_out=st1[:, half:half + 1]), r=[pb], w=[junk2b, st1b])
                pp.append((p_, pb))
            dve.op(lambda h: h.tensor_tensor(out=st1[:, 2:3], in0=st1[:, 0:1], in1=st1[:, 1:2], op=ALU.add), r=[st1b], w=[st1b])
            act.op(lambda h: h.activation(out=st1[:, 3:4], in_=st1[:, 2:3], func=AF.Ln, bias=EPS, scale=1.0 / D), r=[st1b], w=[st1b])
            act.op(lambda h: h.activation(out=st1[:, 4:5], in_=st1[:, 3:4], func=AF.Exp, scale=-0.5), r=[st1b], w=[st1b])
            for half in range(2):
                p_, pb = pp[half]
                hs = slice(half * 512, (half + 1) * 512)
                dve.op(lambda h: h.scalar_tensor_tensor(out=ot[s][:, hs], in0=p_[:, :], scalar=st1[:, 4:5], in1=lnp[:, hs], op0=ALU.mult, op1=ALU.mult),
                       r=[pb, st1b, lnpb], w=[otb[s]])
                pool.op(lambda h: h.tensor_tensor(out=ot[s][:, hs], in0=ot[s][:, hs], in1=xr[s][:, hs], op=ALU.add), r=[otb[s], xrb[s]], w=[otb[s]])
            sp.dma(out_d[tsl, :], ot[s][:], sl_out, r=[otb[s]])
    else:
        zt = sb("zt", [128, D]); ztb = Buf("zt")
        pool.op(lambda h: h.memset(zt[:], 0.0), w=[ztb])
        for tt in range(NT):
            sp.dma(out_d[tt * 128:(tt + 1) * 128, :], zt[:], sl_out, r=[ztb])
    sp.h.wait_ge(sl_out.sem, sl_out.cnt)
    if sl_dbg.cnt:
        sp.h.wait_ge(sl_dbg.sem, sl_dbg.cnt)
    return nc


_INPUT_NAMES = ["ln_pre_w", "w_in", "conv_w", "a_log_fwd", "a_log_bwd", "dt_bias_fwd", "dt_bias_bwd", "gdn_norm_w", "w_proj_gdn",
                "gk_w2_fwd", "gk_b2_fwd", "gk_w2_bwd", "gk_b2_bwd", "gla_norm_w", "w_proj_gla", "w_out", "ln_post_w"]


def kernel(**inputs):
    x = np.ascontiguousarray(np.asarray(inputs["x"], dtype=np.float32))
    shared = {n: np.ascontiguousarray(np.asarray(inputs[n], dtype=np.float32)) for n in _INPUT_NAMES}
    nc = build_nc()
    in_maps = [dict(shared, x=x[b]) for b in range(8)]
    res = run_bass_kernel_spmd(nc, in_maps, core_ids=list(range(8)))
    return np.stack([np.asarray(r["out"], dtype=np.float32) for r in res.results], axis=0)
```

```python
import numpy as np
import concourse.bass as bass
import concourse.mybir as mybir
from concourse.bass_utils import run_bass_kernel_spmd

F32 = mybir.dt.float32
BF16 = mybir.dt.bfloat16
AF = mybir.ActivationFunctionType
ALU = mybir.AluOpType

T = 2048
NT = 16
D = 1024
NIN = 9280
EPS = 1e-6
BIG = 30000.0
MAXACT = 6
OFF_Q, OFF_K, OFF_V, OFF_Z = 0, 1024, 2048, 3072
OFF_SM = 4096
OFF_QB, OFF_KB, OFF_VB, OFF_GB = 4128, 4640, 5152, 6176
OFF_R = 7200
OFF_GA, OFF_GBG = 7232, 8256


class Ev:
    __slots__ = ("sem", "val", "key")

    def __init__(self, sem, val, key):
        self.sem, self.val, self.key = sem, val, key


class Buf:
    __slots__ = ("name", "wev", "revs")

    def __init__(self, name):
        self.name, self.wev, self.revs = name, None, {}


class Slot:
    registry = None

    def __init__(self, nc, name):
        if Slot.registry is not None:
            Slot.registry.append(self)
        self.name = name
        self.sem = nc.semaphore("ds_" + name).__enter__()
        self.cnt = 0


class Eng:
    def __init__(self, nc, name, h, selfsync):
        self.name, self.h, self.selfsync = name, h, selfsync
        self.sem = nc.semaphore("es_" + name).__enter__()
        self.cnt = 0
        self.seen = {}

    def wait(self, ev):
        if ev is None:
            return
        if ev.sem is self.sem and not self.selfsync:
            return
        if self.seen.get(ev.key, 0) >= ev.val:
            return
        self.h.wait_ge(ev.sem, ev.val)
        self.seen[ev.key] = ev.val

    def _deps(self, r, w):
        for b in r:
            self.wait(b.wev)
        for b in w:
            self.wait(b.wev)
            for ev in b.revs.values():
                self.wait(ev)

    def _mark(self, ev, r, w):
        for b in r:
            b.revs[ev.key] = ev
        for b in w:
            b.wev = ev
            b.revs = {}

    def op(self, fn, r=(), w=()):
        self._deps(r, w)
        ins = fn(self.h)
        self.cnt += 1
        ins.then_inc(self.sem, 1)
        ev = Ev(self.sem, self.cnt, self.name)
        self._mark(ev, r, w)
        return ev

    def dma(self, out, in_, slot, r=(), w=(), **kw):
        self._deps(r, w)
        ins = self.h.dma_start(out=out, in_=in_, **kw)
        slot.cnt += 16
        ins.then_inc(slot.sem, 16)
        ev = Ev(slot.sem, slot.cnt, slot.name)
        self._mark(ev, r, w)
        return ev


def build_nc(debug=(), gdn_heads=8, gla_heads=4, do_final=True, gla_stop=99):
    nc = bass.Bass("TRN2", target_bir_lowering=False)
    dram_in = lambda n, s: nc.dram_tensor(n, s, F32, kind="ExternalInput").ap()
    x_d = dram_in("x", [T, D])
    ln_pre_d = dram_in("ln_pre_w", [1, D])
    w_in_d = dram_in("w_in", [1, D, NIN])
    conv_d = dram_in("conv_w", [1, 5, 3072])
    alog_f_d = dram_in("a_log_fwd", [1, 8]); alog_b_d = dram_in("a_log_bwd", [1, 8])
    dtb_f_d = dram_in("dt_bias_fwd", [1, 8]); dtb_b_d = dram_in("dt_bias_bwd", [1, 8])
    gdn_nw_d = dram_in("gdn_norm_w", [1, 128])
    wpg_d = dram_in("w_proj_gdn", [1, D, D])
    gkw_f_d = dram_in("gk_w2_fwd", [1, 16, 512]); gkb_f_d = dram_in("gk_b2_fwd", [1, 512])
    gkw_b_d = dram_in("gk_w2_bwd", [1, 16, 512]); gkb_b_d = dram_in("gk_b2_bwd", [1, 512])
    gla_nw_d = dram_in("gla_norm_w", [1, 256])
    wpl_d = dram_in("w_proj_gla", [1, D, D])
    wout_d = dram_in("w_out", [1, D, D])
    ln_post_d = dram_in("ln_post_w", [1, D])
    out_d = nc.dram_tensor("out", [T, D], F32, kind="ExternalOutput").ap()
    dbg = {}
    for name, shape in debug:
        dbg[name] = nc.dram_tensor("dbg_" + name, shape, F32, kind="ExternalOutput").ap()

    pe = Eng(nc, "pe", nc.tensor, False)
    act = Eng(nc, "act", nc.scalar, True)
    dve = Eng(nc, "dve", nc.vector, True)
    pool = Eng(nc, "pool", nc.gpsimd, True)
    sp = Eng(nc, "sp", nc.sync, False)

    allocs = []

    def sb(name, shape, dt=F32):
        cm = nc.sbuf_tensor(name, shape, dt)
        t = cm.__enter__()
        allocs.append(cm)
        return t

    all_slots = []
    Slot.registry = all_slots

    def barrier():
        engs = (pe, act, dve, pool, sp)
        for e in engs:
            for f in engs:
                if f is not e and f.cnt:
                    e.wait(Ev(f.sem, f.cnt, f.name))
            for sl in all_slots:
                if sl.cnt:
                    e.wait(Ev(sl.sem, sl.cnt, sl.name))

    def free_to(mark):
        barrier()
        while len(allocs) > mark:
            allocs.pop().__exit__(None, None, None)

    NPS = 7
    ps = [nc.psum_tensor(f"ps{i}", [128, 512], F32).__enter__() for i in range(NPS)]
    psb = [Buf(f"ps{i}") for i in range(NPS)]
    pst = nc.psum_tensor("pst", [128, 1024], BF16).__enter__()
    pstb = Buf("pst")
    ps_rr = {}
    PS_POOLS = {"any": list(range(NPS)), "prep": [0, 1, 2, 3], "scan": [4, 5, 6]}

    def next_ps(pool_="any"):
        lst = PS_POOLS[pool_]
        c = ps_rr.get(pool_, 0)
        ps_rr[pool_] = c + 1
        i = lst[c % len(lst)]
        return ps[i], psb[i]

    ps_busy = [False] * NPS

    def acquire(pool_):
        while True:
            for i in PS_POOLS[pool_]:
                if not ps_busy[i]:
                    ps_busy[i] = True
                    return i
            yield

    def release(i):
        ps_busy[i] = False

    class Task:
        def __init__(self, gen, deps=()):
            self.gen, self.deps, self.done = gen, [d for d in deps if d is not None], False

    def run_tasks(tasks, max_active=6):
        pending = list(tasks)
        active = []
        while pending or active:
            for t in list(pending):
                if len(active) >= max_active:
                    break
                if all(d.done for d in t.deps):
                    pending.remove(t)
                    active.append(t)
            assert active, "task deadlock"
            for t in list(active):
                try:
                    next(t.gen)
                except StopIteration:
                    t.done = True
                    active.remove(t)

    cst = Buf("const")
    ident32 = sb("ident32", [128, 128]); ident_bf = sb("ident_bf", [128, 128], BF16)
    ones32 = sb("ones32", [128, 128]); ones_bf = sb("ones_bf", [128, 128], BF16)
    zeros32 = sb("zeros32", [128, 128])
    pool.op(lambda h: h.memset(ones32[:], 1.0), w=[cst])
    pool.op(lambda h: h.memset(zeros32[:], 0.0), w=[cst])
    pool.op(lambda h: h.affine_select(out=ident32[:], in_=zeros32[:], pattern=[[-1, 128]], compare_op=ALU.not_equal,
                                      fill=1.0, base=0, channel_multiplier=1), r=[cst], w=[cst])
    pool.op(lambda h: h.tensor_copy(out=ident_bf[:], in_=ident32[:]), r=[cst], w=[cst])
    pool.op(lambda h: h.tensor_copy(out=ones_bf[:], in_=ones32[:]), r=[cst], w=[cst])

    src_cache = {}

    def tri_const(name, n, inval, fillval, offval, step, cm, cmp):
        t = sb(name, [128, 128])
        pool.op(lambda h: h.memset(t[:], offval), w=[cst])
        if inval not in src_cache:
            src_cache[inval] = sb(f"src_{len(src_cache)}", [128, 128])
            pool.op(lambda h: h.memset(src_cache[inval][:], inval), w=[cst])
        src = src_cache[inval]
        for b0 in range(0, 128, n):
            pool.op(lambda h: h.affine_select(out=t[b0:b0 + n, b0:b0 + n], in_=src[b0:b0 + n, b0:b0 + n], pattern=[[step, n]],
                                              compare_op=cmp, fill=fillval, base=0, channel_multiplier=cm), r=[cst], w=[cst])
        return t

    TRI = {
        0: tri_const("tri_f", 64, 1.0, 0.0, 0.0, 1, -1, ALU.is_ge),
        1: tri_const("tri_b", 64, 1.0, 0.0, 0.0, -1, 1, ALU.is_ge),
    }
    NEG_A = {
        0: tri_const("nega_f", 64, 0.0, BIG, BIG, -1, 1, ALU.is_gt),
        1: tri_const("nega_b", 64, 0.0, BIG, BIG, 1, -1, ALU.is_gt),
    }
    NEG_B = {
        0: tri_const("negb_f", 64, 0.0, -BIG, -BIG, 1, -1, ALU.is_gt),
        1: tri_const("negb_b", 64, 0.0, -BIG, -BIG, -1, 1, ALU.is_gt),
    }
    NEG_C = {
        0: tri_const("negc_f", 64, 0.0, -BIG, -BIG, 1, -1, ALU.is_ge),
        1: tri_const("negc_b", 64, 0.0, -BIG, -BIG, -1, 1, ALU.is_ge),
    }
    BDONES = tri_const("bdones", 64, 1.0, 1.0, 0.0, 1, 1, ALU.is_ge)
    SEL0 = sb("sel0", [128, 128]); SEL1 = sb("sel1", [128, 128])
    pool.op(lambda h: h.memset(SEL0[:], 0.0), w=[cst]); pool.op(lambda h: h.memset(SEL1[:], 0.0), w=[cst])
    pool.op(lambda h: h.memset(SEL0[0:64, :], 1.0), w=[cst]); pool.op(lambda h: h.memset(SEL1[64:128, :], 1.0), w=[cst])
    GS = -1.0 / 16.0
    TRIG = {0: tri_const("trig_f", 128, GS, 0.0, 0.0, 1, -1, ALU.is_ge),
            1: tri_const("trig_b", 128, GS, 0.0, 0.0, -1, 1, ALU.is_ge)}
    DIFG = {0: tri_const("difg_f", 128, 0.0, GS, 0.0, 1, -1, ALU.is_ge),
            1: tri_const("difg_b", 128, 0.0, GS, 0.0, -1, 1, ALU.is_ge)}
    MASKG = {0: tri_const("maskg_f", 128, 1.0, 0.0, 0.0, 1, -1, ALU.is_ge),
             1: tri_const("maskg_b", 128, 1.0, 0.0, 0.0, -1, 1, ALU.is_ge)}

    prm = Buf("params")
    sl_prm = Slot(nc, "prm")
    lnw_T = sb("lnw_T", [128, 8])
    sp.dma(lnw_T[:], ln_pre_d[0, :].rearrange("(k p) -> p k", p=128), sl_prm, w=[prm], allow_slow_non_contiguous=True)
    cw = sb("cw", [128, 24, 5])
    for t_ in range(5):
        sp.dma(cw[:, :, t_], conv_d[0, t_, :].rearrange("(b p) -> p b", p=128), sl_prm, w=[prm], allow_slow_non_contiguous=True)
    prm16 = sb("prm16", [128, 2, 16])
    sp.dma(prm16[:, 0, 0:8], alog_f_d[0, :].partition_broadcast(128), sl_prm, w=[prm])
    sp.dma(prm16[:, 0, 8:16], alog_b_d[0, :].partition_broadcast(128), sl_prm, w=[prm])
    sp.dma(prm16[:, 1, 0:8], dtb_f_d[0, :].partition_broadcast(128), sl_prm, w=[prm])
    sp.dma(prm16[:, 1, 8:16], dtb_b_d[0, :].partition_broadcast(128), sl_prm, w=[prm])
    gdn_nw = sb("gdn_nw", [128, 1])
    sp.dma(gdn_nw[:], gdn_nw_d[0, :].rearrange("(p o) -> p o", o=1), sl_prm, w=[prm])
    gla_nw = sb("gla_nw", [128, 2])
    sp.dma(gla_nw[:], gla_nw_d[0, :].rearrange("(k p) -> p k", p=128), sl_prm, w=[prm], allow_slow_non_contiguous=True)
    w2cat = sb("w2cat", [33, 2, 512])
    pool.op(lambda h: h.memset(w2cat[0:32, :, :], 0.0), w=[prm])
    sp.dma(w2cat[0:16, 0, :], gkw_f_d[0, :, :], sl_prm, r=[prm], w=[prm])
    sp.dma(w2cat[16:32, 1, :], gkw_b_d[0, :, :], sl_prm, w=[prm])
    sp.dma(w2cat[32:33, 0, :], gkb_f_d[0:1, :], sl_prm, w=[prm])
    sp.dma(w2cat[32:33, 1, :], gkb_b_d[0:1, :], sl_prm, w=[prm])

    def dbg_out(name, src_ap, rbufs, dst=None):
        if name not in dbg:
            return
        d = dbg[name] if dst is None else dst
        sp.dma(d, src_ap, sl_dbg, r=rbufs)

    sl_dbg = Slot(nc, "dbg")
    sl_out = Slot(nc, "out")

    hT = sb("hT", [128, 8, T], BF16); hTb = Buf("hT")
    oaT = sb("oaT", [128, 8, T], BF16); oaTb = Buf("oaT")
    base_mark = len(allocs)

    xt = [sb(f"xt{i}", [128, D]) for i in range(2)]; xtb = [Buf(f"xt{i}") for i in range(2)]
    sl_x = [Slot(nc, f"x{i}") for i in range(2)]
    junk = sb("junk", [128, D], BF16); junkb = Buf("junk")
    xn = sb("xn", [128, D], BF16); xnb = Buf("xn")
    st0 = sb("st0", [128, 4]); st0b = Buf("st0")
    for tt in range(NT):
        s = tt % 2
        sp.dma(xt[s][:], x_d[tt * 128:(tt + 1) * 128, :], sl_x[s], w=[xtb[s]])
        act.op(lambda h: h.activation(out=junk[:], in_=xt[s][:], func=AF.Square, accum_out=st0[:, 0:1]), r=[xtb[s]], w=[junkb, st0b])
        act.op(lambda h: h.activation(out=st0[:, 1:2], in_=st0[:, 0:1], func=AF.Ln, bias=EPS, scale=1.0 / D), r=[st0b], w=[st0b])
        act.op(lambda h: h.activation(out=st0[:, 2:3], in_=st0[:, 1:2], func=AF.Exp, scale=-0.5), r=[st0b], w=[st0b])
        dve.op(lambda h: h.tensor_scalar(out=xn[:], in0=xt[s][:], scalar1=st0[:, 2:3], scalar2=None, op0=ALU.mult), r=[xtb[s], st0b], w=[xnb])
        for k in range(8):
            pe.op(lambda h: h.transpose(pst[:, k * 128:(k + 1) * 128], xn[:, k * 128:(k + 1) * 128], ident_bf[:]), r=[xnb, cst], w=[pstb])
        dve.op(lambda h: h.tensor_tensor(out=hT[:, :, tt * 128:(tt + 1) * 128], in0=pst[:].rearrange("p (k t) -> p k t", k=8),
                                         in1=lnw_T[:, :].unsqueeze(2).to_broadcast([128, 8, 128]), op=ALU.mult), r=[pstb, prm], w=[hTb])
    if "st0" in dbg:
        sp.dma(dbg["st0"], st0[:], sl_dbg, r=[st0b])
        tmpx = sb("dbg_tmpx", [128, D]); tbx = Buf("dbg_tmpx")
        act.op(lambda h: h.copy(out=tmpx[:], in_=xn[:]), r=[xnb], w=[tbx])
        sp.dma(dbg["xn"], tmpx[:], sl_dbg, r=[tbx])
    if "hT" in dbg:
        tmp = sb("dbg_tmp", [128, T])
        tb = Buf("dbg_tmp")
        for k in range(8):
            act.op(lambda h: h.copy(out=tmp[:], in_=hT[:, k, :]), r=[hTb], w=[tb])
            sp.dma(dbg["hT"][k * 128:(k + 1) * 128, :], tmp[:], sl_dbg, r=[tb])
    free_to(base_mark)

    NWS = 3
    wblk = [sb(f"wblk{i}", [128, 8, 128], BF16) for i in range(NWS)]
    wblkb = [Buf(f"wblk{i}") for i in range(NWS)]
    sl_w = [Slot(nc, f"w{i}") for i in range(NWS)]
    w_rr = [0]

    def load_wblk(src_ap):
        i = w_rr[0] % NWS
        w_rr[0] += 1
        pool.dma(wblk[i][:], src_ap.rearrange("(k p) c -> p k c", p=128), sl_w[i], w=[wblkb[i]])
        return wblk[i], wblkb[i]

    def proj_fm(wt, wb, tg, pst_, pb, rhsT=hT, rb=hTb):
        for k in range(8):
            pe.op(lambda h: h.matmul(pst_[:, :], lhsT=wt[:, k, :], rhs=rhsT[:, k, tg * 512:(tg + 1) * 512], start=(k == 0), stop=(k == 7)),
                  r=[wb, rb], w=[pb])

    mix_mark = len(allocs)

    NCOL = 16
    LG = sb("LG", [128, NT, NCOL]); LNB = sb("LNB", [128, NT, NCOL]); BETA = sb("BETA", [128, NT, NCOL])
    A_tok = sb("A_tok", [128, NT, NCOL]); NG_tok = sb("NG_tok", [128, NT, NCOL]); BG = sb("BG", [128, NT, NCOL])
    KD = sb("KD", [128, NT, NCOL]); ET0 = sb("ET0", [128, NT, NCOL]); ET1 = sb("ET1", [128, NT, NCOL])
    scal = Buf("scal")
    s1_mark = len(allocs)
    wsm = sb("wsm", [128, 8, 32], BF16); wsmb = Buf("wsm")
    sl_wsm = Slot(nc, "wsm")
    pool.dma(wsm[:], w_in_d[0, :, OFF_SM:OFF_SM + 32].rearrange("(k p) c -> p k c", p=128), sl_wsm, w=[wsmb])
    asm = sb("asm", [128, NT, 32]); asmb = Buf("asm")
    for tt in range(NT):
        p_, pb = next_ps()
        for k in range(8):
            pe.op(lambda h: h.matmul(p_[:, 0:32], lhsT=hT[:, k, tt * 128:(tt + 1) * 128], rhs=wsm[:, k, :], start=(k == 0), stop=(k == 7)),
                  r=[hTb, wsmb], w=[pb])
        act.op(lambda h: h.copy(out=asm[:, tt, :], in_=p_[:, 0:32]), r=[pb], w=[asmb])
    t16 = sb("t16", [128, NT, NCOL]); t16b = Buf("t16")
    dve.op(lambda h: h.tensor_tensor(out=t16[:], in0=asm[:, :, 0:16], in1=prm16[:, 1:2, :].to_broadcast([128, NT, NCOL]), op=ALU.add), r=[asmb, prm], w=[t16b])
    act.op(lambda h: h.activation(out=t16[:], in_=t16[:], func=AF.Exp), r=[t16b], w=[t16b])
    act.op(lambda h: h.activation(out=t16[:], in_=t16[:], func=AF.Ln, bias=1.0), r=[t16b], w=[t16b])
    nA = sb("nA", [128, 1, NCOL]); nAb = Buf("nA")
    act.op(lambda h: h.activation(out=nA[:], in_=prm16[:, 0:1, :], func=AF.Exp), r=[prm], w=[nAb])
    dve.op(lambda h: h.scalar_tensor_tensor(out=LG[:], in0=t16[:], scalar=-1.0, in1=nA[:].to_broadcast([128, NT, NCOL]), op0=ALU.mult, op1=ALU.mult),
           r=[t16b, nAb], w=[scal])
    act.op(lambda h: h.activation(out=t16[:], in_=asm[:, :, 16:32], func=AF.Exp, scale=-1.0), r=[asmb, scal], w=[t16b])
    act.op(lambda h: h.activation(out=t16[:], in_=t16[:], func=AF.Ln, bias=1.0), r=[t16b], w=[t16b])
    act.op(lambda h: h.mul(out=LNB[:], in_=t16[:], mul=-1.0), r=[t16b], w=[scal])
    act.op(lambda h: h.activation(out=BETA[:], in_=LNB[:], func=AF.Exp), r=[scal], w=[scal])
    G_tok = sb("G_tok", [128, NT, NCOL]); TOTO = sb("TOTO", [128, NT, NCOL])
    for tt in range(NT):
        p_, pb = next_ps()
        for i, M in enumerate([TRI[0], TRI[1], BDONES, SEL0, SEL1]):
            pe.op(lambda h: h.matmul(p_[:, i * 16:(i + 1) * 16], lhsT=M[:], rhs=LG[:, tt, :], start=True, stop=True), r=[cst, scal], w=[pb])
        act.op(lambda h: h.copy(out=G_tok[:, tt, 0:8], in_=p_[:, 0:8]), r=[pb], w=[scal])
        act.op(lambda h: h.copy(out=G_tok[:, tt, 8:16], in_=p_[:, 24:32]), r=[pb], w=[scal])
        act.op(lambda h: h.copy(out=TOTO[:, tt, :], in_=p_[:, 32:48]), r=[pb], w=[scal])
        act.op(lambda h: h.activation(out=ET0[:, tt, :], in_=p_[:, 48:64], func=AF.Exp), r=[pb], w=[scal])
        act.op(lambda h: h.activation(out=ET1[:, tt, :], in_=p_[:, 64:80], func=AF.Exp), r=[pb], w=[scal])
    dve.op(lambda h: h.tensor_tensor(out=A_tok[:], in0=G_tok[:], in1=LNB[:], op=ALU.add), r=[scal], w=[scal])
    act.op(lambda h: h.mul(out=NG_tok[:], in_=G_tok[:], mul=-1.0), r=[scal], w=[scal])
    act.op(lambda h: h.activation(out=BG[:], in_=A_tok[:], func=AF.Exp), r=[scal], w=[scal])
    dve.op(lambda h: h.tensor_tensor(out=KD[:], in0=TOTO[:], in1=G_tok[:], op=ALU.subtract), r=[scal], w=[scal])
    act.op(lambda h: h.activation(out=KD[:], in_=KD[:], func=AF.Exp), r=[scal], w=[scal])
    if "LG" in dbg:
        sp.dma(dbg["LG"].rearrange("(t p) c -> p t c", p=128), LG[:], sl_dbg, r=[scal])
        sp.dma(dbg["BETA"].rearrange("(t p) c -> p t c", p=128), BETA[:], sl_dbg, r=[scal])
        sp.dma(dbg["G_tok"].rearrange("(t p) c -> p t c", p=128), G_tok[:], sl_dbg, r=[scal])

    free_to(s1_mark)
    gdn_mark = len(allocs)
    if gdn_heads > 0:
        pre = sb("pre", [128, T + 4]); preb = Buf("pre")
        pool.op(lambda h: h.memset(pre[:, 0:2], 0.0), w=[preb]); pool.op(lambda h: h.memset(pre[:, T + 2:T + 4], 0.0), w=[preb])
        acc = sb("acc", [128, T]); accb = Buf("acc")
        sqt = sb("sqt", [128, 512], BF16); sqtb = Buf("sqt")
        rn = sb("rn", [128, 512]); rnb = Buf("rn")
        qT = sb("qT", [128, T], BF16); qTb = Buf("qT")
        kT = sb("kT", [128, T], BF16); kTb = Buf("kT")
        vT = sb("vT", [128, T], BF16); vTb = Buf("vT")
        zs = sb("zs", [128, T], BF16); zsb = Buf("zs")
        k_tok = sb("k_tok", [128, NT, 128], BF16); ktb = Buf("k_tok")
        v_tok = sb("v_tok", [128, NT, 128], BF16); vtb = Buf("v_tok")
        oTa = sb("oTa", [128, T]); oTab = Buf("oTa")
        NSL = 3
        WK = {}
        ST = {}
        for d_ in range(2):
            for sl_i in range(NSL):
                W_ = {}
                for nm, shp, dt in [("rhs1", [128, 128], F32), ("rhsb", [128, 128], F32), ("D1", [128, 128], F32), ("D1T", [128, 128], F32),
                                    ("D2T", [128, 128], BF16), ("grep", [128, 128], BF16),
                                    ("LU", [128, 2, 256], BF16),
                                    ("LU2", [128, 2, 256], BF16),
                                    ("attnT", [128, 128], BF16), ("qdT", [128, 128], BF16), ("khat", [128, 128], BF16), ("bv", [128, 128], BF16),
                                    ("kd", [128, 128], BF16), ("wT", [128, 128], BF16), ("u", [128, 128], F32), ("vnew", [128, 128], BF16)]:
                    W_[nm] = sb(f"{nm}_{d_}_{sl_i}", shp, dt)
                    W_[nm + "_b"] = Buf(f"{nm}_{d_}_{sl_i}")
                WK[(d_, sl_i)] = W_
            S_ = {}
            for nm, shp, dt in [("S32", [128, 128], F32), ("S16", [128, 128], BF16)]:
                S_[nm] = sb(f"{nm}_{d_}", shp, dt)
                S_[nm + "_b"] = Buf(f"{nm}_{d_}")
            ST[d_] = S_

    for hh in range(gdn_heads):
        for which, off in (("q", OFF_Q), ("k", OFF_K), ("v", OFF_V)):
            wt, wb = load_wblk(w_in_d[0, :, off + hh * 128: off + (hh + 1) * 128])
            for tg in range(4):
                p_, pb = next_ps()
                proj_fm(wt, wb, tg, p_, pb)
                act.op(lambda h: h.copy(out=pre[:, 2 + tg * 512: 2 + (tg + 1) * 512], in_=p_[:, :]), r=[pb], w=[preb])
            blk = off // 128 + hh
            dve.op(lambda h: h.tensor_scalar(out=acc[:], in0=pre[:, 0:T], scalar1=cw[:, blk, 0:1], scalar2=None, op0=ALU.mult), r=[preb, prm], w=[accb])
            for t_ in range(1, 5):
                dve.op(lambda h: h.scalar_tensor_tensor(out=acc[:], in0=pre[:, t_:t_ + T], scalar=cw[:, blk, t_:t_ + 1], in1=acc[:], op0=ALU.mult, op1=ALU.add),
                       r=[preb, prm, accb], w=[accb])
            if which == "v":
                act.op(lambda h: h.activation(out=vT[:], in_=acc[:], func=AF.Silu), r=[accb], w=[vTb])
                continue
            act.op(lambda h: h.activation(out=acc[:], in_=acc[:], func=AF.Silu), r=[accb], w=[accb])
            dstT, dstb = (qT, qTb) if which == "q" else (kT, kTb)
            post = (128.0 ** -0.5) if which == "q" else 1.0
            for tg in range(4):
                sl_ = slice(tg * 512, (tg + 1) * 512)
                act.op(lambda h: h.activation(out=sqt[:], in_=acc[:, sl_], func=AF.Square), r=[accb], w=[sqtb])
                p_, pb = next_ps()
                pe.op(lambda h: h.matmul(p_[:, :], lhsT=ones_bf[:], rhs=sqt[:], start=True, stop=True), r=[cst, sqtb], w=[pb])
                act.op(lambda h: h.activation(out=rn[:], in_=p_[:, :], func=AF.Ln, bias=EPS), r=[pb], w=[rnb])
                act.op(lambda h: h.activation(out=rn[:], in_=rn[:], func=AF.Exp, scale=-0.5), r=[rnb], w=[rnb])
                dve.op(lambda h: h.scalar_tensor_tensor(out=dstT[:, sl_], in0=acc[:, sl_], scalar=post, in1=rn[:], op0=ALU.mult, op1=ALU.mult),
                       r=[accb, rnb], w=[dstb])
        wt, wb = load_wblk(w_in_d[0, :, OFF_Z + hh * 128: OFF_Z + (hh + 1) * 128])
        for tg in range(4):
            p_, pb = next_ps()
            proj_fm(wt, wb, tg, p_, pb)
            act.op(lambda h: h.activation(out=zs[:, tg * 512:(tg + 1) * 512], in_=p_[:, :], func=AF.Silu), r=[pb], w=[zsb])
        for srcT, srcb, dst, dstb in ((kT, kTb, k_tok, ktb), (vT, vTb, v_tok, vtb)):
            for g8 in range(2):
                for j in range(8):
                    tt = g8 * 8 + j
                    pe.op(lambda h: h.transpose(pst[:, j * 128:(j + 1) * 128], srcT[:, tt * 128:(tt + 1) * 128], ident_bf[:]), r=[srcb, cst], w=[pstb])
                act.op(lambda h: h.copy(out=dst[:, g8 * 8:(g8 + 1) * 8, :], in_=pst[:].rearrange("p (j c) -> p j c", j=8)), r=[pstb], w=[dstb])
        if hh == 0 and "qT" in dbg:
            tmp = sb("dbg_tmp2", [128, T]); tb = Buf("dbg_tmp2")
            for nm, src, sbf in (("qT", qT, qTb), ("kT", kT, kTb), ("vT", vT, vTb)):
                act.op(lambda h: h.copy(out=tmp[:], in_=src[:]), r=[sbf], w=[tb])
                sp.dma(dbg[nm], tmp[:], sl_dbg, r=[tb])

        def gdn_prep(d_, tt, W_, hh=hh):
            col = d_ * 8 + hh
            tsl = slice(tt * 128, (tt + 1) * 128)
            act.op(lambda h: h.activation(out=W_["rhs1"][:], in_=TRI[d_][:], func=AF.Copy, scale=LG[:, tt, col:col + 1]),
                   r=[cst, scal], w=[W_["rhs1_b"]])
            act.op(lambda h: h.activation(out=W_["rhsb"][:], in_=ident32[:], func=AF.Copy, scale=LNB[:, tt, col:col + 1]),
                   r=[cst, scal], w=[W_["rhsb_b"]])
            pool.op(lambda h: h.tensor_scalar(out=W_["khat"][:], in0=k_tok[:, tt, :], scalar1=BG[:, tt, col:col + 1], scalar2=None, op0=ALU.mult),
                    r=[ktb, scal], w=[W_["khat_b"]])
            pool.op(lambda h: h.tensor_scalar(out=W_["bv"][:], in0=v_tok[:, tt, :], scalar1=BETA[:, tt, col:col + 1], scalar2=None, op0=ALU.mult),
                    r=[vtb, scal], w=[W_["bv_b"]])
            pool.op(lambda h: h.tensor_scalar(out=W_["kd"][:], in0=k_tok[:, tt, :], scalar1=KD[:, tt, col:col + 1], scalar2=None, op0=ALU.mult),
                    r=[ktb, scal], w=[W_["kd_b"]])
            yield
            ia = yield from acquire("prep")
            pa, pab = ps[ia], psb[ia]
            pe.op(lambda h: h.matmul(pa[:, 0:128], lhsT=ones32[:], rhs=W_["rhs1"][:], start=True, stop=False), r=[cst, W_["rhs1_b"]], w=[pab])
            pe.op(lambda h: h.matmul(pa[:, 0:128], lhsT=ident32[:], rhs=NEG_A[d_][:], start=False, stop=True), r=[cst], w=[pab])
            pe.op(lambda h: h.matmul(pa[:, 128:256], lhsT=ones32[:], rhs=W_["rhs1"][:], start=True, stop=False), r=[cst, W_["rhs1_b"]], w=[pab])
            pe.op(lambda h: h.matmul(pa[:, 128:256], lhsT=ones32[:], rhs=W_["rhsb"][:], start=False, stop=False), r=[cst, W_["rhsb_b"]], w=[pab])
            pe.op(lambda h: h.matmul(pa[:, 128:256], lhsT=ident32[:], rhs=NEG_B[d_][:], start=False, stop=True), r=[cst], w=[pab])
            pe.op(lambda h: h.matmul(pa[:, 256:384], lhsT=ones32[:], rhs=W_["rhs1"][:], start=True, stop=False), r=[cst, W_["rhs1_b"]], w=[pab])
            pe.op(lambda h: h.matmul(pa[:, 256:384], lhsT=ident32[:], rhs=NEG_C[d_][:], start=False, stop=True), r=[cst], w=[pab])
            pe.op(lambda h: h.matmul(pa[:, 384:512], lhsT=ones32[:], rhs=W_["rhs1"][:], start=True, stop=True), r=[cst, W_["rhs1_b"]], w=[pab])
            yield
            act.op(lambda h: h.activation(out=W_["D1"][:], in_=pa[:, 0:128], func=AF.Exp, bias=A_tok[:, tt, col:col + 1], scale=-1.0),
                   r=[pab, scal], w=[W_["D1_b"]])
            act.op(lambda h: h.activation(out=W_["D1T"][:], in_=pa[:, 128:256], func=AF.Exp, bias=NG_tok[:, tt, col:col + 1], scale=1.0),
                   r=[pab, scal], w=[W_["D1T_b"]])
            act.op(lambda h: h.activation(out=W_["D2T"][:], in_=pa[:, 256:384], func=AF.Exp, bias=NG_tok[:, tt, col:col + 1], scale=1.0),
                   r=[pab, scal], w=[W_["D2T_b"]])
            act.op(lambda h: h.activation(out=W_["grep"][:], in_=pa[:, 384:512], func=AF.Exp), r=[pab], w=[W_["grep_b"]])
            release(ia)
            yield
            ikq = yield from acquire("prep")
            pkq, pkqb = ps[ikq], psb[ikq]
            pe.op(lambda h: h.matmul(pkq[:, 0:128], lhsT=kT[:, tsl], rhs=kT[:, tsl], start=True, stop=True), r=[kTb], w=[pkqb])
            pe.op(lambda h: h.matmul(pkq[:, 128:256], lhsT=kT[:, tsl], rhs=qT[:, tsl], start=True, stop=True), r=[kTb, qTb], w=[pkqb])
            yield
            LU, LUb, LU2, LU2b = W_["LU"], W_["LU_b"], W_["LU2"], W_["LU2_b"]
            dve.op(lambda h: h.tensor_tensor(out=LU[:, 1, 0:128], in0=pkq[:, 0:128], in1=W_["D1"][:], op=ALU.mult), r=[pkqb, W_["D1_b"]], w=[LUb])
            dve.op(lambda h: h.tensor_tensor(out=LU[:, 0, 0:128], in0=pkq[:, 0:128], in1=W_["D1T"][:], op=ALU.mult), r=[pkqb, W_["D1T_b"]], w=[LUb])
            dve.op(lambda h: h.tensor_tensor(out=LU2[:, 0, 128:256], in0=ident32[:], in1=LU[:, 0, 0:128], op=ALU.subtract), r=[cst, LUb], w=[LU2b])
            dve.op(lambda h: h.tensor_tensor(out=W_["attnT"][:], in0=pkq[:, 128:256], in1=W_["D2T"][:], op=ALU.mult), r=[pkqb, W_["D2T_b"]], w=[W_["attnT_b"]])
            dve.op(lambda h: h.tensor_tensor(out=W_["qdT"][:], in0=qT[:, tsl], in1=W_["grep"][:], op=ALU.mult), r=[qTb, W_["grep_b"]], w=[W_["qdT_b"]])
            release(ikq)
            yield
            cur, curb, nxt, nxtb = LU, LUb, LU2, LU2b
            for lev in range(6):
                ipn = yield from acquire("prep")
                pn, pnb = ps[ipn], psb[ipn]
                if lev == 0:
                    pe.op(lambda h: h.matmul(pn[:, 0:128], lhsT=cur[:, 1, 0:128], rhs=cur[:, 0, 0:128], start=True, stop=True), r=[curb], w=[pnb])
                elif lev < 5:
                    pe.op(lambda h: h.matmul(pn[:, 0:256], lhsT=cur[:, 1, 0:128], rhs=cur[:, 0, 0:256], start=True, stop=True), r=[curb], w=[pnb])
                else:
                    pe.op(lambda h: h.matmul(pn[:, 128:256], lhsT=cur[:, 1, 0:128], rhs=cur[:, 0, 128:256], start=True, stop=True), r=[curb], w=[pnb])
                if lev < 5:
                    pe.op(lambda h: h.matmul(pn[:, 256:384], lhsT=cur[:, 0, 0:128], rhs=cur[:, 1, 0:128], start=True, stop=True), r=[curb], w=[pnb])
                yield
                if lev < 5:
                    act.op(lambda h: h.copy(out=nxt[:, 0, 0:128], in_=pn[:, 0:128]), r=[pnb], w=[nxtb])
                    act.op(lambda h: h.copy(out=nxt[:, 1, 0:128], in_=pn[:, 256:384]), r=[pnb], w=[nxtb])
                if lev > 0:
                    dve.op(lambda h: h.tensor_tensor(out=nxt[:, 0, 128:256], in0=pn[:, 128:256], in1=cur[:, 0, 128:256], op=ALU.add), r=[pnb, curb], w=[nxtb])
                cur, curb, nxt, nxtb = nxt, nxtb, cur, curb
                release(ipn)
                yield
            Wm = cur[:, 0, 128:256]; Wmb = curb
            ipw = yield from acquire("prep")
            pw, pwb = ps[ipw], psb[ipw]
            pe.op(lambda h: h.matmul(pw[:, 0:128], lhsT=W_["khat"][:], rhs=Wm, start=True, stop=True), r=[W_["khat_b"], Wmb], w=[pwb])
            pe.op(lambda h: h.matmul(pw[:, 128:256], lhsT=Wm, rhs=W_["bv"][:], start=True, stop=True), r=[W_["bv_b"], Wmb], w=[pwb])
            yield
            act.op(lambda h: h.copy(out=W_["wT"][:], in_=pw[:, 0:128]), r=[pwb], w=[W_["wT_b"]])
            act.op(lambda h: h.copy(out=W_["u"][:], in_=pw[:, 128:256]), r=[pwb], w=[W_["u_b"]])
            release(ipw)
            yield

        def gdn_scan(d_, tt, W_, S_, hh=hh):
            col = d_ * 8 + hh
            for c in ((0, 1) if d_ == 0 else (1, 0)):
                rs = slice(c * 64, (c + 1) * 64)
                ETc = ET0 if c == 0 else ET1
                ip1 = yield from acquire("scan")
                p1, p1b = ps[ip1], psb[ip1]
                pe.op(lambda h: h.matmul(p1[rs, 0:128], lhsT=W_["wT"][:, rs], rhs=S_["S16"][:], start=True, stop=True), r=[W_["wT_b"], S_["S16_b"]], w=[p1b])
                yield
                dve.op(lambda h: h.tensor_tensor(out=W_["vnew"][rs, :], in0=W_["u"][rs, :], in1=p1[rs, 0:128], op=ALU.subtract),
                       r=[W_["u_b"], p1b], w=[W_["vnew_b"]])
                release(ip1)
                yield
                ip2 = yield from acquire("scan")
                p2, p2b = ps[ip2], psb[ip2]
                pe.op(lambda h: h.matmul(p2[:, 0:64], lhsT=S_["S16"][:], rhs=W_["qdT"][:, rs], start=True, stop=False), r=[S_["S16_b"], W_["qdT_b"]], w=[p2b])
                pe.op(lambda h: h.matmul(p2[:, 0:64], lhsT=W_["vnew"][rs, :], rhs=W_["attnT"][rs, rs], start=False, stop=True),
                      r=[W_["vnew_b"], W_["attnT_b"]], w=[p2b])
                pe.op(lambda h: h.matmul(p2[:, 128:256], lhsT=W_["kd"][rs, :], rhs=W_["vnew"][rs, :], start=True, stop=True),
                      r=[W_["kd_b"], W_["vnew_b"]], w=[p2b])
                yield
                dve.op(lambda h: h.scalar_tensor_tensor(out=S_["S32"][:], in0=S_["S32"][:], scalar=ETc[:, tt, col:col + 1], in1=p2[:, 128:256],
                                                        op0=ALU.mult, op1=ALU.add), r=[S_["S32_b"], scal, p2b], w=[S_["S32_b"]])
                act.op(lambda h: h.copy(out=S_["S16"][:], in_=S_["S32"][:]), r=[S_["S32_b"]], w=[S_["S16_b"]])
                osl = slice(tt * 128 + c * 64, tt * 128 + (c + 1) * 64)
                first = (tt < NT // 2) if d_ == 0 else (tt >= NT // 2)
                if first:
                    act.op(lambda h: h.copy(out=oTa[:, osl], in_=p2[:, 0:64]), r=[p2b], w=[oTab])
                else:
                    dve.op(lambda h: h.tensor_tensor(out=oTa[:, osl], in0=p2[:, 0:64], in1=oTa[:, osl], op=ALU.add), r=[p2b, oTab], w=[oTab])
                release(ip2)
                yield

        for d_ in range(2):
            pool.op(lambda h: h.memset(ST[d_]["S32"][:], 0.0), w=[ST[d_]["S32_b"]])
            pool.op(lambda h: h.memset(ST[d_]["S16"][:], 0.0), w=[ST[d_]["S16_b"]])
        order = {0: list(range(NT)), 1: list(range(NT - 1, -1, -1))}
        P = {0: [], 1: []}
        S = {0: [], 1: []}
        tasks = []
        for i in range(NT):
            for d_ in range(2):
                tt = order[d_][i]
                W_ = WK[(d_, i % NSL)]
                pt = Task(gdn_prep(d_, tt, W_), deps=[S[d_][i - NSL] if i >= NSL else None])
                P[d_].append(pt)
                tasks.append(pt)
            for d_ in range(2):
                tt = order[d_][i]
                W_ = WK[(d_, i % NSL)]
                other = S[1 - d_][NT - 1 - i] if (i >= NT // 2 and len(S[1 - d_]) > NT - 1 - i) else None
                stt = Task(gdn_scan(d_, tt, W_, ST[d_]), deps=[P[d_][i], S[d_][i - 1] if i >= 1 else None, other])
                S[d_].append(stt)
                tasks.append(stt)
        run_tasks(tasks, max_active=MAXACT)
        if f"oa{hh}" in dbg:
            sp.dma(dbg[f"oa{hh}"], oTa[:], sl_dbg, r=[oTab])
        for tg in range(4):
            sl_ = slice(tg * 512, (tg + 1) * 512)
            act.op(lambda h: h.activation(out=sqt[:], in_=oTa[:, sl_], func=AF.Square), r=[oTab], w=[sqtb])
            p_, pb = next_ps()
            pe.op(lambda h: h.matmul(p_[:, :], lhsT=ones_bf[:], rhs=sqt[:], start=True, stop=True), r=[cst, sqtb], w=[pb])
            act.op(lambda h: h.activation(out=rn[:], in_=p_[:, :], func=AF.Ln, bias=EPS, scale=1.0 / 128), r=[pb], w=[rnb])
            act.op(lambda h: h.activation(out=rn[:], in_=rn[:], func=AF.Exp, scale=-0.5), r=[rnb], w=[rnb])
            dve.op(lambda h: h.scalar_tensor_tensor(out=rn[:], in0=oTa[:, sl_], scalar=gdn_nw[:, 0:1], in1=rn[:], op0=ALU.mult, op1=ALU.mult),
                   r=[oTab, rnb, prm], w=[rnb])
            dve.op(lambda h: h.tensor_tensor(out=oaT[:, hh, sl_], in0=rn[:], in1=zs[:, sl_], op=ALU.mult), r=[rnb, zsb], w=[oaTb])
    if "oaT" in dbg:
        tmp = sb("dbg_tmp3", [128, T]); tb = Buf("dbg_tmp3")
        for k in range(8):
            act.op(lambda h: h.copy(out=tmp[:], in_=oaT[:, k, :]), r=[oaTb], w=[tb])
            sp.dma(dbg["oaT"][k * 128:(k + 1) * 128, :], tmp[:], sl_dbg, r=[tb])
    free_to(s1_mark)

    obT = sb("obT", [128, 8, T], BF16); obTb = Buf("obT")
    gla_mark = len(allocs)
    if gla_heads > 0:
        GW = {}
        for d_ in range(2):
            W_ = {}
            for nm, shp, dt in [("ekd", [128, 128], F32), ("eg", [128, 128], F32), ("eng", [128, 128], F32), ("etot", [128, 1], F32),
                                ("kdg", [128, 128], BF16), ("qgT", [128, 128], BF16), ("kgT", [128, 128], BF16), ("attnT", [128, 128], BF16),
                                ("S32", [128, 256], F32), ("S16", [128, 256], BF16)]:
                W_[nm] = sb(f"g{nm}_{d_}", shp, dt)
                W_[nm + "_b"] = Buf(f"g{nm}_{d_}")
            GW[d_] = W_

        rT1 = sb("rT1", [33, T]); rT1b = Buf("rT1")
        wr = sb("wr", [128, 8, 32], BF16); wrb = Buf("wr")
        sl_wr = Slot(nc, "wr")
        pool.dma(wr[:], w_in_d[0, :, OFF_R:OFF_R + 32].rearrange("(k p) c -> p k c", p=128), sl_wr, w=[wrb])
        pool.op(lambda h: h.memset(rT1[32:33, :], 1.0), w=[rT1b])
        for tg in range(4):
            p_, pb = next_ps()
            for k in range(8):
                pe.op(lambda h: h.matmul(p_[0:32, :], lhsT=wr[:, k, :], rhs=hT[:, k, tg * 512:(tg + 1) * 512], start=(k == 0), stop=(k == 7)),
                      r=[wrb, hTb], w=[pb])
            act.op(lambda h: h.copy(out=rT1[0:32, tg * 512:(tg + 1) * 512], in_=p_[0:32, :]), r=[pb], w=[rT1b])
        wkv = sb("wkv", [128, 8, 384], BF16); wkvb = Buf("wkv"); sl_wkv = Slot(nc, "wkv")
        qTg = sb("qTg", [128, T], BF16); qTgb = Buf("qTg")
        kTg = sb("kTg", [128, T], BF16); kTgb = Buf("kTg")
        kg_tok = sb("kg_tok", [128, NT, 128], BF16); kgtb = Buf("kg_tok")
        vg_tok = sb("vg_tok", [128, NT, 256], BF16); vgtb = Buf("vg_tok")
        gsT = sb("gsT", [128, 2, T], BF16); gsTb = Buf("gsT")
        GKP1 = sb("GKP", [128, NT, 128]); GKP = [GKP1, GKP1]; GKP1b = Buf("GKP"); GKPb = [GKP1b, GKP1b]
        obTa = sb("obTa", [128, 2, T]); obTab = Buf("obTa")
        sqg = sb("sqg", [128, 512], BF16); sqgb = Buf("sqg")
        rng = sb("rng", [128, 512]); rngb = Buf("rng")
        tg1 = sb("tg1", [128, 512]); tg1b = Buf("tg1")
    for hb in range(gla_heads):
        if gla_stop < 1:
            break
        for off, dst, dstb, scl in ((OFF_QB, qTg, qTgb, 128.0 ** -0.5), (OFF_KB, kTg, kTgb, 1.0)):
            wt, wb = load_wblk(w_in_d[0, :, off + hb * 128: off + (hb + 1) * 128])
            for tg in range(4):
                p_, pb = next_ps()
                proj_fm(wt, wb, tg, p_, pb)
                act.op(lambda h: h.mul(out=dst[:, tg * 512:(tg + 1) * 512], in_=p_[:, :], mul=scl), r=[pb], w=[dstb])
        for eb in range(2):
            wt, wb = load_wblk(w_in_d[0, :, OFF_GB + hb * 256 + eb * 128: OFF_GB + hb * 256 + (eb + 1) * 128])
            for tg in range(4):
                p_, pb = next_ps()
                proj_fm(wt, wb, tg, p_, pb)
                act.op(lambda h: h.activation(out=gsT[:, eb, tg * 512:(tg + 1) * 512], in_=p_[:, :], func=AF.Silu), r=[pb], w=[gsTb])
        if gla_stop < 2:
            break
        pool.dma(wkv[:, :, 0:128], w_in_d[0, :, OFF_KB + hb * 128: OFF_KB + (hb + 1) * 128].rearrange("(k p) c -> p k c", p=128), sl_wkv, w=[wkvb])
        pool.dma(wkv[:, :, 128:384], w_in_d[0, :, OFF_VB + hb * 256: OFF_VB + (hb + 1) * 256].rearrange("(k p) c -> p k c", p=128), sl_wkv, w=[wkvb])
        for tt in range(NT):
            p_, pb = next_ps()
            for k in range(8):
                pe.op(lambda h: h.matmul(p_[:, 0:384], lhsT=hT[:, k, tt * 128:(tt + 1) * 128], rhs=wkv[:, k, :], start=(k == 0), stop=(k == 7)),
                      r=[hTb, wkvb], w=[pb])
            act.op(lambda h: h.copy(out=kg_tok[:, tt, :], in_=p_[:, 0:128]), r=[pb], w=[kgtb])
            act.op(lambda h: h.copy(out=vg_tok[:, tt, :], in_=p_[:, 128:384]), r=[pb], w=[vgtb])
        if gla_stop < 3:
            break
        for d_ in range(2):
            for tt in range(NT):
                p_, pb = next_ps()
                pe.op(lambda h: h.matmul(p_[:, 0:128], lhsT=rT1[:, tt * 128:(tt + 1) * 128], rhs=w2cat[:, d_, hb * 128:(hb + 1) * 128], start=True, stop=True),
                      r=[rT1b, prm], w=[pb])
                act.op(lambda h: h.activation(out=GKP[d_][:, tt, :], in_=p_[:, 0:128], func=AF.Exp, scale=-1.0), r=[pb], w=[GKPb[d_]])
            act.op(lambda h: h.activation(out=GKP[d_][:], in_=GKP[d_][:], func=AF.Ln, bias=1.0), r=[GKPb[d_]], w=[GKPb[d_]])
            W_ = GW[d_]
            pool.op(lambda h: h.memset(W_["S32"][:], 0.0), w=[W_["S32_b"]])
            pool.op(lambda h: h.memset(W_["S16"][:], 0.0), w=[W_["S16_b"]])
            tiles = range(NT) if d_ == 0 else range(NT - 1, -1, -1)
            if gla_stop < 4:
                tiles = []
            for tt in tiles:
                tsl = slice(tt * 128, (tt + 1) * 128)
                pg, pgb = next_ps()
                gk_t = GKP[d_][:, tt, :]
                pe.op(lambda h: h.matmul(pg[:, 0:128], lhsT=DIFG[d_][:], rhs=gk_t, start=True, stop=True), r=[cst, GKPb[d_]], w=[pgb])
                pe.op(lambda h: h.matmul(pg[:, 128:256], lhsT=gk_t, rhs=TRIG[d_][:], start=True, stop=True), r=[cst, GKPb[d_]], w=[pgb])
                act.op(lambda h: h.activation(out=W_["ekd"][:], in_=pg[:, 0:128], func=AF.Exp), r=[pgb], w=[W_["ekd_b"]])
                act.op(lambda h: h.activation(out=W_["eg"][:], in_=pg[:, 128:256], func=AF.Exp), r=[pgb], w=[W_["eg_b"]])
                act.op(lambda h: h.activation(out=W_["eng"][:], in_=pg[:, 128:256], func=AF.Exp, scale=-1.0), r=[pgb], w=[W_["eng_b"]])
                lastc = 128 + (127 if d_ == 0 else 0)
                act.op(lambda h: h.activation(out=W_["etot"][:], in_=pg[:, lastc:lastc + 1], func=AF.Exp), r=[pgb], w=[W_["etot_b"]])
                if gla_stop < 5:
                    continue
                dve.op(lambda h: h.tensor_tensor(out=W_["kdg"][:], in0=kg_tok[:, tt, :], in1=W_["ekd"][:], op=ALU.mult), r=[kgtb, W_["ekd_b"]], w=[W_["kdg_b"]])
                dve.op(lambda h: h.tensor_tensor(out=W_["qgT"][:], in0=qTg[:, tsl], in1=W_["eg"][:], op=ALU.mult), r=[qTgb, W_["eg_b"]], w=[W_["qgT_b"]])
                pool.op(lambda h: h.tensor_tensor(out=W_["kgT"][:], in0=kTg[:, tsl], in1=W_["eng"][:], op=ALU.mult), r=[kTgb, W_["eng_b"]], w=[W_["kgT_b"]])
                if gla_stop < 6:
                    continue
                pa_, pab_ = next_ps()
                pe.op(lambda h: h.matmul(pa_[:, 0:128], lhsT=W_["kgT"][:], rhs=W_["qgT"][:], start=True, stop=True), r=[W_["kgT_b"], W_["qgT_b"]], w=[pab_])
                dve.op(lambda h: h.tensor_tensor(out=W_["attnT"][:], in0=pa_[:, 0:128], in1=MASKG[d_][:], op=ALU.mult), r=[pab_, cst], w=[W_["attnT_b"]])
                if gla_stop < 7:
                    continue
                po, pob = next_ps()
                for eb in range(2):
                    pe.op(lambda h: h.matmul(po[:, eb * 128:(eb + 1) * 128], lhsT=W_["S16"][:, eb * 128:(eb + 1) * 128], rhs=W_["qgT"][:], start=True, stop=False),
                          r=[W_["S16_b"], W_["qgT_b"]], w=[pob])
                    pe.op(lambda h: h.matmul(po[:, eb * 128:(eb + 1) * 128], lhsT=vg_tok[:, tt, eb * 128:(eb + 1) * 128], rhs=W_["attnT"][:], start=False, stop=True),
                          r=[vgtb, W_["attnT_b"]], w=[pob])
                pS, pSb = next_ps()
                pe.op(lambda h: h.matmul(pS[:, 0:256], lhsT=W_["kdg"][:], rhs=vg_tok[:, tt, :], start=True, stop=True), r=[W_["kdg_b"], vgtb], w=[pSb])
                if gla_stop < 8:
                    continue
                if d_ == 0:
                    act.op(lambda h: h.copy(out=obTa[:, :, tsl], in_=po[:, 0:256].rearrange("p (e t) -> p e t", e=2)), r=[pob], w=[obTab])
                else:
                    dve.op(lambda h: h.tensor_tensor(out=obTa[:, :, tsl], in0=po[:, 0:256].rearrange("p (e t) -> p e t", e=2), in1=obTa[:, :, tsl], op=ALU.add),
                           r=[pob, obTab], w=[obTab])
                if gla_stop < 9:
                    continue
                dve.op(lambda h: h.scalar_tensor_tensor(out=W_["S32"][:], in0=W_["S32"][:], scalar=W_["etot"][:, 0:1], in1=pS[:, 0:256],
                                                        op0=ALU.mult, op1=ALU.add), r=[W_["S32_b"], W_["etot_b"], pSb], w=[W_["S32_b"]])
                if gla_stop < 10:
                    continue
                act.op(lambda h: h.copy(out=W_["S16"][:], in_=W_["S32"][:]), r=[W_["S32_b"]], w=[W_["S16_b"]])
        if f"ob{hb}" in dbg:
            for eb in range(2):
                sp.dma(dbg[f"ob{hb}"][eb * 128:(eb + 1) * 128, :], obTa[:, eb, :], sl_dbg, r=[obTab])
        for tg in range(4):
            sl_ = slice(tg * 512, (tg + 1) * 512)
            p_, pb = next_ps()
            for eb in range(2):
                act.op(lambda h: h.activation(out=sqg[:], in_=obTa[:, eb, sl_], func=AF.Square), r=[obTab], w=[sqgb])
                pe.op(lambda h: h.matmul(p_[:, :], lhsT=ones_bf[:], rhs=sqg[:], start=(eb == 0), stop=(eb == 1)), r=[cst, sqgb], w=[pb])
            act.op(lambda h: h.activation(out=rng[:], in_=p_[:, :], func=AF.Ln, bias=EPS, scale=1.0 / 256), r=[pb], w=[rngb])
            act.op(lambda h: h.activation(out=rng[:], in_=rng[:], func=AF.Exp, scale=-0.5), r=[rngb], w=[rngb])
            for eb in range(2):
                dve.op(lambda h: h.scalar_tensor_tensor(out=tg1[:], in0=obTa[:, eb, sl_], scalar=gla_nw[:, eb:eb + 1], in1=rng[:], op0=ALU.mult, op1=ALU.mult),
                       r=[obTab, rngb, prm], w=[tg1b])
                dve.op(lambda h: h.tensor_tensor(out=obT[:, hb * 2 + eb, sl_], in0=tg1[:], in1=gsT[:, eb, sl_], op=ALU.mult), r=[tg1b, gsTb], w=[obTb])
    if "obT" in dbg:
        tmp = sb("dbg_tmp4", [128, T]); tb = Buf("dbg_tmp4")
        for k in range(8):
            act.op(lambda h: h.copy(out=tmp[:], in_=obT[:, k, :]), r=[obTb], w=[tb])
            sp.dma(dbg["obT"][k * 128:(k + 1) * 128, :], tmp[:], sl_dbg, r=[tb])
    free_to(gla_mark)

    if do_final:
        mT = sb("mT", [128, 8, T], BF16); mTb = Buf("mT")
        sga = sb("sga", [128, 512]); sgab = Buf("sga")
        sgb = sb("sgb", [128, 512]); sgbb = Buf("sgb")
        t1 = sb("t1", [128, 512]); t1b = Buf("t1")
        t2 = sb("t2", [128, 512]); t2b = Buf("t2")
        NW2 = 8
        wb2 = [sb(f"wb2_{i}", [128, 8, 128], BF16) for i in range(NW2)]; wb2b = [Buf(f"wb2_{i}") for i in range(NW2)]
        sl_w2 = [Slot(nc, f"w2_{i}") for i in range(NW2)]
        rr2 = [0]

        def load2(src_ap):
            i = rr2[0] % NW2
            rr2[0] += 1
            pool.dma(wb2[i][:], src_ap.rearrange("(k p) c -> p k c", p=128), sl_w2[i], w=[wb2b[i]])
            return wb2[i], wb2b[i]

        for m in range(8):
            msl = slice(m * 128, (m + 1) * 128)
            wg, wgb_ = load2(wpg_d[0, :, msl])
            wl, wlb_ = load2(wpl_d[0, :, msl])
            wa, wab_ = load2(w_in_d[0, :, OFF_GA + m * 128: OFF_GA + (m + 1) * 128])
            wbb, wbbb_ = load2(w_in_d[0, :, OFF_GBG + m * 128: OFF_GBG + (m + 1) * 128])
            for tg in range(4):
                sl_ = slice(tg * 512, (tg + 1) * 512)
                pga, pgab = next_ps(); proj_fm(wa, wab_, tg, pga, pgab)
                act.op(lambda h: h.activation(out=sga[:], in_=pga[:, :], func=AF.Sigmoid), r=[pgab], w=[sgab])
                pgb_, pgbb = next_ps(); proj_fm(wbb, wbbb_, tg, pgb_, pgbb)
                act.op(lambda h: h.activation(out=sgb[:], in_=pgb_[:, :], func=AF.Sigmoid), r=[pgbb], w=[sgbb])
                pya, pyab = next_ps(); proj_fm(wg, wgb_, tg, pya, pyab, rhsT=oaT, rb=oaTb)
                dve.op(lambda h: h.tensor_tensor(out=t1[:], in0=pya[:, :], in1=sga[:], op=ALU.mult), r=[pyab, sgab], w=[t1b])
                pyb, pybb = next_ps(); proj_fm(wl, wlb_, tg, pyb, pybb, rhsT=obT, rb=obTb)
                dve.op(lambda h: h.tensor_tensor(out=t2[:], in0=pyb[:, :], in1=sgb[:], op=ALU.mult), r=[pybb, sgbb], w=[t2b])
                pool.op(lambda h: h.tensor_tensor(out=mT[:, m, sl_], in0=t1[:], in1=t2[:], op=ALU.add), r=[t1b, t2b], w=[mTb])
        if "mT" in dbg:
            tmp = sb("dbg_tmp5", [128, T]); tb = Buf("dbg_tmp5")
            for k in range(8):
                act.op(lambda h: h.copy(out=tmp[:], in_=mT[:, k, :]), r=[mTb], w=[tb])
                sp.dma(dbg["mT"][k * 128:(k + 1) * 128, :], tmp[:], sl_dbg, r=[tb])
        wob = hTb; sl_wo = Slot(nc, "wo")
        for k in range(8):
            pool.dma(hT[:, k, 0:D], wout_d[0, k * 128:(k + 1) * 128, :], sl_wo, w=[wob])
        lnp = sb("lnp", [128, D]); sl_lnp = Slot(nc, "lnp"); lnpb = Buf("lnp")
        sp.dma(lnp[:], ln_post_d[0, :].partition_broadcast(128), sl_lnp, w=[lnpb])
        xr = [sb(f"xr{i}", [128, D]) for i in range(2)]; xrb = [Buf(f"xr{i}") for i in range(2)]
        sl_xr = [Slot(nc, f"xr{i}") for i in range(2)]
        ot = [sb(f"ot{i}", [128, D]) for i in range(2)]; otb = [Buf(f"ot{i}") for i in range(2)]
        st1 = sb("st1", [128, 8]); st1b = Buf("st1")
        junk2 = sb("junk2", [128, 512], BF16); junk2b = Buf("junk2")
        for tt in range(NT):
            s = tt % 2
            tsl = slice(tt * 128, (tt + 1) * 128)
            sp.dma(xr[s][:], x_d[tsl, :], sl_xr[s], w=[xrb[s]])
            pp = []
            for half in range(2):
                p_, pb = next_ps()
                for m in range(8):
                    pe.op(lambda h: h.matmul(p_[:, :], lhsT=mT[:, m, tsl], rhs=hT[:, m, half * 512:(half + 1) * 512], start=(m == 0), stop=(m == 7)),
                          r=[mTb, wob], w=[pb])
                act.op(lambda h: h.activation(out=junk2[:], in_=p_[:, :], func=AF.Square, accum_out=st1[:, half:half + 1]), r=[pb], w=[junk2b, st1b])
                pp.append((p_, pb))
            dve.op(lambda h: h.tensor_tensor(out=st1[:, 2:3], in0=st1[:, 0:1], in1=st1[:, 1:2], op=ALU.add), r=[st1b], w=[st1b])
            act.op(lambda h: h.activation(out=st1[:, 3:4], in_=st1[:, 2:3], func=AF.Ln, bias=EPS, scale=1.0 / D), r=[st1b], w=[st1b])
            act.op(lambda h: h.activation(out=st1[:, 4:5], in_=st1[:, 3:4], func=AF.Exp, scale=-0.5), r=[st1b], w=[st1b])
            for half in range(2):
                p_, pb = pp[half]
                hs = slice(half * 512, (half + 1) * 512)
                dve.op(lambda h: h.scalar_tensor_tensor(out=ot[s][:, hs], in0=p_[:, :], scalar=st1[:, 4:5], in1=lnp[:, hs], op0=ALU.mult, op1=ALU.mult),
                       r=[pb, st1b, lnpb], w=[otb[s]])
                pool.op(lambda h: h.tensor_tensor(out=ot[s][:, hs], in0=ot[s][:, hs], in1=xr[s][:, hs], op=ALU.add), r=[otb[s], xrb[s]], w=[otb[s]])
            sp.dma(out_d[tsl, :], ot[s][:], sl_out, r=[otb[s]])
    else:
        zt = sb("zt", [128, D]); ztb = Buf("zt")
        pool.op(lambda h: h.memset(zt[:], 0.0), w=[ztb])
        for tt in range(NT):
            sp.dma(out_d[tt * 128:(tt + 1) * 128, :], zt[:], sl_out, r=[ztb])
    sp.h.wait_ge(sl_out.sem, sl_out.cnt)
    if sl_dbg.cnt:
        sp.h.wait_ge(sl_dbg.sem, sl_dbg.cnt)
    return nc


_INPUT_NAMES = ["ln_pre_w", "w_in", "conv_w", "a_log_fwd", "a_log_bwd", "dt_bias_fwd", "dt_bias_bwd", "gdn_norm_w", "w_proj_gdn",
                "gk_w2_fwd", "gk_b2_fwd", "gk_w2_bwd", "gk_b2_bwd", "gla_norm_w", "w_proj_gla", "w_out", "ln_post_w"]


def kernel(**inputs):
    x = np.ascontiguousarray(np.asarray(inputs["x"], dtype=np.float32))
    shared = {n: np.ascontiguousarray(np.asarray(inputs[n], dtype=np.float32)) for n in _INPUT_NAMES}
    nc = build_nc()
    in_maps = [dict(shared, x=x[b]) for b in range(8)]
    res = run_bass_kernel_spmd(nc, in_maps, core_ids=list(range(8)))
    return np.stack([np.asarray(r["out"], dtype=np.float32) for r in res.results], axis=0)
```

```python
import numpy as np
import concourse.bass as bass
import concourse.mybir as mybir
from concourse.bass_utils import run_bass_kernel_spmd

F32 = mybir.dt.float32
F32R = mybir.dt.float32r
BF16 = mybir.dt.bfloat16
AF = mybir.ActivationFunctionType
ALU = mybir.AluOpType

T = 2048
NT = 16
D = 1024
NIN = 9280
EPS = 1e-6
BIG = 30000.0
MAXACT = 6
OFF_Q, OFF_K, OFF_V, OFF_Z = 0, 1024, 2048, 3072
OFF_SM = 4096
OFF_QB, OFF_KB, OFF_VB, OFF_GB = 4128, 4640, 5152, 6176
OFF_R = 7200
OFF_GA, OFF_GBG = 7232, 8256


class Ev:
    __slots__ = ("sem", "val", "key")

    def __init__(self, sem, val, key):
        self.sem, self.val, self.key = sem, val, key


class Buf:
    __slots__ = ("name", "wev", "revs")

    def __init__(self, name):
        self.name, self.wev, self.revs = name, None, {}


class Slot:
    registry = None

    def __init__(self, nc, name):
        if Slot.registry is not None:
            Slot.registry.append(self)
        self.name = name
        self.sem = nc.semaphore("ds_" + name).__enter__()
        self.cnt = 0


class Eng:
    def __init__(self, nc, name, h, selfsync):
        self.name, self.h, self.selfsync = name, h, selfsync
        self.sem = nc.semaphore("es_" + name).__enter__()
        self.cnt = 0
        self.seen = {}

    def wait(self, ev):
        if ev is None:
            return
        if ev.sem is self.sem and not self.selfsync:
            return
        if self.seen.get(ev.key, 0) >= ev.val:
            return
        self.h.wait_ge(ev.sem, ev.val)
        self.seen[ev.key] = ev.val

    def _deps(self, r, w):
        for b in r:
            self.wait(b.wev)
        for b in w:
            self.wait(b.wev)
            for ev in b.revs.values():
                self.wait(ev)

    def _mark(self, ev, r, w):
        for b in r:
            b.revs[ev.key] = ev
        for b in w:
            b.wev = ev
            b.revs = {}

    def op(self, fn, r=(), w=()):
        self._deps(r, w)
        ins = fn(self.h)
        self.cnt += 1
        ins.then_inc(self.sem, 1)
        ev = Ev(self.sem, self.cnt, self.name)
        self._mark(ev, r, w)
        return ev

    def dma(self, out, in_, slot, r=(), w=(), **kw):
        self._deps(r, w)
        ins = self.h.dma_start(out=out, in_=in_, **kw)
        slot.cnt += 16
        ins.then_inc(slot.sem, 16)
        ev = Ev(slot.sem, slot.cnt, slot.name)
        self._mark(ev, r, w)
        return ev


def build_nc(debug=(), gdn_heads=8, gla_heads=4, do_final=True):
    nc = bass.Bass("TRN2", target_bir_lowering=False)
    dram_in = lambda n, s: nc.dram_tensor(n, s, F32, kind="ExternalInput").ap()
    x_d = dram_in("x", [T, D])
    ln_pre_d = dram_in("ln_pre_w", [1, D])
    w_in_d = dram_in("w_in", [1, D, NIN])
    conv_d = dram_in("conv_w", [1, 5, 3072])
    alog_f_d = dram_in("a_log_fwd", [1, 8]); alog_b_d = dram_in("a_log_bwd", [1, 8])
    dtb_f_d = dram_in("dt_bias_fwd", [1, 8]); dtb_b_d = dram_in("dt_bias_bwd", [1, 8])
    gdn_nw_d = dram_in("gdn_norm_w", [1, 128])
    wpg_d = dram_in("w_proj_gdn", [1, D, D])
    gkw_f_d = dram_in("gk_w2_fwd", [1, 16, 512]); gkb_f_d = dram_in("gk_b2_fwd", [1, 512])
    gkw_b_d = dram_in("gk_w2_bwd", [1, 16, 512]); gkb_b_d = dram_in("gk_b2_bwd", [1, 512])
    gla_nw_d = dram_in("gla_norm_w", [1, 256])
    wpl_d = dram_in("w_proj_gla", [1, D, D])
    wout_d = dram_in("w_out", [1, D, D])
    ln_post_d = dram_in("ln_post_w", [1, D])
    out_d = nc.dram_tensor("out", [T, D], F32, kind="ExternalOutput").ap()
    dbg = {}
    for name, shape in debug:
        dbg[name] = nc.dram_tensor("dbg_" + name, shape, F32, kind="ExternalOutput").ap()

    pe = Eng(nc, "pe", nc.tensor, False)
    act = Eng(nc, "act", nc.scalar, True)
    dve = Eng(nc, "dve", nc.vector, True)
    pool = Eng(nc, "pool", nc.gpsimd, True)
    sp = Eng(nc, "sp", nc.sync, False)

    allocs = []

    def sb(name, shape, dt=F32):
        cm = nc.sbuf_tensor(name, shape, dt)
        t = cm.__enter__()
        allocs.append(cm)
        return t

    all_slots = []
    Slot.registry = all_slots

    def barrier():
        engs = (pe, act, dve, pool, sp)
        for e in engs:
            for f in engs:
                if f is not e and f.cnt:
                    e.wait(Ev(f.sem, f.cnt, f.name))
            for sl in all_slots:
                if sl.cnt:
                    e.wait(Ev(sl.sem, sl.cnt, sl.name))

    def free_to(mark):
        barrier()
        while len(allocs) > mark:
            allocs.pop().__exit__(None, None, None)

    NPS = 7
    ps = [nc.psum_tensor(f"ps{i}", [128, 512], F32).__enter__() for i in range(NPS)]
    psb = [Buf(f"ps{i}") for i in range(NPS)]
    pst = nc.psum_tensor("pst", [128, 1024], BF16).__enter__()
    pstb = Buf("pst")
    ps_rr = {}
    PS_POOLS = {"any": list(range(NPS)), "prep": [0, 1, 2], "scan": [3, 4], "proj": [5, 6], "gprep": [0, 1, 2], "gscan": [3, 4, 5, 6]}

    def next_ps(pool_="any"):
        lst = PS_POOLS[pool_]
        c = ps_rr.get(pool_, 0)
        ps_rr[pool_] = c + 1
        i = lst[c % len(lst)]
        return ps[i], psb[i]

    ps_busy = [False] * NPS

    def acquire(pool_):
        while True:
            for i in PS_POOLS[pool_]:
                if not ps_busy[i]:
                    ps_busy[i] = True
                    return i
            yield

    def release(i):
        ps_busy[i] = False

    class Task:
        def __init__(self, gen, deps=()):
            self.gen, self.deps, self.done = gen, [d for d in deps if d is not None], False

    stall = [0]

    def run_tasks(tasks, max_active=6):
        pending = list(tasks)
        active = []
        while pending or active:
            for t in list(pending):
                if len(active) >= max_active:
                    break
                if all(d.done for d in t.deps):
                    pending.remove(t)
                    active.append(t)
            assert active, "task deadlock"
            before = (pe.cnt, act.cnt, dve.cnt, pool.cnt, len(active), len(pending))
            for t in list(active):
                try:
                    next(t.gen)
                except StopIteration:
                    t.done = True
                    active.remove(t)
            if before == (pe.cnt, act.cnt, dve.cnt, pool.cnt, len(active), len(pending)):
                stall[0] += 1
                assert stall[0] < 200, ("scheduler stall", [getattr(t.gen, "__name__", "?") for t in active], list(ps_busy), len(pending))
            else:
                stall[0] = 0

    cst = Buf("const")
    ident32 = sb("ident32", [128, 128]); ident_bf = sb("ident_bf", [128, 128], BF16)
    ones32 = sb("ones32", [128, 128]); ones_bf = sb("ones_bf", [128, 128], BF16)
    zeros32 = sb("zeros32", [128, 128])
    pool.op(lambda h: h.memset(ones32[:], 1.0), w=[cst])
    pool.op(lambda h: h.memset(zeros32[:], 0.0), w=[cst])
    pool.op(lambda h: h.affine_select(out=ident32[:], in_=zeros32[:], pattern=[[-1, 128]], compare_op=ALU.not_equal,
                                      fill=1.0, base=0, channel_multiplier=1), r=[cst], w=[cst])
    pool.op(lambda h: h.tensor_copy(out=ident_bf[:], in_=ident32[:]), r=[cst], w=[cst])
    pool.op(lambda h: h.tensor_copy(out=ones_bf[:], in_=ones32[:]), r=[cst], w=[cst])

    src_cache = {}

    def tri_const(name, n, inval, fillval, offval, step, cm, cmp):
        t = sb(name, [128, 128])
        pool.op(lambda h: h.memset(t[:], offval), w=[cst])
        if inval not in src_cache:
            src_cache[inval] = sb(f"src_{len(src_cache)}", [128, 128])
            pool.op(lambda h: h.memset(src_cache[inval][:], inval), w=[cst])
        src = src_cache[inval]
        for b0 in range(0, 128, n):
            pool.op(lambda h: h.affine_select(out=t[b0:b0 + n, b0:b0 + n], in_=src[b0:b0 + n, b0:b0 + n], pattern=[[step, n]],
                                              compare_op=cmp, fill=fillval, base=0, channel_multiplier=cm), r=[cst], w=[cst])
        return t

    TRI = {
        0: tri_const("tri_f", 64, 1.0, 0.0, 0.0, 1, -1, ALU.is_ge),
        1: tri_const("tri_b", 64, 1.0, 0.0, 0.0, -1, 1, ALU.is_ge),
    }
    NEG_A = {
        0: tri_const("nega_f", 64, 0.0, BIG, BIG, -1, 1, ALU.is_gt),
        1: tri_const("nega_b", 64, 0.0, BIG, BIG, 1, -1, ALU.is_gt),
    }
    NEG_B = {
        0: tri_const("negb_f", 64, 0.0, -BIG, -BIG, 1, -1, ALU.is_gt),
        1: tri_const("negb_b", 64, 0.0, -BIG, -BIG, -1, 1, ALU.is_gt),
    }
    NEG_C = {
        0: tri_const("negc_f", 64, 0.0, -BIG, -BIG, 1, -1, ALU.is_ge),
        1: tri_const("negc_b", 64, 0.0, -BIG, -BIG, -1, 1, ALU.is_ge),
    }
    BDONES = tri_const("bdones", 64, 1.0, 1.0, 0.0, 1, 1, ALU.is_ge)
    ones_r = sb("ones_r", [128, 128], F32R); ident_r = sb("ident_r", [128, 128], F32R)
    pool.op(lambda h: h.tensor_copy(out=ones_r[:], in_=ones32[:]), r=[cst], w=[cst])
    pool.op(lambda h: h.tensor_copy(out=ident_r[:], in_=ident32[:]), r=[cst], w=[cst])
    MASKS_R = {}
    for d__ in range(2):
        mk = sb(f"masks_r{d__}", [128, 512], F32R)
        pool.op(lambda h: h.tensor_copy(out=mk[:, 0:128], in_=NEG_A[d__][:]), r=[cst], w=[cst])
        pool.op(lambda h: h.tensor_copy(out=mk[:, 128:256], in_=NEG_B[d__][:]), r=[cst], w=[cst])
        pool.op(lambda h: h.tensor_copy(out=mk[:, 256:384], in_=NEG_C[d__][:]), r=[cst], w=[cst])
        pool.op(lambda h: h.tensor_copy(out=mk[:, 384:512], in_=zeros32[:]), r=[cst], w=[cst])
        MASKS_R[d__] = mk
    SEL0 = sb("sel0", [128, 128]); SEL1 = sb("sel1", [128, 128])
    pool.op(lambda h: h.memset(SEL0[:], 0.0), w=[cst]); pool.op(lambda h: h.memset(SEL1[:], 0.0), w=[cst])
    pool.op(lambda h: h.memset(SEL0[0:64, :], 1.0), w=[cst]); pool.op(lambda h: h.memset(SEL1[64:128, :], 1.0), w=[cst])
    GS = -1.0 / 16.0
    TRIG = {0: tri_const("trig_f", 128, GS, 0.0, 0.0, 1, -1, ALU.is_ge),
            1: tri_const("trig_b", 128, GS, 0.0, 0.0, -1, 1, ALU.is_ge)}
    DIFG = {0: tri_const("difg_f", 128, 0.0, GS, 0.0, 1, -1, ALU.is_ge),
            1: tri_const("difg_b", 128, 0.0, GS, 0.0, -1, 1, ALU.is_ge)}
    MASKG = {0: tri_const("maskg_f", 128, 1.0, 0.0, 0.0, 1, -1, ALU.is_ge),
             1: tri_const("maskg_b", 128, 1.0, 0.0, 0.0, -1, 1, ALU.is_ge)}

    prm = Buf("params")
    sl_prm = Slot(nc, "prm")
    lnw_T = sb("lnw_T", [128, 8])
    sp.dma(lnw_T[:], ln_pre_d[0, :].rearrange("(k p) -> p k", p=128), sl_prm, w=[prm], allow_slow_non_contiguous=True)
    cw = sb("cw", [128, 24, 5])
    for t_ in range(5):
        sp.dma(cw[:, :, t_], conv_d[0, t_, :].rearrange("(b p) -> p b", p=128), sl_prm, w=[prm], allow_slow_non_contiguous=True)
    prm16 = sb("prm16", [128, 2, 16])
    sp.dma(prm16[:, 0, 0:8], alog_f_d[0, :].partition_broadcast(128), sl_prm, w=[prm])
    sp.dma(prm16[:, 0, 8:16], alog_b_d[0, :].partition_broadcast(128), sl_prm, w=[prm])
    sp.dma(prm16[:, 1, 0:8], dtb_f_d[0, :].partition_broadcast(128), sl_prm, w=[prm])
    sp.dma(prm16[:, 1, 8:16], dtb_b_d[0, :].partition_broadcast(128), sl_prm, w=[prm])
    gdn_nw = sb("gdn_nw", [128, 1])
    sp.dma(gdn_nw[:], gdn_nw_d[0, :].rearrange("(p o) -> p o", o=1), sl_prm, w=[prm])
    gla_nw = sb("gla_nw", [128, 2])
    sp.dma(gla_nw[:], gla_nw_d[0, :].rearrange("(k p) -> p k", p=128), sl_prm, w=[prm], allow_slow_non_contiguous=True)
    w2cat = sb("w2cat", [33, 2, 512])
    pool.op(lambda h: h.memset(w2cat[0:32, :, :], 0.0), w=[prm])
    sp.dma(w2cat[0:16, 0, :], gkw_f_d[0, :, :], sl_prm, r=[prm], w=[prm])
    sp.dma(w2cat[16:32, 1, :], gkw_b_d[0, :, :], sl_prm, w=[prm])
    sp.dma(w2cat[32:33, 0, :], gkb_f_d[0:1, :], sl_prm, w=[prm])
    sp.dma(w2cat[32:33, 1, :], gkb_b_d[0:1, :], sl_prm, w=[prm])

    def dbg_out(name, src_ap, rbufs, dst=None):
        if name not in dbg:
            return
        d = dbg[name] if dst is None else dst
        sp.dma(d, src_ap, sl_dbg, r=rbufs)

    sl_dbg = Slot(nc, "dbg")
    sl_out = Slot(nc, "out")

    hT = sb("hT", [128, 8, T], BF16); hTb = Buf("hT")
    oaT = sb("oaT", [128, 8, T], BF16); oaTb = Buf("oaT")
    base_mark = len(allocs)

    xt = [sb(f"xt{i}", [128, D]) for i in range(2)]; xtb = [Buf(f"xt{i}") for i in range(2)]
    sl_x = [Slot(nc, f"x{i}") for i in range(2)]
    junk = sb("junk", [128, D], BF16); junkb = Buf("junk")
    xn = sb("xn", [128, D], BF16); xnb = Buf("xn")
    st0 = sb("st0", [128, 4]); st0b = Buf("st0")
    for tt in range(NT):
        s = tt % 2
        sp.dma(xt[s][:], x_d[tt * 128:(tt + 1) * 128, :], sl_x[s], w=[xtb[s]])
        act.op(lambda h: h.activation(out=junk[:], in_=xt[s][:], func=AF.Square, accum_out=st0[:, 0:1]), r=[xtb[s]], w=[junkb, st0b])
        act.op(lambda h: h.activation(out=st0[:, 1:2], in_=st0[:, 0:1], func=AF.Ln, bias=EPS, scale=1.0 / D), r=[st0b], w=[st0b])
        act.op(lambda h: h.activation(out=st0[:, 2:3], in_=st0[:, 1:2], func=AF.Exp, scale=-0.5), r=[st0b], w=[st0b])
        dve.op(lambda h: h.tensor_scalar(out=xn[:], in0=xt[s][:], scalar1=st0[:, 2:3], scalar2=None, op0=ALU.mult), r=[xtb[s], st0b], w=[xnb])
        for k in range(8):
            pe.op(lambda h: h.transpose(pst[:, k * 128:(k + 1) * 128], xn[:, k * 128:(k + 1) * 128], ident_bf[:]), r=[xnb, cst], w=[pstb])
        dve.op(lambda h: h.tensor_tensor(out=hT[:, :, tt * 128:(tt + 1) * 128], in0=pst[:].rearrange("p (k t) -> p k t", k=8),
                                         in1=lnw_T[:, :].unsqueeze(2).to_broadcast([128, 8, 128]), op=ALU.mult), r=[pstb, prm], w=[hTb])
    if "st0" in dbg:
        sp.dma(dbg["st0"], st0[:], sl_dbg, r=[st0b])
        tmpx = sb("dbg_tmpx", [128, D]); tbx = Buf("dbg_tmpx")
        act.op(lambda h: h.copy(out=tmpx[:], in_=xn[:]), r=[xnb], w=[tbx])
        sp.dma(dbg["xn"], tmpx[:], sl_dbg, r=[tbx])
    if "hT" in dbg:
        tmp = sb("dbg_tmp", [128, T])
        tb = Buf("dbg_tmp")
        for k in range(8):
            act.op(lambda h: h.copy(out=tmp[:], in_=hT[:, k, :]), r=[hTb], w=[tb])
            sp.dma(dbg["hT"][k * 128:(k + 1) * 128, :], tmp[:], sl_dbg, r=[tb])
    free_to(base_mark)

    NWS = 2
    wblk = [sb(f"wblk{i}", [128, 8, 128], BF16) for i in range(NWS)]
    wblkb = [Buf(f"wblk{i}") for i in range(NWS)]
    sl_w = [Slot(nc, f"w{i}") for i in range(NWS)]
    w_rr = [0]

    def load_wblk(src_ap):
        i = w_rr[0] % NWS
        w_rr[0] += 1
        pool.dma(wblk[i][:], src_ap.rearrange("(k p) c -> p k c", p=128), sl_w[i], w=[wblkb[i]])
        return wblk[i], wblkb[i]

    def proj_fm(wt, wb, tg, pst_, pb, rhsT=hT, rb=hTb):
        for k in range(8):
            pe.op(lambda h: h.matmul(pst_[:, :], lhsT=wt[:, k, :], rhs=rhsT[:, k, tg * 512:(tg + 1) * 512], start=(k == 0), stop=(k == 7)),
                  r=[wb, rb], w=[pb])

    mix_mark = len(allocs)

    NCOL = 16
    LG = sb("LG", [128, NT, NCOL]); LNB = sb("LNB", [128, NT, NCOL]); BETA = sb("BETA", [128, NT, NCOL])
    A_tok = sb("A_tok", [128, NT, NCOL]); NG_tok = sb("NG_tok", [128, NT, NCOL]); BG = sb("BG", [128, NT, NCOL])
    KD = sb("KD", [128, NT, NCOL]); ET0 = sb("ET0", [128, NT, NCOL]); ET1 = sb("ET1", [128, NT, NCOL])
    scal = Buf("scal")
    s1_mark = len(allocs)
    wsm = sb("wsm", [128, 8, 32], BF16); wsmb = Buf("wsm")
    sl_wsm = Slot(nc, "wsm")
    pool.dma(wsm[:], w_in_d[0, :, OFF_SM:OFF_SM + 32].rearrange("(k p) c -> p k c", p=128), sl_wsm, w=[wsmb])
    asm = sb("asm", [128, NT, 32]); asmb = Buf("asm")
    for tt in range(NT):
        p_, pb = next_ps()
        for k in range(8):
            pe.op(lambda h: h.matmul(p_[:, 0:32], lhsT=hT[:, k, tt * 128:(tt + 1) * 128], rhs=wsm[:, k, :], start=(k == 0), stop=(k == 7)),
                  r=[hTb, wsmb], w=[pb])
        act.op(lambda h: h.copy(out=asm[:, tt, :], in_=p_[:, 0:32]), r=[pb], w=[asmb])
    t16 = sb("t16", [128, NT, NCOL]); t16b = Buf("t16")
    dve.op(lambda h: h.tensor_tensor(out=t16[:], in0=asm[:, :, 0:16], in1=prm16[:, 1:2, :].to_broadcast([128, NT, NCOL]), op=ALU.add), r=[asmb, prm], w=[t16b])
    act.op(lambda h: h.activation(out=t16[:], in_=t16[:], func=AF.Exp), r=[t16b], w=[t16b])
    act.op(lambda h: h.activation(out=t16[:], in_=t16[:], func=AF.Ln, bias=1.0), r=[t16b], w=[t16b])
    nA = sb("nA", [128, 1, NCOL]); nAb = Buf("nA")
    act.op(lambda h: h.activation(out=nA[:], in_=prm16[:, 0:1, :], func=AF.Exp), r=[prm], w=[nAb])
    dve.op(lambda h: h.scalar_tensor_tensor(out=LG[:], in0=t16[:], scalar=-1.0, in1=nA[:].to_broadcast([128, NT, NCOL]), op0=ALU.mult, op1=ALU.mult),
           r=[t16b, nAb], w=[scal])
    act.op(lambda h: h.activation(out=t16[:], in_=asm[:, :, 16:32], func=AF.Exp, scale=-1.0), r=[asmb, scal], w=[t16b])
    act.op(lambda h: h.activation(out=t16[:], in_=t16[:], func=AF.Ln, bias=1.0), r=[t16b], w=[t16b])
    act.op(lambda h: h.mul(out=LNB[:], in_=t16[:], mul=-1.0), r=[t16b], w=[scal])
    act.op(lambda h: h.activation(out=BETA[:], in_=LNB[:], func=AF.Exp), r=[scal], w=[scal])
    G_tok = sb("G_tok", [128, NT, NCOL]); TOTO = sb("TOTO", [128, NT, NCOL])
    for tt in range(NT):
        p_, pb = next_ps()
        for i, M in enumerate([TRI[0], TRI[1], BDONES, SEL0, SEL1]):
            pe.op(lambda h: h.matmul(p_[:, i * 16:(i + 1) * 16], lhsT=M[:], rhs=LG[:, tt, :], start=True, stop=True), r=[cst, scal], w=[pb])
        act.op(lambda h: h.copy(out=G_tok[:, tt, 0:8], in_=p_[:, 0:8]), r=[pb], w=[scal])
        act.op(lambda h: h.copy(out=G_tok[:, tt, 8:16], in_=p_[:, 24:32]), r=[pb], w=[scal])
        act.op(lambda h: h.copy(out=TOTO[:, tt, :], in_=p_[:, 32:48]), r=[pb], w=[scal])
        act.op(lambda h: h.activation(out=ET0[:, tt, :], in_=p_[:, 48:64], func=AF.Exp), r=[pb], w=[scal])
        act.op(lambda h: h.activation(out=ET1[:, tt, :], in_=p_[:, 64:80], func=AF.Exp), r=[pb], w=[scal])
    dve.op(lambda h: h.tensor_tensor(out=A_tok[:], in0=G_tok[:], in1=LNB[:], op=ALU.add), r=[scal], w=[scal])
    act.op(lambda h: h.mul(out=NG_tok[:], in_=G_tok[:], mul=-1.0), r=[scal], w=[scal])
    act.op(lambda h: h.activation(out=BG[:], in_=A_tok[:], func=AF.Exp), r=[scal], w=[scal])
    dve.op(lambda h: h.tensor_tensor(out=KD[:], in0=TOTO[:], in1=G_tok[:], op=ALU.subtract), r=[scal], w=[scal])
    act.op(lambda h: h.activation(out=KD[:], in_=KD[:], func=AF.Exp), r=[scal], w=[scal])
    if "LG" in dbg:
        sp.dma(dbg["LG"].rearrange("(t p) c -> p t c", p=128), LG[:], sl_dbg, r=[scal])
        sp.dma(dbg["BETA"].rearrange("(t p) c -> p t c", p=128), BETA[:], sl_dbg, r=[scal])
        sp.dma(dbg["G_tok"].rearrange("(t p) c -> p t c", p=128), G_tok[:], sl_dbg, r=[scal])

    free_to(s1_mark)
    gdn_mark = len(allocs)
    if gdn_heads > 0:
        pre = sb("pre", [128, T + 4]); preb = Buf("pre")
        pool.op(lambda h: h.memset(pre[:, 0:2], 0.0), w=[preb]); pool.op(lambda h: h.memset(pre[:, T + 2:T + 4], 0.0), w=[preb])
        acc = sb("acc", [128, T]); accb = Buf("acc")
        sqt = sb("sqt", [128, 512], BF16); sqtb = Buf("sqt")
        rn = sb("rn", [128, 512]); rnb = Buf("rn")
        HB = []
        for bs in range(2):
            HB.append(dict(qT=sb(f"qT{bs}", [128, T], BF16), qTb=Buf(f"qT{bs}"), kT=sb(f"kT{bs}", [128, T], BF16), kTb=Buf(f"kT{bs}"),
                           k_tok=sb(f"k_tok{bs}", [128, NT, 128], BF16), ktb=Buf(f"k_tok{bs}"),
                           v_tok=sb(f"v_tok{bs}", [128, NT, 128], BF16), vtb=Buf(f"v_tok{bs}")))
        vT = sb("vT", [128, T], BF16); vTb = Buf("vT")
        zs = sb("zs", [128, T], BF16); zsb = Buf("zs")
        oTa = sb("oTa", [128, T]); oTab = Buf("oTa")
        NSL = 2
        WK = {}
        ST = {}
        for d_ in range(2):
            for sl_i in range(NSL):
                W_ = {}
                for nm, shp, dt in [("rhs1", [128, 512], F32R), ("rhsb", [128, 128], F32R), ("D1", [128, 128], F32), ("D1T", [128, 128], F32),
                                    ("D2T", [128, 128], BF16), ("grep", [128, 128], F32),
                                    ("LU", [128, 2, 256], BF16),
                                    ("LU2", [128, 2, 256], BF16),
                                    ("attnT", [128, 128], BF16), ("qdT", [128, 128], BF16), ("khat", [128, 128], BF16), ("bv", [128, 128], BF16),
                                    ("kd", [128, 128], BF16), ("wT", [128, 128], BF16), ("u", [128, 128], F32), ("vnew", [128, 128], BF16)]:
                    W_[nm] = sb(f"{nm}_{d_}_{sl_i}", shp, dt)
                    W_[nm + "_b"] = Buf(f"{nm}_{d_}_{sl_i}")
                WK[(d_, sl_i)] = W_
            S_ = {}
            for nm, shp, dt in [("S32", [128, 128], F32), ("S16", [128, 128], BF16)]:
                S_[nm] = sb(f"{nm}_{d_}", shp, dt)
                S_[nm + "_b"] = Buf(f"{nm}_{d_}")
            ST[d_] = S_

    def gdn_projA(hh, B_):
        for which, off in (("q", OFF_Q), ("k", OFF_K), ("v", OFF_V)):
            wt, wb = load_wblk(w_in_d[0, :, off + hh * 128: off + (hh + 1) * 128])
            for tg in range(4):
                ipj = yield from acquire("proj"); p_, pb = ps[ipj], psb[ipj]
                proj_fm(wt, wb, tg, p_, pb)
                act.op(lambda h: h.copy(out=pre[:, 2 + tg * 512: 2 + (tg + 1) * 512], in_=p_[:, :]), r=[pb], w=[preb])
                release(ipj)
                yield
            blk = off // 128 + hh
            dve.op(lambda h: h.tensor_scalar(out=acc[:], in0=pre[:, 0:T], scalar1=cw[:, blk, 0:1], scalar2=None, op0=ALU.mult), r=[preb, prm], w=[accb])
            for t_ in range(1, 5):
                dve.op(lambda h: h.scalar_tensor_tensor(out=acc[:], in0=pre[:, t_:t_ + T], scalar=cw[:, blk, t_:t_ + 1], in1=acc[:], op0=ALU.mult, op1=ALU.add),
                       r=[preb, prm, accb], w=[accb])
                yield
            if which == "v":
                act.op(lambda h: h.activation(out=vT[:], in_=acc[:], func=AF.Silu), r=[accb], w=[vTb])
                continue
            act.op(lambda h: h.activation(out=acc[:], in_=acc[:], func=AF.Silu), r=[accb], w=[accb])
            dstT, dstb = (B_["qT"], B_["qTb"]) if which == "q" else (B_["kT"], B_["kTb"])
            post = (128.0 ** -0.5) if which == "q" else 1.0
            for tg in range(4):
                sl_ = slice(tg * 512, (tg + 1) * 512)
                act.op(lambda h: h.activation(out=sqt[:], in_=acc[:, sl_], func=AF.Square), r=[accb], w=[sqtb])
                ipj = yield from acquire("proj"); p_, pb = ps[ipj], psb[ipj]
                pe.op(lambda h: h.matmul(p_[:, :], lhsT=ones_bf[:], rhs=sqt[:], start=True, stop=True), r=[cst, sqtb], w=[pb])
                yield
                act.op(lambda h: h.activation(out=rn[:], in_=p_[:, :], func=AF.Ln, bias=EPS), r=[pb], w=[rnb])
                release(ipj)
                act.op(lambda h: h.activation(out=rn[:], in_=rn[:], func=AF.Exp, scale=-0.5), r=[rnb], w=[rnb])
                dve.op(lambda h: h.scalar_tensor_tensor(out=dstT[:, sl_], in0=acc[:, sl_], scalar=post, in1=rn[:], op0=ALU.mult, op1=ALU.mult),
                       r=[accb, rnb], w=[dstb])
                yield
        for srcT, srcb, dst, dstb in ((B_["kT"], B_["kTb"], B_["k_tok"], B_["ktb"]), (vT, vTb, B_["v_tok"], B_["vtb"])):
            for g8 in range(2):
                for j in range(8):
                    tt = g8 * 8 + j
                    pe.op(lambda h: h.transpose(pst[:, j * 128:(j + 1) * 128], srcT[:, tt * 128:(tt + 1) * 128], ident_bf[:]), r=[srcb, cst], w=[pstb])
                act.op(lambda h: h.copy(out=dst[:, g8 * 8:(g8 + 1) * 8, :], in_=pst[:].rearrange("p (j c) -> p j c", j=8)), r=[pstb], w=[dstb])
                yield

    def gdn_projZ(hh):
        wt, wb = load_wblk(w_in_d[0, :, OFF_Z + hh * 128: OFF_Z + (hh + 1) * 128])
        for tg in range(4):
            ipj = yield from acquire("proj"); p_, pb = ps[ipj], psb[ipj]
            proj_fm(wt, wb, tg, p_, pb)
            act.op(lambda h: h.activation(out=zs[:, tg * 512:(tg + 1) * 512], in_=p_[:, :], func=AF.Silu), r=[pb], w=[zsb])
            release(ipj)
            yield

    def gdn_norm(hh):
        if f"oa{hh}" in dbg:
            sp.dma(dbg[f"oa{hh}"], oTa[:], sl_dbg, r=[oTab])
        for tg in range(4):
            sl_ = slice(tg * 512, (tg + 1) * 512)
            act.op(lambda h: h.activation(out=sqt[:], in_=oTa[:, sl_], func=AF.Square), r=[oTab], w=[sqtb])
            ipj = yield from acquire("proj")
            p_, pb = ps[ipj], psb[ipj]
            pe.op(lambda h: h.matmul(p_[:, :], lhsT=ones_bf[:], rhs=sqt[:], start=True, stop=True), r=[cst, sqtb], w=[pb])
            yield
            act.op(lambda h: h.activation(out=rn[:], in_=p_[:, :], func=AF.Ln, bias=EPS, scale=1.0 / 128), r=[pb], w=[rnb])
            release(ipj)
            act.op(lambda h: h.activation(out=rn[:], in_=rn[:], func=AF.Exp, scale=-0.5), r=[rnb], w=[rnb])
            dve.op(lambda h: h.scalar_tensor_tensor(out=rn[:], in0=oTa[:, sl_], scalar=gdn_nw[:, 0:1], in1=rn[:], op0=ALU.mult, op1=ALU.mult),
                   r=[oTab, rnb, prm], w=[rnb])
            dve.op(lambda h: h.tensor_tensor(out=oaT[:, hh, sl_], in0=rn[:], in1=zs[:, sl_], op=ALU.mult), r=[rnb, zsb], w=[oaTb])
            yield

    all_tasks = []
    PRA, PRZ, NORM = [], [], []
    for hh in range(gdn_heads):
        B_ = HB[hh % 2]
        pra = Task(gdn_projA(hh, B_), deps=[PRA[hh - 1] if hh >= 1 else None, NORM[hh - 2] if hh >= 2 else None, PRZ[hh - 1] if hh >= 1 else None])
        PRA.append(pra)
        all_tasks.append(pra)
        prz = Task(gdn_projZ(hh), deps=[pra, NORM[hh - 1] if hh >= 1 else None])
        PRZ.append(prz)
        def gdn_prep(d_, tt, W_, hh=hh, B_=B_):
            qT, qTb, kT, kTb, k_tok, ktb, v_tok, vtb = B_["qT"], B_["qTb"], B_["kT"], B_["kTb"], B_["k_tok"], B_["ktb"], B_["v_tok"], B_["vtb"]
            col = d_ * 8 + hh
            tsl = slice(tt * 128, (tt + 1) * 128)
            act.op(lambda h: h.activation(out=W_["rhs1"][:].rearrange("p (a b) -> p a b", a=4), in_=TRI[d_][:, :].unsqueeze(1).to_broadcast([128, 4, 128]),
                                          func=AF.Copy, scale=LG[:, tt, col:col + 1]), r=[cst, scal], w=[W_["rhs1_b"]])
            act.op(lambda h: h.activation(out=W_["rhsb"][:], in_=ident32[:], func=AF.Copy, scale=LNB[:, tt, col:col + 1]),
                   r=[cst, scal], w=[W_["rhsb_b"]])
            act.op(lambda h: h.activation(out=W_["khat"][:], in_=k_tok[:, tt, :], func=AF.Copy, scale=BG[:, tt, col:col + 1]),
                   r=[ktb, scal], w=[W_["khat_b"]])
            act.op(lambda h: h.activation(out=W_["bv"][:], in_=v_tok[:, tt, :], func=AF.Copy, scale=BETA[:, tt, col:col + 1]),
                   r=[vtb, scal], w=[W_["bv_b"]])
            act.op(lambda h: h.activation(out=W_["kd"][:], in_=k_tok[:, tt, :], func=AF.Copy, scale=KD[:, tt, col:col + 1]),
                   r=[ktb, scal], w=[W_["kd_b"]])
            yield
            ia = yield from acquire("prep")
            pa, pab = ps[ia], psb[ia]
            pe.op(lambda h: h.matmul(pa[:, :], lhsT=ones_r[:], rhs=W_["rhs1"][:], start=True, stop=False), r=[cst, W_["rhs1_b"]], w=[pab])
            pe.op(lambda h: h.matmul(pa[:, :], lhsT=ident_r[:], rhs=MASKS_R[d_][:], start=False, stop=False), r=[cst], w=[pab])
            pe.op(lambda h: h.matmul(pa[:, 128:256], lhsT=ones_r[:], rhs=W_["rhsb"][:], start=False, stop=True), r=[cst, W_["rhsb_b"]], w=[pab])
            yield
            act.op(lambda h: h.activation(out=W_["D1"][:], in_=pa[:, 0:128], func=AF.Exp, bias=A_tok[:, tt, col:col + 1], scale=-1.0),
                   r=[pab, scal], w=[W_["D1_b"]])
            act.op(lambda h: h.activation(out=W_["D1T"][:], in_=pa[:, 128:256], func=AF.Exp, bias=NG_tok[:, tt, col:col + 1], scale=1.0),
                   r=[pab, scal], w=[W_["D1T_b"]])
            act.op(lambda h: h.activation(out=W_["D2T"][:], in_=pa[:, 256:384], func=AF.Exp, bias=NG_tok[:, tt, col:col + 1], scale=1.0),
                   r=[pab, scal], w=[W_["D2T_b"]])
            act.op(lambda h: h.activation(out=W_["grep"][:], in_=pa[:, 384:512], func=AF.Exp), r=[pab], w=[W_["grep_b"]])
            release(ia)
            yield
            ikq = yield from acquire("prep")
            pkq, pkqb = ps[ikq], psb[ikq]
            pe.op(lambda h: h.matmul(pkq[:, 0:128], lhsT=kT[:, tsl], rhs=kT[:, tsl], start=True, stop=True), r=[kTb], w=[pkqb])
            pe.op(lambda h: h.matmul(pkq[:, 128:256], lhsT=kT[:, tsl], rhs=qT[:, tsl], start=True, stop=True), r=[kTb, qTb], w=[pkqb])
            yield
            LU, LUb, LU2, LU2b = W_["LU"], W_["LU_b"], W_["LU2"], W_["LU2_b"]
            dve.op(lambda h: h.tensor_tensor(out=LU[:, 1, 0:128], in0=pkq[:, 0:128], in1=W_["D1"][:], op=ALU.mult), r=[pkqb, W_["D1_b"]], w=[LUb])
            dve.op(lambda h: h.tensor_tensor(out=LU[:, 0, 0:128], in0=pkq[:, 0:128], in1=W_["D1T"][:], op=ALU.mult), r=[pkqb, W_["D1T_b"]], w=[LUb])
            dve.op(lambda h: h.tensor_tensor(out=LU2[:, 0, 128:256], in0=ident32[:], in1=LU[:, 0, 0:128], op=ALU.subtract), r=[cst, LUb], w=[LU2b])
            dve.op(lambda h: h.tensor_tensor(out=W_["attnT"][:], in0=pkq[:, 128:256], in1=W_["D2T"][:], op=ALU.mult), r=[pkqb, W_["D2T_b"]], w=[W_["attnT_b"]])
            dve.op(lambda h: h.tensor_tensor(out=W_["qdT"][:], in0=qT[:, tsl], in1=W_["grep"][:], op=ALU.mult), r=[qTb, W_["grep_b"]], w=[W_["qdT_b"]])
            release(ikq)
            yield
            cur, curb, nxt, nxtb = LU, LUb, LU2, LU2b
            for lev in range(6):
                ipn = yield from acquire("prep")
                pn, pnb = ps[ipn], psb[ipn]
                if lev == 0:
                    pe.op(lambda h: h.matmul(pn[:, 0:128], lhsT=cur[:, 1, 0:128], rhs=cur[:, 0, 0:128], start=True, stop=True), r=[curb], w=[pnb])
                elif lev < 5:
                    pe.op(lambda h: h.matmul(pn[:, 0:256], lhsT=cur[:, 1, 0:128], rhs=cur[:, 0, 0:256], start=True, stop=True), r=[curb], w=[pnb])
                else:
                    pe.op(lambda h: h.matmul(pn[:, 128:256], lhsT=cur[:, 1, 0:128], rhs=cur[:, 0, 128:256], start=True, stop=True), r=[curb], w=[pnb])
                if lev < 5:
                    pe.op(lambda h: h.matmul(pn[:, 256:384], lhsT=cur[:, 0, 0:128], rhs=cur[:, 1, 0:128], start=True, stop=True), r=[curb], w=[pnb])
                yield
                if lev < 5:
                    act.op(lambda h: h.copy(out=nxt[:, :, 0:128], in_=pn[:, :].rearrange("p (a b) -> p a b", a=2)[:, :, 0:128]), r=[pnb], w=[nxtb])
                if lev > 0:
                    dve.op(lambda h: h.tensor_tensor(out=nxt[:, 0, 128:256], in0=pn[:, 128:256], in1=cur[:, 0, 128:256], op=ALU.add), r=[pnb, curb], w=[nxtb])
                cur, curb, nxt, nxtb = nxt, nxtb, cur, curb
                release(ipn)
                yield
            Wm = cur[:, 0, 128:256]; Wmb = curb
            ipw = yield from acquire("prep")
            pw, pwb = ps[ipw], psb[ipw]
            pe.op(lambda h: h.matmul(pw[:, 0:128], lhsT=W_["khat"][:], rhs=Wm, start=True, stop=True), r=[W_["khat_b"], Wmb], w=[pwb])
            pe.op(lambda h: h.matmul(pw[:, 128:256], lhsT=Wm, rhs=W_["bv"][:], start=True, stop=True), r=[W_["bv_b"], Wmb], w=[pwb])
            yield
            act.op(lambda h: h.copy(out=W_["wT"][:], in_=pw[:, 0:128]), r=[pwb], w=[W_["wT_b"]])
            act.op(lambda h: h.copy(out=W_["u"][:], in_=pw[:, 128:256]), r=[pwb], w=[W_["u_b"]])
            release(ipw)
            yield

        def gdn_scan(d_, tt, W_, S_, hh=hh):
            col = d_ * 8 + hh
            for c in ((0, 1) if d_ == 0 else (1, 0)):
                rs = slice(c * 64, (c + 1) * 64)
                ETc = ET0 if c == 0 else ET1
                ip1 = yield from acquire("scan")
                p1, p1b = ps[ip1], psb[ip1]
                pe.op(lambda h: h.matmul(p1[rs, 0:128], lhsT=W_["wT"][:, rs], rhs=S_["S16"][:], start=True, stop=True), r=[W_["wT_b"], S_["S16_b"]], w=[p1b])
                yield
                dve.op(lambda h: h.tensor_tensor(out=W_["vnew"][rs, :], in0=W_["u"][rs, :], in1=p1[rs, 0:128], op=ALU.subtract),
                       r=[W_["u_b"], p1b], w=[W_["vnew_b"]])
                release(ip1)
                yield
                ip2 = yield from acquire("scan")
                p2, p2b = ps[ip2], psb[ip2]
                pe.op(lambda h: h.matmul(p2[:, 0:64], lhsT=S_["S16"][:], rhs=W_["qdT"][:, rs], start=True, stop=False), r=[S_["S16_b"], W_["qdT_b"]], w=[p2b])
                pe.op(lambda h: h.matmul(p2[:, 0:64], lhsT=W_["vnew"][rs, :], rhs=W_["attnT"][rs, rs], start=False, stop=True),
                      r=[W_["vnew_b"], W_["attnT_b"]], w=[p2b])
                pe.op(lambda h: h.matmul(p2[:, 128:256], lhsT=W_["kd"][rs, :], rhs=W_["vnew"][rs, :], start=True, stop=True),
                      r=[W_["kd_b"], W_["vnew_b"]], w=[p2b])
                yield
                dve.op(lambda h: h.scalar_tensor_tensor(out=S_["S32"][:], in0=S_["S32"][:], scalar=ETc[:, tt, col:col + 1], in1=p2[:, 128:256],
                                                        op0=ALU.mult, op1=ALU.add), r=[S_["S32_b"], scal, p2b], w=[S_["S32_b"]])
                act.op(lambda h: h.copy(out=S_["S16"][:], in_=S_["S32"][:]), r=[S_["S32_b"]], w=[S_["S16_b"]])
                osl = slice(tt * 128 + c * 64, tt * 128 + (c + 1) * 64)
                first = (tt < NT // 2) if d_ == 0 else (tt >= NT // 2)
                if first:
                    act.op(lambda h: h.copy(out=oTa[:, osl], in_=p2[:, 0:64]), r=[p2b], w=[oTab])
                else:
                    dve.op(lambda h: h.tensor_tensor(out=oTa[:, osl], in0=p2[:, 0:64], in1=oTa[:, osl], op=ALU.add), r=[p2b, oTab], w=[oTab])
                release(ip2)
                yield

        def gdn_init():
            for d_ in range(2):
                pool.op(lambda h: h.memset(ST[d_]["S32"][:], 0.0), w=[ST[d_]["S32_b"]])
                pool.op(lambda h: h.memset(ST[d_]["S16"][:], 0.0), w=[ST[d_]["S16_b"]])
            yield
        init_t = Task(gdn_init(), deps=[NORM[hh - 1] if hh >= 1 else None])
        all_tasks.append(init_t)
        order = {0: list(range(NT)), 1: list(range(NT - 1, -1, -1))}
        P = {0: [], 1: []}
        S = {0: [], 1: []}
        tasks = all_tasks
        for i in range(NT):
            for d_ in range(2):
                tt = order[d_][i]
                W_ = WK[(d_, i % NSL)]
                pt = Task(gdn_prep(d_, tt, W_), deps=[S[d_][i - NSL] if i >= NSL else None, PRA[hh], init_t])
                P[d_].append(pt)
                tasks.append(pt)
            for d_ in range(2):
                tt = order[d_][i]
                W_ = WK[(d_, i % NSL)]
                other = S[1 - d_][NT - 1 - i] if (i >= NT // 2 and len(S[1 - d_]) > NT - 1 - i) else None
                stt = Task(gdn_scan(d_, tt, W_, ST[d_]), deps=[P[d_][i], S[d_][i - 1] if i >= 1 else None, other])
                S[d_].append(stt)
                tasks.append(stt)
        all_tasks.append(prz)
        nt = Task(gdn_norm(hh), deps=[prz] + S[0] + S[1])
        NORM.append(nt)
        all_tasks.append(nt)
    if gdn_heads > 0:
        for hh in range(gdn_heads - 1):
            NORM[hh].deps.append(PRA[hh + 1])
        run_tasks(all_tasks, max_active=MAXACT + 2)
    if "oaT" in dbg:
        tmp = sb("dbg_tmp3", [128, T]); tb = Buf("dbg_tmp3")
        for k in range(8):
            act.op(lambda h: h.copy(out=tmp[:], in_=oaT[:, k, :]), r=[oaTb], w=[tb])
            sp.dma(dbg["oaT"][k * 128:(k + 1) * 128, :], tmp[:], sl_dbg, r=[tb])
    free_to(s1_mark)

    obT = sb("obT", [128, 8, T], BF16); obTb = Buf("obT")
    gla_mark = len(allocs)
    if gla_heads > 0:
        NSG = 2
        GW = {}
        GS_ = {}
        for d_ in range(2):
            for sl_i in range(NSG):
                W_ = {}
                for nm, shp, dt in [("gk", [128, 128], F32), ("ekd", [128, 128], F32), ("eg", [128, 128], F32), ("eng", [128, 128], F32), ("etot", [128, 1], F32),
                                    ("kdg", [128, 128], BF16), ("qgT", [128, 128], BF16), ("kgT", [128, 128], BF16), ("attnT", [128, 128], BF16)]:
                    W_[nm] = sb(f"g{nm}_{d_}_{sl_i}", shp, dt)
                    W_[nm + "_b"] = Buf(f"g{nm}_{d_}_{sl_i}")
                GW[(d_, sl_i)] = W_
            S_ = {}
            for nm, shp, dt in [("S32", [128, 256], F32), ("S16", [128, 256], BF16)]:
                S_[nm] = sb(f"g{nm}_{d_}", shp, dt)
                S_[nm + "_b"] = Buf(f"g{nm}_{d_}")
            GS_[d_] = S_

        rT1 = sb("rT1", [33, T]); rT1b = Buf("rT1")
        wr = sb("wr", [128, 8, 32], BF16); wrb = Buf("wr")
        sl_wr = Slot(nc, "wr")
        pool.dma(wr[:], w_in_d[0, :, OFF_R:OFF_R + 32].rearrange("(k p) c -> p k c", p=128), sl_wr, w=[wrb])
        pool.op(lambda h: h.memset(rT1[32:33, :], 1.0), w=[rT1b])
        for tg in range(4):
            p_, pb = next_ps()
            for k in range(8):
                pe.op(lambda h: h.matmul(p_[0:32, :], lhsT=wr[:, k, :], rhs=hT[:, k, tg * 512:(tg + 1) * 512], start=(k == 0), stop=(k == 7)),
                      r=[wrb, hTb], w=[pb])
            act.op(lambda h: h.copy(out=rT1[0:32, tg * 512:(tg + 1) * 512], in_=p_[0:32, :]), r=[pb], w=[rT1b])
        wkv = sb("wkv", [128, 8, 384], BF16); wkvb = Buf("wkv"); sl_wkv = Slot(nc, "wkv")
        qTg = sb("qTg", [128, T], BF16); qTgb = Buf("qTg")
        kTg = sb("kTg", [128, T], BF16); kTgb = Buf("kTg")
        kg_tok = sb("kg_tok", [128, NT, 128], BF16); kgtb = Buf("kg_tok")
        vg_tok = sb("vg_tok", [128, NT, 256], BF16); vgtb = Buf("vg_tok")
        gsT = sb("gsT", [128, 2, T], BF16); gsTb = Buf("gsT")
        obTa = sb("obTa", [128, 2, T]); obTab = Buf("obTa")
        sqg = sb("sqg", [128, 512], BF16); sqgb = Buf("sqg")
        rng = sb("rng", [128, 512]); rngb = Buf("rng")
    for hb in range(gla_heads):
        for off, dst, dstb, scl in ((OFF_QB, qTg, qTgb, 128.0 ** -0.5), (OFF_KB, kTg, kTgb, 1.0)):
            wt, wb = load_wblk(w_in_d[0, :, off + hb * 128: off + (hb + 1) * 128])
            for tg in range(4):
                p_, pb = next_ps()
                proj_fm(wt, wb, tg, p_, pb)
                act.op(lambda h: h.mul(out=dst[:, tg * 512:(tg + 1) * 512], in_=p_[:, :], mul=scl), r=[pb], w=[dstb])
        for eb in range(2):
            wt, wb = load_wblk(w_in_d[0, :, OFF_GB + hb * 256 + eb * 128: OFF_GB + hb * 256 + (eb + 1) * 128])
            for tg in range(4):
                p_, pb = next_ps()
                proj_fm(wt, wb, tg, p_, pb)
                act.op(lambda h: h.activation(out=gsT[:, eb, tg * 512:(tg + 1) * 512], in_=p_[:, :], func=AF.Silu), r=[pb], w=[gsTb])
        pool.dma(wkv[:, :, 0:128], w_in_d[0, :, OFF_KB + hb * 128: OFF_KB + (hb + 1) * 128].rearrange("(k p) c -> p k c", p=128), sl_wkv, w=[wkvb])
        pool.dma(wkv[:, :, 128:384], w_in_d[0, :, OFF_VB + hb * 256: OFF_VB + (hb + 1) * 256].rearrange("(k p) c -> p k c", p=128), sl_wkv, w=[wkvb])
        for tt in range(NT):
            p_, pb = next_ps()
            for k in range(8):
                pe.op(lambda h: h.matmul(p_[:, 0:384], lhsT=hT[:, k, tt * 128:(tt + 1) * 128], rhs=wkv[:, k, :], start=(k == 0), stop=(k == 7)),
                      r=[hTb, wkvb], w=[pb])
            act.op(lambda h: h.copy(out=kg_tok[:, tt, :], in_=p_[:, 0:128]), r=[pb], w=[kgtb])
            act.op(lambda h: h.copy(out=vg_tok[:, tt, :], in_=p_[:, 128:384]), r=[pb], w=[vgtb])

        def gla_prep(d_, tt, W_, hb=hb):
            tsl = slice(tt * 128, (tt + 1) * 128)
            ix = yield from acquire("gprep")
            px, pxb = ps[ix], psb[ix]
            pe.op(lambda h: h.matmul(px[:, 0:128], lhsT=rT1[:, tsl], rhs=w2cat[:, d_, hb * 128:(hb + 1) * 128], start=True, stop=True),
                  r=[rT1b, prm], w=[pxb])
            yield
            act.op(lambda h: h.activation(out=W_["gk"][:], in_=px[:, 0:128], func=AF.Exp, scale=-1.0), r=[pxb], w=[W_["gk_b"]])
            release(ix)
            act.op(lambda h: h.activation(out=W_["gk"][:], in_=W_["gk"][:], func=AF.Ln, bias=1.0), r=[W_["gk_b"]], w=[W_["gk_b"]])
            yield
            ig = yield from acquire("gprep")
            pg, pgb = ps[ig], psb[ig]
            pe.op(lambda h: h.matmul(pg[:, 0:128], lhsT=DIFG[d_][:], rhs=W_["gk"][:], start=True, stop=True), r=[cst, W_["gk_b"]], w=[pgb])
            pe.op(lambda h: h.matmul(pg[:, 128:256], lhsT=W_["gk"][:], rhs=TRIG[d_][:], start=True, stop=True), r=[cst, W_["gk_b"]], w=[pgb])
            yield
            act.op(lambda h: h.activation(out=W_["eg"][:], in_=pg[:, 128:256], func=AF.Exp), r=[pgb], w=[W_["eg_b"]])
            act.op(lambda h: h.activation(out=W_["eng"][:], in_=pg[:, 128:256], func=AF.Exp, scale=-1.0), r=[pgb], w=[W_["eng_b"]])
            act.op(lambda h: h.activation(out=W_["ekd"][:], in_=pg[:, 0:128], func=AF.Exp), r=[pgb], w=[W_["ekd_b"]])
            lastc = 128 + (127 if d_ == 0 else 0)
            act.op(lambda h: h.activation(out=W_["etot"][:], in_=pg[:, lastc:lastc + 1], func=AF.Exp), r=[pgb], w=[W_["etot_b"]])
            release(ig)
            yield
            dve.op(lambda h: h.tensor_tensor(out=W_["qgT"][:], in0=qTg[:, tsl], in1=W_["eg"][:], op=ALU.mult), r=[qTgb, W_["eg_b"]], w=[W_["qgT_b"]])
            pool.op(lambda h: h.tensor_tensor(out=W_["kgT"][:], in0=kTg[:, tsl], in1=W_["eng"][:], op=ALU.mult), r=[kTgb, W_["eng_b"]], w=[W_["kgT_b"]])
            dve.op(lambda h: h.tensor_tensor(out=W_["kdg"][:], in0=kg_tok[:, tt, :], in1=W_["ekd"][:], op=ALU.mult), r=[kgtb, W_["ekd_b"]], w=[W_["kdg_b"]])
            yield
            ia = yield from acquire("gprep")
            pa_, pab_ = ps[ia], psb[ia]
            pe.op(lambda h: h.matmul(pa_[:, 0:128], lhsT=W_["kgT"][:], rhs=W_["qgT"][:], start=True, stop=True), r=[W_["kgT_b"], W_["qgT_b"]], w=[pab_])
            yield
            dve.op(lambda h: h.tensor_tensor(out=W_["attnT"][:], in0=pa_[:, 0:128], in1=MASKG[d_][:], op=ALU.mult), r=[pab_, cst], w=[W_["attnT_b"]])
            release(ia)
            yield

        def gla_scan(d_, tt, W_, S_):
            tsl = slice(tt * 128, (tt + 1) * 128)
            io = yield from acquire("gscan")
            po, pob = ps[io], psb[io]
            for eb in range(2):
                pe.op(lambda h: h.matmul(po[:, eb * 128:(eb + 1) * 128], lhsT=S_["S16"][:, eb * 128:(eb + 1) * 128], rhs=W_["qgT"][:], start=True, stop=False),
                      r=[S_["S16_b"], W_["qgT_b"]], w=[pob])
                pe.op(lambda h: h.matmul(po[:, eb * 128:(eb + 1) * 128], lhsT=vg_tok[:, tt, eb * 128:(eb + 1) * 128], rhs=W_["attnT"][:], start=False, stop=True),
                      r=[vgtb, W_["attnT_b"]], w=[pob])
            iS = yield from acquire("gscan")
            pS, pSb = ps[iS], psb[iS]
            pe.op(lambda h: h.matmul(pS[:, 0:256], lhsT=W_["kdg"][:], rhs=vg_tok[:, tt, :], start=True, stop=True), r=[W_["kdg_b"], vgtb], w=[pSb])
            yield
            dve.op(lambda h: h.scalar_tensor_tensor(out=S_["S32"][:], in0=S_["S32"][:], scalar=W_["etot"][:, 0:1], in1=pS[:, 0:256],
                                                    op0=ALU.mult, op1=ALU.add), r=[S_["S32_b"], W_["etot_b"], pSb], w=[S_["S32_b"]])
            release(iS)
            act.op(lambda h: h.copy(out=S_["S16"][:], in_=S_["S32"][:]), r=[S_["S32_b"]], w=[S_["S16_b"]])
            first = (tt < NT // 2) if d_ == 0 else (tt >= NT // 2)
            if first:
                act.op(lambda h: h.copy(out=obTa[:, :, tsl], in_=po[:, 0:256].rearrange("p (e t) -> p e t", e=2)), r=[pob], w=[obTab])
            else:
                dve.op(lambda h: h.tensor_tensor(out=obTa[:, :, tsl], in0=po[:, 0:256].rearrange("p (e t) -> p e t", e=2), in1=obTa[:, :, tsl], op=ALU.add),
                       r=[pob, obTab], w=[obTab])
            release(io)
            yield

        for d_ in range(2):
            pool.op(lambda h: h.memset(GS_[d_]["S32"][:], 0.0), w=[GS_[d_]["S32_b"]])
            pool.op(lambda h: h.memset(GS_[d_]["S16"][:], 0.0), w=[GS_[d_]["S16_b"]])
        order = {0: list(range(NT)), 1: list(range(NT - 1, -1, -1))}
        P = {0: [], 1: []}
        S = {0: [], 1: []}
        tasks = []
        for i in range(NT):
            for d_ in range(2):
                pt = Task(gla_prep(d_, order[d_][i], GW[(d_, i % NSG)]), deps=[S[d_][i - NSG] if i >= NSG else None])
                P[d_].append(pt)
                tasks.append(pt)
            for d_ in range(2):
                other = S[1 - d_][NT - 1 - i] if (i >= NT // 2 and len(S[1 - d_]) > NT - 1 - i) else None
                stt = Task(gla_scan(d_, order[d_][i], GW[(d_, i % NSG)], GS_[d_]), deps=[P[d_][i], S[d_][i - 1] if i >= 1 else None, other])
                S[d_].append(stt)
                tasks.append(stt)
        run_tasks(tasks, max_active=MAXACT)
        if f"ob{hb}" in dbg:
            for eb in range(2):
                sp.dma(dbg[f"ob{hb}"][eb * 128:(eb + 1) * 128, :], obTa[:, eb, :], sl_dbg, r=[obTab])
        for tg in range(4):
            sl_ = slice(tg * 512, (tg + 1) * 512)
            p_, pb = next_ps()
            for eb in range(2):
                act.op(lambda h: h.activation(out=sqg[:], in_=obTa[:, eb, sl_], func=AF.Square), r=[obTab], w=[sqgb])
                pe.op(lambda h: h.matmul(p_[:, :], lhsT=ones_bf[:], rhs=sqg[:], start=(eb == 0), stop=(eb == 1)), r=[cst, sqgb], w=[pb])
            act.op(lambda h: h.activation(out=rng[:], in_=p_[:, :], func=AF.Ln, bias=EPS, scale=1.0 / 256), r=[pb], w=[rngb])
            act.op(lambda h: h.activation(out=rng[:], in_=rng[:], func=AF.Exp, scale=-0.5), r=[rngb], w=[rngb])
            for eb in range(2):
                dve.op(lambda h: h.scalar_tensor_tensor(out=obTa[:, eb, sl_], in0=obTa[:, eb, sl_], scalar=gla_nw[:, eb:eb + 1], in1=rng[:], op0=ALU.mult, op1=ALU.mult),
                       r=[obTab, rngb, prm], w=[obTab])
                dve.op(lambda h: h.tensor_tensor(out=obT[:, hb * 2 + eb, sl_], in0=obTa[:, eb, sl_], in1=gsT[:, eb, sl_], op=ALU.mult), r=[obTab, gsTb], w=[obTb])
    if "obT" in dbg:
        tmp = sb("dbg_tmp4", [128, T]); tb = Buf("dbg_tmp4")
        for k in range(8):
            act.op(lambda h: h.copy(out=tmp[:], in_=obT[:, k, :]), r=[obTb], w=[tb])
            sp.dma(dbg["obT"][k * 128:(k + 1) * 128, :], tmp[:], sl_dbg, r=[tb])
    free_to(gla_mark)

    if do_final:
        mT = sb("mT", [128, 8, T], BF16); mTb = Buf("mT")
        fin_mark = len(allocs)
        sga = sb("sga", [128, 512]); sgab = Buf("sga")
        sgb = sb("sgb", [128, 512]); sgbb = Buf("sgb")
        t1 = sb("t1", [128, 512]); t1b = Buf("t1")
        t2 = sb("t2", [128, 512]); t2b = Buf("t2")
        NW2 = 8
        wb2 = [sb(f"wb2_{i}", [128, 8, 128], BF16) for i in range(NW2)]; wb2b = [Buf(f"wb2_{i}") for i in range(NW2)]
        sl_w2 = [Slot(nc, f"w2_{i}") for i in range(NW2)]
        rr2 = [0]

        def load2(src_ap):
            i = rr2[0] % NW2
            rr2[0] += 1
            pool.dma(wb2[i][:], src_ap.rearrange("(k p) c -> p k c", p=128), sl_w2[i], w=[wb2b[i]])
            return wb2[i], wb2b[i]

        for m in range(8):
            msl = slice(m * 128, (m + 1) * 128)
            wg, wgb_ = load2(wpg_d[0, :, msl])
            wl, wlb_ = load2(wpl_d[0, :, msl])
            wa, wab_ = load2(w_in_d[0, :, OFF_GA + m * 128: OFF_GA + (m + 1) * 128])
            wbb, wbbb_ = load2(w_in_d[0, :, OFF_GBG + m * 128: OFF_GBG + (m + 1) * 128])
            for tg in range(4):
                sl_ = slice(tg * 512, (tg + 1) * 512)
                pga, pgab = next_ps(); proj_fm(wa, wab_, tg, pga, pgab)
                act.op(lambda h: h.activation(out=sga[:], in_=pga[:, :], func=AF.Sigmoid), r=[pgab], w=[sgab])
                pgb_, pgbb = next_ps(); proj_fm(wbb, wbbb_, tg, pgb_, pgbb)
                act.op(lambda h: h.activation(out=sgb[:], in_=pgb_[:, :], func=AF.Sigmoid), r=[pgbb], w=[sgbb])
                pya, pyab = next_ps(); proj_fm(wg, wgb_, tg, pya, pyab, rhsT=oaT, rb=oaTb)
                dve.op(lambda h: h.tensor_tensor(out=t1[:], in0=pya[:, :], in1=sga[:], op=ALU.mult), r=[pyab, sgab], w=[t1b])
                pyb, pybb = next_ps(); proj_fm(wl, wlb_, tg, pyb, pybb, rhsT=obT, rb=obTb)
                dve.op(lambda h: h.tensor_tensor(out=t2[:], in0=pyb[:, :], in1=sgb[:], op=ALU.mult), r=[pybb, sgbb], w=[t2b])
                dve.op(lambda h: h.tensor_tensor(out=mT[:, m, sl_], in0=t1[:], in1=t2[:], op=ALU.add), r=[t1b, t2b], w=[mTb])
        if "mT" in dbg:
            tmp = sb("dbg_tmp5", [128, T]); tb = Buf("dbg_tmp5")
            for k in range(8):
                act.op(lambda h: h.copy(out=tmp[:], in_=mT[:, k, :]), r=[mTb], w=[tb])
                sp.dma(dbg["mT"][k * 128:(k + 1) * 128, :], tmp[:], sl_dbg, r=[tb])
        free_to(fin_mark)
        wob = hTb; sl_wo = Slot(nc, "wo")
        for k in range(8):
            pool.dma(hT[:, k, 0:D], wout_d[0, k * 128:(k + 1) * 128, :], sl_wo, w=[wob])
        lnp = sb("lnp", [128, D]); sl_lnp = Slot(nc, "lnp"); lnpb = Buf("lnp")
        sp.dma(lnp[:], ln_post_d[0, :].partition_broadcast(128), sl_lnp, w=[lnpb])
        xr = [sb(f"xr{i}", [128, D]) for i in range(2)]; xrb = [Buf(f"xr{i}") for i in range(2)]
        sl_xr = [Slot(nc, f"xr{i}") for i in range(2)]
        ot = [sb(f"ot{i}", [128, D]) for i in range(2)]; otb = [Buf(f"ot{i}") for i in range(2)]
        st1 = sb("st1", [128, 8]); st1b = Buf("st1")
        junk2 = sb("junk2", [128, 512], BF16); junk2b = Buf("junk2")
        for tt in range(NT):
            s = tt % 2
            tsl = slice(tt * 128, (tt + 1) * 128)
            sp.dma(xr[s][:], x_d[tsl, :], sl_xr[s], w=[xrb[s]])
            pp = []
            for half in range(2):
                p_, pb = next_ps()
                for m in range(8):
                    pe.op(lambda h: h.matmul(p_[:, :], lhsT=mT[:, m, tsl], rhs=hT[:, m, half * 512:(half + 1) * 512], start=(m == 0), stop=(m == 7)),
                          r=[mTb, wob], w=[pb])
                act.op(lambda h: h.activation(out=junk2[:], in_=p_[:, :], func=AF.Square, accum_out=st1[:, half:half + 1]), r=[pb], w=[junk2b, st1b])
                pp.append((p_, pb))
            dve.op(lambda h: h.tensor_tensor(out=st1[:, 2:3], in0=st1[:, 0:1], in1=st1[:, 1:2], op=ALU.add), r=[st1b], w=[st1b])
            act.op(lambda h: h.activation(out=st1[:, 3:4], in_=st1[:, 2:3], func=AF.Ln, bias=EPS, scale=1.0 / D), r=[st1b], w=[st1b])
            act.op(lambda h: h.activation(out=st1[:, 4:5], in_=st1[:, 3:4], func=AF.Exp, scale=-0.5), r=[st1b], w=[st1b])
            for half in range(2):
                p_, pb = pp[half]
                hs = slice(half * 512, (half + 1) * 512)
                dve.op(lambda h: h.scalar_tensor_tensor(out=ot[s][:, hs], in0=p_[:, :], scalar=st1[:, 4:5], in1=lnp[:, hs], op0=ALU.mult, op1=ALU.mult),
                       r=[pb, st1b, lnpb], w=[otb[s]])
                dve.op(lambda h: h.tensor_tensor(out=ot[s][:, hs], in0=ot[s][:, hs], in1=xr[s][:, hs], op=ALU.add), r=[otb[s], xrb[s]], w=[otb[s]])
            sp.dma(out_d[tsl, :], ot[s][:], sl_out, r=[otb[s]])
    else:
        zt = sb("zt", [128, D]); ztb = Buf("zt")
        pool.op(lambda h: h.memset(zt[:], 0.0), w=[ztb])
        for tt in range(NT):
            sp.dma(out_d[tt * 128:(tt + 1) * 128, :], zt[:], sl_out, r=[ztb])
    sp.h.wait_ge(sl_out.sem, sl_out.cnt)
    if sl_dbg.cnt:
        sp.h.wait_ge(sl_dbg.sem, sl_dbg.cnt)
    return nc


_INPUT_NAMES = ["ln_pre_w", "w_in", "conv_w", "a_log_fwd", "a_log_bwd", "dt_bias_fwd", "dt_bias_bwd", "gdn_norm_w", "w_proj_gdn",
                "gk_w2_fwd", "gk_b2_fwd", "gk_w2_bwd", "gk_b2_bwd", "gla_norm_w", "w_proj_gla", "w_out", "ln_post_w"]


def kernel(**inputs):
    x = np.ascontiguousarray(np.asarray(inputs["x"], dtype=np.float32))
    shared = {n: np.ascontiguousarray(np.asarray(inputs[n], dtype=np.float32)) for n in _INPUT_NAMES}
    nc = build_nc()
    in_maps = [dict(shared, x=x[b]) for b in range(8)]
    res = run_bass_kernel_spmd(nc, in_maps, core_ids=list(range(8)))
    return np.stack([np.asarray(r["out"], dtype=np.float32) for r in res.results], axis=0)
```

```python
import numpy as np
import concourse.bass as bass
import concourse.mybir as mybir
from concourse.bass_utils import run_bass_kernel_spmd

F32 = mybir.dt.float32
F32R = mybir.dt.float32r
BF16 = mybir.dt.bfloat16
AF = mybir.ActivationFunctionType
ALU = mybir.AluOpType

T = 2048
NT = 16
D = 1024
NIN = 9280
EPS = 1e-6
BIG = 30000.0
MAXACT = 6
OFF_Q, OFF_K, OFF_V, OFF_Z = 0, 1024, 2048, 3072
OFF_SM = 4096
OFF_QB, OFF_KB, OFF_VB, OFF_GB = 4128, 4640, 5152, 6176
OFF_R = 7200
OFF_GA, OFF_GBG = 7232, 8256


class Ev:
    __slots__ = ("sem", "val", "key")

    def __init__(self, sem, val, key):
        self.sem, self.val, self.key = sem, val, key


class Buf:
    __slots__ = ("name", "wev", "revs")

    def __init__(self, name):
        self.name, self.wev, self.revs = name, None, {}


class Slot:
    registry = None

    def __init__(self, nc, name):
        if Slot.registry is not None:
            Slot.registry.append(self)
        self.name = name
        self.sem = nc.semaphore("ds_" + name).__enter__()
        self.cnt = 0


class Eng:
    def __init__(self, nc, name, h, selfsync):
        self.name, self.h, self.selfsync = name, h, selfsync
        self.sem = nc.semaphore("es_" + name).__enter__()
        self.cnt = 0
        self.seen = {}

    def wait(self, ev):
        if ev is None:
            return
        if ev.sem is self.sem and not self.selfsync:
            return
        if self.seen.get(ev.key, 0) >= ev.val:
            return
        self.h.wait_ge(ev.sem, ev.val)
        self.seen[ev.key] = ev.val

    def _deps(self, r, w):
        for b in r:
            self.wait(b.wev)
        for b in w:
            self.wait(b.wev)
            for ev in b.revs.values():
                self.wait(ev)

    def _mark(self, ev, r, w):
        for b in r:
            b.revs[ev.key] = ev
        for b in w:
            b.wev = ev
            b.revs = {}

    def op(self, fn, r=(), w=()):
        self._deps(r, w)
        ins = fn(self.h)
        self.cnt += 1
        ins.then_inc(self.sem, 1)
        ev = Ev(self.sem, self.cnt, self.name)
        self._mark(ev, r, w)
        return ev

    def dma(self, out, in_, slot, r=(), w=(), **kw):
        self._deps(r, w)
        ins = self.h.dma_start(out=out, in_=in_, **kw)
        slot.cnt += 16
        ins.then_inc(slot.sem, 16)
        ev = Ev(slot.sem, slot.cnt, slot.name)
        self._mark(ev, r, w)
        return ev


def build_nc(debug=(), gdn_heads=8, gla_heads=4, do_final=True):
    nc = bass.Bass("TRN2", target_bir_lowering=False)
    dram_in = lambda n, s: nc.dram_tensor(n, s, F32, kind="ExternalInput").ap()
    x_d = dram_in("x", [T, D])
    ln_pre_d = dram_in("ln_pre_w", [1, D])
    w_in_d = dram_in("w_in", [1, D, NIN])
    conv_d = dram_in("conv_w", [1, 5, 3072])
    alog_f_d = dram_in("a_log_fwd", [1, 8]); alog_b_d = dram_in("a_log_bwd", [1, 8])
    dtb_f_d = dram_in("dt_bias_fwd", [1, 8]); dtb_b_d = dram_in("dt_bias_bwd", [1, 8])
    gdn_nw_d = dram_in("gdn_norm_w", [1, 128])
    wpg_d = dram_in("w_proj_gdn", [1, D, D])
    gkw_f_d = dram_in("gk_w2_fwd", [1, 16, 512]); gkb_f_d = dram_in("gk_b2_fwd", [1, 512])
    gkw_b_d = dram_in("gk_w2_bwd", [1, 16, 512]); gkb_b_d = dram_in("gk_b2_bwd", [1, 512])
    gla_nw_d = dram_in("gla_norm_w", [1, 256])
    wpl_d = dram_in("w_proj_gla", [1, D, D])
    wout_d = dram_in("w_out", [1, D, D])
    ln_post_d = dram_in("ln_post_w", [1, D])
    out_d = nc.dram_tensor("out", [T, D], F32, kind="ExternalOutput").ap()
    dbg = {}
    for name, shape in debug:
        dbg[name] = nc.dram_tensor("dbg_" + name, shape, F32, kind="ExternalOutput").ap()

    pe = Eng(nc, "pe", nc.tensor, False)
    act = Eng(nc, "act", nc.scalar, True)
    dve = Eng(nc, "dve", nc.vector, True)
    pool = Eng(nc, "pool", nc.gpsimd, True)
    sp = Eng(nc, "sp", nc.sync, False)

    allocs = []

    def sb(name, shape, dt=F32):
        cm = nc.sbuf_tensor(name, shape, dt)
        t = cm.__enter__()
        allocs.append(cm)
        return t

    all_slots = []
    Slot.registry = all_slots

    def barrier():
        engs = (pe, act, dve, pool, sp)
        for e in engs:
            for f in engs:
                if f is not e and f.cnt:
                    e.wait(Ev(f.sem, f.cnt, f.name))
            for sl in all_slots:
                if sl.cnt:
                    e.wait(Ev(sl.sem, sl.cnt, sl.name))

    def free_to(mark):
        barrier()
        while len(allocs) > mark:
            allocs.pop().__exit__(None, None, None)

    NPS = 7
    ps = [nc.psum_tensor(f"ps{i}", [128, 512], F32).__enter__() for i in range(NPS)]
    psb = [Buf(f"ps{i}") for i in range(NPS)]
    pst = nc.psum_tensor("pst", [128, 1024], BF16).__enter__()
    pstb = Buf("pst")
    ps_rr = {}
    PS_POOLS = {"any": list(range(NPS)), "prep": [0, 1, 2], "scan": [3, 4], "proj": [5, 6], "gprep": [0, 1, 2], "gscan": [3, 4, 5, 6]}

    def next_ps(pool_="any"):
        lst = PS_POOLS[pool_]
        c = ps_rr.get(pool_, 0)
        ps_rr[pool_] = c + 1
        i = lst[c % len(lst)]
        return ps[i], psb[i]

    ps_busy = [False] * NPS

    def acquire(pool_):
        while True:
            for i in PS_POOLS[pool_]:
                if not ps_busy[i]:
                    ps_busy[i] = True
                    return i
            yield

    def release(i):
        ps_busy[i] = False

    class Task:
        def __init__(self, gen, deps=()):
            self.gen, self.deps, self.done = gen, [d for d in deps if d is not None], False

    stall = [0]

    def run_tasks(tasks, max_active=6):
        pending = list(tasks)
        active = []
        while pending or active:
            for t in list(pending):
                if len(active) >= max_active:
                    break
                if all(d.done for d in t.deps):
                    pending.remove(t)
                    active.append(t)
            assert active, "task deadlock"
            before = (pe.cnt, act.cnt, dve.cnt, pool.cnt, len(active), len(pending))
            for t in list(active):
                try:
                    next(t.gen)
                except StopIteration:
                    t.done = True
                    active.remove(t)
            if before == (pe.cnt, act.cnt, dve.cnt, pool.cnt, len(active), len(pending)):
                stall[0] += 1
                assert stall[0] < 200, ("scheduler stall", [getattr(t.gen, "__name__", "?") for t in active], list(ps_busy), len(pending))
            else:
                stall[0] = 0

    cst = Buf("const")
    ident32 = sb("ident32", [128, 128]); ident_bf = sb("ident_bf", [128, 128], BF16)
    ones32 = sb("ones32", [128, 128]); ones_bf = sb("ones_bf", [128, 128], BF16)
    zeros32 = sb("zeros32", [128, 128])
    pool.op(lambda h: h.memset(ones32[:], 1.0), w=[cst])
    pool.op(lambda h: h.memset(zeros32[:], 0.0), w=[cst])
    pool.op(lambda h: h.affine_select(out=ident32[:], in_=zeros32[:], pattern=[[-1, 128]], compare_op=ALU.not_equal,
                                      fill=1.0, base=0, channel_multiplier=1), r=[cst], w=[cst])
    pool.op(lambda h: h.tensor_copy(out=ident_bf[:], in_=ident32[:]), r=[cst], w=[cst])
    pool.op(lambda h: h.tensor_copy(out=ones_bf[:], in_=ones32[:]), r=[cst], w=[cst])

    src_cache = {}

    def tri_const(name, n, inval, fillval, offval, step, cm, cmp):
        t = sb(name, [128, 128])
        pool.op(lambda h: h.memset(t[:], offval), w=[cst])
        if inval not in src_cache:
            src_cache[inval] = sb(f"src_{len(src_cache)}", [128, 128])
            pool.op(lambda h: h.memset(src_cache[inval][:], inval), w=[cst])
        src = src_cache[inval]
        for b0 in range(0, 128, n):
            pool.op(lambda h: h.affine_select(out=t[b0:b0 + n, b0:b0 + n], in_=src[b0:b0 + n, b0:b0 + n], pattern=[[step, n]],
                                              compare_op=cmp, fill=fillval, base=0, channel_multiplier=cm), r=[cst], w=[cst])
        return t

    TRI = {
        0: tri_const("tri_f", 64, 1.0, 0.0, 0.0, 1, -1, ALU.is_ge),
        1: tri_const("tri_b", 64, 1.0, 0.0, 0.0, -1, 1, ALU.is_ge),
    }
    NEG_A = {
        0: tri_const("nega_f", 64, 0.0, BIG, BIG, -1, 1, ALU.is_gt),
        1: tri_const("nega_b", 64, 0.0, BIG, BIG, 1, -1, ALU.is_gt),
    }
    NEG_B = {
        0: tri_const("negb_f", 64, 0.0, -BIG, -BIG, 1, -1, ALU.is_gt),
        1: tri_const("negb_b", 64, 0.0, -BIG, -BIG, -1, 1, ALU.is_gt),
    }
    NEG_C = {
        0: tri_const("negc_f", 64, 0.0, -BIG, -BIG, 1, -1, ALU.is_ge),
        1: tri_const("negc_b", 64, 0.0, -BIG, -BIG, -1, 1, ALU.is_ge),
    }
    BDONES = tri_const("bdones", 64, 1.0, 1.0, 0.0, 1, 1, ALU.is_ge)
    ones_r = sb("ones_r", [128, 128], F32R); ident_r = sb("ident_r", [128, 128], F32R)
    pool.op(lambda h: h.tensor_copy(out=ones_r[:], in_=ones32[:]), r=[cst], w=[cst])
    pool.op(lambda h: h.tensor_copy(out=ident_r[:], in_=ident32[:]), r=[cst], w=[cst])
    MASKS_R = {}
    for d__ in range(2):
        mk = sb(f"masks_r{d__}", [128, 512], F32R)
        pool.op(lambda h: h.tensor_copy(out=mk[:, 0:128], in_=NEG_A[d__][:]), r=[cst], w=[cst])
        pool.op(lambda h: h.tensor_copy(out=mk[:, 128:256], in_=NEG_B[d__][:]), r=[cst], w=[cst])
        pool.op(lambda h: h.tensor_copy(out=mk[:, 256:384], in_=NEG_C[d__][:]), r=[cst], w=[cst])
        pool.op(lambda h: h.tensor_copy(out=mk[:, 384:512], in_=zeros32[:]), r=[cst], w=[cst])
        MASKS_R[d__] = mk
    SEL0 = sb("sel0", [128, 128]); SEL1 = sb("sel1", [128, 128])
    pool.op(lambda h: h.memset(SEL0[:], 0.0), w=[cst]); pool.op(lambda h: h.memset(SEL1[:], 0.0), w=[cst])
    pool.op(lambda h: h.memset(SEL0[0:64, :], 1.0), w=[cst]); pool.op(lambda h: h.memset(SEL1[64:128, :], 1.0), w=[cst])
    GS = -1.0 / 16.0
    TRIG = {0: tri_const("trig_f", 128, GS, 0.0, 0.0, 1, -1, ALU.is_ge),
            1: tri_const("trig_b", 128, GS, 0.0, 0.0, -1, 1, ALU.is_ge)}
    DIFG = {0: tri_const("difg_f", 128, 0.0, GS, 0.0, 1, -1, ALU.is_ge),
            1: tri_const("difg_b", 128, 0.0, GS, 0.0, -1, 1, ALU.is_ge)}
    MASKG = {0: tri_const("maskg_f", 128, 1.0, 0.0, 0.0, 1, -1, ALU.is_ge),
             1: tri_const("maskg_b", 128, 1.0, 0.0, 0.0, -1, 1, ALU.is_ge)}

    prm = Buf("params")
    sl_prm = Slot(nc, "prm")
    lnw_T = sb("lnw_T", [128, 8])
    sp.dma(lnw_T[:], ln_pre_d[0, :].rearrange("(k p) -> p k", p=128), sl_prm, w=[prm], allow_slow_non_contiguous=True)
    cw = sb("cw", [128, 24, 5])
    for t_ in range(5):
        sp.dma(cw[:, :, t_], conv_d[0, t_, :].rearrange("(b p) -> p b", p=128), sl_prm, w=[prm], allow_slow_non_contiguous=True)
    prm16 = sb("prm16", [128, 2, 16])
    sp.dma(prm16[:, 0, 0:8], alog_f_d[0, :].partition_broadcast(128), sl_prm, w=[prm])
    sp.dma(prm16[:, 0, 8:16], alog_b_d[0, :].partition_broadcast(128), sl_prm, w=[prm])
    sp.dma(prm16[:, 1, 0:8], dtb_f_d[0, :].partition_broadcast(128), sl_prm, w=[prm])
    sp.dma(prm16[:, 1, 8:16], dtb_b_d[0, :].partition_broadcast(128), sl_prm, w=[prm])
    gdn_nw = sb("gdn_nw", [128, 1])
    sp.dma(gdn_nw[:], gdn_nw_d[0, :].rearrange("(p o) -> p o", o=1), sl_prm, w=[prm])
    gla_nw = sb("gla_nw", [128, 2])
    sp.dma(gla_nw[:], gla_nw_d[0, :].rearrange("(k p) -> p k", p=128), sl_prm, w=[prm], allow_slow_non_contiguous=True)
    w2cat = sb("w2cat", [33, 2, 512])
    pool.op(lambda h: h.memset(w2cat[0:32, :, :], 0.0), w=[prm])
    sp.dma(w2cat[0:16, 0, :], gkw_f_d[0, :, :], sl_prm, r=[prm], w=[prm])
    sp.dma(w2cat[16:32, 1, :], gkw_b_d[0, :, :], sl_prm, w=[prm])
    sp.dma(w2cat[32:33, 0, :], gkb_f_d[0:1, :], sl_prm, w=[prm])
    sp.dma(w2cat[32:33, 1, :], gkb_b_d[0:1, :], sl_prm, w=[prm])

    def dbg_out(name, src_ap, rbufs, dst=None):
        if name not in dbg:
            return
        d = dbg[name] if dst is None else dst
        sp.dma(d, src_ap, sl_dbg, r=rbufs)

    sl_dbg = Slot(nc, "dbg")
    sl_out = Slot(nc, "out")

    hT = sb("hT", [128, 8, T], BF16); hTb = Buf("hT")
    oaT = sb("oaT", [128, 8, T], BF16); oaTb = Buf("oaT")
    base_mark = len(allocs)

    xt = [sb(f"xt{i}", [128, D]) for i in range(2)]; xtb = [Buf(f"xt{i}") for i in range(2)]
    sl_x = [Slot(nc, f"x{i}") for i in range(2)]
    junk = sb("junk", [128, D], BF16); junkb = Buf("junk")
    xn = sb("xn", [128, D], BF16); xnb = Buf("xn")
    st0 = sb("st0", [128, 4]); st0b = Buf("st0")
    for tt in range(NT):
        s = tt % 2
        sp.dma(xt[s][:], x_d[tt * 128:(tt + 1) * 128, :], sl_x[s], w=[xtb[s]])
        act.op(lambda h: h.activation(out=junk[:], in_=xt[s][:], func=AF.Square, accum_out=st0[:, 0:1]), r=[xtb[s]], w=[junkb, st0b])
        act.op(lambda h: h.activation(out=st0[:, 1:2], in_=st0[:, 0:1], func=AF.Ln, bias=EPS, scale=1.0 / D), r=[st0b], w=[st0b])
        act.op(lambda h: h.activation(out=st0[:, 2:3], in_=st0[:, 1:2], func=AF.Exp, scale=-0.5), r=[st0b], w=[st0b])
        act.op(lambda h: h.activation(out=xn[:], in_=xt[s][:], func=AF.Copy, scale=st0[:, 2:3]), r=[xtb[s], st0b], w=[xnb])
        for k in range(8):
            pe.op(lambda h: h.transpose(pst[:, k * 128:(k + 1) * 128], xn[:, k * 128:(k + 1) * 128], ident_bf[:]), r=[xnb, cst], w=[pstb])
        dve.op(lambda h: h.tensor_tensor(out=hT[:, :, tt * 128:(tt + 1) * 128], in0=pst[:].rearrange("p (k t) -> p k t", k=8),
                                         in1=lnw_T[:, :].unsqueeze(2).to_broadcast([128, 8, 128]), op=ALU.mult), r=[pstb, prm], w=[hTb])
    if "st0" in dbg:
        sp.dma(dbg["st0"], st0[:], sl_dbg, r=[st0b])
        tmpx = sb("dbg_tmpx", [128, D]); tbx = Buf("dbg_tmpx")
        act.op(lambda h: h.copy(out=tmpx[:], in_=xn[:]), r=[xnb], w=[tbx])
        sp.dma(dbg["xn"], tmpx[:], sl_dbg, r=[tbx])
    if "hT" in dbg:
        tmp = sb("dbg_tmp", [128, T])
        tb = Buf("dbg_tmp")
        for k in range(8):
            act.op(lambda h: h.copy(out=tmp[:], in_=hT[:, k, :]), r=[hTb], w=[tb])
            sp.dma(dbg["hT"][k * 128:(k + 1) * 128, :], tmp[:], sl_dbg, r=[tb])
    free_to(base_mark)

    NWS = 2
    wblk = [sb(f"wblk{i}", [128, 8, 128], BF16) for i in range(NWS)]
    wblkb = [Buf(f"wblk{i}") for i in range(NWS)]
    sl_w = [Slot(nc, f"w{i}") for i in range(NWS)]
    w_rr = [0]

    def load_wblk(src_ap):
        i = w_rr[0] % NWS
        w_rr[0] += 1
        pool.dma(wblk[i][:], src_ap.rearrange("(k p) c -> p k c", p=128), sl_w[i], w=[wblkb[i]])
        return wblk[i], wblkb[i]

    def proj_fm(wt, wb, tg, pst_, pb, rhsT=hT, rb=hTb):
        for k in range(8):
            pe.op(lambda h: h.matmul(pst_[:, :], lhsT=wt[:, k, :], rhs=rhsT[:, k, tg * 512:(tg + 1) * 512], start=(k == 0), stop=(k == 7)),
                  r=[wb, rb], w=[pb])

    mix_mark = len(allocs)

    NCOL = 16
    LG = sb("LG", [128, NT, NCOL]); LNB = sb("LNB", [128, NT, NCOL]); BETA = sb("BETA", [128, NT, NCOL])
    A_tok = sb("A_tok", [128, NT, NCOL]); NG_tok = sb("NG_tok", [128, NT, NCOL]); BG = sb("BG", [128, NT, NCOL])
    KD = sb("KD", [128, NT, NCOL]); ET0 = sb("ET0", [128, NT, NCOL]); ET1 = sb("ET1", [128, NT, NCOL])
    scal = Buf("scal")
    s1_mark = len(allocs)
    wsm = sb("wsm", [128, 8, 32], BF16); wsmb = Buf("wsm")
    sl_wsm = Slot(nc, "wsm")
    pool.dma(wsm[:], w_in_d[0, :, OFF_SM:OFF_SM + 32].rearrange("(k p) c -> p k c", p=128), sl_wsm, w=[wsmb])
    asm = sb("asm", [128, NT, 32]); asmb = Buf("asm")
    for tt in range(NT):
        p_, pb = next_ps()
        for k in range(8):
            pe.op(lambda h: h.matmul(p_[:, 0:32], lhsT=hT[:, k, tt * 128:(tt + 1) * 128], rhs=wsm[:, k, :], start=(k == 0), stop=(k == 7)),
                  r=[hTb, wsmb], w=[pb])
        act.op(lambda h: h.copy(out=asm[:, tt, :], in_=p_[:, 0:32]), r=[pb], w=[asmb])
    t16 = sb("t16", [128, NT, NCOL]); t16b = Buf("t16")
    dve.op(lambda h: h.tensor_tensor(out=t16[:], in0=asm[:, :, 0:16], in1=prm16[:, 1:2, :].to_broadcast([128, NT, NCOL]), op=ALU.add), r=[asmb, prm], w=[t16b])
    act.op(lambda h: h.activation(out=t16[:], in_=t16[:], func=AF.Exp), r=[t16b], w=[t16b])
    act.op(lambda h: h.activation(out=t16[:], in_=t16[:], func=AF.Ln, bias=1.0), r=[t16b], w=[t16b])
    nA = sb("nA", [128, 1, NCOL]); nAb = Buf("nA")
    act.op(lambda h: h.activation(out=nA[:], in_=prm16[:, 0:1, :], func=AF.Exp), r=[prm], w=[nAb])
    dve.op(lambda h: h.scalar_tensor_tensor(out=LG[:], in0=t16[:], scalar=-1.0, in1=nA[:].to_broadcast([128, NT, NCOL]), op0=ALU.mult, op1=ALU.mult),
           r=[t16b, nAb], w=[scal])
    act.op(lambda h: h.activation(out=t16[:], in_=asm[:, :, 16:32], func=AF.Exp, scale=-1.0), r=[asmb, scal], w=[t16b])
    act.op(lambda h: h.activation(out=t16[:], in_=t16[:], func=AF.Ln, bias=1.0), r=[t16b], w=[t16b])
    act.op(lambda h: h.mul(out=LNB[:], in_=t16[:], mul=-1.0), r=[t16b], w=[scal])
    act.op(lambda h: h.activation(out=BETA[:], in_=LNB[:], func=AF.Exp), r=[scal], w=[scal])
    G_tok = sb("G_tok", [128, NT, NCOL]); TOTO = sb("TOTO", [128, NT, NCOL])
    for tt in range(NT):
        p_, pb = next_ps()
        for i, M in enumerate([TRI[0], TRI[1], BDONES, SEL0, SEL1]):
            pe.op(lambda h: h.matmul(p_[:, i * 16:(i + 1) * 16], lhsT=M[:], rhs=LG[:, tt, :], start=True, stop=True), r=[cst, scal], w=[pb])
        act.op(lambda h: h.copy(out=G_tok[:, tt, 0:8], in_=p_[:, 0:8]), r=[pb], w=[scal])
        act.op(lambda h: h.copy(out=G_tok[:, tt, 8:16], in_=p_[:, 24:32]), r=[pb], w=[scal])
        act.op(lambda h: h.copy(out=TOTO[:, tt, :], in_=p_[:, 32:48]), r=[pb], w=[scal])
        act.op(lambda h: h.activation(out=ET0[:, tt, :], in_=p_[:, 48:64], func=AF.Exp), r=[pb], w=[scal])
        act.op(lambda h: h.activation(out=ET1[:, tt, :], in_=p_[:, 64:80], func=AF.Exp), r=[pb], w=[scal])
    dve.op(lambda h: h.tensor_tensor(out=A_tok[:], in0=G_tok[:], in1=LNB[:], op=ALU.add), r=[scal], w=[scal])
    act.op(lambda h: h.mul(out=NG_tok[:], in_=G_tok[:], mul=-1.0), r=[scal], w=[scal])
    act.op(lambda h: h.activation(out=BG[:], in_=A_tok[:], func=AF.Exp), r=[scal], w=[scal])
    dve.op(lambda h: h.tensor_tensor(out=KD[:], in0=TOTO[:], in1=G_tok[:], op=ALU.subtract), r=[scal], w=[scal])
    act.op(lambda h: h.activation(out=KD[:], in_=KD[:], func=AF.Exp), r=[scal], w=[scal])
    if "LG" in dbg:
        sp.dma(dbg["LG"].rearrange("(t p) c -> p t c", p=128), LG[:], sl_dbg, r=[scal])
        sp.dma(dbg["BETA"].rearrange("(t p) c -> p t c", p=128), BETA[:], sl_dbg, r=[scal])
        sp.dma(dbg["G_tok"].rearrange("(t p) c -> p t c", p=128), G_tok[:], sl_dbg, r=[scal])

    free_to(s1_mark)
    gdn_mark = len(allocs)
    if gdn_heads > 0:
        pre = sb("pre", [128, T + 4]); preb = Buf("pre")
        pool.op(lambda h: h.memset(pre[:, 0:2], 0.0), w=[preb]); pool.op(lambda h: h.memset(pre[:, T + 2:T + 4], 0.0), w=[preb])
        acc = sb("acc", [128, T]); accb = Buf("acc")
        sqt = sb("sqt", [128, 512], BF16); sqtb = Buf("sqt")
        rn = sb("rn", [128, 512]); rnb = Buf("rn")
        HB = []
        for bs in range(2):
            HB.append(dict(qT=sb(f"qT{bs}", [128, T], BF16), qTb=Buf(f"qT{bs}"), kT=sb(f"kT{bs}", [128, T], BF16), kTb=Buf(f"kT{bs}"),
                           k_tok=sb(f"k_tok{bs}", [128, NT, 128], BF16), ktb=Buf(f"k_tok{bs}"),
                           v_tok=sb(f"v_tok{bs}", [128, NT, 128], BF16), vtb=Buf(f"v_tok{bs}")))
        vT = sb("vT", [128, T], BF16); vTb = Buf("vT")
        zs = sb("zs", [128, T], BF16); zsb = Buf("zs")
        oTa = sb("oTa", [128, T]); oTab = Buf("oTa")
        NSL = 2
        WK = {}
        ST = {}
        for d_ in range(2):
            for sl_i in range(NSL):
                W_ = {}
                for nm, shp, dt in [("rhs1", [128, 512], F32R), ("rhsb", [128, 128], F32R), ("D1", [128, 128], F32), ("D1T", [128, 128], F32),
                                    ("D2T", [128, 128], BF16), ("grep", [128, 128], F32),
                                    ("LU", [128, 2, 256], BF16),
                                    ("LU2", [128, 2, 256], BF16),
                                    ("attnT", [128, 128], BF16), ("qdT", [128, 128], BF16), ("khat", [128, 128], BF16), ("bv", [128, 128], BF16),
                                    ("kd", [128, 128], BF16), ("wT", [128, 128], BF16), ("u", [128, 128], F32), ("vnew", [128, 128], BF16)]:
                    W_[nm] = sb(f"{nm}_{d_}_{sl_i}", shp, dt)
                    W_[nm + "_b"] = Buf(f"{nm}_{d_}_{sl_i}")
                WK[(d_, sl_i)] = W_
            S_ = {}
            for nm, shp, dt in [("S32", [128, 128], F32), ("S16", [128, 128], BF16)]:
                S_[nm] = sb(f"{nm}_{d_}", shp, dt)
                S_[nm + "_b"] = Buf(f"{nm}_{d_}")
            ST[d_] = S_

    def gdn_projA(hh, B_):
        for which, off in (("q", OFF_Q), ("k", OFF_K), ("v", OFF_V)):
            wt, wb = load_wblk(w_in_d[0, :, off + hh * 128: off + (hh + 1) * 128])
            for tg in range(4):
                ipj = yield from acquire("proj"); p_, pb = ps[ipj], psb[ipj]
                proj_fm(wt, wb, tg, p_, pb)
                act.op(lambda h: h.copy(out=pre[:, 2 + tg * 512: 2 + (tg + 1) * 512], in_=p_[:, :]), r=[pb], w=[preb])
                release(ipj)
                yield
            blk = off // 128 + hh
            act.op(lambda h: h.activation(out=acc[:], in_=pre[:, 0:T], func=AF.Copy, scale=cw[:, blk, 0:1]), r=[preb, prm], w=[accb])
            for t_ in range(1, 5):
                dve.op(lambda h: h.scalar_tensor_tensor(out=acc[:], in0=pre[:, t_:t_ + T], scalar=cw[:, blk, t_:t_ + 1], in1=acc[:], op0=ALU.mult, op1=ALU.add),
                       r=[preb, prm, accb], w=[accb])
                yield
            if which == "v":
                act.op(lambda h: h.activation(out=vT[:], in_=acc[:], func=AF.Silu), r=[accb], w=[vTb])
                continue
            act.op(lambda h: h.activation(out=acc[:], in_=acc[:], func=AF.Silu), r=[accb], w=[accb])
            dstT, dstb = (B_["qT"], B_["qTb"]) if which == "q" else (B_["kT"], B_["kTb"])
            post = (128.0 ** -0.5) if which == "q" else 1.0
            for tg in range(4):
                sl_ = slice(tg * 512, (tg + 1) * 512)
                act.op(lambda h: h.activation(out=sqt[:], in_=acc[:, sl_], func=AF.Square), r=[accb], w=[sqtb])
                ipj = yield from acquire("proj"); p_, pb = ps[ipj], psb[ipj]
                pe.op(lambda h: h.matmul(p_[:, :], lhsT=ones_bf[:], rhs=sqt[:], start=True, stop=True), r=[cst, sqtb], w=[pb])
                yield
                act.op(lambda h: h.activation(out=rn[:], in_=p_[:, :], func=AF.Ln, bias=EPS), r=[pb], w=[rnb])
                release(ipj)
                act.op(lambda h: h.activation(out=rn[:], in_=rn[:], func=AF.Exp, scale=-0.5), r=[rnb], w=[rnb])
                dve.op(lambda h: h.scalar_tensor_tensor(out=dstT[:, sl_], in0=acc[:, sl_], scalar=post, in1=rn[:], op0=ALU.mult, op1=ALU.mult),
                       r=[accb, rnb], w=[dstb])
                yield
        for srcT, srcb, dst, dstb in ((B_["kT"], B_["kTb"], B_["k_tok"], B_["ktb"]), (vT, vTb, B_["v_tok"], B_["vtb"])):
            for g8 in range(2):
                for j in range(8):
                    tt = g8 * 8 + j
                    pe.op(lambda h: h.transpose(pst[:, j * 128:(j + 1) * 128], srcT[:, tt * 128:(tt + 1) * 128], ident_bf[:]), r=[srcb, cst], w=[pstb])
                act.op(lambda h: h.copy(out=dst[:, g8 * 8:(g8 + 1) * 8, :], in_=pst[:].rearrange("p (j c) -> p j c", j=8)), r=[pstb], w=[dstb])
                yield

    def gdn_projZ(hh):
        wt, wb = load_wblk(w_in_d[0, :, OFF_Z + hh * 128: OFF_Z + (hh + 1) * 128])
        for tg in range(4):
            ipj = yield from acquire("proj"); p_, pb = ps[ipj], psb[ipj]
            proj_fm(wt, wb, tg, p_, pb)
            act.op(lambda h: h.activation(out=zs[:, tg * 512:(tg + 1) * 512], in_=p_[:, :], func=AF.Silu), r=[pb], w=[zsb])
            release(ipj)
            yield

    def gdn_norm(hh):
        if f"oa{hh}" in dbg:
            sp.dma(dbg[f"oa{hh}"], oTa[:], sl_dbg, r=[oTab])
        for tg in range(4):
            sl_ = slice(tg * 512, (tg + 1) * 512)
            act.op(lambda h: h.activation(out=sqt[:], in_=oTa[:, sl_], func=AF.Square), r=[oTab], w=[sqtb])
            ipj = yield from acquire("proj")
            p_, pb = ps[ipj], psb[ipj]
            pe.op(lambda h: h.matmul(p_[:, :], lhsT=ones_bf[:], rhs=sqt[:], start=True, stop=True), r=[cst, sqtb], w=[pb])
            yield
            act.op(lambda h: h.activation(out=rn[:], in_=p_[:, :], func=AF.Ln, bias=EPS, scale=1.0 / 128), r=[pb], w=[rnb])
            release(ipj)
            act.op(lambda h: h.activation(out=rn[:], in_=rn[:], func=AF.Exp, scale=-0.5), r=[rnb], w=[rnb])
            dve.op(lambda h: h.scalar_tensor_tensor(out=rn[:], in0=oTa[:, sl_], scalar=gdn_nw[:, 0:1], in1=rn[:], op0=ALU.mult, op1=ALU.mult),
                   r=[oTab, rnb, prm], w=[rnb])
            dve.op(lambda h: h.tensor_tensor(out=oaT[:, hh, sl_], in0=rn[:], in1=zs[:, sl_], op=ALU.mult), r=[rnb, zsb], w=[oaTb])
            yield

    all_tasks = []
    PRA, PRZ, NORM = [], [], []
    for hh in range(gdn_heads):
        B_ = HB[hh % 2]
        pra = Task(gdn_projA(hh, B_), deps=[PRA[hh - 1] if hh >= 1 else None, NORM[hh - 2] if hh >= 2 else None, PRZ[hh - 1] if hh >= 1 else None])
        PRA.append(pra)
        all_tasks.append(pra)
        prz = Task(gdn_projZ(hh), deps=[pra, NORM[hh - 1] if hh >= 1 else None])
        PRZ.append(prz)
        def gdn_prep(d_, tt, W_, hh=hh, B_=B_):
            qT, qTb, kT, kTb, k_tok, ktb, v_tok, vtb = B_["qT"], B_["qTb"], B_["kT"], B_["kTb"], B_["k_tok"], B_["ktb"], B_["v_tok"], B_["vtb"]
            col = d_ * 8 + hh
            tsl = slice(tt * 128, (tt + 1) * 128)
            act.op(lambda h: h.activation(out=W_["rhs1"][:].rearrange("p (a b) -> p a b", a=4), in_=TRI[d_][:, :].unsqueeze(1).to_broadcast([128, 4, 128]),
                                          func=AF.Copy, scale=LG[:, tt, col:col + 1]), r=[cst, scal], w=[W_["rhs1_b"]])
            act.op(lambda h: h.activation(out=W_["rhsb"][:], in_=ident32[:], func=AF.Copy, scale=LNB[:, tt, col:col + 1]),
                   r=[cst, scal], w=[W_["rhsb_b"]])
            act.op(lambda h: h.activation(out=W_["khat"][:], in_=k_tok[:, tt, :], func=AF.Copy, scale=BG[:, tt, col:col + 1]),
                   r=[ktb, scal], w=[W_["khat_b"]])
            act.op(lambda h: h.activation(out=W_["bv"][:], in_=v_tok[:, tt, :], func=AF.Copy, scale=BETA[:, tt, col:col + 1]),
                   r=[vtb, scal], w=[W_["bv_b"]])
            pool.op(lambda h: h.tensor_scalar(out=W_["kd"][:], in0=k_tok[:, tt, :], scalar1=KD[:, tt, col:col + 1], scalar2=None, op0=ALU.mult),
                    r=[ktb, scal], w=[W_["kd_b"]])
            yield
            ia = yield from acquire("prep")
            pa, pab = ps[ia], psb[ia]
            pe.op(lambda h: h.matmul(pa[:, :], lhsT=ones_r[:], rhs=W_["rhs1"][:], start=True, stop=False), r=[cst, W_["rhs1_b"]], w=[pab])
            pe.op(lambda h: h.matmul(pa[:, :], lhsT=ident_r[:], rhs=MASKS_R[d_][:], start=False, stop=False), r=[cst], w=[pab])
            pe.op(lambda h: h.matmul(pa[:, 128:256], lhsT=ones_r[:], rhs=W_["rhsb"][:], start=False, stop=True), r=[cst, W_["rhsb_b"]], w=[pab])
            yield
            act.op(lambda h: h.activation(out=W_["D1"][:], in_=pa[:, 0:128], func=AF.Exp, bias=A_tok[:, tt, col:col + 1], scale=-1.0),
                   r=[pab, scal], w=[W_["D1_b"]])
            act.op(lambda h: h.activation(out=W_["D1T"][:], in_=pa[:, 128:256], func=AF.Exp, bias=NG_tok[:, tt, col:col + 1], scale=1.0),
                   r=[pab, scal], w=[W_["D1T_b"]])
            act.op(lambda h: h.activation(out=W_["D2T"][:], in_=pa[:, 256:384], func=AF.Exp, bias=NG_tok[:, tt, col:col + 1], scale=1.0),
                   r=[pab, scal], w=[W_["D2T_b"]])
            act.op(lambda h: h.activation(out=W_["grep"][:], in_=pa[:, 384:512], func=AF.Exp), r=[pab], w=[W_["grep_b"]])
            release(ia)
            yield
            ikq = yield from acquire("prep")
            pkq, pkqb = ps[ikq], psb[ikq]
            pe.op(lambda h: h.matmul(pkq[:, 0:128], lhsT=kT[:, tsl], rhs=kT[:, tsl], start=True, stop=True), r=[kTb], w=[pkqb])
            pe.op(lambda h: h.matmul(pkq[:, 128:256], lhsT=kT[:, tsl], rhs=qT[:, tsl], start=True, stop=True), r=[kTb, qTb], w=[pkqb])
            yield
            LU, LUb, LU2, LU2b = W_["LU"], W_["LU_b"], W_["LU2"], W_["LU2_b"]
            dve.op(lambda h: h.tensor_tensor(out=LU[:, 1, 0:128], in0=pkq[:, 0:128], in1=W_["D1"][:], op=ALU.mult), r=[pkqb, W_["D1_b"]], w=[LUb])
            dve.op(lambda h: h.tensor_tensor(out=LU[:, 0, 0:128], in0=pkq[:, 0:128], in1=W_["D1T"][:], op=ALU.mult), r=[pkqb, W_["D1T_b"]], w=[LUb])
            dve.op(lambda h: h.tensor_tensor(out=LU2[:, 0, 128:256], in0=ident32[:], in1=LU[:, 0, 0:128], op=ALU.subtract), r=[cst, LUb], w=[LU2b])
            dve.op(lambda h: h.tensor_tensor(out=W_["attnT"][:], in0=pkq[:, 128:256], in1=W_["D2T"][:], op=ALU.mult), r=[pkqb, W_["D2T_b"]], w=[W_["attnT_b"]])
            dve.op(lambda h: h.tensor_tensor(out=W_["qdT"][:], in0=qT[:, tsl], in1=W_["grep"][:], op=ALU.mult), r=[qTb, W_["grep_b"]], w=[W_["qdT_b"]])
            release(ikq)
            yield
            cur, curb, nxt, nxtb = LU, LUb, LU2, LU2b
            for lev in range(6):
                ipn = yield from acquire("prep")
                pn, pnb = ps[ipn], psb[ipn]
                if lev == 0:
                    pe.op(lambda h: h.matmul(pn[:, 0:128], lhsT=cur[:, 1, 0:128], rhs=cur[:, 0, 0:128], start=True, stop=True), r=[curb], w=[pnb])
                elif lev < 5:
                    pe.op(lambda h: h.matmul(pn[:, 0:256], lhsT=cur[:, 1, 0:128], rhs=cur[:, 0, 0:256], start=True, stop=True), r=[curb], w=[pnb])
                else:
                    pe.op(lambda h: h.matmul(pn[:, 128:256], lhsT=cur[:, 1, 0:128], rhs=cur[:, 0, 128:256], start=True, stop=True), r=[curb], w=[pnb])
                if lev < 5:
                    pe.op(lambda h: h.matmul(pn[:, 256:384], lhsT=cur[:, 0, 0:128], rhs=cur[:, 1, 0:128], start=True, stop=True), r=[curb], w=[pnb])
                yield
                if lev < 5:
                    ev_eng = act
                    if ev_eng is dve:
                        dve.op(lambda h: h.tensor_copy(out=nxt[:, :, 0:128], in_=pn[:, :].rearrange("p (a b) -> p a b", a=2)[:, :, 0:128]), r=[pnb], w=[nxtb])
                    else:
                        act.op(lambda h: h.copy(out=nxt[:, :, 0:128], in_=pn[:, :].rearrange("p (a b) -> p a b", a=2)[:, :, 0:128]), r=[pnb], w=[nxtb])
                if lev > 0:
                    dve.op(lambda h: h.tensor_tensor(out=nxt[:, 0, 128:256], in0=pn[:, 128:256], in1=cur[:, 0, 128:256], op=ALU.add), r=[pnb, curb], w=[nxtb])
                cur, curb, nxt, nxtb = nxt, nxtb, cur, curb
                release(ipn)
                yield
            Wm = cur[:, 0, 128:256]; Wmb = curb
            ipw = yield from acquire("prep")
            pw, pwb = ps[ipw], psb[ipw]
            pe.op(lambda h: h.matmul(pw[:, 0:128], lhsT=W_["khat"][:], rhs=Wm, start=True, stop=True), r=[W_["khat_b"], Wmb], w=[pwb])
            pe.op(lambda h: h.matmul(pw[:, 128:256], lhsT=Wm, rhs=W_["bv"][:], start=True, stop=True), r=[W_["bv_b"], Wmb], w=[pwb])
            yield
            act.op(lambda h: h.copy(out=W_["wT"][:], in_=pw[:, 0:128]), r=[pwb], w=[W_["wT_b"]])
            act.op(lambda h: h.copy(out=W_["u"][:], in_=pw[:, 128:256]), r=[pwb], w=[W_["u_b"]])
            release(ipw)
            yield

        def gdn_scan(d_, tt, W_, S_, hh=hh):
            col = d_ * 8 + hh
            for c in ((0, 1) if d_ == 0 else (1, 0)):
                rs = slice(c * 64, (c + 1) * 64)
                ETc = ET0 if c == 0 else ET1
                ip1 = yield from acquire("scan")
                p1, p1b = ps[ip1], psb[ip1]
                pe.op(lambda h: h.matmul(p1[rs, 0:128], lhsT=W_["wT"][:, rs], rhs=S_["S16"][:], start=True, stop=True), r=[W_["wT_b"], S_["S16_b"]], w=[p1b])
                yield
                dve.op(lambda h: h.tensor_tensor(out=W_["vnew"][rs, :], in0=W_["u"][rs, :], in1=p1[rs, 0:128], op=ALU.subtract),
                       r=[W_["u_b"], p1b], w=[W_["vnew_b"]])
                release(ip1)
                yield
                ip2 = yield from acquire("scan")
                p2, p2b = ps[ip2], psb[ip2]
                pe.op(lambda h: h.matmul(p2[:, 0:64], lhsT=S_["S16"][:], rhs=W_["qdT"][:, rs], start=True, stop=False), r=[S_["S16_b"], W_["qdT_b"]], w=[p2b])
                pe.op(lambda h: h.matmul(p2[:, 0:64], lhsT=W_["vnew"][rs, :], rhs=W_["attnT"][rs, rs], start=False, stop=True),
                      r=[W_["vnew_b"], W_["attnT_b"]], w=[p2b])
                pe.op(lambda h: h.matmul(p2[:, 128:256], lhsT=W_["kd"][rs, :], rhs=W_["vnew"][rs, :], start=True, stop=True),
                      r=[W_["kd_b"], W_["vnew_b"]], w=[p2b])
                yield
                dve.op(lambda h: h.scalar_tensor_tensor(out=S_["S32"][:], in0=S_["S32"][:], scalar=ETc[:, tt, col:col + 1], in1=p2[:, 128:256],
                                                        op0=ALU.mult, op1=ALU.add), r=[S_["S32_b"], scal, p2b], w=[S_["S32_b"]])
                act.op(lambda h: h.copy(out=S_["S16"][:], in_=S_["S32"][:]), r=[S_["S32_b"]], w=[S_["S16_b"]])
                osl = slice(tt * 128 + c * 64, tt * 128 + (c + 1) * 64)
                first = (tt < NT // 2) if d_ == 0 else (tt >= NT // 2)
                if first:
                    act.op(lambda h: h.copy(out=oTa[:, osl], in_=p2[:, 0:64]), r=[p2b], w=[oTab])
                else:
                    dve.op(lambda h: h.tensor_tensor(out=oTa[:, osl], in0=p2[:, 0:64], in1=oTa[:, osl], op=ALU.add), r=[p2b, oTab], w=[oTab])
                release(ip2)
                yield

        def gdn_init():
            for d_ in range(2):
                pool.op(lambda h: h.memset(ST[d_]["S32"][:], 0.0), w=[ST[d_]["S32_b"]])
                pool.op(lambda h: h.memset(ST[d_]["S16"][:], 0.0), w=[ST[d_]["S16_b"]])
            yield
        init_t = Task(gdn_init(), deps=[NORM[hh - 1] if hh >= 1 else None])
        all_tasks.append(init_t)
        order = {0: list(range(NT)), 1: list(range(NT - 1, -1, -1))}
        P = {0: [], 1: []}
        S = {0: [], 1: []}
        tasks = all_tasks
        for i in range(NT):
            for d_ in range(2):
                tt = order[d_][i]
                W_ = WK[(d_, i % NSL)]
                pt = Task(gdn_prep(d_, tt, W_), deps=[S[d_][i - NSL] if i >= NSL else None, PRA[hh], init_t])
                P[d_].append(pt)
                tasks.append(pt)
            for d_ in range(2):
                tt = order[d_][i]
                W_ = WK[(d_, i % NSL)]
                other = S[1 - d_][NT - 1 - i] if (i >= NT // 2 and len(S[1 - d_]) > NT - 1 - i) else None
                stt = Task(gdn_scan(d_, tt, W_, ST[d_]), deps=[P[d_][i], S[d_][i - 1] if i >= 1 else None, other])
                S[d_].append(stt)
                tasks.append(stt)
        all_tasks.append(prz)
        nt = Task(gdn_norm(hh), deps=[prz] + S[0] + S[1])
        NORM.append(nt)
        all_tasks.append(nt)
    if gdn_heads > 0:
        for hh in range(gdn_heads - 1):
            NORM[hh].deps.append(PRA[hh + 1])
        run_tasks(all_tasks, max_active=MAXACT + 2)
    if "oaT" in dbg:
        tmp = sb("dbg_tmp3", [128, T]); tb = Buf("dbg_tmp3")
        for k in range(8):
            act.op(lambda h: h.copy(out=tmp[:], in_=oaT[:, k, :]), r=[oaTb], w=[tb])
            sp.dma(dbg["oaT"][k * 128:(k + 1) * 128, :], tmp[:], sl_dbg, r=[tb])
    free_to(s1_mark)

    obT = sb("obT", [128, 8, T], BF16); obTb = Buf("obT")
    gla_mark = len(allocs)
    if gla_heads > 0:
        NSG = 2
        GW = {}
        GS_ = {}
        for d_ in range(2):
            for sl_i in range(NSG):
                W_ = {}
                for nm, shp, dt in [("gk", [128, 128], F32), ("ekd", [128, 128], F32), ("eg", [128, 128], F32), ("eng", [128, 128], F32), ("etot", [128, 1], F32),
                                    ("kdg", [128, 128], BF16), ("qgT", [128, 128], BF16), ("kgT", [128, 128], BF16), ("attnT", [128, 128], BF16)]:
                    W_[nm] = sb(f"g{nm}_{d_}_{sl_i}", shp, dt)
                    W_[nm + "_b"] = Buf(f"g{nm}_{d_}_{sl_i}")
                GW[(d_, sl_i)] = W_
            S_ = {}
            for nm, shp, dt in [("S32", [128, 256], F32), ("S16", [128, 256], BF16)]:
                S_[nm] = sb(f"g{nm}_{d_}", shp, dt)
                S_[nm + "_b"] = Buf(f"g{nm}_{d_}")
            GS_[d_] = S_

        rT1 = sb("rT1", [33, T]); rT1b = Buf("rT1")
        wr = sb("wr", [128, 8, 32], BF16); wrb = Buf("wr")
        sl_wr = Slot(nc, "wr")
        pool.dma(wr[:], w_in_d[0, :, OFF_R:OFF_R + 32].rearrange("(k p) c -> p k c", p=128), sl_wr, w=[wrb])
        pool.op(lambda h: h.memset(rT1[32:33, :], 1.0), w=[rT1b])
        for tg in range(4):
            p_, pb = next_ps()
            for k in range(8):
                pe.op(lambda h: h.matmul(p_[0:32, :], lhsT=wr[:, k, :], rhs=hT[:, k, tg * 512:(tg + 1) * 512], start=(k == 0), stop=(k == 7)),
                      r=[wrb, hTb], w=[pb])
            act.op(lambda h: h.copy(out=rT1[0:32, tg * 512:(tg + 1) * 512], in_=p_[0:32, :]), r=[pb], w=[rT1b])
        wkv = sb("wkv", [128, 8, 384], BF16); wkvb = Buf("wkv"); sl_wkv = Slot(nc, "wkv")
        qTg = sb("qTg", [128, T], BF16); qTgb = Buf("qTg")
        kTg = sb("kTg", [128, T], BF16); kTgb = Buf("kTg")
        kg_tok = sb("kg_tok", [128, NT, 128], BF16); kgtb = Buf("kg_tok")
        vg_tok = sb("vg_tok", [128, NT, 256], BF16); vgtb = Buf("vg_tok")
        gsT = sb("gsT", [128, 2, T], BF16); gsTb = Buf("gsT")
        obTa = sb("obTa", [128, 2, T]); obTab = Buf("obTa")
        sqg = sb("sqg", [128, 512], BF16); sqgb = Buf("sqg")
        rng = sb("rng", [128, 512]); rngb = Buf("rng")
    for hb in range(gla_heads):
        for off, dst, dstb, scl in ((OFF_QB, qTg, qTgb, 128.0 ** -0.5), (OFF_KB, kTg, kTgb, 1.0)):
            wt, wb = load_wblk(w_in_d[0, :, off + hb * 128: off + (hb + 1) * 128])
            for tg in range(4):
                p_, pb = next_ps()
                proj_fm(wt, wb, tg, p_, pb)
                act.op(lambda h: h.mul(out=dst[:, tg * 512:(tg + 1) * 512], in_=p_[:, :], mul=scl), r=[pb], w=[dstb])
        for eb in range(2):
            wt, wb = load_wblk(w_in_d[0, :, OFF_GB + hb * 256 + eb * 128: OFF_GB + hb * 256 + (eb + 1) * 128])
            for tg in range(4):
                p_, pb = next_ps()
                proj_fm(wt, wb, tg, p_, pb)
                act.op(lambda h: h.activation(out=gsT[:, eb, tg * 512:(tg + 1) * 512], in_=p_[:, :], func=AF.Silu), r=[pb], w=[gsTb])
        pool.dma(wkv[:, :, 0:128], w_in_d[0, :, OFF_KB + hb * 128: OFF_KB + (hb + 1) * 128].rearrange("(k p) c -> p k c", p=128), sl_wkv, w=[wkvb])
        pool.dma(wkv[:, :, 128:384], w_in_d[0, :, OFF_VB + hb * 256: OFF_VB + (hb + 1) * 256].rearrange("(k p) c -> p k c", p=128), sl_wkv, w=[wkvb])
        for tt in range(NT):
            p_, pb = next_ps()
            for k in range(8):
                pe.op(lambda h: h.matmul(p_[:, 0:384], lhsT=hT[:, k, tt * 128:(tt + 1) * 128], rhs=wkv[:, k, :], start=(k == 0), stop=(k == 7)),
                      r=[hTb, wkvb], w=[pb])
            act.op(lambda h: h.copy(out=kg_tok[:, tt, :], in_=p_[:, 0:128]), r=[pb], w=[kgtb])
            act.op(lambda h: h.copy(out=vg_tok[:, tt, :], in_=p_[:, 128:384]), r=[pb], w=[vgtb])

        def gla_prep(d_, tt, W_, hb=hb):
            tsl = slice(tt * 128, (tt + 1) * 128)
            ix = yield from acquire("gprep")
            px, pxb = ps[ix], psb[ix]
            pe.op(lambda h: h.matmul(px[:, 0:128], lhsT=rT1[:, tsl], rhs=w2cat[:, d_, hb * 128:(hb + 1) * 128], start=True, stop=True),
                  r=[rT1b, prm], w=[pxb])
            yield
            act.op(lambda h: h.activation(out=W_["gk"][:], in_=px[:, 0:128], func=AF.Exp, scale=-1.0), r=[pxb], w=[W_["gk_b"]])
            release(ix)
            act.op(lambda h: h.activation(out=W_["gk"][:], in_=W_["gk"][:], func=AF.Ln, bias=1.0), r=[W_["gk_b"]], w=[W_["gk_b"]])
            yield
            ig = yield from acquire("gprep")
            pg, pgb = ps[ig], psb[ig]
            pe.op(lambda h: h.matmul(pg[:, 0:128], lhsT=DIFG[d_][:], rhs=W_["gk"][:], start=True, stop=True), r=[cst, W_["gk_b"]], w=[pgb])
            pe.op(lambda h: h.matmul(pg[:, 128:256], lhsT=W_["gk"][:], rhs=TRIG[d_][:], start=True, stop=True), r=[cst, W_["gk_b"]], w=[pgb])
            yield
            act.op(lambda h: h.activation(out=W_["eg"][:], in_=pg[:, 128:256], func=AF.Exp), r=[pgb], w=[W_["eg_b"]])
            act.op(lambda h: h.activation(out=W_["eng"][:], in_=pg[:, 128:256], func=AF.Exp, scale=-1.0), r=[pgb], w=[W_["eng_b"]])
            act.op(lambda h: h.activation(out=W_["ekd"][:], in_=pg[:, 0:128], func=AF.Exp), r=[pgb], w=[W_["ekd_b"]])
            lastc = 128 + (127 if d_ == 0 else 0)
            act.op(lambda h: h.activation(out=W_["etot"][:], in_=pg[:, lastc:lastc + 1], func=AF.Exp), r=[pgb], w=[W_["etot_b"]])
            release(ig)
            yield
            dve.op(lambda h: h.tensor_tensor(out=W_["qgT"][:], in0=qTg[:, tsl], in1=W_["eg"][:], op=ALU.mult), r=[qTgb, W_["eg_b"]], w=[W_["qgT_b"]])
            pool.op(lambda h: h.tensor_tensor(out=W_["kgT"][:], in0=kTg[:, tsl], in1=W_["eng"][:], op=ALU.mult), r=[kTgb, W_["eng_b"]], w=[W_["kgT_b"]])
            dve.op(lambda h: h.tensor_tensor(out=W_["kdg"][:], in0=kg_tok[:, tt, :], in1=W_["ekd"][:], op=ALU.mult), r=[kgtb, W_["ekd_b"]], w=[W_["kdg_b"]])
            yield
            ia = yield from acquire("gprep")
            pa_, pab_ = ps[ia], psb[ia]
            pe.op(lambda h: h.matmul(pa_[:, 0:128], lhsT=W_["kgT"][:], rhs=W_["qgT"][:], start=True, stop=True), r=[W_["kgT_b"], W_["qgT_b"]], w=[pab_])
            yield
            dve.op(lambda h: h.tensor_tensor(out=W_["attnT"][:], in0=pa_[:, 0:128], in1=MASKG[d_][:], op=ALU.mult), r=[pab_, cst], w=[W_["attnT_b"]])
            release(ia)
            yield

        def gla_scan(d_, tt, W_, S_):
            tsl = slice(tt * 128, (tt + 1) * 128)
            io = yield from acquire("gscan")
            po, pob = ps[io], psb[io]
            for eb in range(2):
                pe.op(lambda h: h.matmul(po[:, eb * 128:(eb + 1) * 128], lhsT=S_["S16"][:, eb * 128:(eb + 1) * 128], rhs=W_["qgT"][:], start=True, stop=False),
                      r=[S_["S16_b"], W_["qgT_b"]], w=[pob])
                pe.op(lambda h: h.matmul(po[:, eb * 128:(eb + 1) * 128], lhsT=vg_tok[:, tt, eb * 128:(eb + 1) * 128], rhs=W_["attnT"][:], start=False, stop=True),
                      r=[vgtb, W_["attnT_b"]], w=[pob])
            iS = yield from acquire("gscan")
            pS, pSb = ps[iS], psb[iS]
            pe.op(lambda h: h.matmul(pS[:, 0:256], lhsT=W_["kdg"][:], rhs=vg_tok[:, tt, :], start=True, stop=True), r=[W_["kdg_b"], vgtb], w=[pSb])
            yield
            dve.op(lambda h: h.scalar_tensor_tensor(out=S_["S32"][:], in0=S_["S32"][:], scalar=W_["etot"][:, 0:1], in1=pS[:, 0:256],
                                                    op0=ALU.mult, op1=ALU.add), r=[S_["S32_b"], W_["etot_b"], pSb], w=[S_["S32_b"]])
            release(iS)
            act.op(lambda h: h.copy(out=S_["S16"][:], in_=S_["S32"][:]), r=[S_["S32_b"]], w=[S_["S16_b"]])
            first = (tt < NT // 2) if d_ == 0 else (tt >= NT // 2)
            if first:
                act.op(lambda h: h.copy(out=obTa[:, :, tsl], in_=po[:, 0:256].rearrange("p (e t) -> p e t", e=2)), r=[pob], w=[obTab])
            else:
                dve.op(lambda h: h.tensor_tensor(out=obTa[:, :, tsl], in0=po[:, 0:256].rearrange("p (e t) -> p e t", e=2), in1=obTa[:, :, tsl], op=ALU.add),
                       r=[pob, obTab], w=[obTab])
            release(io)
            yield

        for d_ in range(2):
            pool.op(lambda h: h.memset(GS_[d_]["S32"][:], 0.0), w=[GS_[d_]["S32_b"]])
            pool.op(lambda h: h.memset(GS_[d_]["S16"][:], 0.0), w=[GS_[d_]["S16_b"]])
        order = {0: list(range(NT)), 1: list(range(NT - 1, -1, -1))}
        P = {0: [], 1: []}
        S = {0: [], 1: []}
        tasks = []
        for i in range(NT):
            for d_ in range(2):
                pt = Task(gla_prep(d_, order[d_][i], GW[(d_, i % NSG)]), deps=[S[d_][i - NSG] if i >= NSG else None])
                P[d_].append(pt)
                tasks.append(pt)
            for d_ in range(2):
                other = S[1 - d_][NT - 1 - i] if (i >= NT // 2 and len(S[1 - d_]) > NT - 1 - i) else None
                stt = Task(gla_scan(d_, order[d_][i], GW[(d_, i % NSG)], GS_[d_]), deps=[P[d_][i], S[d_][i - 1] if i >= 1 else None, other])
                S[d_].append(stt)
                tasks.append(stt)
        run_tasks(tasks, max_active=MAXACT)
        if f"ob{hb}" in dbg:
            for eb in range(2):
                sp.dma(dbg[f"ob{hb}"][eb * 128:(eb + 1) * 128, :], obTa[:, eb, :], sl_dbg, r=[obTab])
        for tg in range(4):
            sl_ = slice(tg * 512, (tg + 1) * 512)
            p_, pb = next_ps()
            for eb in range(2):
                act.op(lambda h: h.activation(out=sqg[:], in_=obTa[:, eb, sl_], func=AF.Square), r=[obTab], w=[sqgb])
                pe.op(lambda h: h.matmul(p_[:, :], lhsT=ones_bf[:], rhs=sqg[:], start=(eb == 0), stop=(eb == 1)), r=[cst, sqgb], w=[pb])
            act.op(lambda h: h.activation(out=rng[:], in_=p_[:, :], func=AF.Ln, bias=EPS, scale=1.0 / 256), r=[pb], w=[rngb])
            act.op(lambda h: h.activation(out=rng[:], in_=rng[:], func=AF.Exp, scale=-0.5), r=[rngb], w=[rngb])
            for eb in range(2):
                dve.op(lambda h: h.scalar_tensor_tensor(out=obTa[:, eb, sl_], in0=obTa[:, eb, sl_], scalar=gla_nw[:, eb:eb + 1], in1=rng[:], op0=ALU.mult, op1=ALU.mult),
                       r=[obTab, rngb, prm], w=[obTab])
                dve.op(lambda h: h.tensor_tensor(out=obT[:, hb * 2 + eb, sl_], in0=obTa[:, eb, sl_], in1=gsT[:, eb, sl_], op=ALU.mult), r=[obTab, gsTb], w=[obTb])
    if "obT" in dbg:
        tmp = sb("dbg_tmp4", [128, T]); tb = Buf("dbg_tmp4")
        for k in range(8):
            act.op(lambda h: h.copy(out=tmp[:], in_=obT[:, k, :]), r=[obTb], w=[tb])
            sp.dma(dbg["obT"][k * 128:(k + 1) * 128, :], tmp[:], sl_dbg, r=[tb])
    free_to(gla_mark)

    if do_final:
        mT = sb("mT", [128, 8, T], BF16); mTb = Buf("mT")
        fin_mark = len(allocs)
        sga = sb("sga", [128, 512]); sgab = Buf("sga")
        sgb = sb("sgb", [128, 512]); sgbb = Buf("sgb")
        t1 = sb("t1", [128, 512]); t1b = Buf("t1")
        t2 = sb("t2", [128, 512]); t2b = Buf("t2")
        NW2 = 8
        wb2 = [sb(f"wb2_{i}", [128, 8, 128], BF16) for i in range(NW2)]; wb2b = [Buf(f"wb2_{i}") for i in range(NW2)]
        sl_w2 = [Slot(nc, f"w2_{i}") for i in range(NW2)]
        rr2 = [0]

        def load2(src_ap):
            i = rr2[0] % NW2
            rr2[0] += 1
            pool.dma(wb2[i][:], src_ap.rearrange("(k p) c -> p k c", p=128), sl_w2[i], w=[wb2b[i]])
            return wb2[i], wb2b[i]

        for m in range(8):
            msl = slice(m * 128, (m + 1) * 128)
            wg, wgb_ = load2(wpg_d[0, :, msl])
            wl, wlb_ = load2(wpl_d[0, :, msl])
            wa, wab_ = load2(w_in_d[0, :, OFF_GA + m * 128: OFF_GA + (m + 1) * 128])
            wbb, wbbb_ = load2(w_in_d[0, :, OFF_GBG + m * 128: OFF_GBG + (m + 1) * 128])
            for tg in range(4):
                sl_ = slice(tg * 512, (tg + 1) * 512)
                pga, pgab = next_ps(); proj_fm(wa, wab_, tg, pga, pgab)
                act.op(lambda h: h.activation(out=sga[:], in_=pga[:, :], func=AF.Sigmoid), r=[pgab], w=[sgab])
                pgb_, pgbb = next_ps(); proj_fm(wbb, wbbb_, tg, pgb_, pgbb)
                act.op(lambda h: h.activation(out=sgb[:], in_=pgb_[:, :], func=AF.Sigmoid), r=[pgbb], w=[sgbb])
                pya, pyab = next_ps(); proj_fm(wg, wgb_, tg, pya, pyab, rhsT=oaT, rb=oaTb)
                dve.op(lambda h: h.tensor_tensor(out=t1[:], in0=pya[:, :], in1=sga[:], op=ALU.mult), r=[pyab, sgab], w=[t1b])
                pyb, pybb = next_ps(); proj_fm(wl, wlb_, tg, pyb, pybb, rhsT=obT, rb=obTb)
                dve.op(lambda h: h.tensor_tensor(out=t2[:], in0=pyb[:, :], in1=sgb[:], op=ALU.mult), r=[pybb, sgbb], w=[t2b])
                dve.op(lambda h: h.tensor_tensor(out=mT[:, m, sl_], in0=t1[:], in1=t2[:], op=ALU.add), r=[t1b, t2b], w=[mTb])
        if "mT" in dbg:
            tmp = sb("dbg_tmp5", [128, T]); tb = Buf("dbg_tmp5")
            for k in range(8):
                act.op(lambda h: h.copy(out=tmp[:], in_=mT[:, k, :]), r=[mTb], w=[tb])
                sp.dma(dbg["mT"][k * 128:(k + 1) * 128, :], tmp[:], sl_dbg, r=[tb])
        free_to(fin_mark)
        wob = hTb; sl_wo = Slot(nc, "wo")
        for k in range(8):
            pool.dma(hT[:, k, 0:D], wout_d[0, k * 128:(k + 1) * 128, :], sl_wo, w=[wob])
        lnp = sb("lnp", [128, D]); sl_lnp = Slot(nc, "lnp"); lnpb = Buf("lnp")
        sp.dma(lnp[:], ln_post_d[0, :].partition_broadcast(128), sl_lnp, w=[lnpb])
        xr = [sb(f"xr{i}", [128, D]) for i in range(2)]; xrb = [Buf(f"xr{i}") for i in range(2)]
        sl_xr = [Slot(nc, f"xr{i}") for i in range(2)]
        ot = [sb(f"ot{i}", [128, D]) for i in range(2)]; otb = [Buf(f"ot{i}") for i in range(2)]
        st1 = sb("st1", [128, 8]); st1b = Buf("st1")
        junk2 = sb("junk2", [128, 512], BF16); junk2b = Buf("junk2")
        for tt in range(NT):
            s = tt % 2
            tsl = slice(tt * 128, (tt + 1) * 128)
            sp.dma(xr[s][:], x_d[tsl, :], sl_xr[s], w=[xrb[s]])
            pp = []
            for half in range(2):
                p_, pb = next_ps()
                for m in range(8):
                    pe.op(lambda h: h.matmul(p_[:, :], lhsT=mT[:, m, tsl], rhs=hT[:, m, half * 512:(half + 1) * 512], start=(m == 0), stop=(m == 7)),
                          r=[mTb, wob], w=[pb])
                act.op(lambda h: h.activation(out=junk2[:], in_=p_[:, :], func=AF.Square, accum_out=st1[:, half:half + 1]), r=[pb], w=[junk2b, st1b])
                pp.append((p_, pb))
            dve.op(lambda h: h.tensor_tensor(out=st1[:, 2:3], in0=st1[:, 0:1], in1=st1[:, 1:2], op=ALU.add), r=[st1b], w=[st1b])
            act.op(lambda h: h.activation(out=st1[:, 3:4], in_=st1[:, 2:3], func=AF.Ln, bias=EPS, scale=1.0 / D), r=[st1b], w=[st1b])
            act.op(lambda h: h.activation(out=st1[:, 4:5], in_=st1[:, 3:4], func=AF.Exp, scale=-0.5), r=[st1b], w=[st1b])
            for half in range(2):
                p_, pb = pp[half]
                hs = slice(half * 512, (half + 1) * 512)
                dve.op(lambda h: h.scalar_tensor_tensor(out=ot[s][:, hs], in0=p_[:, :], scalar=st1[:, 4:5], in1=lnp[:, hs], op0=ALU.mult, op1=ALU.mult),
                       r=[pb, st1b, lnpb], w=[otb[s]])
                dve.op(lambda h: h.tensor_tensor(out=ot[s][:, hs], in0=ot[s][:, hs], in1=xr[s][:, hs], op=ALU.add), r=[otb[s], xrb[s]], w=[otb[s]])
            sp.dma(out_d[tsl, :], ot[s][:], sl_out, r=[otb[s]])
    else:
        zt = sb("zt", [128, D]); ztb = Buf("zt")
        pool.op(lambda h: h.memset(zt[:], 0.0), w=[ztb])
        for tt in range(NT):
            sp.dma(out_d[tt * 128:(tt + 1) * 128, :], zt[:], sl_out, r=[ztb])
    sp.h.wait_ge(sl_out.sem, sl_out.cnt)
    if sl_dbg.cnt:
        sp.h.wait_ge(sl_dbg.sem, sl_dbg.cnt)
    return nc


_INPUT_NAMES = ["ln_pre_w", "w_in", "conv_w", "a_log_fwd", "a_log_bwd", "dt_bias_fwd", "dt_bias_bwd", "gdn_norm_w", "w_proj_gdn",
                "gk_w2_fwd", "gk_b2_fwd", "gk_w2_bwd", "gk_b2_bwd", "gla_norm_w", "w_proj_gla", "w_out", "ln_post_w"]


def kernel(**inputs):
    x = np.ascontiguousarray(np.asarray(inputs["x"], dtype=np.float32))
    shared = {n: np.ascontiguousarray(np.asarray(inputs[n], dtype=np.float32)) for n in _INPUT_NAMES}
    nc = build_nc()
    in_maps = [dict(shared, x=x[b]) for b in range(8)]
    res = run_bass_kernel_spmd(nc, in_maps, core_ids=list(range(8)))
    return np.stack([np.asarray(r["out"], dtype=np.float32) for r in res.results], axis=0)
```

```python
import numpy as np
import concourse.bass as bass
import concourse.mybir as mybir
from concourse.bass_utils import run_bass_kernel_spmd

F32 = mybir.dt.float32
F32R = mybir.dt.float32r
BF16 = mybir.dt.bfloat16
AF = mybir.ActivationFunctionType
ALU = mybir.AluOpType

T = 2048
NT = 16
D = 1024
NIN = 9280
EPS = 1e-6
BIG = 30000.0
MAXACT = 6
OFF_Q, OFF_K, OFF_V, OFF_Z = 0, 1024, 2048, 3072
OFF_SM = 4096
OFF_QB, OFF_KB, OFF_VB, OFF_GB = 4128, 4640, 5152, 6176
OFF_R = 7200
OFF_GA, OFF_GBG = 7232, 8256


class Ev:
    __slots__ = ("sem", "val", "key")

    def __init__(self, sem, val, key):
        self.sem, self.val, self.key = sem, val, key


class Buf:
    __slots__ = ("name", "wev", "revs")

    def __init__(self, name):
        self.name, self.wev, self.revs = name, None, {}


class Slot:
    registry = None

    def __init__(self, nc, name):
        if Slot.registry is not None:
            Slot.registry.append(self)
        self.name = name
        self.sem = nc.semaphore("ds_" + name).__enter__()
        self.cnt = 0


class Eng:
    def __init__(self, nc, name, h, selfsync):
        self.name, self.h, self.selfsync = name, h, selfsync
        self.sem = nc.semaphore("es_" + name).__enter__()
        self.cnt = 0
        self.seen = {}

    def wait(self, ev):
        if ev is None:
            return
        if ev.sem is self.sem and not self.selfsync:
            return
        if self.seen.get(ev.key, 0) >= ev.val:
            return
        self.h.wait_ge(ev.sem, ev.val)
        self.seen[ev.key] = ev.val

    def _deps(self, r, w):
        for b in r:
            self.wait(b.wev)
        for b in w:
            self.wait(b.wev)
            for ev in b.revs.values():
                self.wait(ev)

    def _mark(self, ev, r, w):
        for b in r:
            b.revs[ev.key] = ev
        for b in w:
            b.wev = ev
            b.revs = {}

    def op(self, fn, r=(), w=()):
        self._deps(r, w)
        ins = fn(self.h)
        self.cnt += 1
        ins.then_inc(self.sem, 1)
        ev = Ev(self.sem, self.cnt, self.name)
        self._mark(ev, r, w)
        return ev

    def dma(self, out, in_, slot, r=(), w=(), **kw):
        self._deps(r, w)
        ins = self.h.dma_start(out=out, in_=in_, **kw)
        slot.cnt += 16
        ins.then_inc(slot.sem, 16)
        ev = Ev(slot.sem, slot.cnt, slot.name)
        self._mark(ev, r, w)
        return ev


def build_nc(debug=(), gdn_heads=8, gla_heads=4, do_final=True):
    nc = bass.Bass("TRN2", target_bir_lowering=False)
    dram_in = lambda n, s: nc.dram_tensor(n, s, F32, kind="ExternalInput").ap()
    x_d = dram_in("x", [T, D])
    ln_pre_d = dram_in("ln_pre_w", [1, D])
    w_in_d = dram_in("w_in", [1, D, NIN])
    conv_d = dram_in("conv_w", [1, 5, 3072])
    alog_f_d = dram_in("a_log_fwd", [1, 8]); alog_b_d = dram_in("a_log_bwd", [1, 8])
    dtb_f_d = dram_in("dt_bias_fwd", [1, 8]); dtb_b_d = dram_in("dt_bias_bwd", [1, 8])
    gdn_nw_d = dram_in("gdn_norm_w", [1, 128])
    wpg_d = dram_in("w_proj_gdn", [1, D, D])
    gkw_f_d = dram_in("gk_w2_fwd", [1, 16, 512]); gkb_f_d = dram_in("gk_b2_fwd", [1, 512])
    gkw_b_d = dram_in("gk_w2_bwd", [1, 16, 512]); gkb_b_d = dram_in("gk_b2_bwd", [1, 512])
    gla_nw_d = dram_in("gla_norm_w", [1, 256])
    wpl_d = dram_in("w_proj_gla", [1, D, D])
    wout_d = dram_in("w_out", [1, D, D])
    ln_post_d = dram_in("ln_post_w", [1, D])
    out_d = nc.dram_tensor("out", [T, D], F32, kind="ExternalOutput").ap()
    dbg = {}
    for name, shape in debug:
        dbg[name] = nc.dram_tensor("dbg_" + name, shape, F32, kind="ExternalOutput").ap()

    pe = Eng(nc, "pe", nc.tensor, False)
    act = Eng(nc, "act", nc.scalar, True)
    dve = Eng(nc, "dve", nc.vector, True)
    pool = Eng(nc, "pool", nc.gpsimd, True)
    sp = Eng(nc, "sp", nc.sync, False)

    allocs = []

    def sb(name, shape, dt=F32):
        cm = nc.sbuf_tensor(name, shape, dt)
        t = cm.__enter__()
        allocs.append(cm)
        return t

    all_slots = []
    Slot.registry = all_slots

    def barrier():
        engs = (pe, act, dve, pool, sp)
        for e in engs:
            for f in engs:
                if f is not e and f.cnt:
                    e.wait(Ev(f.sem, f.cnt, f.name))
            for sl in all_slots:
                if sl.cnt:
                    e.wait(Ev(sl.sem, sl.cnt, sl.name))

    def free_to(mark):
        barrier()
        while len(allocs) > mark:
            allocs.pop().__exit__(None, None, None)

    NPS = 7
    ps = [nc.psum_tensor(f"ps{i}", [128, 512], F32).__enter__() for i in range(NPS)]
    psb = [Buf(f"ps{i}") for i in range(NPS)]
    pst = nc.psum_tensor("pst", [128, 1024], BF16).__enter__()
    pstb = Buf("pst")
    ps_rr = {}
    PS_POOLS = {"any": list(range(NPS)), "prep": [0, 1, 2], "scan": [3, 4], "proj": [5, 6], "gprep": [0, 1, 2], "gscan": [3, 4, 5, 6]}

    def next_ps(pool_="any"):
        lst = PS_POOLS[pool_]
        c = ps_rr.get(pool_, 0)
        ps_rr[pool_] = c + 1
        i = lst[c % len(lst)]
        return ps[i], psb[i]

    ps_busy = [False] * NPS

    def acquire(pool_):
        while True:
            for i in PS_POOLS[pool_]:
                if not ps_busy[i]:
                    ps_busy[i] = True
                    return i
            yield

    def release(i):
        ps_busy[i] = False

    class Task:
        def __init__(self, gen, deps=()):
            self.gen, self.deps, self.done = gen, [d for d in deps if d is not None], False

    stall = [0]

    def run_tasks(tasks, max_active=6):
        pending = list(tasks)
        active = []
        while pending or active:
            for t in list(pending):
                if len(active) >= max_active:
                    break
                if all(d.done for d in t.deps):
                    pending.remove(t)
                    active.append(t)
            assert active, "task deadlock"
            before = (pe.cnt, act.cnt, dve.cnt, pool.cnt, len(active), len(pending))
            for t in list(active):
                try:
                    next(t.gen)
                except StopIteration:
                    t.done = True
                    active.remove(t)
            if before == (pe.cnt, act.cnt, dve.cnt, pool.cnt, len(active), len(pending)):
                stall[0] += 1
                assert stall[0] < 200, ("scheduler stall", [getattr(t.gen, "__name__", "?") for t in active], list(ps_busy), len(pending))
            else:
                stall[0] = 0

    cst = Buf("const")
    ident32 = sb("ident32", [128, 128]); ident_bf = sb("ident_bf", [128, 128], BF16)
    ones32 = sb("ones32", [128, 128]); ones_bf = sb("ones_bf", [128, 128], BF16)
    zeros32 = sb("zeros32", [128, 128])
    pool.op(lambda h: h.memset(ones32[:], 1.0), w=[cst])
    pool.op(lambda h: h.memset(zeros32[:], 0.0), w=[cst])
    pool.op(lambda h: h.affine_select(out=ident32[:], in_=zeros32[:], pattern=[[-1, 128]], compare_op=ALU.not_equal,
                                      fill=1.0, base=0, channel_multiplier=1), r=[cst], w=[cst])
    pool.op(lambda h: h.tensor_copy(out=ident_bf[:], in_=ident32[:]), r=[cst], w=[cst])
    pool.op(lambda h: h.tensor_copy(out=ones_bf[:], in_=ones32[:]), r=[cst], w=[cst])

    src_cache = {}

    def tri_const(name, n, inval, fillval, offval, step, cm, cmp):
        t = sb(name, [128, 128])
        pool.op(lambda h: h.memset(t[:], offval), w=[cst])
        if inval not in src_cache:
            src_cache[inval] = sb(f"src_{len(src_cache)}", [128, 128])
            pool.op(lambda h: h.memset(src_cache[inval][:], inval), w=[cst])
        src = src_cache[inval]
        for b0 in range(0, 128, n):
            pool.op(lambda h: h.affine_select(out=t[b0:b0 + n, b0:b0 + n], in_=src[b0:b0 + n, b0:b0 + n], pattern=[[step, n]],
                                              compare_op=cmp, fill=fillval, base=0, channel_multiplier=cm), r=[cst], w=[cst])
        return t

    TRI = {
        0: tri_const("tri_f", 64, 1.0, 0.0, 0.0, 1, -1, ALU.is_ge),
        1: tri_const("tri_b", 64, 1.0, 0.0, 0.0, -1, 1, ALU.is_ge),
    }
    NEG_A = {
        0: tri_const("nega_f", 64, 0.0, BIG, BIG, -1, 1, ALU.is_gt),
        1: tri_const("nega_b", 64, 0.0, BIG, BIG, 1, -1, ALU.is_gt),
    }
    NEG_B = {
        0: tri_const("negb_f", 64, 0.0, -BIG, -BIG, 1, -1, ALU.is_gt),
        1: tri_const("negb_b", 64, 0.0, -BIG, -BIG, -1, 1, ALU.is_gt),
    }
    NEG_C = {
        0: tri_const("negc_f", 64, 0.0, -BIG, -BIG, 1, -1, ALU.is_ge),
        1: tri_const("negc_b", 64, 0.0, -BIG, -BIG, -1, 1, ALU.is_ge),
    }
    BDONES = tri_const("bdones", 64, 1.0, 1.0, 0.0, 1, 1, ALU.is_ge)
    ones_r = sb("ones_r", [128, 128], F32R); ident_r = sb("ident_r", [128, 128], F32R)
    pool.op(lambda h: h.tensor_copy(out=ones_r[:], in_=ones32[:]), r=[cst], w=[cst])
    pool.op(lambda h: h.tensor_copy(out=ident_r[:], in_=ident32[:]), r=[cst], w=[cst])
    MASKS_R = {}
    for d__ in range(2):
        mk = sb(f"masks_r{d__}", [128, 512], F32R)
        pool.op(lambda h: h.tensor_copy(out=mk[:, 0:128], in_=NEG_A[d__][:]), r=[cst], w=[cst])
        pool.op(lambda h: h.tensor_copy(out=mk[:, 128:256], in_=NEG_B[d__][:]), r=[cst], w=[cst])
        pool.op(lambda h: h.tensor_copy(out=mk[:, 256:384], in_=NEG_C[d__][:]), r=[cst], w=[cst])
        pool.op(lambda h: h.tensor_copy(out=mk[:, 384:512], in_=zeros32[:]), r=[cst], w=[cst])
        MASKS_R[d__] = mk
    SEL0 = sb("sel0", [128, 128]); SEL1 = sb("sel1", [128, 128])
    pool.op(lambda h: h.memset(SEL0[:], 0.0), w=[cst]); pool.op(lambda h: h.memset(SEL1[:], 0.0), w=[cst])
    pool.op(lambda h: h.memset(SEL0[0:64, :], 1.0), w=[cst]); pool.op(lambda h: h.memset(SEL1[64:128, :], 1.0), w=[cst])
    GS = -1.0 / 16.0
    TRIG = {0: tri_const("trig_f", 128, GS, 0.0, 0.0, 1, -1, ALU.is_ge),
            1: tri_const("trig_b", 128, GS, 0.0, 0.0, -1, 1, ALU.is_ge)}
    DIFG = {0: tri_const("difg_f", 128, 0.0, GS, 0.0, 1, -1, ALU.is_ge),
            1: tri_const("difg_b", 128, 0.0, GS, 0.0, -1, 1, ALU.is_ge)}
    MASKG = {0: tri_const("maskg_f", 128, 1.0, 0.0, 0.0, 1, -1, ALU.is_ge),
             1: tri_const("maskg_b", 128, 1.0, 0.0, 0.0, -1, 1, ALU.is_ge)}

    prm = Buf("params")
    sl_prm = Slot(nc, "prm")
    lnw_T = sb("lnw_T", [128, 8])
    sp.dma(lnw_T[:], ln_pre_d[0, :].rearrange("(k p) -> p k", p=128), sl_prm, w=[prm], allow_slow_non_contiguous=True)
    cw = sb("cw", [128, 24, 5])
    for t_ in range(5):
        sp.dma(cw[:, :, t_], conv_d[0, t_, :].rearrange("(b p) -> p b", p=128), sl_prm, w=[prm], allow_slow_non_contiguous=True)
    prm16 = sb("prm16", [128, 2, 16])
    sp.dma(prm16[:, 0, 0:8], alog_f_d[0, :].partition_broadcast(128), sl_prm, w=[prm])
    sp.dma(prm16[:, 0, 8:16], alog_b_d[0, :].partition_broadcast(128), sl_prm, w=[prm])
    sp.dma(prm16[:, 1, 0:8], dtb_f_d[0, :].partition_broadcast(128), sl_prm, w=[prm])
    sp.dma(prm16[:, 1, 8:16], dtb_b_d[0, :].partition_broadcast(128), sl_prm, w=[prm])
    gdn_nw = sb("gdn_nw", [128, 1])
    sp.dma(gdn_nw[:], gdn_nw_d[0, :].rearrange("(p o) -> p o", o=1), sl_prm, w=[prm])
    gla_nw = sb("gla_nw", [128, 2])
    sp.dma(gla_nw[:], gla_nw_d[0, :].rearrange("(k p) -> p k", p=128), sl_prm, w=[prm], allow_slow_non_contiguous=True)
    w2cat = sb("w2cat", [33, 2, 512])
    pool.op(lambda h: h.memset(w2cat[0:32, :, :], 0.0), w=[prm])
    sp.dma(w2cat[0:16, 0, :], gkw_f_d[0, :, :], sl_prm, r=[prm], w=[prm])
    sp.dma(w2cat[16:32, 1, :], gkw_b_d[0, :, :], sl_prm, w=[prm])
    sp.dma(w2cat[32:33, 0, :], gkb_f_d[0:1, :], sl_prm, w=[prm])
    sp.dma(w2cat[32:33, 1, :], gkb_b_d[0:1, :], sl_prm, w=[prm])

    def dbg_out(name, src_ap, rbufs, dst=None):
        if name not in dbg:
            return
        d = dbg[name] if dst is None else dst
        sp.dma(d, src_ap, sl_dbg, r=rbufs)

    sl_dbg = Slot(nc, "dbg")
    sl_out2 = [Slot(nc, "out0"), Slot(nc, "out1")]

    hT = sb("hT", [128, 8, T], BF16); hTb = Buf("hT")
    oaT = sb("oaT", [128, 8, T], BF16); oaTb = Buf("oaT")
    base_mark = len(allocs)

    xt = [sb(f"xt{i}", [128, D]) for i in range(2)]; xtb = [Buf(f"xt{i}") for i in range(2)]
    sl_x = [Slot(nc, f"x{i}") for i in range(2)]
    junk = sb("junk", [128, D], BF16); junkb = Buf("junk")
    xn = sb("xn", [128, D], BF16); xnb = Buf("xn")
    st0 = sb("st0", [128, 4]); st0b = Buf("st0")
    for tt in range(NT):
        s = tt % 2
        sp.dma(xt[s][:], x_d[tt * 128:(tt + 1) * 128, :], sl_x[s], w=[xtb[s]])
        act.op(lambda h: h.activation(out=junk[:], in_=xt[s][:], func=AF.Square, accum_out=st0[:, 0:1]), r=[xtb[s]], w=[junkb, st0b])
        act.op(lambda h: h.activation(out=st0[:, 1:2], in_=st0[:, 0:1], func=AF.Ln, bias=EPS, scale=1.0 / D), r=[st0b], w=[st0b])
        act.op(lambda h: h.activation(out=st0[:, 2:3], in_=st0[:, 1:2], func=AF.Exp, scale=-0.5), r=[st0b], w=[st0b])
        act.op(lambda h: h.activation(out=xn[:], in_=xt[s][:], func=AF.Copy, scale=st0[:, 2:3]), r=[xtb[s], st0b], w=[xnb])
        for k in range(8):
            pe.op(lambda h: h.transpose(pst[:, k * 128:(k + 1) * 128], xn[:, k * 128:(k + 1) * 128], ident_bf[:]), r=[xnb, cst], w=[pstb])
        dve.op(lambda h: h.tensor_tensor(out=hT[:, :, tt * 128:(tt + 1) * 128], in0=pst[:].rearrange("p (k t) -> p k t", k=8),
                                         in1=lnw_T[:, :].unsqueeze(2).to_broadcast([128, 8, 128]), op=ALU.mult), r=[pstb, prm], w=[hTb])
    if "st0" in dbg:
        sp.dma(dbg["st0"], st0[:], sl_dbg, r=[st0b])
        tmpx = sb("dbg_tmpx", [128, D]); tbx = Buf("dbg_tmpx")
        act.op(lambda h: h.copy(out=tmpx[:], in_=xn[:]), r=[xnb], w=[tbx])
        sp.dma(dbg["xn"], tmpx[:], sl_dbg, r=[tbx])
    if "hT" in dbg:
        tmp = sb("dbg_tmp", [128, T])
        tb = Buf("dbg_tmp")
        for k in range(8):
            act.op(lambda h: h.copy(out=tmp[:], in_=hT[:, k, :]), r=[hTb], w=[tb])
            sp.dma(dbg["hT"][k * 128:(k + 1) * 128, :], tmp[:], sl_dbg, r=[tb])
    free_to(base_mark)

    NWS = 2
    wblk = [sb(f"wblk{i}", [128, 8, 128], BF16) for i in range(NWS)]
    wblkb = [Buf(f"wblk{i}") for i in range(NWS)]
    sl_w = [Slot(nc, f"w{i}") for i in range(NWS)]
    w_rr = [0]

    def load_wblk(src_ap):
        i = w_rr[0] % NWS
        w_rr[0] += 1
        pool.dma(wblk[i][:], src_ap.rearrange("(k p) c -> p k c", p=128), sl_w[i], w=[wblkb[i]])
        return wblk[i], wblkb[i]

    def proj_fm(wt, wb, tg, pst_, pb, rhsT=hT, rb=hTb):
        for k in range(8):
            pe.op(lambda h: h.matmul(pst_[:, :], lhsT=wt[:, k, :], rhs=rhsT[:, k, tg * 512:(tg + 1) * 512], start=(k == 0), stop=(k == 7)),
                  r=[wb, rb], w=[pb])

    mix_mark = len(allocs)

    NCOL = 16
    LG = sb("LG", [128, NT, NCOL]); LNB = sb("LNB", [128, NT, NCOL]); BETA = sb("BETA", [128, NT, NCOL])
    A_tok = sb("A_tok", [128, NT, NCOL]); NG_tok = sb("NG_tok", [128, NT, NCOL]); BG = sb("BG", [128, NT, NCOL])
    KD = sb("KD", [128, NT, NCOL]); ET0 = sb("ET0", [128, NT, NCOL]); ET1 = sb("ET1", [128, NT, NCOL])
    scal = Buf("scal")
    s1_mark = len(allocs)
    wsm = sb("wsm", [128, 8, 32], BF16); wsmb = Buf("wsm")
    sl_wsm = Slot(nc, "wsm")
    pool.dma(wsm[:], w_in_d[0, :, OFF_SM:OFF_SM + 32].rearrange("(k p) c -> p k c", p=128), sl_wsm, w=[wsmb])
    asm = sb("asm", [128, NT, 32]); asmb = Buf("asm")
    for tt in range(NT):
        p_, pb = next_ps()
        for k in range(8):
            pe.op(lambda h: h.matmul(p_[:, 0:32], lhsT=hT[:, k, tt * 128:(tt + 1) * 128], rhs=wsm[:, k, :], start=(k == 0), stop=(k == 7)),
                  r=[hTb, wsmb], w=[pb])
        act.op(lambda h: h.copy(out=asm[:, tt, :], in_=p_[:, 0:32]), r=[pb], w=[asmb])
    t16 = sb("t16", [128, NT, NCOL]); t16b = Buf("t16")
    dve.op(lambda h: h.tensor_tensor(out=t16[:], in0=asm[:, :, 0:16], in1=prm16[:, 1:2, :].to_broadcast([128, NT, NCOL]), op=ALU.add), r=[asmb, prm], w=[t16b])
    act.op(lambda h: h.activation(out=t16[:], in_=t16[:], func=AF.Exp), r=[t16b], w=[t16b])
    act.op(lambda h: h.activation(out=t16[:], in_=t16[:], func=AF.Ln, bias=1.0), r=[t16b], w=[t16b])
    nA = sb("nA", [128, 1, NCOL]); nAb = Buf("nA")
    act.op(lambda h: h.activation(out=nA[:], in_=prm16[:, 0:1, :], func=AF.Exp), r=[prm], w=[nAb])
    dve.op(lambda h: h.scalar_tensor_tensor(out=LG[:], in0=t16[:], scalar=-1.0, in1=nA[:].to_broadcast([128, NT, NCOL]), op0=ALU.mult, op1=ALU.mult),
           r=[t16b, nAb], w=[scal])
    act.op(lambda h: h.activation(out=t16[:], in_=asm[:, :, 16:32], func=AF.Exp, scale=-1.0), r=[asmb, scal], w=[t16b])
    act.op(lambda h: h.activation(out=t16[:], in_=t16[:], func=AF.Ln, bias=1.0), r=[t16b], w=[t16b])
    act.op(lambda h: h.mul(out=LNB[:], in_=t16[:], mul=-1.0), r=[t16b], w=[scal])
    act.op(lambda h: h.activation(out=BETA[:], in_=LNB[:], func=AF.Exp), r=[scal], w=[scal])
    G_tok = sb("G_tok", [128, NT, NCOL]); TOTO = sb("TOTO", [128, NT, NCOL])
    for tt in range(NT):
        p_, pb = next_ps()
        for i, M in enumerate([TRI[0], TRI[1], BDONES, SEL0, SEL1]):
            pe.op(lambda h: h.matmul(p_[:, i * 16:(i + 1) * 16], lhsT=M[:], rhs=LG[:, tt, :], start=True, stop=True), r=[cst, scal], w=[pb])
        act.op(lambda h: h.copy(out=G_tok[:, tt, 0:8], in_=p_[:, 0:8]), r=[pb], w=[scal])
        act.op(lambda h: h.copy(out=G_tok[:, tt, 8:16], in_=p_[:, 24:32]), r=[pb], w=[scal])
        act.op(lambda h: h.copy(out=TOTO[:, tt, :], in_=p_[:, 32:48]), r=[pb], w=[scal])
        act.op(lambda h: h.activation(out=ET0[:, tt, :], in_=p_[:, 48:64], func=AF.Exp), r=[pb], w=[scal])
        act.op(lambda h: h.activation(out=ET1[:, tt, :], in_=p_[:, 64:80], func=AF.Exp), r=[pb], w=[scal])
    dve.op(lambda h: h.tensor_tensor(out=A_tok[:], in0=G_tok[:], in1=LNB[:], op=ALU.add), r=[scal], w=[scal])
    act.op(lambda h: h.mul(out=NG_tok[:], in_=G_tok[:], mul=-1.0), r=[scal], w=[scal])
    act.op(lambda h: h.activation(out=BG[:], in_=A_tok[:], func=AF.Exp), r=[scal], w=[scal])
    dve.op(lambda h: h.tensor_tensor(out=KD[:], in0=TOTO[:], in1=G_tok[:], op=ALU.subtract), r=[scal], w=[scal])
    act.op(lambda h: h.activation(out=KD[:], in_=KD[:], func=AF.Exp), r=[scal], w=[scal])
    if "LG" in dbg:
        sp.dma(dbg["LG"].rearrange("(t p) c -> p t c", p=128), LG[:], sl_dbg, r=[scal])
        sp.dma(dbg["BETA"].rearrange("(t p) c -> p t c", p=128), BETA[:], sl_dbg, r=[scal])
        sp.dma(dbg["G_tok"].rearrange("(t p) c -> p t c", p=128), G_tok[:], sl_dbg, r=[scal])

    free_to(s1_mark)
    gdn_mark = len(allocs)
    if gdn_heads > 0:
        pre = sb("pre", [128, T + 4]); preb = Buf("pre")
        pool.op(lambda h: h.memset(pre[:, 0:2], 0.0), w=[preb]); pool.op(lambda h: h.memset(pre[:, T + 2:T + 4], 0.0), w=[preb])
        acc = sb("acc", [128, T]); accb = Buf("acc")
        sqt = sb("sqt", [128, 512], BF16); sqtb = Buf("sqt")
        rn = sb("rn", [128, 512]); rnb = Buf("rn")
        HB = []
        for bs in range(2):
            HB.append(dict(qT=sb(f"qT{bs}", [128, T], BF16), qTb=Buf(f"qT{bs}"), kT=sb(f"kT{bs}", [128, T], BF16), kTb=Buf(f"kT{bs}"),
                           k_tok=sb(f"k_tok{bs}", [128, NT, 128], BF16), ktb=Buf(f"k_tok{bs}"),
                           v_tok=sb(f"v_tok{bs}", [128, NT, 128], BF16), vtb=Buf(f"v_tok{bs}")))
        vT = sb("vT", [128, T], BF16); vTb = Buf("vT")
        zs = sb("zs", [128, T], BF16); zsb = Buf("zs")
        oTa = sb("oTa", [128, T]); oTab = Buf("oTa")
        NSL = 2
        WK = {}
        ST = {}
        for d_ in range(2):
            for sl_i in range(NSL):
                W_ = {}
                for nm, shp, dt in [("rhs1", [128, 512], F32R), ("rhsb", [128, 128], F32R), ("D1", [128, 128], F32), ("D1T", [128, 128], F32),
                                    ("D2T", [128, 128], BF16), ("grep", [128, 128], F32),
                                    ("LU", [128, 2, 256], BF16),
                                    ("LU2", [128, 2, 256], BF16),
                                    ("attnT", [128, 128], BF16), ("qdT", [128, 128], BF16), ("khat", [128, 128], BF16), ("bv", [128, 128], BF16),
                                    ("kd", [128, 128], BF16), ("wT", [128, 128], BF16), ("u", [128, 128], F32), ("vnew", [128, 128], BF16)]:
                    W_[nm] = sb(f"{nm}_{d_}_{sl_i}", shp, dt)
                    W_[nm + "_b"] = Buf(f"{nm}_{d_}_{sl_i}")
                WK[(d_, sl_i)] = W_
            S_ = {}
            for nm, shp, dt in [("S32", [128, 128], F32), ("S16", [128, 128], BF16)]:
                S_[nm] = sb(f"{nm}_{d_}", shp, dt)
                S_[nm + "_b"] = Buf(f"{nm}_{d_}")
            ST[d_] = S_

    def gdn_projA(hh, B_):
        for which, off in (("q", OFF_Q), ("k", OFF_K), ("v", OFF_V)):
            wt, wb = load_wblk(w_in_d[0, :, off + hh * 128: off + (hh + 1) * 128])
            for tg in range(4):
                ipj = yield from acquire("proj"); p_, pb = ps[ipj], psb[ipj]
                proj_fm(wt, wb, tg, p_, pb)
                act.op(lambda h: h.copy(out=pre[:, 2 + tg * 512: 2 + (tg + 1) * 512], in_=p_[:, :]), r=[pb], w=[preb])
                release(ipj)
                yield
            blk = off // 128 + hh
            act.op(lambda h: h.activation(out=acc[:], in_=pre[:, 0:T], func=AF.Copy, scale=cw[:, blk, 0:1]), r=[preb, prm], w=[accb])
            for t_ in range(1, 5):
                dve.op(lambda h: h.scalar_tensor_tensor(out=acc[:], in0=pre[:, t_:t_ + T], scalar=cw[:, blk, t_:t_ + 1], in1=acc[:], op0=ALU.mult, op1=ALU.add),
                       r=[preb, prm, accb], w=[accb])
                yield
            if which == "v":
                act.op(lambda h: h.activation(out=vT[:], in_=acc[:], func=AF.Silu), r=[accb], w=[vTb])
                continue
            act.op(lambda h: h.activation(out=acc[:], in_=acc[:], func=AF.Silu), r=[accb], w=[accb])
            dstT, dstb = (B_["qT"], B_["qTb"]) if which == "q" else (B_["kT"], B_["kTb"])
            post = (128.0 ** -0.5) if which == "q" else 1.0
            for tg in range(4):
                sl_ = slice(tg * 512, (tg + 1) * 512)
                act.op(lambda h: h.activation(out=sqt[:], in_=acc[:, sl_], func=AF.Square), r=[accb], w=[sqtb])
                ipj = yield from acquire("proj"); p_, pb = ps[ipj], psb[ipj]
                pe.op(lambda h: h.matmul(p_[:, :], lhsT=ones_bf[:], rhs=sqt[:], start=True, stop=True), r=[cst, sqtb], w=[pb])
                yield
                act.op(lambda h: h.activation(out=rn[:], in_=p_[:, :], func=AF.Ln, bias=EPS), r=[pb], w=[rnb])
                release(ipj)
                act.op(lambda h: h.activation(out=rn[:], in_=rn[:], func=AF.Exp, scale=-0.5), r=[rnb], w=[rnb])
                dve.op(lambda h: h.scalar_tensor_tensor(out=dstT[:, sl_], in0=acc[:, sl_], scalar=post, in1=rn[:], op0=ALU.mult, op1=ALU.mult),
                       r=[accb, rnb], w=[dstb])
                yield
        for srcT, srcb, dst, dstb in ((B_["kT"], B_["kTb"], B_["k_tok"], B_["ktb"]), (vT, vTb, B_["v_tok"], B_["vtb"])):
            for g8 in range(2):
                for j in range(8):
                    tt = g8 * 8 + j
                    pe.op(lambda h: h.transpose(pst[:, j * 128:(j + 1) * 128], srcT[:, tt * 128:(tt + 1) * 128], ident_bf[:]), r=[srcb, cst], w=[pstb])
                act.op(lambda h: h.copy(out=dst[:, g8 * 8:(g8 + 1) * 8, :], in_=pst[:].rearrange("p (j c) -> p j c", j=8)), r=[pstb], w=[dstb])
                yield

    def gdn_projZ(hh):
        wt, wb = load_wblk(w_in_d[0, :, OFF_Z + hh * 128: OFF_Z + (hh + 1) * 128])
        for tg in range(4):
            ipj = yield from acquire("proj"); p_, pb = ps[ipj], psb[ipj]
            proj_fm(wt, wb, tg, p_, pb)
            act.op(lambda h: h.activation(out=zs[:, tg * 512:(tg + 1) * 512], in_=p_[:, :], func=AF.Silu), r=[pb], w=[zsb])
            release(ipj)
            yield

    def gdn_norm(hh):
        if f"oa{hh}" in dbg:
            sp.dma(dbg[f"oa{hh}"], oTa[:], sl_dbg, r=[oTab])
        for tg in range(4):
            sl_ = slice(tg * 512, (tg + 1) * 512)
            act.op(lambda h: h.activation(out=sqt[:], in_=oTa[:, sl_], func=AF.Square), r=[oTab], w=[sqtb])
            ipj = yield from acquire("proj")
            p_, pb = ps[ipj], psb[ipj]
            pe.op(lambda h: h.matmul(p_[:, :], lhsT=ones_bf[:], rhs=sqt[:], start=True, stop=True), r=[cst, sqtb], w=[pb])
            yield
            act.op(lambda h: h.activation(out=rn[:], in_=p_[:, :], func=AF.Ln, bias=EPS, scale=1.0 / 128), r=[pb], w=[rnb])
            release(ipj)
            act.op(lambda h: h.activation(out=rn[:], in_=rn[:], func=AF.Exp, scale=-0.5), r=[rnb], w=[rnb])
            dve.op(lambda h: h.scalar_tensor_tensor(out=rn[:], in0=oTa[:, sl_], scalar=gdn_nw[:, 0:1], in1=rn[:], op0=ALU.mult, op1=ALU.mult),
                   r=[oTab, rnb, prm], w=[rnb])
            dve.op(lambda h: h.tensor_tensor(out=oaT[:, hh, sl_], in0=rn[:], in1=zs[:, sl_], op=ALU.mult), r=[rnb, zsb], w=[oaTb])
            yield

    all_tasks = []
    PRA, PRZ, NORM = [], [], []
    for hh in range(gdn_heads):
        B_ = HB[hh % 2]
        pra = Task(gdn_projA(hh, B_), deps=[PRA[hh - 1] if hh >= 1 else None, NORM[hh - 2] if hh >= 2 else None, PRZ[hh - 1] if hh >= 1 else None])
        PRA.append(pra)
        all_tasks.append(pra)
        prz = Task(gdn_projZ(hh), deps=[pra, NORM[hh - 1] if hh >= 1 else None])
        PRZ.append(prz)
        def gdn_prep(d_, tt, W_, hh=hh, B_=B_):
            qT, qTb, kT, kTb, k_tok, ktb, v_tok, vtb = B_["qT"], B_["qTb"], B_["kT"], B_["kTb"], B_["k_tok"], B_["ktb"], B_["v_tok"], B_["vtb"]
            col = d_ * 8 + hh
            tsl = slice(tt * 128, (tt + 1) * 128)
            act.op(lambda h: h.activation(out=W_["rhs1"][:].rearrange("p (a b) -> p a b", a=4), in_=TRI[d_][:, :].unsqueeze(1).to_broadcast([128, 4, 128]),
                                          func=AF.Copy, scale=LG[:, tt, col:col + 1]), r=[cst, scal], w=[W_["rhs1_b"]])
            act.op(lambda h: h.activation(out=W_["rhsb"][:], in_=ident32[:], func=AF.Copy, scale=LNB[:, tt, col:col + 1]),
                   r=[cst, scal], w=[W_["rhsb_b"]])
            act.op(lambda h: h.activation(out=W_["khat"][:], in_=k_tok[:, tt, :], func=AF.Copy, scale=BG[:, tt, col:col + 1]),
                   r=[ktb, scal], w=[W_["khat_b"]])
            act.op(lambda h: h.activation(out=W_["bv"][:], in_=v_tok[:, tt, :], func=AF.Copy, scale=BETA[:, tt, col:col + 1]),
                   r=[vtb, scal], w=[W_["bv_b"]])
            pool.op(lambda h: h.tensor_scalar(out=W_["kd"][:], in0=k_tok[:, tt, :], scalar1=KD[:, tt, col:col + 1], scalar2=None, op0=ALU.mult),
                    r=[ktb, scal], w=[W_["kd_b"]])
            yield
            ia = yield from acquire("prep")
            pa, pab = ps[ia], psb[ia]
            pe.op(lambda h: h.matmul(pa[:, :], lhsT=ones_r[:], rhs=W_["rhs1"][:], start=True, stop=False), r=[cst, W_["rhs1_b"]], w=[pab])
            pe.op(lambda h: h.matmul(pa[:, :], lhsT=ident_r[:], rhs=MASKS_R[d_][:], start=False, stop=False), r=[cst], w=[pab])
            pe.op(lambda h: h.matmul(pa[:, 128:256], lhsT=ones_r[:], rhs=W_["rhsb"][:], start=False, stop=True), r=[cst, W_["rhsb_b"]], w=[pab])
            yield
            act.op(lambda h: h.activation(out=W_["D1"][:], in_=pa[:, 0:128], func=AF.Exp, bias=A_tok[:, tt, col:col + 1], scale=-1.0),
                   r=[pab, scal], w=[W_["D1_b"]])
            act.op(lambda h: h.activation(out=W_["D1T"][:], in_=pa[:, 128:256], func=AF.Exp, bias=NG_tok[:, tt, col:col + 1], scale=1.0),
                   r=[pab, scal], w=[W_["D1T_b"]])
            act.op(lambda h: h.activation(out=W_["D2T"][:], in_=pa[:, 256:384], func=AF.Exp, bias=NG_tok[:, tt, col:col + 1], scale=1.0),
                   r=[pab, scal], w=[W_["D2T_b"]])
            act.op(lambda h: h.activation(out=W_["grep"][:], in_=pa[:, 384:512], func=AF.Exp), r=[pab], w=[W_["grep_b"]])
            release(ia)
            yield
            ikq = yield from acquire("prep")
            pkq, pkqb = ps[ikq], psb[ikq]
            pe.op(lambda h: h.matmul(pkq[:, 0:128], lhsT=kT[:, tsl], rhs=kT[:, tsl], start=True, stop=True), r=[kTb], w=[pkqb])
            pe.op(lambda h: h.matmul(pkq[:, 128:256], lhsT=kT[:, tsl], rhs=qT[:, tsl], start=True, stop=True), r=[kTb, qTb], w=[pkqb])
            yield
            LU, LUb, LU2, LU2b = W_["LU"], W_["LU_b"], W_["LU2"], W_["LU2_b"]
            dve.op(lambda h: h.tensor_tensor(out=LU[:, 1, 0:128], in0=pkq[:, 0:128], in1=W_["D1"][:], op=ALU.mult), r=[pkqb, W_["D1_b"]], w=[LUb])
            dve.op(lambda h: h.tensor_tensor(out=LU[:, 0, 0:128], in0=pkq[:, 0:128], in1=W_["D1T"][:], op=ALU.mult), r=[pkqb, W_["D1T_b"]], w=[LUb])
            dve.op(lambda h: h.tensor_tensor(out=LU2[:, 0, 128:256], in0=ident32[:], in1=LU[:, 0, 0:128], op=ALU.subtract), r=[cst, LUb], w=[LU2b])
            dve.op(lambda h: h.tensor_tensor(out=W_["attnT"][:], in0=pkq[:, 128:256], in1=W_["D2T"][:], op=ALU.mult), r=[pkqb, W_["D2T_b"]], w=[W_["attnT_b"]])
            dve.op(lambda h: h.tensor_tensor(out=W_["qdT"][:], in0=qT[:, tsl], in1=W_["grep"][:], op=ALU.mult), r=[qTb, W_["grep_b"]], w=[W_["qdT_b"]])
            release(ikq)
            yield
            cur, curb, nxt, nxtb = LU, LUb, LU2, LU2b
            for lev in range(6):
                ipn = yield from acquire("prep")
                pn, pnb = ps[ipn], psb[ipn]
                if lev == 0:
                    pe.op(lambda h: h.matmul(pn[:, 0:128], lhsT=cur[:, 1, 0:128], rhs=cur[:, 0, 0:128], start=True, stop=True), r=[curb], w=[pnb])
                elif lev < 5:
                    pe.op(lambda h: h.matmul(pn[:, 0:256], lhsT=cur[:, 1, 0:128], rhs=cur[:, 0, 0:256], start=True, stop=True), r=[curb], w=[pnb])
                else:
                    pe.op(lambda h: h.matmul(pn[:, 128:256], lhsT=cur[:, 1, 0:128], rhs=cur[:, 0, 128:256], start=True, stop=True), r=[curb], w=[pnb])
                if lev < 5:
                    pe.op(lambda h: h.matmul(pn[:, 256:384], lhsT=cur[:, 0, 0:128], rhs=cur[:, 1, 0:128], start=True, stop=True), r=[curb], w=[pnb])
                yield
                if lev < 5:
                    ev_eng = act
                    if ev_eng is dve:
                        dve.op(lambda h: h.tensor_copy(out=nxt[:, :, 0:128], in_=pn[:, :].rearrange("p (a b) -> p a b", a=2)[:, :, 0:128]), r=[pnb], w=[nxtb])
                    else:
                        act.op(lambda h: h.copy(out=nxt[:, :, 0:128], in_=pn[:, :].rearrange("p (a b) -> p a b", a=2)[:, :, 0:128]), r=[pnb], w=[nxtb])
                if lev > 0:
                    dve.op(lambda h: h.tensor_tensor(out=nxt[:, 0, 128:256], in0=pn[:, 128:256], in1=cur[:, 0, 128:256], op=ALU.add), r=[pnb, curb], w=[nxtb])
                cur, curb, nxt, nxtb = nxt, nxtb, cur, curb
                release(ipn)
                yield
            Wm = cur[:, 0, 128:256]; Wmb = curb
            ipw = yield from acquire("prep")
            pw, pwb = ps[ipw], psb[ipw]
            pe.op(lambda h: h.matmul(pw[:, 0:128], lhsT=W_["khat"][:], rhs=Wm, start=True, stop=True), r=[W_["khat_b"], Wmb], w=[pwb])
            pe.op(lambda h: h.matmul(pw[:, 128:256], lhsT=Wm, rhs=W_["bv"][:], start=True, stop=True), r=[W_["bv_b"], Wmb], w=[pwb])
            yield
            act.op(lambda h: h.copy(out=W_["wT"][:], in_=pw[:, 0:128]), r=[pwb], w=[W_["wT_b"]])
            act.op(lambda h: h.copy(out=W_["u"][:], in_=pw[:, 128:256]), r=[pwb], w=[W_["u_b"]])
            release(ipw)
            yield

        def gdn_scan(d_, tt, W_, S_, hh=hh):
            col = d_ * 8 + hh
            for c in ((0, 1) if d_ == 0 else (1, 0)):
                rs = slice(c * 64, (c + 1) * 64)
                ETc = ET0 if c == 0 else ET1
                ip1 = yield from acquire("scan")
                p1, p1b = ps[ip1], psb[ip1]
                pe.op(lambda h: h.matmul(p1[rs, 0:128], lhsT=W_["wT"][:, rs], rhs=S_["S16"][:], start=True, stop=True), r=[W_["wT_b"], S_["S16_b"]], w=[p1b])
                yield
                dve.op(lambda h: h.tensor_tensor(out=W_["vnew"][rs, :], in0=W_["u"][rs, :], in1=p1[rs, 0:128], op=ALU.subtract),
                       r=[W_["u_b"], p1b], w=[W_["vnew_b"]])
                release(ip1)
                yield
                ip2 = yield from acquire("scan")
                p2, p2b = ps[ip2], psb[ip2]
                pe.op(lambda h: h.matmul(p2[:, 0:64], lhsT=S_["S16"][:], rhs=W_["qdT"][:, rs], start=True, stop=False), r=[S_["S16_b"], W_["qdT_b"]], w=[p2b])
                pe.op(lambda h: h.matmul(p2[:, 0:64], lhsT=W_["vnew"][rs, :], rhs=W_["attnT"][rs, rs], start=False, stop=True),
                      r=[W_["vnew_b"], W_["attnT_b"]], w=[p2b])
                pe.op(lambda h: h.matmul(p2[:, 128:256], lhsT=W_["kd"][rs, :], rhs=W_["vnew"][rs, :], start=True, stop=True),
                      r=[W_["kd_b"], W_["vnew_b"]], w=[p2b])
                yield
                dve.op(lambda h: h.scalar_tensor_tensor(out=S_["S32"][:], in0=S_["S32"][:], scalar=ETc[:, tt, col:col + 1], in1=p2[:, 128:256],
                                                        op0=ALU.mult, op1=ALU.add), r=[S_["S32_b"], scal, p2b], w=[S_["S32_b"]])
                act.op(lambda h: h.copy(out=S_["S16"][:], in_=S_["S32"][:]), r=[S_["S32_b"]], w=[S_["S16_b"]])
                osl = slice(tt * 128 + c * 64, tt * 128 + (c + 1) * 64)
                first = (tt < NT // 2) if d_ == 0 else (tt >= NT // 2)
                if first:
                    act.op(lambda h: h.copy(out=oTa[:, osl], in_=p2[:, 0:64]), r=[p2b], w=[oTab])
                else:
                    dve.op(lambda h: h.tensor_tensor(out=oTa[:, osl], in0=p2[:, 0:64], in1=oTa[:, osl], op=ALU.add), r=[p2b, oTab], w=[oTab])
                release(ip2)
                yield

        def gdn_init():
            for d_ in range(2):
                pool.op(lambda h: h.memset(ST[d_]["S32"][:], 0.0), w=[ST[d_]["S32_b"]])
                pool.op(lambda h: h.memset(ST[d_]["S16"][:], 0.0), w=[ST[d_]["S16_b"]])
            yield
        init_t = Task(gdn_init(), deps=[NORM[hh - 1] if hh >= 1 else None])
        all_tasks.append(init_t)
        order = {0: list(range(NT)), 1: list(range(NT - 1, -1, -1))}
        P = {0: [], 1: []}
        S = {0: [], 1: []}
        tasks = all_tasks
        for i in range(NT):
            for d_ in range(2):
                tt = order[d_][i]
                W_ = WK[(d_, i % NSL)]
                pt = Task(gdn_prep(d_, tt, W_), deps=[S[d_][i - NSL] if i >= NSL else None, PRA[hh], init_t])
                P[d_].append(pt)
                tasks.append(pt)
            for d_ in range(2):
                tt = order[d_][i]
                W_ = WK[(d_, i % NSL)]
                other = S[1 - d_][NT - 1 - i] if (i >= NT // 2 and len(S[1 - d_]) > NT - 1 - i) else None
                stt = Task(gdn_scan(d_, tt, W_, ST[d_]), deps=[P[d_][i], S[d_][i - 1] if i >= 1 else None, other])
                S[d_].append(stt)
                tasks.append(stt)
        all_tasks.append(prz)
        nt = Task(gdn_norm(hh), deps=[prz] + S[0] + S[1])
        NORM.append(nt)
        all_tasks.append(nt)
    if gdn_heads > 0:
        for hh in range(gdn_heads - 1):
            NORM[hh].deps.append(PRA[hh + 1])
        run_tasks(all_tasks, max_active=MAXACT + 2)
    if "oaT" in dbg:
        tmp = sb("dbg_tmp3", [128, T]); tb = Buf("dbg_tmp3")
        for k in range(8):
            act.op(lambda h: h.copy(out=tmp[:], in_=oaT[:, k, :]), r=[oaTb], w=[tb])
            sp.dma(dbg["oaT"][k * 128:(k + 1) * 128, :], tmp[:], sl_dbg, r=[tb])
    free_to(s1_mark)

    obT = sb("obT", [128, 8, T], BF16); obTb = Buf("obT")
    gla_mark = len(allocs)
    if gla_heads > 0:
        NSG = 2
        GW = {}
        GS_ = {}
        for d_ in range(2):
            for sl_i in range(NSG):
                W_ = {}
                for nm, shp, dt in [("gk", [128, 128], F32), ("ekd", [128, 128], F32), ("eg", [128, 128], F32), ("eng", [128, 128], F32), ("etot", [128, 1], F32),
                                    ("kdg", [128, 128], BF16), ("qgT", [128, 128], BF16), ("kgT", [128, 128], BF16), ("attnT", [128, 128], BF16)]:
                    W_[nm] = sb(f"g{nm}_{d_}_{sl_i}", shp, dt)
                    W_[nm + "_b"] = Buf(f"g{nm}_{d_}_{sl_i}")
                GW[(d_, sl_i)] = W_
            S_ = {}
            for nm, shp, dt in [("S32", [128, 256], F32), ("S16", [128, 256], BF16)]:
                S_[nm] = sb(f"g{nm}_{d_}", shp, dt)
                S_[nm + "_b"] = Buf(f"g{nm}_{d_}")
            GS_[d_] = S_

        rT1 = sb("rT1", [33, T]); rT1b = Buf("rT1")
        wr = sb("wr", [128, 8, 32], BF16); wrb = Buf("wr")
        sl_wr = Slot(nc, "wr")
        pool.dma(wr[:], w_in_d[0, :, OFF_R:OFF_R + 32].rearrange("(k p) c -> p k c", p=128), sl_wr, w=[wrb])
        pool.op(lambda h: h.memset(rT1[32:33, :], 1.0), w=[rT1b])
        for tg in range(4):
            p_, pb = next_ps()
            for k in range(8):
                pe.op(lambda h: h.matmul(p_[0:32, :], lhsT=wr[:, k, :], rhs=hT[:, k, tg * 512:(tg + 1) * 512], start=(k == 0), stop=(k == 7)),
                      r=[wrb, hTb], w=[pb])
            act.op(lambda h: h.copy(out=rT1[0:32, tg * 512:(tg + 1) * 512], in_=p_[0:32, :]), r=[pb], w=[rT1b])
        wkv = sb("wkv", [128, 8, 384], BF16); wkvb = Buf("wkv"); sl_wkv = Slot(nc, "wkv")
        qTg = sb("qTg", [128, T], BF16); qTgb = Buf("qTg")
        kTg = sb("kTg", [128, T], BF16); kTgb = Buf("kTg")
        kg_tok = sb("kg_tok", [128, NT, 128], BF16); kgtb = Buf("kg_tok")
        vg_tok = sb("vg_tok", [128, NT, 256], BF16); vgtb = Buf("vg_tok")
        gsT = sb("gsT", [128, 2, T], BF16); gsTb = Buf("gsT")
        obTa = sb("obTa", [128, 2, T]); obTab = Buf("obTa")
        sqg = sb("sqg", [128, 512], BF16); sqgb = Buf("sqg")
        rng = sb("rng", [128, 512]); rngb = Buf("rng")
    for hb in range(gla_heads):
        for off, dst, dstb, scl in ((OFF_QB, qTg, qTgb, 128.0 ** -0.5), (OFF_KB, kTg, kTgb, 1.0)):
            wt, wb = load_wblk(w_in_d[0, :, off + hb * 128: off + (hb + 1) * 128])
            for tg in range(4):
                p_, pb = next_ps()
                proj_fm(wt, wb, tg, p_, pb)
                act.op(lambda h: h.mul(out=dst[:, tg * 512:(tg + 1) * 512], in_=p_[:, :], mul=scl), r=[pb], w=[dstb])
        for eb in range(2):
            wt, wb = load_wblk(w_in_d[0, :, OFF_GB + hb * 256 + eb * 128: OFF_GB + hb * 256 + (eb + 1) * 128])
            for tg in range(4):
                p_, pb = next_ps()
                proj_fm(wt, wb, tg, p_, pb)
                act.op(lambda h: h.activation(out=gsT[:, eb, tg * 512:(tg + 1) * 512], in_=p_[:, :], func=AF.Silu), r=[pb], w=[gsTb])
        pool.dma(wkv[:, :, 0:128], w_in_d[0, :, OFF_KB + hb * 128: OFF_KB + (hb + 1) * 128].rearrange("(k p) c -> p k c", p=128), sl_wkv, w=[wkvb])
        pool.dma(wkv[:, :, 128:384], w_in_d[0, :, OFF_VB + hb * 256: OFF_VB + (hb + 1) * 256].rearrange("(k p) c -> p k c", p=128), sl_wkv, w=[wkvb])
        for tt in range(NT):
            p_, pb = next_ps()
            for k in range(8):
                pe.op(lambda h: h.matmul(p_[:, 0:384], lhsT=hT[:, k, tt * 128:(tt + 1) * 128], rhs=wkv[:, k, :], start=(k == 0), stop=(k == 7)),
                      r=[hTb, wkvb], w=[pb])
            act.op(lambda h: h.copy(out=kg_tok[:, tt, :], in_=p_[:, 0:128]), r=[pb], w=[kgtb])
            act.op(lambda h: h.copy(out=vg_tok[:, tt, :], in_=p_[:, 128:384]), r=[pb], w=[vgtb])

        def gla_prep(d_, tt, W_, hb=hb):
            tsl = slice(tt * 128, (tt + 1) * 128)
            ix = yield from acquire("gprep")
            px, pxb = ps[ix], psb[ix]
            pe.op(lambda h: h.matmul(px[:, 0:128], lhsT=rT1[:, tsl], rhs=w2cat[:, d_, hb * 128:(hb + 1) * 128], start=True, stop=True),
                  r=[rT1b, prm], w=[pxb])
            yield
            act.op(lambda h: h.activation(out=W_["gk"][:], in_=px[:, 0:128], func=AF.Exp, scale=-1.0), r=[pxb], w=[W_["gk_b"]])
            release(ix)
            act.op(lambda h: h.activation(out=W_["gk"][:], in_=W_["gk"][:], func=AF.Ln, bias=1.0), r=[W_["gk_b"]], w=[W_["gk_b"]])
            yield
            ig = yield from acquire("gprep")
            pg, pgb = ps[ig], psb[ig]
            pe.op(lambda h: h.matmul(pg[:, 0:128], lhsT=DIFG[d_][:], rhs=W_["gk"][:], start=True, stop=True), r=[cst, W_["gk_b"]], w=[pgb])
            pe.op(lambda h: h.matmul(pg[:, 128:256], lhsT=W_["gk"][:], rhs=TRIG[d_][:], start=True, stop=True), r=[cst, W_["gk_b"]], w=[pgb])
            yield
            act.op(lambda h: h.activation(out=W_["eg"][:], in_=pg[:, 128:256], func=AF.Exp), r=[pgb], w=[W_["eg_b"]])
            act.op(lambda h: h.activation(out=W_["eng"][:], in_=pg[:, 128:256], func=AF.Exp, scale=-1.0), r=[pgb], w=[W_["eng_b"]])
            act.op(lambda h: h.activation(out=W_["ekd"][:], in_=pg[:, 0:128], func=AF.Exp), r=[pgb], w=[W_["ekd_b"]])
            lastc = 128 + (127 if d_ == 0 else 0)
            act.op(lambda h: h.activation(out=W_["etot"][:], in_=pg[:, lastc:lastc + 1], func=AF.Exp), r=[pgb], w=[W_["etot_b"]])
            release(ig)
            yield
            dve.op(lambda h: h.tensor_tensor(out=W_["qgT"][:], in0=qTg[:, tsl], in1=W_["eg"][:], op=ALU.mult), r=[qTgb, W_["eg_b"]], w=[W_["qgT_b"]])
            pool.op(lambda h: h.tensor_tensor(out=W_["kgT"][:], in0=kTg[:, tsl], in1=W_["eng"][:], op=ALU.mult), r=[kTgb, W_["eng_b"]], w=[W_["kgT_b"]])
            dve.op(lambda h: h.tensor_tensor(out=W_["kdg"][:], in0=kg_tok[:, tt, :], in1=W_["ekd"][:], op=ALU.mult), r=[kgtb, W_["ekd_b"]], w=[W_["kdg_b"]])
            yield
            ia = yield from acquire("gprep")
            pa_, pab_ = ps[ia], psb[ia]
            pe.op(lambda h: h.matmul(pa_[:, 0:128], lhsT=W_["kgT"][:], rhs=W_["qgT"][:], start=True, stop=True), r=[W_["kgT_b"], W_["qgT_b"]], w=[pab_])
            yield
            dve.op(lambda h: h.tensor_tensor(out=W_["attnT"][:], in0=pa_[:, 0:128], in1=MASKG[d_][:], op=ALU.mult), r=[pab_, cst], w=[W_["attnT_b"]])
            release(ia)
            yield

        def gla_scan(d_, tt, W_, S_):
            tsl = slice(tt * 128, (tt + 1) * 128)
            io = yield from acquire("gscan")
            po, pob = ps[io], psb[io]
            for eb in range(2):
                pe.op(lambda h: h.matmul(po[:, eb * 128:(eb + 1) * 128], lhsT=S_["S16"][:, eb * 128:(eb + 1) * 128], rhs=W_["qgT"][:], start=True, stop=False),
                      r=[S_["S16_b"], W_["qgT_b"]], w=[pob])
                pe.op(lambda h: h.matmul(po[:, eb * 128:(eb + 1) * 128], lhsT=vg_tok[:, tt, eb * 128:(eb + 1) * 128], rhs=W_["attnT"][:], start=False, stop=True),
                      r=[vgtb, W_["attnT_b"]], w=[pob])
            iS = yield from acquire("gscan")
            pS, pSb = ps[iS], psb[iS]
            pe.op(lambda h: h.matmul(pS[:, 0:256], lhsT=W_["kdg"][:], rhs=vg_tok[:, tt, :], start=True, stop=True), r=[W_["kdg_b"], vgtb], w=[pSb])
            yield
            dve.op(lambda h: h.scalar_tensor_tensor(out=S_["S32"][:], in0=S_["S32"][:], scalar=W_["etot"][:, 0:1], in1=pS[:, 0:256],
                                                    op0=ALU.mult, op1=ALU.add), r=[S_["S32_b"], W_["etot_b"], pSb], w=[S_["S32_b"]])
            release(iS)
            act.op(lambda h: h.copy(out=S_["S16"][:], in_=S_["S32"][:]), r=[S_["S32_b"]], w=[S_["S16_b"]])
            first = (tt < NT // 2) if d_ == 0 else (tt >= NT // 2)
            if first:
                act.op(lambda h: h.copy(out=obTa[:, :, tsl], in_=po[:, 0:256].rearrange("p (e t) -> p e t", e=2)), r=[pob], w=[obTab])
            else:
                dve.op(lambda h: h.tensor_tensor(out=obTa[:, :, tsl], in0=po[:, 0:256].rearrange("p (e t) -> p e t", e=2), in1=obTa[:, :, tsl], op=ALU.add),
                       r=[pob, obTab], w=[obTab])
            release(io)
            yield

        for d_ in range(2):
            pool.op(lambda h: h.memset(GS_[d_]["S32"][:], 0.0), w=[GS_[d_]["S32_b"]])
            pool.op(lambda h: h.memset(GS_[d_]["S16"][:], 0.0), w=[GS_[d_]["S16_b"]])
        order = {0: list(range(NT)), 1: list(range(NT - 1, -1, -1))}
        P = {0: [], 1: []}
        S = {0: [], 1: []}
        tasks = []
        for i in range(NT):
            for d_ in range(2):
                pt = Task(gla_prep(d_, order[d_][i], GW[(d_, i % NSG)]), deps=[S[d_][i - NSG] if i >= NSG else None])
                P[d_].append(pt)
                tasks.append(pt)
            for d_ in range(2):
                other = S[1 - d_][NT - 1 - i] if (i >= NT // 2 and len(S[1 - d_]) > NT - 1 - i) else None
                stt = Task(gla_scan(d_, order[d_][i], GW[(d_, i % NSG)], GS_[d_]), deps=[P[d_][i], S[d_][i - 1] if i >= 1 else None, other])
                S[d_].append(stt)
                tasks.append(stt)
        run_tasks(tasks, max_active=MAXACT)
        if f"ob{hb}" in dbg:
            for eb in range(2):
                sp.dma(dbg[f"ob{hb}"][eb * 128:(eb + 1) * 128, :], obTa[:, eb, :], sl_dbg, r=[obTab])
        for tg in range(4):
            sl_ = slice(tg * 512, (tg + 1) * 512)
            p_, pb = next_ps()
            for eb in range(2):
                act.op(lambda h: h.activation(out=sqg[:], in_=obTa[:, eb, sl_], func=AF.Square), r=[obTab], w=[sqgb])
                pe.op(lambda h: h.matmul(p_[:, :], lhsT=ones_bf[:], rhs=sqg[:], start=(eb == 0), stop=(eb == 1)), r=[cst, sqgb], w=[pb])
            act.op(lambda h: h.activation(out=rng[:], in_=p_[:, :], func=AF.Ln, bias=EPS, scale=1.0 / 256), r=[pb], w=[rngb])
            act.op(lambda h: h.activation(out=rng[:], in_=rng[:], func=AF.Exp, scale=-0.5), r=[rngb], w=[rngb])
            for eb in range(2):
                dve.op(lambda h: h.scalar_tensor_tensor(out=obTa[:, eb, sl_], in0=obTa[:, eb, sl_], scalar=gla_nw[:, eb:eb + 1], in1=rng[:], op0=ALU.mult, op1=ALU.mult),
                       r=[obTab, rngb, prm], w=[obTab])
                dve.op(lambda h: h.tensor_tensor(out=obT[:, hb * 2 + eb, sl_], in0=obTa[:, eb, sl_], in1=gsT[:, eb, sl_], op=ALU.mult), r=[obTab, gsTb], w=[obTb])
    if "obT" in dbg:
        tmp = sb("dbg_tmp4", [128, T]); tb = Buf("dbg_tmp4")
        for k in range(8):
            act.op(lambda h: h.copy(out=tmp[:], in_=obT[:, k, :]), r=[obTb], w=[tb])
            sp.dma(dbg["obT"][k * 128:(k + 1) * 128, :], tmp[:], sl_dbg, r=[tb])
    free_to(gla_mark)

    if do_final:
        mT = sb("mT", [128, 8, T], BF16); mTb = Buf("mT")
        fin_mark = len(allocs)
        sga = sb("sga", [128, 512]); sgab = Buf("sga")
        sgb = sb("sgb", [128, 512]); sgbb = Buf("sgb")
        t1 = sb("t1", [128, 512]); t1b = Buf("t1")
        t2 = sb("t2", [128, 512]); t2b = Buf("t2")
        NW2 = 8
        wb2 = [sb(f"wb2_{i}", [128, 8, 128], BF16) for i in range(NW2)]; wb2b = [Buf(f"wb2_{i}") for i in range(NW2)]
        sl_w2 = [Slot(nc, f"w2_{i}") for i in range(NW2)]
        rr2 = [0]

        def load2(src_ap):
            i = rr2[0] % NW2
            rr2[0] += 1
            pool.dma(wb2[i][:], src_ap.rearrange("(k p) c -> p k c", p=128), sl_w2[i], w=[wb2b[i]])
            return wb2[i], wb2b[i]

        for m in range(8):
            msl = slice(m * 128, (m + 1) * 128)
            wg, wgb_ = load2(wpg_d[0, :, msl])
            wl, wlb_ = load2(wpl_d[0, :, msl])
            wa, wab_ = load2(w_in_d[0, :, OFF_GA + m * 128: OFF_GA + (m + 1) * 128])
            wbb, wbbb_ = load2(w_in_d[0, :, OFF_GBG + m * 128: OFF_GBG + (m + 1) * 128])
            for tg in range(4):
                sl_ = slice(tg * 512, (tg + 1) * 512)
                pga, pgab = next_ps(); proj_fm(wa, wab_, tg, pga, pgab)
                act.op(lambda h: h.activation(out=sga[:], in_=pga[:, :], func=AF.Sigmoid), r=[pgab], w=[sgab])
                pgb_, pgbb = next_ps(); proj_fm(wbb, wbbb_, tg, pgb_, pgbb)
                act.op(lambda h: h.activation(out=sgb[:], in_=pgb_[:, :], func=AF.Sigmoid), r=[pgbb], w=[sgbb])
                pya, pyab = next_ps(); proj_fm(wg, wgb_, tg, pya, pyab, rhsT=oaT, rb=oaTb)
                dve.op(lambda h: h.tensor_tensor(out=t1[:], in0=pya[:, :], in1=sga[:], op=ALU.mult), r=[pyab, sgab], w=[t1b])
                pyb, pybb = next_ps(); proj_fm(wl, wlb_, tg, pyb, pybb, rhsT=obT, rb=obTb)
                dve.op(lambda h: h.tensor_tensor(out=t2[:], in0=pyb[:, :], in1=sgb[:], op=ALU.mult), r=[pybb, sgbb], w=[t2b])
                dve.op(lambda h: h.tensor_tensor(out=mT[:, m, sl_], in0=t1[:], in1=t2[:], op=ALU.add), r=[t1b, t2b], w=[mTb])
        if "mT" in dbg:
            tmp = sb("dbg_tmp5", [128, T]); tb = Buf("dbg_tmp5")
            for k in range(8):
                act.op(lambda h: h.copy(out=tmp[:], in_=mT[:, k, :]), r=[mTb], w=[tb])
                sp.dma(dbg["mT"][k * 128:(k + 1) * 128, :], tmp[:], sl_dbg, r=[tb])
        free_to(fin_mark)
        wob = hTb; sl_wo = Slot(nc, "wo")
        for k in range(8):
            pool.dma(hT[:, k, 0:D], wout_d[0, k * 128:(k + 1) * 128, :], sl_wo, w=[wob])
        lnp = sb("lnp", [128, D]); sl_lnp = Slot(nc, "lnp"); lnpb = Buf("lnp")
        sp.dma(lnp[:], ln_post_d[0, :].partition_broadcast(128), sl_lnp, w=[lnpb])
        xr = [sb(f"xr{i}", [128, D]) for i in range(2)]; xrb = [Buf(f"xr{i}") for i in range(2)]
        sl_xr = [Slot(nc, f"xr{i}") for i in range(2)]
        ot = [sb(f"ot{i}", [128, D]) for i in range(2)]; otb = [Buf(f"ot{i}") for i in range(2)]
        st1 = sb("st1", [128, 8]); st1b = Buf("st1")
        junk2 = sb("junk2", [128, 512], BF16); junk2b = Buf("junk2")
        for tt in range(NT):
            s = tt % 2
            tsl = slice(tt * 128, (tt + 1) * 128)
            sp.dma(xr[s][:], x_d[tsl, :], sl_xr[s], w=[xrb[s]])
            pp = []
            for half in range(2):
                p_, pb = next_ps()
                for m in range(8):
                    pe.op(lambda h: h.matmul(p_[:, :], lhsT=mT[:, m, tsl], rhs=hT[:, m, half * 512:(half + 1) * 512], start=(m == 0), stop=(m == 7)),
                          r=[mTb, wob], w=[pb])
                act.op(lambda h: h.activation(out=junk2[:], in_=p_[:, :], func=AF.Square, accum_out=st1[:, half:half + 1]), r=[pb], w=[junk2b, st1b])
                pp.append((p_, pb))
            dve.op(lambda h: h.tensor_tensor(out=st1[:, 2:3], in0=st1[:, 0:1], in1=st1[:, 1:2], op=ALU.add), r=[st1b], w=[st1b])
            act.op(lambda h: h.activation(out=st1[:, 3:4], in_=st1[:, 2:3], func=AF.Ln, bias=EPS, scale=1.0 / D), r=[st1b], w=[st1b])
            act.op(lambda h: h.activation(out=st1[:, 4:5], in_=st1[:, 3:4], func=AF.Exp, scale=-0.5), r=[st1b], w=[st1b])
            for half in range(2):
                p_, pb = pp[half]
                hs = slice(half * 512, (half + 1) * 512)
                dve.op(lambda h: h.scalar_tensor_tensor(out=ot[s][:, hs], in0=p_[:, :], scalar=st1[:, 4:5], in1=lnp[:, hs], op0=ALU.mult, op1=ALU.mult),
                       r=[pb, st1b, lnpb], w=[otb[s]])
                dve.op(lambda h: h.tensor_tensor(out=ot[s][:, hs], in0=ot[s][:, hs], in1=xr[s][:, hs], op=ALU.add), r=[otb[s], xrb[s]], w=[otb[s]])
            sp.dma(out_d[tsl, :], ot[s][:], sl_out2[s], r=[otb[s]])
    else:
        zt = sb("zt", [128, D]); ztb = Buf("zt")
        pool.op(lambda h: h.memset(zt[:], 0.0), w=[ztb])
        for tt in range(NT):
            sp.dma(out_d[tt * 128:(tt + 1) * 128, :], zt[:], sl_out2[tt % 2], r=[ztb])
    for so_ in sl_out2:
        if so_.cnt:
            sp.h.wait_ge(so_.sem, so_.cnt)
    if sl_dbg.cnt:
        sp.h.wait_ge(sl_dbg.sem, sl_dbg.cnt)
    return nc


_INPUT_NAMES = ["ln_pre_w", "w_in", "conv_w", "a_log_fwd", "a_log_bwd", "dt_bias_fwd", "dt_bias_bwd", "gdn_norm_w", "w_proj_gdn",
                "gk_w2_fwd", "gk_b2_fwd", "gk_w2_bwd", "gk_b2_bwd", "gla_norm_w", "w_proj_gla", "w_out", "ln_post_w"]


def kernel(**inputs):
    x = np.ascontiguousarray(np.asarray(inputs["x"], dtype=np.float32))
    shared = {n: np.ascontiguousarray(np.asarray(inputs[n], dtype=np.float32)) for n in _INPUT_NAMES}
    nc = build_nc()
    in_maps = [dict(shared, x=x[b]) for b in range(8)]
    res = run_bass_kernel_spmd(nc, in_maps, core_ids=list(range(8)))
    return np.stack([np.asarray(r["out"], dtype=np.float32) for r in res.results], axis=0)
```

```python
import numpy as np
import concourse.bass as bass
import concourse.mybir as mybir
from concourse.bass_utils import run_bass_kernel_spmd

F32 = mybir.dt.float32
F32R = mybir.dt.float32r
BF16 = mybir.dt.bfloat16
AF = mybir.ActivationFunctionType
ALU = mybir.AluOpType

T = 2048
NT = 16
D = 1024
NIN = 9280
EPS = 1e-6
BIG = 30000.0
MAXACT = 6
OFF_Q, OFF_K, OFF_V, OFF_Z = 0, 1024, 2048, 3072
OFF_SM = 4096
OFF_QB, OFF_KB, OFF_VB, OFF_GB = 4128, 4640, 5152, 6176
OFF_R = 7200
OFF_GA, OFF_GBG = 7232, 8256


class Ev:
    __slots__ = ("sem", "val", "key")

    def __init__(self, sem, val, key):
        self.sem, self.val, self.key = sem, val, key


class Buf:
    __slots__ = ("name", "wev", "revs")

    def __init__(self, name):
        self.name, self.wev, self.revs = name, None, {}


class Slot:
    registry = None

    def __init__(self, nc, name):
        if Slot.registry is not None:
            Slot.registry.append(self)
        self.name = name
        self.sem = nc.semaphore("ds_" + name).__enter__()
        self.cnt = 0


class Eng:
    def __init__(self, nc, name, h, selfsync):
        self.name, self.h, self.selfsync = name, h, selfsync
        self.sem = nc.semaphore("es_" + name).__enter__()
        self.cnt = 0
        self.seen = {}

    def wait(self, ev):
        if ev is None:
            return
        if ev.sem is self.sem and not self.selfsync:
            return
        if self.seen.get(ev.key, 0) >= ev.val:
            return
        self.h.wait_ge(ev.sem, ev.val)
        self.seen[ev.key] = ev.val

    def _deps(self, r, w):
        for b in r:
            self.wait(b.wev)
        for b in w:
            self.wait(b.wev)
            for ev in b.revs.values():
                self.wait(ev)

    def _mark(self, ev, r, w):
        for b in r:
            b.revs[ev.key] = ev
        for b in w:
            b.wev = ev
            b.revs = {}

    def op(self, fn, r=(), w=()):
        self._deps(r, w)
        ins = fn(self.h)
        self.cnt += 1
        ins.then_inc(self.sem, 1)
        ev = Ev(self.sem, self.cnt, self.name)
        self._mark(ev, r, w)
        return ev

    def dma(self, out, in_, slot, r=(), w=(), **kw):
        self._deps(r, w)
        ins = self.h.dma_start(out=out, in_=in_, **kw)
        slot.cnt += 16
        ins.then_inc(slot.sem, 16)
        ev = Ev(slot.sem, slot.cnt, slot.name)
        self._mark(ev, r, w)
        return ev


def build_nc(debug=(), gdn_heads=8, gla_heads=4, do_final=True):
    nc = bass.Bass("TRN2", target_bir_lowering=False)
    dram_in = lambda n, s: nc.dram_tensor(n, s, F32, kind="ExternalInput").ap()
    x_d = dram_in("x", [T, D])
    ln_pre_d = dram_in("ln_pre_w", [1, D])
    w_in_d = dram_in("w_in", [1, D, NIN])
    conv_d = dram_in("conv_w", [1, 5, 3072])
    alog_f_d = dram_in("a_log_fwd", [1, 8]); alog_b_d = dram_in("a_log_bwd", [1, 8])
    dtb_f_d = dram_in("dt_bias_fwd", [1, 8]); dtb_b_d = dram_in("dt_bias_bwd", [1, 8])
    gdn_nw_d = dram_in("gdn_norm_w", [1, 128])
    wpg_d = dram_in("w_proj_gdn", [1, D, D])
    gkw_f_d = dram_in("gk_w2_fwd", [1, 16, 512]); gkb_f_d = dram_in("gk_b2_fwd", [1, 512])
    gkw_b_d = dram_in("gk_w2_bwd", [1, 16, 512]); gkb_b_d = dram_in("gk_b2_bwd", [1, 512])
    gla_nw_d = dram_in("gla_norm_w", [1, 256])
    wpl_d = dram_in("w_proj_gla", [1, D, D])
    wout_d = dram_in("w_out", [1, D, D])
    ln_post_d = dram_in("ln_post_w", [1, D])
    out_d = nc.dram_tensor("out", [T, D], F32, kind="ExternalOutput").ap()
    dbg = {}
    for name, shape in debug:
        dbg[name] = nc.dram_tensor("dbg_" + name, shape, F32, kind="ExternalOutput").ap()

    pe = Eng(nc, "pe", nc.tensor, False)
    act = Eng(nc, "act", nc.scalar, True)
    dve = Eng(nc, "dve", nc.vector, True)
    pool = Eng(nc, "pool", nc.gpsimd, True)
    sp = Eng(nc, "sp", nc.sync, False)

    allocs = []

    def sb(name, shape, dt=F32):
        cm = nc.sbuf_tensor(name, shape, dt)
        t = cm.__enter__()
        allocs.append(cm)
        return t

    all_slots = []
    Slot.registry = all_slots

    def barrier():
        engs = (pe, act, dve, pool, sp)
        for e in engs:
            for f in engs:
                if f is not e and f.cnt:
                    e.wait(Ev(f.sem, f.cnt, f.name))
            for sl in all_slots:
                if sl.cnt:
                    e.wait(Ev(sl.sem, sl.cnt, sl.name))

    def free_to(mark):
        barrier()
        while len(allocs) > mark:
            allocs.pop().__exit__(None, None, None)

    NPS = 7
    ps = [nc.psum_tensor(f"ps{i}", [128, 512], F32).__enter__() for i in range(NPS)]
    psb = [Buf(f"ps{i}") for i in range(NPS)]
    pst = nc.psum_tensor("pst", [128, 1024], BF16).__enter__()
    pstb = Buf("pst")
    ps_rr = {}
    PS_POOLS = {"any": list(range(NPS)), "prep": [0, 1, 2], "scan": [3, 4], "proj": [5, 6], "gprep": [0, 1, 2], "gscan": [3, 4, 5, 6]}

    def next_ps(pool_="any"):
        lst = PS_POOLS[pool_]
        c = ps_rr.get(pool_, 0)
        ps_rr[pool_] = c + 1
        i = lst[c % len(lst)]
        return ps[i], psb[i]

    ps_busy = [False] * NPS

    def acquire(pool_):
        while True:
            for i in PS_POOLS[pool_]:
                if not ps_busy[i]:
                    ps_busy[i] = True
                    return i
            yield

    def release(i):
        ps_busy[i] = False

    class Task:
        def __init__(self, gen, deps=()):
            self.gen, self.deps, self.done = gen, [d for d in deps if d is not None], False

    stall = [0]

    def run_tasks(tasks, max_active=6):
        pending = list(tasks)
        active = []
        while pending or active:
            for t in list(pending):
                if len(active) >= max_active:
                    break
                if all(d.done for d in t.deps):
                    pending.remove(t)
                    active.append(t)
            assert active, "task deadlock"
            before = (pe.cnt, act.cnt, dve.cnt, pool.cnt, len(active), len(pending))
            for t in list(active):
                try:
                    next(t.gen)
                except StopIteration:
                    t.done = True
                    active.remove(t)
            if before == (pe.cnt, act.cnt, dve.cnt, pool.cnt, len(active), len(pending)):
                stall[0] += 1
                assert stall[0] < 200, ("scheduler stall", [getattr(t.gen, "__name__", "?") for t in active], list(ps_busy), len(pending))
            else:
                stall[0] = 0

    cst = Buf("const")
    ident32 = sb("ident32", [128, 128]); ident_bf = sb("ident_bf", [128, 128], BF16)
    ones32 = sb("ones32", [128, 128]); ones_bf = sb("ones_bf", [128, 128], BF16)
    zeros32 = sb("zeros32", [128, 128])
    pool.op(lambda h: h.memset(ones32[:], 1.0), w=[cst])
    pool.op(lambda h: h.memset(zeros32[:], 0.0), w=[cst])
    pool.op(lambda h: h.affine_select(out=ident32[:], in_=zeros32[:], pattern=[[-1, 128]], compare_op=ALU.not_equal,
                                      fill=1.0, base=0, channel_multiplier=1), r=[cst], w=[cst])
    pool.op(lambda h: h.tensor_copy(out=ident_bf[:], in_=ident32[:]), r=[cst], w=[cst])
    pool.op(lambda h: h.tensor_copy(out=ones_bf[:], in_=ones32[:]), r=[cst], w=[cst])

    src_cache = {}

    def tri_const(name, n, inval, fillval, offval, step, cm, cmp):
        t = sb(name, [128, 128])
        pool.op(lambda h: h.memset(t[:], offval), w=[cst])
        if inval not in src_cache:
            src_cache[inval] = sb(f"src_{len(src_cache)}", [128, 128])
            pool.op(lambda h: h.memset(src_cache[inval][:], inval), w=[cst])
        src = src_cache[inval]
        for b0 in range(0, 128, n):
            pool.op(lambda h: h.affine_select(out=t[b0:b0 + n, b0:b0 + n], in_=src[b0:b0 + n, b0:b0 + n], pattern=[[step, n]],
                                              compare_op=cmp, fill=fillval, base=0, channel_multiplier=cm), r=[cst], w=[cst])
        return t

    TRI = {
        0: tri_const("tri_f", 64, 1.0, 0.0, 0.0, 1, -1, ALU.is_ge),
        1: tri_const("tri_b", 64, 1.0, 0.0, 0.0, -1, 1, ALU.is_ge),
    }
    NEG_A = {
        0: tri_const("nega_f", 64, 0.0, BIG, BIG, -1, 1, ALU.is_gt),
        1: tri_const("nega_b", 64, 0.0, BIG, BIG, 1, -1, ALU.is_gt),
    }
    NEG_B = {
        0: tri_const("negb_f", 64, 0.0, -BIG, -BIG, 1, -1, ALU.is_gt),
        1: tri_const("negb_b", 64, 0.0, -BIG, -BIG, -1, 1, ALU.is_gt),
    }
    NEG_C = {
        0: tri_const("negc_f", 64, 0.0, -BIG, -BIG, 1, -1, ALU.is_ge),
        1: tri_const("negc_b", 64, 0.0, -BIG, -BIG, -1, 1, ALU.is_ge),
    }
    BDONES = tri_const("bdones", 64, 1.0, 1.0, 0.0, 1, 1, ALU.is_ge)
    ones_r = sb("ones_r", [128, 128], F32R); ident_r = sb("ident_r", [128, 128], F32R)
    pool.op(lambda h: h.tensor_copy(out=ones_r[:], in_=ones32[:]), r=[cst], w=[cst])
    pool.op(lambda h: h.tensor_copy(out=ident_r[:], in_=ident32[:]), r=[cst], w=[cst])
    MASKS_R = {}
    for d__ in range(2):
        mk = sb(f"masks_r{d__}", [128, 512], F32R)
        pool.op(lambda h: h.tensor_copy(out=mk[:, 0:128], in_=NEG_A[d__][:]), r=[cst], w=[cst])
        pool.op(lambda h: h.tensor_copy(out=mk[:, 128:256], in_=NEG_B[d__][:]), r=[cst], w=[cst])
        pool.op(lambda h: h.tensor_copy(out=mk[:, 256:384], in_=NEG_C[d__][:]), r=[cst], w=[cst])
        pool.op(lambda h: h.tensor_copy(out=mk[:, 384:512], in_=zeros32[:]), r=[cst], w=[cst])
        MASKS_R[d__] = mk
    SEL0 = sb("sel0", [128, 128]); SEL1 = sb("sel1", [128, 128])
    pool.op(lambda h: h.memset(SEL0[:], 0.0), w=[cst]); pool.op(lambda h: h.memset(SEL1[:], 0.0), w=[cst])
    pool.op(lambda h: h.memset(SEL0[0:64, :], 1.0), w=[cst]); pool.op(lambda h: h.memset(SEL1[64:128, :], 1.0), w=[cst])
    GS = -1.0 / 16.0
    TRIG = {0: tri_const("trig_f", 128, GS, 0.0, 0.0, 1, -1, ALU.is_ge),
            1: tri_const("trig_b", 128, GS, 0.0, 0.0, -1, 1, ALU.is_ge)}
    DIFG = {0: tri_const("difg_f", 128, 0.0, GS, 0.0, 1, -1, ALU.is_ge),
            1: tri_const("difg_b", 128, 0.0, GS, 0.0, -1, 1, ALU.is_ge)}
    MASKG = {0: tri_const("maskg_f", 128, 1.0, 0.0, 0.0, 1, -1, ALU.is_ge),
             1: tri_const("maskg_b", 128, 1.0, 0.0, 0.0, -1, 1, ALU.is_ge)}

    prm = Buf("params")
    sl_prm = Slot(nc, "prm")
    lnw_T = sb("lnw_T", [128, 8])
    sp.dma(lnw_T[:], ln_pre_d[0, :].rearrange("(k p) -> p k", p=128), sl_prm, w=[prm], allow_slow_non_contiguous=True)
    cw = sb("cw", [128, 24, 5])
    for t_ in range(5):
        sp.dma(cw[:, :, t_], conv_d[0, t_, :].rearrange("(b p) -> p b", p=128), sl_prm, w=[prm], allow_slow_non_contiguous=True)
    prm16 = sb("prm16", [128, 2, 16])
    sp.dma(prm16[:, 0, 0:8], alog_f_d[0, :].partition_broadcast(128), sl_prm, w=[prm])
    sp.dma(prm16[:, 0, 8:16], alog_b_d[0, :].partition_broadcast(128), sl_prm, w=[prm])
    sp.dma(prm16[:, 1, 0:8], dtb_f_d[0, :].partition_broadcast(128), sl_prm, w=[prm])
    sp.dma(prm16[:, 1, 8:16], dtb_b_d[0, :].partition_broadcast(128), sl_prm, w=[prm])
    gdn_nw = sb("gdn_nw", [128, 1])
    sp.dma(gdn_nw[:], gdn_nw_d[0, :].rearrange("(p o) -> p o", o=1), sl_prm, w=[prm])
    gla_nw = sb("gla_nw", [128, 2])
    sp.dma(gla_nw[:], gla_nw_d[0, :].rearrange("(k p) -> p k", p=128), sl_prm, w=[prm], allow_slow_non_contiguous=True)
    w2cat = sb("w2cat", [33, 2, 512])
    pool.op(lambda h: h.memset(w2cat[0:32, :, :], 0.0), w=[prm])
    sp.dma(w2cat[0:16, 0, :], gkw_f_d[0, :, :], sl_prm, r=[prm], w=[prm])
    sp.dma(w2cat[16:32, 1, :], gkw_b_d[0, :, :], sl_prm, w=[prm])
    sp.dma(w2cat[32:33, 0, :], gkb_f_d[0:1, :], sl_prm, w=[prm])
    sp.dma(w2cat[32:33, 1, :], gkb_b_d[0:1, :], sl_prm, w=[prm])

    def dbg_out(name, src_ap, rbufs, dst=None):
        if name not in dbg:
            return
        d = dbg[name] if dst is None else dst
        sp.dma(d, src_ap, sl_dbg, r=rbufs)

    sl_dbg = Slot(nc, "dbg")
    sl_out2 = [Slot(nc, "out0"), Slot(nc, "out1")]

    hT = sb("hT", [128, 8, T], BF16); hTb = Buf("hT")
    oaT = sb("oaT", [128, 8, T], BF16); oaTb = Buf("oaT")
    base_mark = len(allocs)

    xt = [sb(f"xt{i}", [128, D]) for i in range(2)]; xtb = [Buf(f"xt{i}") for i in range(2)]
    sl_x = [Slot(nc, f"x{i}") for i in range(2)]
    junk = sb("junk", [128, D], BF16); junkb = Buf("junk")
    xn = sb("xn", [128, D], BF16); xnb = Buf("xn")
    st0 = sb("st0", [128, 4]); st0b = Buf("st0")
    for tt in range(NT):
        s = tt % 2
        sp.dma(xt[s][:], x_d[tt * 128:(tt + 1) * 128, :], sl_x[s], w=[xtb[s]])
        act.op(lambda h: h.activation(out=junk[:], in_=xt[s][:], func=AF.Square, accum_out=st0[:, 0:1]), r=[xtb[s]], w=[junkb, st0b])
        act.op(lambda h: h.activation(out=st0[:, 1:2], in_=st0[:, 0:1], func=AF.Ln, bias=EPS, scale=1.0 / D), r=[st0b], w=[st0b])
        act.op(lambda h: h.activation(out=st0[:, 2:3], in_=st0[:, 1:2], func=AF.Exp, scale=-0.5), r=[st0b], w=[st0b])
        act.op(lambda h: h.activation(out=xn[:], in_=xt[s][:], func=AF.Copy, scale=st0[:, 2:3]), r=[xtb[s], st0b], w=[xnb])
        for k in range(8):
            pe.op(lambda h: h.transpose(pst[:, k * 128:(k + 1) * 128], xn[:, k * 128:(k + 1) * 128], ident_bf[:]), r=[xnb, cst], w=[pstb])
        dve.op(lambda h: h.tensor_tensor(out=hT[:, :, tt * 128:(tt + 1) * 128], in0=pst[:].rearrange("p (k t) -> p k t", k=8),
                                         in1=lnw_T[:, :].unsqueeze(2).to_broadcast([128, 8, 128]), op=ALU.mult), r=[pstb, prm], w=[hTb])
    if "st0" in dbg:
        sp.dma(dbg["st0"], st0[:], sl_dbg, r=[st0b])
        tmpx = sb("dbg_tmpx", [128, D]); tbx = Buf("dbg_tmpx")
        act.op(lambda h: h.copy(out=tmpx[:], in_=xn[:]), r=[xnb], w=[tbx])
        sp.dma(dbg["xn"], tmpx[:], sl_dbg, r=[tbx])
    if "hT" in dbg:
        tmp = sb("dbg_tmp", [128, T])
        tb = Buf("dbg_tmp")
        for k in range(8):
            act.op(lambda h: h.copy(out=tmp[:], in_=hT[:, k, :]), r=[hTb], w=[tb])
            sp.dma(dbg["hT"][k * 128:(k + 1) * 128, :], tmp[:], sl_dbg, r=[tb])
    free_to(base_mark)

    NWS = 2
    wblk = [sb(f"wblk{i}", [128, 8, 128], BF16) for i in range(NWS)]
    wblkb = [Buf(f"wblk{i}") for i in range(NWS)]
    sl_w = [Slot(nc, f"w{i}") for i in range(NWS)]
    w_rr = [0]

    def load_wblk(src_ap):
        i = w_rr[0] % NWS
        w_rr[0] += 1
        pool.dma(wblk[i][:], src_ap.rearrange("(k p) c -> p k c", p=128), sl_w[i], w=[wblkb[i]])
        return wblk[i], wblkb[i]

    def proj_fm(wt, wb, tg, pst_, pb, rhsT=hT, rb=hTb):
        for k in range(8):
            pe.op(lambda h: h.matmul(pst_[:, :], lhsT=wt[:, k, :], rhs=rhsT[:, k, tg * 512:(tg + 1) * 512], start=(k == 0), stop=(k == 7)),
                  r=[wb, rb], w=[pb])

    mix_mark = len(allocs)

    NCOL = 16
    LG = sb("LG", [128, NT, NCOL]); LNB = sb("LNB", [128, NT, NCOL]); BETA = sb("BETA", [128, NT, NCOL])
    A_tok = sb("A_tok", [128, NT, NCOL]); NG_tok = sb("NG_tok", [128, NT, NCOL]); BG = sb("BG", [128, NT, NCOL])
    KD = sb("KD", [128, NT, NCOL]); ET0 = sb("ET0", [128, NT, NCOL]); ET1 = sb("ET1", [128, NT, NCOL])
    scal = Buf("scal")
    s1_mark = len(allocs)
    wsm = sb("wsm", [128, 8, 32], BF16); wsmb = Buf("wsm")
    sl_wsm = Slot(nc, "wsm")
    pool.dma(wsm[:], w_in_d[0, :, OFF_SM:OFF_SM + 32].rearrange("(k p) c -> p k c", p=128), sl_wsm, w=[wsmb])
    asm = sb("asm", [128, NT, 32]); asmb = Buf("asm")
    for tt in range(NT):
        p_, pb = next_ps()
        for k in range(8):
            pe.op(lambda h: h.matmul(p_[:, 0:32], lhsT=hT[:, k, tt * 128:(tt + 1) * 128], rhs=wsm[:, k, :], start=(k == 0), stop=(k == 7)),
                  r=[hTb, wsmb], w=[pb])
        act.op(lambda h: h.copy(out=asm[:, tt, :], in_=p_[:, 0:32]), r=[pb], w=[asmb])
    t16 = sb("t16", [128, NT, NCOL]); t16b = Buf("t16")
    dve.op(lambda h: h.tensor_tensor(out=t16[:], in0=asm[:, :, 0:16], in1=prm16[:, 1:2, :].to_broadcast([128, NT, NCOL]), op=ALU.add), r=[asmb, prm], w=[t16b])
    act.op(lambda h: h.activation(out=t16[:], in_=t16[:], func=AF.Exp), r=[t16b], w=[t16b])
    act.op(lambda h: h.activation(out=t16[:], in_=t16[:], func=AF.Ln, bias=1.0), r=[t16b], w=[t16b])
    nA = sb("nA", [128, 1, NCOL]); nAb = Buf("nA")
    act.op(lambda h: h.activation(out=nA[:], in_=prm16[:, 0:1, :], func=AF.Exp), r=[prm], w=[nAb])
    dve.op(lambda h: h.scalar_tensor_tensor(out=LG[:], in0=t16[:], scalar=-1.0, in1=nA[:].to_broadcast([128, NT, NCOL]), op0=ALU.mult, op1=ALU.mult),
           r=[t16b, nAb], w=[scal])
    act.op(lambda h: h.activation(out=t16[:], in_=asm[:, :, 16:32], func=AF.Exp, scale=-1.0), r=[asmb, scal], w=[t16b])
    act.op(lambda h: h.activation(out=t16[:], in_=t16[:], func=AF.Ln, bias=1.0), r=[t16b], w=[t16b])
    act.op(lambda h: h.mul(out=LNB[:], in_=t16[:], mul=-1.0), r=[t16b], w=[scal])
    act.op(lambda h: h.activation(out=BETA[:], in_=LNB[:], func=AF.Exp), r=[scal], w=[scal])
    G_tok = sb("G_tok", [128, NT, NCOL]); TOTO = sb("TOTO", [128, NT, NCOL])
    for tt in range(NT):
        p_, pb = next_ps()
        for i, M in enumerate([TRI[0], TRI[1], BDONES, SEL0, SEL1]):
            pe.op(lambda h: h.matmul(p_[:, i * 16:(i + 1) * 16], lhsT=M[:], rhs=LG[:, tt, :], start=True, stop=True), r=[cst, scal], w=[pb])
        act.op(lambda h: h.copy(out=G_tok[:, tt, 0:8], in_=p_[:, 0:8]), r=[pb], w=[scal])
        act.op(lambda h: h.copy(out=G_tok[:, tt, 8:16], in_=p_[:, 24:32]), r=[pb], w=[scal])
        act.op(lambda h: h.copy(out=TOTO[:, tt, :], in_=p_[:, 32:48]), r=[pb], w=[scal])
        act.op(lambda h: h.activation(out=ET0[:, tt, :], in_=p_[:, 48:64], func=AF.Exp), r=[pb], w=[scal])
        act.op(lambda h: h.activation(out=ET1[:, tt, :], in_=p_[:, 64:80], func=AF.Exp), r=[pb], w=[scal])
    dve.op(lambda h: h.tensor_tensor(out=A_tok[:], in0=G_tok[:], in1=LNB[:], op=ALU.add), r=[scal], w=[scal])
    act.op(lambda h: h.mul(out=NG_tok[:], in_=G_tok[:], mul=-1.0), r=[scal], w=[scal])
    act.op(lambda h: h.activation(out=BG[:], in_=A_tok[:], func=AF.Exp), r=[scal], w=[scal])
    dve.op(lambda h: h.tensor_tensor(out=KD[:], in0=TOTO[:], in1=G_tok[:], op=ALU.subtract), r=[scal], w=[scal])
    act.op(lambda h: h.activation(out=KD[:], in_=KD[:], func=AF.Exp), r=[scal], w=[scal])
    if "LG" in dbg:
        sp.dma(dbg["LG"].rearrange("(t p) c -> p t c", p=128), LG[:], sl_dbg, r=[scal])
        sp.dma(dbg["BETA"].rearrange("(t p) c -> p t c", p=128), BETA[:], sl_dbg, r=[scal])
        sp.dma(dbg["G_tok"].rearrange("(t p) c -> p t c", p=128), G_tok[:], sl_dbg, r=[scal])

    free_to(s1_mark)
    gdn_mark = len(allocs)
    if gdn_heads > 0:
        pre = sb("pre", [128, T + 4]); preb = Buf("pre")
        pool.op(lambda h: h.memset(pre[:, 0:2], 0.0), w=[preb]); pool.op(lambda h: h.memset(pre[:, T + 2:T + 4], 0.0), w=[preb])
        acc = sb("acc", [128, T]); accb = Buf("acc")
        sqt = sb("sqt", [128, 512], BF16); sqtb = Buf("sqt")
        rn = sb("rn", [128, 512]); rnb = Buf("rn")
        HB = []
        for bs in range(2):
            HB.append(dict(qT=sb(f"qT{bs}", [128, T], BF16), qTb=Buf(f"qT{bs}"), kT=sb(f"kT{bs}", [128, T], BF16), kTb=Buf(f"kT{bs}"),
                           k_tok=sb(f"k_tok{bs}", [128, NT, 128], BF16), ktb=Buf(f"k_tok{bs}"),
                           v_tok=sb(f"v_tok{bs}", [128, NT, 128], BF16), vtb=Buf(f"v_tok{bs}")))
        vT = sb("vT", [128, T], BF16); vTb = Buf("vT")
        zs = sb("zs", [128, T], BF16); zsb = Buf("zs")
        oTa = sb("oTa", [128, T]); oTabs = [[Buf(f"oTa{t_}_{c_}") for c_ in range(2)] for t_ in range(NT)]
        oTa_all = [b_ for l_ in oTabs for b_ in l_]
        NSL = 2
        WK = {}
        ST = {}
        for d_ in range(2):
            for sl_i in range(NSL):
                W_ = {}
                for nm, shp, dt in [("rhs1", [128, 512], F32R), ("rhsb", [128, 128], F32R), ("D1", [128, 128], F32), ("D1T", [128, 128], F32),
                                    ("D2T", [128, 128], BF16), ("grep", [128, 128], F32),
                                    ("LU", [128, 2, 256], BF16),
                                    ("LU2", [128, 2, 256], BF16),
                                    ("attnT", [128, 128], BF16), ("qdT", [128, 128], BF16), ("khat", [128, 128], BF16), ("bv", [128, 128], BF16),
                                    ("kd", [128, 128], BF16), ("wT", [128, 128], BF16), ("u", [128, 128], F32), ("vnew", [128, 128], BF16)]:
                    W_[nm] = sb(f"{nm}_{d_}_{sl_i}", shp, dt)
                    W_[nm + "_b"] = Buf(f"{nm}_{d_}_{sl_i}")
                WK[(d_, sl_i)] = W_
            S_ = {}
            for nm, shp, dt in [("S32", [128, 128], F32), ("S16", [128, 128], BF16)]:
                S_[nm] = sb(f"{nm}_{d_}", shp, dt)
                S_[nm + "_b"] = Buf(f"{nm}_{d_}")
            ST[d_] = S_

    def gdn_projA(hh, B_):
        for which, off in (("q", OFF_Q), ("k", OFF_K), ("v", OFF_V)):
            wt, wb = load_wblk(w_in_d[0, :, off + hh * 128: off + (hh + 1) * 128])
            for tg in range(4):
                ipj = yield from acquire("proj"); p_, pb = ps[ipj], psb[ipj]
                proj_fm(wt, wb, tg, p_, pb)
                act.op(lambda h: h.copy(out=pre[:, 2 + tg * 512: 2 + (tg + 1) * 512], in_=p_[:, :]), r=[pb], w=[preb])
                release(ipj)
                yield
            blk = off // 128 + hh
            act.op(lambda h: h.activation(out=acc[:], in_=pre[:, 0:T], func=AF.Copy, scale=cw[:, blk, 0:1]), r=[preb, prm], w=[accb])
            for t_ in range(1, 5):
                dve.op(lambda h: h.scalar_tensor_tensor(out=acc[:], in0=pre[:, t_:t_ + T], scalar=cw[:, blk, t_:t_ + 1], in1=acc[:], op0=ALU.mult, op1=ALU.add),
                       r=[preb, prm, accb], w=[accb])
                yield
            if which == "v":
                act.op(lambda h: h.activation(out=vT[:], in_=acc[:], func=AF.Silu), r=[accb], w=[vTb])
                continue
            act.op(lambda h: h.activation(out=acc[:], in_=acc[:], func=AF.Silu), r=[accb], w=[accb])
            dstT, dstb = (B_["qT"], B_["qTb"]) if which == "q" else (B_["kT"], B_["kTb"])
            post = (128.0 ** -0.5) if which == "q" else 1.0
            for tg in range(4):
                sl_ = slice(tg * 512, (tg + 1) * 512)
                act.op(lambda h: h.activation(out=sqt[:], in_=acc[:, sl_], func=AF.Square), r=[accb], w=[sqtb])
                ipj = yield from acquire("proj"); p_, pb = ps[ipj], psb[ipj]
                pe.op(lambda h: h.matmul(p_[:, :], lhsT=ones_bf[:], rhs=sqt[:], start=True, stop=True), r=[cst, sqtb], w=[pb])
                yield
                act.op(lambda h: h.activation(out=rn[:], in_=p_[:, :], func=AF.Ln, bias=EPS), r=[pb], w=[rnb])
                release(ipj)
                act.op(lambda h: h.activation(out=rn[:], in_=rn[:], func=AF.Exp, scale=-0.5), r=[rnb], w=[rnb])
                dve.op(lambda h: h.scalar_tensor_tensor(out=dstT[:, sl_], in0=acc[:, sl_], scalar=post, in1=rn[:], op0=ALU.mult, op1=ALU.mult),
                       r=[accb, rnb], w=[dstb])
                yield
        for srcT, srcb, dst, dstb in ((B_["kT"], B_["kTb"], B_["k_tok"], B_["ktb"]), (vT, vTb, B_["v_tok"], B_["vtb"])):
            for g8 in range(2):
                for j in range(8):
                    tt = g8 * 8 + j
                    pe.op(lambda h: h.transpose(pst[:, j * 128:(j + 1) * 128], srcT[:, tt * 128:(tt + 1) * 128], ident_bf[:]), r=[srcb, cst], w=[pstb])
                act.op(lambda h: h.copy(out=dst[:, g8 * 8:(g8 + 1) * 8, :], in_=pst[:].rearrange("p (j c) -> p j c", j=8)), r=[pstb], w=[dstb])
                yield

    def gdn_projZ(hh):
        wt, wb = load_wblk(w_in_d[0, :, OFF_Z + hh * 128: OFF_Z + (hh + 1) * 128])
        for tg in range(4):
            ipj = yield from acquire("proj"); p_, pb = ps[ipj], psb[ipj]
            proj_fm(wt, wb, tg, p_, pb)
            act.op(lambda h: h.activation(out=zs[:, tg * 512:(tg + 1) * 512], in_=p_[:, :], func=AF.Silu), r=[pb], w=[zsb])
            release(ipj)
            yield

    def gdn_norm(hh):
        if f"oa{hh}" in dbg:
            sp.dma(dbg[f"oa{hh}"], oTa[:], sl_dbg, r=oTa_all)
        for tg in range(4):
            sl_ = slice(tg * 512, (tg + 1) * 512)
            act.op(lambda h: h.activation(out=sqt[:], in_=oTa[:, sl_], func=AF.Square), r=[b_ for t_ in range(tg * 4, tg * 4 + 4) for b_ in oTabs[t_]], w=[sqtb])
            ipj = yield from acquire("proj")
            p_, pb = ps[ipj], psb[ipj]
            pe.op(lambda h: h.matmul(p_[:, :], lhsT=ones_bf[:], rhs=sqt[:], start=True, stop=True), r=[cst, sqtb], w=[pb])
            yield
            act.op(lambda h: h.activation(out=rn[:], in_=p_[:, :], func=AF.Ln, bias=EPS, scale=1.0 / 128), r=[pb], w=[rnb])
            release(ipj)
            act.op(lambda h: h.activation(out=rn[:], in_=rn[:], func=AF.Exp, scale=-0.5), r=[rnb], w=[rnb])
            dve.op(lambda h: h.scalar_tensor_tensor(out=rn[:], in0=oTa[:, sl_], scalar=gdn_nw[:, 0:1], in1=rn[:], op0=ALU.mult, op1=ALU.mult),
                   r=[b_ for t_ in range(tg * 4, tg * 4 + 4) for b_ in oTabs[t_]] + [rnb, prm], w=[rnb])
            dve.op(lambda h: h.tensor_tensor(out=oaT[:, hh, sl_], in0=rn[:], in1=zs[:, sl_], op=ALU.mult), r=[rnb, zsb], w=[oaTb])
            yield

    all_tasks = []
    PRA, PRZ, NORM = [], [], []
    for hh in range(gdn_heads):
        B_ = HB[hh % 2]
        pra = Task(gdn_projA(hh, B_), deps=[PRA[hh - 1] if hh >= 1 else None, NORM[hh - 2] if hh >= 2 else None, PRZ[hh - 1] if hh >= 1 else None])
        PRA.append(pra)
        all_tasks.append(pra)
        prz = Task(gdn_projZ(hh), deps=[pra, NORM[hh - 1] if hh >= 1 else None])
        PRZ.append(prz)
        def gdn_prep(d_, tt, W_, hh=hh, B_=B_):
            qT, qTb, kT, kTb, k_tok, ktb, v_tok, vtb = B_["qT"], B_["qTb"], B_["kT"], B_["kTb"], B_["k_tok"], B_["ktb"], B_["v_tok"], B_["vtb"]
            col = d_ * 8 + hh
            tsl = slice(tt * 128, (tt + 1) * 128)
            act.op(lambda h: h.activation(out=W_["rhs1"][:].rearrange("p (a b) -> p a b", a=4), in_=TRI[d_][:, :].unsqueeze(1).to_broadcast([128, 4, 128]),
                                          func=AF.Copy, scale=LG[:, tt, col:col + 1]), r=[cst, scal], w=[W_["rhs1_b"]])
            act.op(lambda h: h.activation(out=W_["rhsb"][:], in_=ident32[:], func=AF.Copy, scale=LNB[:, tt, col:col + 1]),
                   r=[cst, scal], w=[W_["rhsb_b"]])
            act.op(lambda h: h.activation(out=W_["khat"][:], in_=k_tok[:, tt, :], func=AF.Copy, scale=BG[:, tt, col:col + 1]),
                   r=[ktb, scal], w=[W_["khat_b"]])
            act.op(lambda h: h.activation(out=W_["bv"][:], in_=v_tok[:, tt, :], func=AF.Copy, scale=BETA[:, tt, col:col + 1]),
                   r=[vtb, scal], w=[W_["bv_b"]])
            pool.op(lambda h: h.tensor_scalar(out=W_["kd"][:], in0=k_tok[:, tt, :], scalar1=KD[:, tt, col:col + 1], scalar2=None, op0=ALU.mult),
                    r=[ktb, scal], w=[W_["kd_b"]])
            yield
            ia = yield from acquire("prep")
            pa, pab = ps[ia], psb[ia]
            pe.op(lambda h: h.matmul(pa[:, :], lhsT=ones_r[:], rhs=W_["rhs1"][:], start=True, stop=False), r=[cst, W_["rhs1_b"]], w=[pab])
            pe.op(lambda h: h.matmul(pa[:, :], lhsT=ident_r[:], rhs=MASKS_R[d_][:], start=False, stop=False), r=[cst], w=[pab])
            pe.op(lambda h: h.matmul(pa[:, 128:256], lhsT=ones_r[:], rhs=W_["rhsb"][:], start=False, stop=True), r=[cst, W_["rhsb_b"]], w=[pab])
            yield
            act.op(lambda h: h.activation(out=W_["D1"][:], in_=pa[:, 0:128], func=AF.Exp, bias=A_tok[:, tt, col:col + 1], scale=-1.0),
                   r=[pab, scal], w=[W_["D1_b"]])
            act.op(lambda h: h.activation(out=W_["D1T"][:], in_=pa[:, 128:256], func=AF.Exp, bias=NG_tok[:, tt, col:col + 1], scale=1.0),
                   r=[pab, scal], w=[W_["D1T_b"]])
            act.op(lambda h: h.activation(out=W_["D2T"][:], in_=pa[:, 256:384], func=AF.Exp, bias=NG_tok[:, tt, col:col + 1], scale=1.0),
                   r=[pab, scal], w=[W_["D2T_b"]])
            act.op(lambda h: h.activation(out=W_["grep"][:], in_=pa[:, 384:512], func=AF.Exp), r=[pab], w=[W_["grep_b"]])
            release(ia)
            yield
            ikq = yield from acquire("prep")
            pkq, pkqb = ps[ikq], psb[ikq]
            pe.op(lambda h: h.matmul(pkq[:, 0:128], lhsT=kT[:, tsl], rhs=kT[:, tsl], start=True, stop=True), r=[kTb], w=[pkqb])
            pe.op(lambda h: h.matmul(pkq[:, 128:256], lhsT=kT[:, tsl], rhs=qT[:, tsl], start=True, stop=True), r=[kTb, qTb], w=[pkqb])
            yield
            LU, LUb, LU2, LU2b = W_["LU"], W_["LU_b"], W_["LU2"], W_["LU2_b"]
            dve.op(lambda h: h.tensor_tensor(out=LU[:, 1, 0:128], in0=pkq[:, 0:128], in1=W_["D1"][:], op=ALU.mult), r=[pkqb, W_["D1_b"]], w=[LUb])
            dve.op(lambda h: h.tensor_tensor(out=LU[:, 0, 0:128], in0=pkq[:, 0:128], in1=W_["D1T"][:], op=ALU.mult), r=[pkqb, W_["D1T_b"]], w=[LUb])
            dve.op(lambda h: h.tensor_tensor(out=LU2[:, 0, 128:256], in0=ident32[:], in1=LU[:, 0, 0:128], op=ALU.subtract), r=[cst, LUb], w=[LU2b])
            dve.op(lambda h: h.tensor_tensor(out=W_["attnT"][:], in0=pkq[:, 128:256], in1=W_["D2T"][:], op=ALU.mult), r=[pkqb, W_["D2T_b"]], w=[W_["attnT_b"]])
            dve.op(lambda h: h.tensor_tensor(out=W_["qdT"][:], in0=qT[:, tsl], in1=W_["grep"][:], op=ALU.mult), r=[qTb, W_["grep_b"]], w=[W_["qdT_b"]])
            release(ikq)
            yield
            cur, curb, nxt, nxtb = LU, LUb, LU2, LU2b
            for lev in range(6):
                ipn = yield from acquire("prep")
                pn, pnb = ps[ipn], psb[ipn]
                if lev == 0:
                    pe.op(lambda h: h.matmul(pn[:, 0:128], lhsT=cur[:, 1, 0:128], rhs=cur[:, 0, 0:128], start=True, stop=True), r=[curb], w=[pnb])
                elif lev < 5:
                    pe.op(lambda h: h.matmul(pn[:, 0:256], lhsT=cur[:, 1, 0:128], rhs=cur[:, 0, 0:256], start=True, stop=True), r=[curb], w=[pnb])
                else:
                    pe.op(lambda h: h.matmul(pn[:, 128:256], lhsT=cur[:, 1, 0:128], rhs=cur[:, 0, 128:256], start=True, stop=True), r=[curb], w=[pnb])
                if lev < 5:
                    pe.op(lambda h: h.matmul(pn[:, 256:384], lhsT=cur[:, 0, 0:128], rhs=cur[:, 1, 0:128], start=True, stop=True), r=[curb], w=[pnb])
                yield
                if lev < 5:
                    ev_eng = act
                    if ev_eng is dve:
                        dve.op(lambda h: h.tensor_copy(out=nxt[:, :, 0:128], in_=pn[:, :].rearrange("p (a b) -> p a b", a=2)[:, :, 0:128]), r=[pnb], w=[nxtb])
                    else:
                        act.op(lambda h: h.copy(out=nxt[:, :, 0:128], in_=pn[:, :].rearrange("p (a b) -> p a b", a=2)[:, :, 0:128]), r=[pnb], w=[nxtb])
                if lev > 0:
                    dve.op(lambda h: h.tensor_tensor(out=nxt[:, 0, 128:256], in0=pn[:, 128:256], in1=cur[:, 0, 128:256], op=ALU.add), r=[pnb, curb], w=[nxtb])
                cur, curb, nxt, nxtb = nxt, nxtb, cur, curb
                release(ipn)
                yield
            Wm = cur[:, 0, 128:256]; Wmb = curb
            ipw = yield from acquire("prep")
            pw, pwb = ps[ipw], psb[ipw]
            pe.op(lambda h: h.matmul(pw[:, 0:128], lhsT=W_["khat"][:], rhs=Wm, start=True, stop=True), r=[W_["khat_b"], Wmb], w=[pwb])
            pe.op(lambda h: h.matmul(pw[:, 128:256], lhsT=Wm, rhs=W_["bv"][:], start=True, stop=True), r=[W_["bv_b"], Wmb], w=[pwb])
            yield
            act.op(lambda h: h.copy(out=W_["wT"][:], in_=pw[:, 0:128]), r=[pwb], w=[W_["wT_b"]])
            act.op(lambda h: h.copy(out=W_["u"][:], in_=pw[:, 128:256]), r=[pwb], w=[W_["u_b"]])
            release(ipw)
            yield

        def gdn_scan(d_, tt, W_, S_, hh=hh):
            col = d_ * 8 + hh
            for c in ((0, 1) if d_ == 0 else (1, 0)):
                rs = slice(c * 64, (c + 1) * 64)
                ETc = ET0 if c == 0 else ET1
                ip1 = yield from acquire("scan")
                p1, p1b = ps[ip1], psb[ip1]
                pe.op(lambda h: h.matmul(p1[rs, 0:128], lhsT=W_["wT"][:, rs], rhs=S_["S16"][:], start=True, stop=True), r=[W_["wT_b"], S_["S16_b"]], w=[p1b])
                yield
                dve.op(lambda h: h.tensor_tensor(out=W_["vnew"][rs, :], in0=W_["u"][rs, :], in1=p1[rs, 0:128], op=ALU.subtract),
                       r=[W_["u_b"], p1b], w=[W_["vnew_b"]])
                release(ip1)
                yield
                ip2 = yield from acquire("scan")
                p2, p2b = ps[ip2], psb[ip2]
                pe.op(lambda h: h.matmul(p2[:, 0:64], lhsT=S_["S16"][:], rhs=W_["qdT"][:, rs], start=True, stop=False), r=[S_["S16_b"], W_["qdT_b"]], w=[p2b])
                pe.op(lambda h: h.matmul(p2[:, 0:64], lhsT=W_["vnew"][rs, :], rhs=W_["attnT"][rs, rs], start=False, stop=True),
                      r=[W_["vnew_b"], W_["attnT_b"]], w=[p2b])
                pe.op(lambda h: h.matmul(p2[:, 128:256], lhsT=W_["kd"][rs, :], rhs=W_["vnew"][rs, :], start=True, stop=True),
                      r=[W_["kd_b"], W_["vnew_b"]], w=[p2b])
                yield
                dve.op(lambda h: h.scalar_tensor_tensor(out=S_["S32"][:], in0=S_["S32"][:], scalar=ETc[:, tt, col:col + 1], in1=p2[:, 128:256],
                                                        op0=ALU.mult, op1=ALU.add), r=[S_["S32_b"], scal, p2b], w=[S_["S32_b"]])
                act.op(lambda h: h.copy(out=S_["S16"][:], in_=S_["S32"][:]), r=[S_["S32_b"]], w=[S_["S16_b"]])
                osl = slice(tt * 128 + c * 64, tt * 128 + (c + 1) * 64)
                first = (tt < NT // 2) if d_ == 0 else (tt >= NT // 2)
                if first:
                    act.op(lambda h: h.copy(out=oTa[:, osl], in_=p2[:, 0:64]), r=[p2b], w=[oTabs[tt][c]])
                else:
                    dve.op(lambda h: h.tensor_tensor(out=oTa[:, osl], in0=p2[:, 0:64], in1=oTa[:, osl], op=ALU.add), r=[p2b, oTabs[tt][c]], w=[oTabs[tt][c]])
                release(ip2)
                yield

        def gdn_init():
            for d_ in range(2):
                pool.op(lambda h: h.memset(ST[d_]["S32"][:], 0.0), w=[ST[d_]["S32_b"]])
                pool.op(lambda h: h.memset(ST[d_]["S16"][:], 0.0), w=[ST[d_]["S16_b"]])
            yield
        init_t = Task(gdn_init(), deps=[NORM[hh - 1] if hh >= 1 else None])
        all_tasks.append(init_t)
        order = {0: list(range(NT)), 1: list(range(NT - 1, -1, -1))}
        P = {0: [], 1: []}
        S = {0: [], 1: []}
        tasks = all_tasks
        for i in range(NT):
            for d_ in range(2):
                tt = order[d_][i]
                W_ = WK[(d_, i % NSL)]
                pt = Task(gdn_prep(d_, tt, W_), deps=[S[d_][i - NSL] if i >= NSL else None, PRA[hh], init_t])
                P[d_].append(pt)
                tasks.append(pt)
            for d_ in range(2):
                tt = order[d_][i]
                W_ = WK[(d_, i % NSL)]
                other = S[1 - d_][NT - 1 - i] if (i >= NT // 2 and len(S[1 - d_]) > NT - 1 - i) else None
                stt = Task(gdn_scan(d_, tt, W_, ST[d_]), deps=[P[d_][i], S[d_][i - 1] if i >= 1 else None, other])
                S[d_].append(stt)
                tasks.append(stt)
        all_tasks.append(prz)
        nt = Task(gdn_norm(hh), deps=[prz] + S[0] + S[1])
        NORM.append(nt)
        all_tasks.append(nt)
    if gdn_heads > 0:
        for hh in range(gdn_heads - 1):
            NORM[hh].deps.append(PRA[hh + 1])
        run_tasks(all_tasks, max_active=MAXACT + 2)
    if "oaT" in dbg:
        tmp = sb("dbg_tmp3", [128, T]); tb = Buf("dbg_tmp3")
        for k in range(8):
            act.op(lambda h: h.copy(out=tmp[:], in_=oaT[:, k, :]), r=[oaTb], w=[tb])
            sp.dma(dbg["oaT"][k * 128:(k + 1) * 128, :], tmp[:], sl_dbg, r=[tb])
    free_to(s1_mark)

    obT = sb("obT", [128, 8, T], BF16); obTb = Buf("obT")
    gla_mark = len(allocs)
    if gla_heads > 0:
        NSG = 2
        GW = {}
        GS_ = {}
        for d_ in range(2):
            for sl_i in range(NSG):
                W_ = {}
                for nm, shp, dt in [("gk", [128, 128], F32), ("ekd", [128, 128], F32), ("eg", [128, 128], F32), ("eng", [128, 128], F32), ("etot", [128, 1], F32),
                                    ("kdg", [128, 128], BF16), ("qgT", [128, 128], BF16), ("kgT", [128, 128], BF16), ("attnT", [128, 128], BF16)]:
                    W_[nm] = sb(f"g{nm}_{d_}_{sl_i}", shp, dt)
                    W_[nm + "_b"] = Buf(f"g{nm}_{d_}_{sl_i}")
                GW[(d_, sl_i)] = W_
            S_ = {}
            for nm, shp, dt in [("S32", [128, 256], F32), ("S16", [128, 256], BF16)]:
                S_[nm] = sb(f"g{nm}_{d_}", shp, dt)
                S_[nm + "_b"] = Buf(f"g{nm}_{d_}")
            GS_[d_] = S_

        rT1 = sb("rT1", [33, T]); rT1b = Buf("rT1")
        wr = sb("wr", [128, 8, 32], BF16); wrb = Buf("wr")
        sl_wr = Slot(nc, "wr")
        pool.dma(wr[:], w_in_d[0, :, OFF_R:OFF_R + 32].rearrange("(k p) c -> p k c", p=128), sl_wr, w=[wrb])
        pool.op(lambda h: h.memset(rT1[32:33, :], 1.0), w=[rT1b])
        for tg in range(4):
            p_, pb = next_ps()
            for k in range(8):
                pe.op(lambda h: h.matmul(p_[0:32, :], lhsT=wr[:, k, :], rhs=hT[:, k, tg * 512:(tg + 1) * 512], start=(k == 0), stop=(k == 7)),
                      r=[wrb, hTb], w=[pb])
            act.op(lambda h: h.copy(out=rT1[0:32, tg * 512:(tg + 1) * 512], in_=p_[0:32, :]), r=[pb], w=[rT1b])
        wkv = sb("wkv", [128, 8, 384], BF16); wkvb = Buf("wkv"); sl_wkv = Slot(nc, "wkv")
        qTg = sb("qTg", [128, T], BF16); qTgb = Buf("qTg")
        kTg = sb("kTg", [128, T], BF16); kTgb = Buf("kTg")
        kg_tok = sb("kg_tok", [128, NT, 128], BF16); kgtb = Buf("kg_tok")
        vg_tok = sb("vg_tok", [128, NT, 256], BF16); vgtb = Buf("vg_tok")
        gsT = sb("gsT", [128, 2, T], BF16); gsTb = Buf("gsT")
        obTa = sb("obTa", [128, 2, T]); obTabs = [Buf(f"obTa{t_}") for t_ in range(NT)]
        sqg = sb("sqg", [128, 512], BF16); sqgb = Buf("sqg")
        rng = sb("rng", [128, 512]); rngb = Buf("rng")
    for hb in range(gla_heads):
        for off, dst, dstb, scl in ((OFF_QB, qTg, qTgb, 128.0 ** -0.5), (OFF_KB, kTg, kTgb, 1.0)):
            wt, wb = load_wblk(w_in_d[0, :, off + hb * 128: off + (hb + 1) * 128])
            for tg in range(4):
                p_, pb = next_ps()
                proj_fm(wt, wb, tg, p_, pb)
                act.op(lambda h: h.mul(out=dst[:, tg * 512:(tg + 1) * 512], in_=p_[:, :], mul=scl), r=[pb], w=[dstb])
        for eb in range(2):
            wt, wb = load_wblk(w_in_d[0, :, OFF_GB + hb * 256 + eb * 128: OFF_GB + hb * 256 + (eb + 1) * 128])
            for tg in range(4):
                p_, pb = next_ps()
                proj_fm(wt, wb, tg, p_, pb)
                act.op(lambda h: h.activation(out=gsT[:, eb, tg * 512:(tg + 1) * 512], in_=p_[:, :], func=AF.Silu), r=[pb], w=[gsTb])
        pool.dma(wkv[:, :, 0:128], w_in_d[0, :, OFF_KB + hb * 128: OFF_KB + (hb + 1) * 128].rearrange("(k p) c -> p k c", p=128), sl_wkv, w=[wkvb])
        pool.dma(wkv[:, :, 128:384], w_in_d[0, :, OFF_VB + hb * 256: OFF_VB + (hb + 1) * 256].rearrange("(k p) c -> p k c", p=128), sl_wkv, w=[wkvb])
        for tt in range(NT):
            p_, pb = next_ps()
            for k in range(8):
                pe.op(lambda h: h.matmul(p_[:, 0:384], lhsT=hT[:, k, tt * 128:(tt + 1) * 128], rhs=wkv[:, k, :], start=(k == 0), stop=(k == 7)),
                      r=[hTb, wkvb], w=[pb])
            act.op(lambda h: h.copy(out=kg_tok[:, tt, :], in_=p_[:, 0:128]), r=[pb], w=[kgtb])
            act.op(lambda h: h.copy(out=vg_tok[:, tt, :], in_=p_[:, 128:384]), r=[pb], w=[vgtb])

        def gla_prep(d_, tt, W_, hb=hb):
            tsl = slice(tt * 128, (tt + 1) * 128)
            ix = yield from acquire("gprep")
            px, pxb = ps[ix], psb[ix]
            pe.op(lambda h: h.matmul(px[:, 0:128], lhsT=rT1[:, tsl], rhs=w2cat[:, d_, hb * 128:(hb + 1) * 128], start=True, stop=True),
                  r=[rT1b, prm], w=[pxb])
            yield
            act.op(lambda h: h.activation(out=W_["gk"][:], in_=px[:, 0:128], func=AF.Exp, scale=-1.0), r=[pxb], w=[W_["gk_b"]])
            release(ix)
            act.op(lambda h: h.activation(out=W_["gk"][:], in_=W_["gk"][:], func=AF.Ln, bias=1.0), r=[W_["gk_b"]], w=[W_["gk_b"]])
            yield
            ig = yield from acquire("gprep")
            pg, pgb = ps[ig], psb[ig]
            pe.op(lambda h: h.matmul(pg[:, 0:128], lhsT=DIFG[d_][:], rhs=W_["gk"][:], start=True, stop=True), r=[cst, W_["gk_b"]], w=[pgb])
            pe.op(lambda h: h.matmul(pg[:, 128:256], lhsT=W_["gk"][:], rhs=TRIG[d_][:], start=True, stop=True), r=[cst, W_["gk_b"]], w=[pgb])
            yield
            act.op(lambda h: h.activation(out=W_["eg"][:], in_=pg[:, 128:256], func=AF.Exp), r=[pgb], w=[W_["eg_b"]])
            act.op(lambda h: h.activation(out=W_["eng"][:], in_=pg[:, 128:256], func=AF.Exp, scale=-1.0), r=[pgb], w=[W_["eng_b"]])
            act.op(lambda h: h.activation(out=W_["ekd"][:], in_=pg[:, 0:128], func=AF.Exp), r=[pgb], w=[W_["ekd_b"]])
            lastc = 128 + (127 if d_ == 0 else 0)
            act.op(lambda h: h.activation(out=W_["etot"][:], in_=pg[:, lastc:lastc + 1], func=AF.Exp), r=[pgb], w=[W_["etot_b"]])
            release(ig)
            yield
            dve.op(lambda h: h.tensor_tensor(out=W_["qgT"][:], in0=qTg[:, tsl], in1=W_["eg"][:], op=ALU.mult), r=[qTgb, W_["eg_b"]], w=[W_["qgT_b"]])
            pool.op(lambda h: h.tensor_tensor(out=W_["kgT"][:], in0=kTg[:, tsl], in1=W_["eng"][:], op=ALU.mult), r=[kTgb, W_["eng_b"]], w=[W_["kgT_b"]])
            dve.op(lambda h: h.tensor_tensor(out=W_["kdg"][:], in0=kg_tok[:, tt, :], in1=W_["ekd"][:], op=ALU.mult), r=[kgtb, W_["ekd_b"]], w=[W_["kdg_b"]])
            yield
            ia = yield from acquire("gprep")
            pa_, pab_ = ps[ia], psb[ia]
            pe.op(lambda h: h.matmul(pa_[:, 0:128], lhsT=W_["kgT"][:], rhs=W_["qgT"][:], start=True, stop=True), r=[W_["kgT_b"], W_["qgT_b"]], w=[pab_])
            yield
            dve.op(lambda h: h.tensor_tensor(out=W_["attnT"][:], in0=pa_[:, 0:128], in1=MASKG[d_][:], op=ALU.mult), r=[pab_, cst], w=[W_["attnT_b"]])
            release(ia)
            yield

        def gla_scan(d_, tt, W_, S_):
            tsl = slice(tt * 128, (tt + 1) * 128)
            io = yield from acquire("gscan")
            po, pob = ps[io], psb[io]
            for eb in range(2):
                pe.op(lambda h: h.matmul(po[:, eb * 128:(eb + 1) * 128], lhsT=S_["S16"][:, eb * 128:(eb + 1) * 128], rhs=W_["qgT"][:], start=True, stop=False),
                      r=[S_["S16_b"], W_["qgT_b"]], w=[pob])
                pe.op(lambda h: h.matmul(po[:, eb * 128:(eb + 1) * 128], lhsT=vg_tok[:, tt, eb * 128:(eb + 1) * 128], rhs=W_["attnT"][:], start=False, stop=True),
                      r=[vgtb, W_["attnT_b"]], w=[pob])
            iS = yield from acquire("gscan")
            pS, pSb = ps[iS], psb[iS]
            pe.op(lambda h: h.matmul(pS[:, 0:256], lhsT=W_["kdg"][:], rhs=vg_tok[:, tt, :], start=True, stop=True), r=[W_["kdg_b"], vgtb], w=[pSb])
            yield
            dve.op(lambda h: h.scalar_tensor_tensor(out=S_["S32"][:], in0=S_["S32"][:], scalar=W_["etot"][:, 0:1], in1=pS[:, 0:256],
                                                    op0=ALU.mult, op1=ALU.add), r=[S_["S32_b"], W_["etot_b"], pSb], w=[S_["S32_b"]])
            release(iS)
            act.op(lambda h: h.copy(out=S_["S16"][:], in_=S_["S32"][:]), r=[S_["S32_b"]], w=[S_["S16_b"]])
            first = (tt < NT // 2) if d_ == 0 else (tt >= NT // 2)
            if first:
                act.op(lambda h: h.copy(out=obTa[:, :, tsl], in_=po[:, 0:256].rearrange("p (e t) -> p e t", e=2)), r=[pob], w=[obTabs[tt]])
            else:
                dve.op(lambda h: h.tensor_tensor(out=obTa[:, :, tsl], in0=po[:, 0:256].rearrange("p (e t) -> p e t", e=2), in1=obTa[:, :, tsl], op=ALU.add),
                       r=[pob, obTabs[tt]], w=[obTabs[tt]])
            release(io)
            yield

        for d_ in range(2):
            pool.op(lambda h: h.memset(GS_[d_]["S32"][:], 0.0), w=[GS_[d_]["S32_b"]])
            pool.op(lambda h: h.memset(GS_[d_]["S16"][:], 0.0), w=[GS_[d_]["S16_b"]])
        order = {0: list(range(NT)), 1: list(range(NT - 1, -1, -1))}
        P = {0: [], 1: []}
        S = {0: [], 1: []}
        tasks = []
        for i in range(NT):
            for d_ in range(2):
                pt = Task(gla_prep(d_, order[d_][i], GW[(d_, i % NSG)]), deps=[S[d_][i - NSG] if i >= NSG else None])
                P[d_].append(pt)
                tasks.append(pt)
            for d_ in range(2):
                other = S[1 - d_][NT - 1 - i] if (i >= NT // 2 and len(S[1 - d_]) > NT - 1 - i) else None
                stt = Task(gla_scan(d_, order[d_][i], GW[(d_, i % NSG)], GS_[d_]), deps=[P[d_][i], S[d_][i - 1] if i >= 1 else None, other])
                S[d_].append(stt)
                tasks.append(stt)
        run_tasks(tasks, max_active=MAXACT)
        if f"ob{hb}" in dbg:
            for eb in range(2):
                sp.dma(dbg[f"ob{hb}"][eb * 128:(eb + 1) * 128, :], obTa[:, eb, :], sl_dbg, r=obTabs)
        for tg in range(4):
            sl_ = slice(tg * 512, (tg + 1) * 512)
            p_, pb = next_ps()
            for eb in range(2):
                act.op(lambda h: h.activation(out=sqg[:], in_=obTa[:, eb, sl_], func=AF.Square), r=obTabs[tg * 4:tg * 4 + 4], w=[sqgb])
                pe.op(lambda h: h.matmul(p_[:, :], lhsT=ones_bf[:], rhs=sqg[:], start=(eb == 0), stop=(eb == 1)), r=[cst, sqgb], w=[pb])
            act.op(lambda h: h.activation(out=rng[:], in_=p_[:, :], func=AF.Ln, bias=EPS, scale=1.0 / 256), r=[pb], w=[rngb])
            act.op(lambda h: h.activation(out=rng[:], in_=rng[:], func=AF.Exp, scale=-0.5), r=[rngb], w=[rngb])
            for eb in range(2):
                dve.op(lambda h: h.scalar_tensor_tensor(out=obTa[:, eb, sl_], in0=obTa[:, eb, sl_], scalar=gla_nw[:, eb:eb + 1], in1=rng[:], op0=ALU.mult, op1=ALU.mult),
                       r=obTabs[tg * 4:tg * 4 + 4] + [rngb, prm], w=obTabs[tg * 4:tg * 4 + 4])
                dve.op(lambda h: h.tensor_tensor(out=obT[:, hb * 2 + eb, sl_], in0=obTa[:, eb, sl_], in1=gsT[:, eb, sl_], op=ALU.mult), r=obTabs[tg * 4:tg * 4 + 4] + [gsTb], w=[obTb])
    if "obT" in dbg:
        tmp = sb("dbg_tmp4", [128, T]); tb = Buf("dbg_tmp4")
        for k in range(8):
            act.op(lambda h: h.copy(out=tmp[:], in_=obT[:, k, :]), r=[obTb], w=[tb])
            sp.dma(dbg["obT"][k * 128:(k + 1) * 128, :], tmp[:], sl_dbg, r=[tb])
    free_to(gla_mark)

    if do_final:
        mT = sb("mT", [128, 8, T], BF16); mTb = Buf("mT")
        fin_mark = len(allocs)
        sga = sb("sga", [128, 512]); sgab = Buf("sga")
        sgb = sb("sgb", [128, 512]); sgbb = Buf("sgb")
        t1 = sb("t1", [128, 512]); t1b = Buf("t1")
        t2 = sb("t2", [128, 512]); t2b = Buf("t2")
        NW2 = 8
        wb2 = [sb(f"wb2_{i}", [128, 8, 128], BF16) for i in range(NW2)]; wb2b = [Buf(f"wb2_{i}") for i in range(NW2)]
        sl_w2 = [Slot(nc, f"w2_{i}") for i in range(NW2)]
        rr2 = [0]

        def load2(src_ap):
            i = rr2[0] % NW2
            rr2[0] += 1
            pool.dma(wb2[i][:], src_ap.rearrange("(k p) c -> p k c", p=128), sl_w2[i], w=[wb2b[i]])
            return wb2[i], wb2b[i]

        for m in range(8):
            msl = slice(m * 128, (m + 1) * 128)
            wg, wgb_ = load2(wpg_d[0, :, msl])
            wl, wlb_ = load2(wpl_d[0, :, msl])
            wa, wab_ = load2(w_in_d[0, :, OFF_GA + m * 128: OFF_GA + (m + 1) * 128])
            wbb, wbbb_ = load2(w_in_d[0, :, OFF_GBG + m * 128: OFF_GBG + (m + 1) * 128])
            for tg in range(4):
                sl_ = slice(tg * 512, (tg + 1) * 512)
                pga, pgab = next_ps(); proj_fm(wa, wab_, tg, pga, pgab)
                act.op(lambda h: h.activation(out=sga[:], in_=pga[:, :], func=AF.Sigmoid), r=[pgab], w=[sgab])
                pgb_, pgbb = next_ps(); proj_fm(wbb, wbbb_, tg, pgb_, pgbb)
                act.op(lambda h: h.activation(out=sgb[:], in_=pgb_[:, :], func=AF.Sigmoid), r=[pgbb], w=[sgbb])
                pya, pyab = next_ps(); proj_fm(wg, wgb_, tg, pya, pyab, rhsT=oaT, rb=oaTb)
                dve.op(lambda h: h.tensor_tensor(out=t1[:], in0=pya[:, :], in1=sga[:], op=ALU.mult), r=[pyab, sgab], w=[t1b])
                pyb, pybb = next_ps(); proj_fm(wl, wlb_, tg, pyb, pybb, rhsT=obT, rb=obTb)
                dve.op(lambda h: h.tensor_tensor(out=t2[:], in0=pyb[:, :], in1=sgb[:], op=ALU.mult), r=[pybb, sgbb], w=[t2b])
                dve.op(lambda h: h.tensor_tensor(out=mT[:, m, sl_], in0=t1[:], in1=t2[:], op=ALU.add), r=[t1b, t2b], w=[mTb])
        if "mT" in dbg:
            tmp = sb("dbg_tmp5", [128, T]); tb = Buf("dbg_tmp5")
            for k in range(8):
                act.op(lambda h: h.copy(out=tmp[:], in_=mT[:, k, :]), r=[mTb], w=[tb])
                sp.dma(dbg["mT"][k * 128:(k + 1) * 128, :], tmp[:], sl_dbg, r=[tb])
        free_to(fin_mark)
        wob = hTb; sl_wo = Slot(nc, "wo")
        for k in range(8):
            pool.dma(hT[:, k, 0:D], wout_d[0, k * 128:(k + 1) * 128, :], sl_wo, w=[wob])
        lnp = sb("lnp", [128, D]); sl_lnp = Slot(nc, "lnp"); lnpb = Buf("lnp")
        sp.dma(lnp[:], ln_post_d[0, :].partition_broadcast(128), sl_lnp, w=[lnpb])
        xr = [sb(f"xr{i}", [128, D]) for i in range(2)]; xrb = [Buf(f"xr{i}") for i in range(2)]
        sl_xr = [Slot(nc, f"xr{i}") for i in range(2)]
        ot = [sb(f"ot{i}", [128, D]) for i in range(2)]; otb = [Buf(f"ot{i}") for i in range(2)]
        st1 = sb("st1", [128, 8]); st1b = Buf("st1")
        junk2 = sb("junk2", [128, 512], BF16); junk2b = Buf("junk2")
        for tt in range(NT):
            s = tt % 2
            tsl = slice(tt * 128, (tt + 1) * 128)
            sp.dma(xr[s][:], x_d[tsl, :], sl_xr[s], w=[xrb[s]])
            pp = []
            for half in range(2):
                p_, pb = next_ps()
                for m in range(8):
                    pe.op(lambda h: h.matmul(p_[:, :], lhsT=mT[:, m, tsl], rhs=hT[:, m, half * 512:(half + 1) * 512], start=(m == 0), stop=(m == 7)),
                          r=[mTb, wob], w=[pb])
                act.op(lambda h: h.activation(out=junk2[:], in_=p_[:, :], func=AF.Square, accum_out=st1[:, half:half + 1]), r=[pb], w=[junk2b, st1b])
                pp.append((p_, pb))
            dve.op(lambda h: h.tensor_tensor(out=st1[:, 2:3], in0=st1[:, 0:1], in1=st1[:, 1:2], op=ALU.add), r=[st1b], w=[st1b])
            act.op(lambda h: h.activation(out=st1[:, 3:4], in_=st1[:, 2:3], func=AF.Ln, bias=EPS, scale=1.0 / D), r=[st1b], w=[st1b])
            act.op(lambda h: h.activation(out=st1[:, 4:5], in_=st1[:, 3:4], func=AF.Exp, scale=-0.5), r=[st1b], w=[st1b])
            for half in range(2):
                p_, pb = pp[half]
                hs = slice(half * 512, (half + 1) * 512)
                dve.op(lambda h: h.scalar_tensor_tensor(out=ot[s][:, hs], in0=p_[:, :], scalar=st1[:, 4:5], in1=lnp[:, hs], op0=ALU.mult, op1=ALU.mult),
                       r=[pb, st1b, lnpb], w=[otb[s]])
                dve.op(lambda h: h.tensor_tensor(out=ot[s][:, hs], in0=ot[s][:, hs], in1=xr[s][:, hs], op=ALU.add), r=[otb[s], xrb[s]], w=[otb[s]])
            sp.dma(out_d[tsl, :], ot[s][:], sl_out2[s], r=[otb[s]])
    else:
        zt = sb("zt", [128, D]); ztb = Buf("zt")
        pool.op(lambda h: h.memset(zt[:], 0.0), w=[ztb])
        for tt in range(NT):
            sp.dma(out_d[tt * 128:(tt + 1) * 128, :], zt[:], sl_out2[tt % 2], r=[ztb])
    for so_ in sl_out2:
        if so_.cnt:
            sp.h.wait_ge(so_.sem, so_.cnt)
    if sl_dbg.cnt:
        sp.h.wait_ge(sl_dbg.sem, sl_dbg.cnt)
    return nc


_INPUT_NAMES = ["ln_pre_w", "w_in", "conv_w", "a_log_fwd", "a_log_bwd", "dt_bias_fwd", "dt_bias_bwd", "gdn_norm_w", "w_proj_gdn",
                "gk_w2_fwd", "gk_b2_fwd", "gk_w2_bwd", "gk_b2_bwd", "gla_norm_w", "w_proj_gla", "w_out", "ln_post_w"]


def kernel(**inputs):
    x = np.ascontiguousarray(np.asarray(inputs["x"], dtype=np.float32))
    shared = {n: np.ascontiguousarray(np.asarray(inputs[n], dtype=np.float32)) for n in _INPUT_NAMES}
    nc = build_nc()
    in_maps = [dict(shared, x=x[b]) for b in range(8)]
    res = run_bass_kernel_spmd(nc, in_maps, core_ids=list(range(8)))
    return np.stack([np.asarray(r["out"], dtype=np.float32) for r in res.results], axis=0)
```

```python
import numpy as np
import concourse.bass as bass
import concourse.mybir as mybir
from concourse.bass_utils import run_bass_kernel_spmd

F32 = mybir.dt.float32
F32R = mybir.dt.float32r
BF16 = mybir.dt.bfloat16
AF = mybir.ActivationFunctionType
ALU = mybir.AluOpType

T = 2048
NT = 16
D = 1024
NIN = 9280
EPS = 1e-6
BIG = 30000.0
MAXACT = 6
OFF_Q, OFF_K, OFF_V, OFF_Z = 0, 1024, 2048, 3072
OFF_SM = 4096
OFF_QB, OFF_KB, OFF_VB, OFF_GB = 4128, 4640, 5152, 6176
OFF_R = 7200
OFF_GA, OFF_GBG = 7232, 8256


class Ev:
    __slots__ = ("sem", "val", "key")

    def __init__(self, sem, val, key):
        self.sem, self.val, self.key = sem, val, key


class Buf:
    __slots__ = ("name", "wev", "revs")

    def __init__(self, name):
        self.name, self.wev, self.revs = name, None, {}


class Slot:
    registry = None

    def __init__(self, nc, name):
        if Slot.registry is not None:
            Slot.registry.append(self)
        self.name = name
        self.sem = nc.semaphore("ds_" + name).__enter__()
        self.cnt = 0


class Eng:
    def __init__(self, nc, name, h, selfsync):
        self.name, self.h, self.selfsync = name, h, selfsync
        self.sem = nc.semaphore("es_" + name).__enter__()
        self.cnt = 0
        self.seen = {}

    def wait(self, ev):
        if ev is None:
            return
        if ev.sem is self.sem and not self.selfsync:
            return
        if self.seen.get(ev.key, 0) >= ev.val:
            return
        self.h.wait_ge(ev.sem, ev.val)
        self.seen[ev.key] = ev.val

    def _deps(self, r, w):
        for b in r:
            self.wait(b.wev)
        for b in w:
            self.wait(b.wev)
            for ev in b.revs.values():
                self.wait(ev)

    def _mark(self, ev, r, w):
        for b in r:
            b.revs[ev.key] = ev
        for b in w:
            b.wev = ev
            b.revs = {}

    def op(self, fn, r=(), w=()):
        self._deps(r, w)
        ins = fn(self.h)
        self.cnt += 1
        ins.then_inc(self.sem, 1)
        ev = Ev(self.sem, self.cnt, self.name)
        self._mark(ev, r, w)
        return ev

    def dma(self, out, in_, slot, r=(), w=(), **kw):
        self._deps(r, w)
        ins = self.h.dma_start(out=out, in_=in_, **kw)
        slot.cnt += 16
        ins.then_inc(slot.sem, 16)
        ev = Ev(slot.sem, slot.cnt, slot.name)
        self._mark(ev, r, w)
        return ev


def build_nc(debug=(), gdn_heads=8, gla_heads=4, do_final=True):
    nc = bass.Bass("TRN2", target_bir_lowering=False)
    dram_in = lambda n, s: nc.dram_tensor(n, s, F32, kind="ExternalInput").ap()
    x_d = dram_in("x", [T, D])
    ln_pre_d = dram_in("ln_pre_w", [1, D])
    w_in_d = dram_in("w_in", [1, D, NIN])
    conv_d = dram_in("conv_w", [1, 5, 3072])
    alog_f_d = dram_in("a_log_fwd", [1, 8]); alog_b_d = dram_in("a_log_bwd", [1, 8])
    dtb_f_d = dram_in("dt_bias_fwd", [1, 8]); dtb_b_d = dram_in("dt_bias_bwd", [1, 8])
    gdn_nw_d = dram_in("gdn_norm_w", [1, 128])
    wpg_d = dram_in("w_proj_gdn", [1, D, D])
    gkw_f_d = dram_in("gk_w2_fwd", [1, 16, 512]); gkb_f_d = dram_in("gk_b2_fwd", [1, 512])
    gkw_b_d = dram_in("gk_w2_bwd", [1, 16, 512]); gkb_b_d = dram_in("gk_b2_bwd", [1, 512])
    gla_nw_d = dram_in("gla_norm_w", [1, 256])
    wpl_d = dram_in("w_proj_gla", [1, D, D])
    wout_d = dram_in("w_out", [1, D, D])
    ln_post_d = dram_in("ln_post_w", [1, D])
    out_d = nc.dram_tensor("out", [T, D], F32, kind="ExternalOutput").ap()
    dbg = {}
    for name, shape in debug:
        dbg[name] = nc.dram_tensor("dbg_" + name, shape, F32, kind="ExternalOutput").ap()

    pe = Eng(nc, "pe", nc.tensor, False)
    act = Eng(nc, "act", nc.scalar, True)
    dve = Eng(nc, "dve", nc.vector, True)
    pool = Eng(nc, "pool", nc.gpsimd, True)
    sp = Eng(nc, "sp", nc.sync, False)

    allocs = []

    def sb(name, shape, dt=F32):
        cm = nc.sbuf_tensor(name, shape, dt)
        t = cm.__enter__()
        allocs.append(cm)
        return t

    all_slots = []
    Slot.registry = all_slots

    def barrier():
        engs = (pe, act, dve, pool, sp)
        for e in engs:
            for f in engs:
                if f is not e and f.cnt:
                    e.wait(Ev(f.sem, f.cnt, f.name))
            for sl in all_slots:
                if sl.cnt:
                    e.wait(Ev(sl.sem, sl.cnt, sl.name))

    def free_to(mark):
        barrier()
        while len(allocs) > mark:
            allocs.pop().__exit__(None, None, None)

    NPS = 7
    ps = [nc.psum_tensor(f"ps{i}", [128, 512], F32).__enter__() for i in range(NPS)]
    psb = [Buf(f"ps{i}") for i in range(NPS)]
    pst = nc.psum_tensor("pst", [128, 1024], BF16).__enter__()
    pstb = Buf("pst")
    ps_rr = {}
    PS_POOLS = {"any": list(range(NPS)), "prep": [0, 1, 2], "scan": [3, 4], "proj": [5, 6], "gprep": [0, 1, 2], "gscan": [3, 4, 5, 6]}

    def next_ps(pool_="any"):
        lst = PS_POOLS[pool_]
        c = ps_rr.get(pool_, 0)
        ps_rr[pool_] = c + 1
        i = lst[c % len(lst)]
        return ps[i], psb[i]

    ps_busy = [False] * NPS

    def acquire(pool_):
        while True:
            for i in PS_POOLS[pool_]:
                if not ps_busy[i]:
                    ps_busy[i] = True
                    return i
            yield

    def release(i):
        ps_busy[i] = False

    class Task:
        def __init__(self, gen, deps=()):
            self.gen, self.deps, self.done = gen, [d for d in deps if d is not None], False

    stall = [0]

    def run_tasks(tasks, max_active=6):
        pending = list(tasks)
        active = []
        while pending or active:
            for t in list(pending):
                if len(active) >= max_active:
                    break
                if all(d.done for d in t.deps):
                    pending.remove(t)
                    active.append(t)
            assert active, "task deadlock"
            before = (pe.cnt, act.cnt, dve.cnt, pool.cnt, len(active), len(pending))
            for t in list(active):
                try:
                    next(t.gen)
                except StopIteration:
                    t.done = True
                    active.remove(t)
            if before == (pe.cnt, act.cnt, dve.cnt, pool.cnt, len(active), len(pending)):
                stall[0] += 1
                assert stall[0] < 200, ("scheduler stall", [getattr(t.gen, "__name__", "?") for t in active], list(ps_busy), len(pending))
            else:
                stall[0] = 0

    cst = Buf("const")
    ident32 = sb("ident32", [128, 128]); ident_bf = sb("ident_bf", [128, 128], BF16)
    ones32 = sb("ones32", [128, 128]); ones_bf = sb("ones_bf", [128, 128], BF16)
    zeros32 = sb("zeros32", [128, 128])
    pool.op(lambda h: h.memset(ones32[:], 1.0), w=[cst])
    pool.op(lambda h: h.memset(zeros32[:], 0.0), w=[cst])
    pool.op(lambda h: h.affine_select(out=ident32[:], in_=zeros32[:], pattern=[[-1, 128]], compare_op=ALU.not_equal,
                                      fill=1.0, base=0, channel_multiplier=1), r=[cst], w=[cst])
    pool.op(lambda h: h.tensor_copy(out=ident_bf[:], in_=ident32[:]), r=[cst], w=[cst])
    pool.op(lambda h: h.tensor_copy(out=ones_bf[:], in_=ones32[:]), r=[cst], w=[cst])

    src_cache = {}

    def tri_const(name, n, inval, fillval, offval, step, cm, cmp):
        t = sb(name, [128, 128])
        pool.op(lambda h: h.memset(t[:], offval), w=[cst])
        if inval not in src_cache:
            src_cache[inval] = sb(f"src_{len(src_cache)}", [128, 128])
            pool.op(lambda h: h.memset(src_cache[inval][:], inval), w=[cst])
        src = src_cache[inval]
        for b0 in range(0, 128, n):
            pool.op(lambda h: h.affine_select(out=t[b0:b0 + n, b0:b0 + n], in_=src[b0:b0 + n, b0:b0 + n], pattern=[[step, n]],
                                              compare_op=cmp, fill=fillval, base=0, channel_multiplier=cm), r=[cst], w=[cst])
        return t

    TRI = {
        0: tri_const("tri_f", 64, 1.0, 0.0, 0.0, 1, -1, ALU.is_ge),
        1: tri_const("tri_b", 64, 1.0, 0.0, 0.0, -1, 1, ALU.is_ge),
    }
    NEG_A = {
        0: tri_const("nega_f", 64, 0.0, BIG, BIG, -1, 1, ALU.is_gt),
        1: tri_const("nega_b", 64, 0.0, BIG, BIG, 1, -1, ALU.is_gt),
    }
    NEG_B = {
        0: tri_const("negb_f", 64, 0.0, -BIG, -BIG, 1, -1, ALU.is_gt),
        1: tri_const("negb_b", 64, 0.0, -BIG, -BIG, -1, 1, ALU.is_gt),
    }
    NEG_C = {
        0: tri_const("negc_f", 64, 0.0, -BIG, -BIG, 1, -1, ALU.is_ge),
        1: tri_const("negc_b", 64, 0.0, -BIG, -BIG, -1, 1, ALU.is_ge),
    }
    BDONES = tri_const("bdones", 64, 1.0, 1.0, 0.0, 1, 1, ALU.is_ge)
    ones_r = sb("ones_r", [128, 128], F32R); ident_r = sb("ident_r", [128, 128], F32R)
    pool.op(lambda h: h.tensor_copy(out=ones_r[:], in_=ones32[:]), r=[cst], w=[cst])
    pool.op(lambda h: h.tensor_copy(out=ident_r[:], in_=ident32[:]), r=[cst], w=[cst])
    TRI4_R = {}
    for d__ in range(2):
        t4 = sb(f"tri4_r{d__}", [128, 512], F32R)
        for rep in range(4):
            pool.op(lambda h: h.tensor_copy(out=t4[:, rep * 128:(rep + 1) * 128], in_=TRI[d__][:]), r=[cst], w=[cst])
        TRI4_R[d__] = t4
    MASKS_R = {}
    for d__ in range(2):
        mk = sb(f"masks_r{d__}", [128, 512], F32R)
        pool.op(lambda h: h.tensor_copy(out=mk[:, 0:128], in_=NEG_A[d__][:]), r=[cst], w=[cst])
        pool.op(lambda h: h.tensor_copy(out=mk[:, 128:256], in_=NEG_B[d__][:]), r=[cst], w=[cst])
        pool.op(lambda h: h.tensor_copy(out=mk[:, 256:384], in_=NEG_C[d__][:]), r=[cst], w=[cst])
        pool.op(lambda h: h.tensor_copy(out=mk[:, 384:512], in_=zeros32[:]), r=[cst], w=[cst])
        MASKS_R[d__] = mk
    SEL0 = sb("sel0", [128, 128]); SEL1 = sb("sel1", [128, 128])
    pool.op(lambda h: h.memset(SEL0[:], 0.0), w=[cst]); pool.op(lambda h: h.memset(SEL1[:], 0.0), w=[cst])
    pool.op(lambda h: h.memset(SEL0[0:64, :], 1.0), w=[cst]); pool.op(lambda h: h.memset(SEL1[64:128, :], 1.0), w=[cst])
    GS = -1.0 / 16.0
    TRIG = {0: tri_const("trig_f", 128, GS, 0.0, 0.0, 1, -1, ALU.is_ge),
            1: tri_const("trig_b", 128, GS, 0.0, 0.0, -1, 1, ALU.is_ge)}
    DIFG = {0: tri_const("difg_f", 128, 0.0, GS, 0.0, 1, -1, ALU.is_ge),
            1: tri_const("difg_b", 128, 0.0, GS, 0.0, -1, 1, ALU.is_ge)}
    MASKG = {0: tri_const("maskg_f", 128, 1.0, 0.0, 0.0, 1, -1, ALU.is_ge),
             1: tri_const("maskg_b", 128, 1.0, 0.0, 0.0, -1, 1, ALU.is_ge)}

    prm = Buf("params")
    sl_prm = Slot(nc, "prm")
    lnw_T = sb("lnw_T", [128, 8])
    sp.dma(lnw_T[:], ln_pre_d[0, :].rearrange("(k p) -> p k", p=128), sl_prm, w=[prm], allow_slow_non_contiguous=True)
    cw = sb("cw", [128, 24, 5])
    for t_ in range(5):
        sp.dma(cw[:, :, t_], conv_d[0, t_, :].rearrange("(b p) -> p b", p=128), sl_prm, w=[prm], allow_slow_non_contiguous=True)
    prm16 = sb("prm16", [128, 2, 16])
    sp.dma(prm16[:, 0, 0:8], alog_f_d[0, :].partition_broadcast(128), sl_prm, w=[prm])
    sp.dma(prm16[:, 0, 8:16], alog_b_d[0, :].partition_broadcast(128), sl_prm, w=[prm])
    sp.dma(prm16[:, 1, 0:8], dtb_f_d[0, :].partition_broadcast(128), sl_prm, w=[prm])
    sp.dma(prm16[:, 1, 8:16], dtb_b_d[0, :].partition_broadcast(128), sl_prm, w=[prm])
    gdn_nw = sb("gdn_nw", [128, 1])
    sp.dma(gdn_nw[:], gdn_nw_d[0, :].rearrange("(p o) -> p o", o=1), sl_prm, w=[prm])
    gla_nw = sb("gla_nw", [128, 2])
    sp.dma(gla_nw[:], gla_nw_d[0, :].rearrange("(k p) -> p k", p=128), sl_prm, w=[prm], allow_slow_non_contiguous=True)
    w2cat = sb("w2cat", [33, 2, 512])
    pool.op(lambda h: h.memset(w2cat[0:32, :, :], 0.0), w=[prm])
    sp.dma(w2cat[0:16, 0, :], gkw_f_d[0, :, :], sl_prm, r=[prm], w=[prm])
    sp.dma(w2cat[16:32, 1, :], gkw_b_d[0, :, :], sl_prm, w=[prm])
    sp.dma(w2cat[32:33, 0, :], gkb_f_d[0:1, :], sl_prm, w=[prm])
    sp.dma(w2cat[32:33, 1, :], gkb_b_d[0:1, :], sl_prm, w=[prm])

    def dbg_out(name, src_ap, rbufs, dst=None):
        if name not in dbg:
            return
        d = dbg[name] if dst is None else dst
        sp.dma(d, src_ap, sl_dbg, r=rbufs)

    sl_dbg = Slot(nc, "dbg")
    sl_out2 = [Slot(nc, "out0"), Slot(nc, "out1")]

    hT = sb("hT", [128, 8, T], BF16); hTb = Buf("hT")
    oaT = sb("oaT", [128, 8, T], BF16); oaTb = Buf("oaT")
    base_mark = len(allocs)

    xt = [sb(f"xt{i}", [128, D]) for i in range(2)]; xtb = [Buf(f"xt{i}") for i in range(2)]
    sl_x = [Slot(nc, f"x{i}") for i in range(2)]
    junk = sb("junk", [128, D], BF16); junkb = Buf("junk")
    xn = sb("xn", [128, D], BF16); xnb = Buf("xn")
    st0 = sb("st0", [128, 4]); st0b = Buf("st0")
    for tt in range(NT):
        s = tt % 2
        sp.dma(xt[s][:], x_d[tt * 128:(tt + 1) * 128, :], sl_x[s], w=[xtb[s]])
        act.op(lambda h: h.activation(out=junk[:], in_=xt[s][:], func=AF.Square, accum_out=st0[:, 0:1]), r=[xtb[s]], w=[junkb, st0b])
        act.op(lambda h: h.activation(out=st0[:, 1:2], in_=st0[:, 0:1], func=AF.Ln, bias=EPS, scale=1.0 / D), r=[st0b], w=[st0b])
        act.op(lambda h: h.activation(out=st0[:, 2:3], in_=st0[:, 1:2], func=AF.Exp, scale=-0.5), r=[st0b], w=[st0b])
        act.op(lambda h: h.activation(out=xn[:], in_=xt[s][:], func=AF.Copy, scale=st0[:, 2:3]), r=[xtb[s], st0b], w=[xnb])
        for k in range(8):
            pe.op(lambda h: h.transpose(pst[:, k * 128:(k + 1) * 128], xn[:, k * 128:(k + 1) * 128], ident_bf[:]), r=[xnb, cst], w=[pstb])
        dve.op(lambda h: h.tensor_tensor(out=hT[:, :, tt * 128:(tt + 1) * 128], in0=pst[:].rearrange("p (k t) -> p k t", k=8),
                                         in1=lnw_T[:, :].unsqueeze(2).to_broadcast([128, 8, 128]), op=ALU.mult), r=[pstb, prm], w=[hTb])
    if "st0" in dbg:
        sp.dma(dbg["st0"], st0[:], sl_dbg, r=[st0b])
        tmpx = sb("dbg_tmpx", [128, D]); tbx = Buf("dbg_tmpx")
        act.op(lambda h: h.copy(out=tmpx[:], in_=xn[:]), r=[xnb], w=[tbx])
        sp.dma(dbg["xn"], tmpx[:], sl_dbg, r=[tbx])
    if "hT" in dbg:
        tmp = sb("dbg_tmp", [128, T])
        tb = Buf("dbg_tmp")
        for k in range(8):
            act.op(lambda h: h.copy(out=tmp[:], in_=hT[:, k, :]), r=[hTb], w=[tb])
            sp.dma(dbg["hT"][k * 128:(k + 1) * 128, :], tmp[:], sl_dbg, r=[tb])
    free_to(base_mark)

    NWS = 2
    wblk = [sb(f"wblk{i}", [128, 8, 128], BF16) for i in range(NWS)]
    wblkb = [Buf(f"wblk{i}") for i in range(NWS)]
    sl_w = [Slot(nc, f"w{i}") for i in range(NWS)]
    w_rr = [0]

    def load_wblk(src_ap):
        i = w_rr[0] % NWS
        w_rr[0] += 1
        pool.dma(wblk[i][:], src_ap.rearrange("(k p) c -> p k c", p=128), sl_w[i], w=[wblkb[i]])
        return wblk[i], wblkb[i]

    def proj_fm(wt, wb, tg, pst_, pb, rhsT=hT, rb=hTb):
        for k in range(8):
            pe.op(lambda h: h.matmul(pst_[:, :], lhsT=wt[:, k, :], rhs=rhsT[:, k, tg * 512:(tg + 1) * 512], start=(k == 0), stop=(k == 7)),
                  r=[wb, rb], w=[pb])

    mix_mark = len(allocs)

    NCOL = 16
    LG = sb("LG", [128, NT, NCOL]); LNB = sb("LNB", [128, NT, NCOL]); BETA = sb("BETA", [128, NT, NCOL])
    LG_R = sb("LG_R", [128, NT, NCOL], F32R); LNB_R = sb("LNB_R", [128, NT, NCOL], F32R)
    A_tok = sb("A_tok", [128, NT, NCOL]); NG_tok = sb("NG_tok", [128, NT, NCOL]); BG = sb("BG", [128, NT, NCOL])
    KD = sb("KD", [128, NT, NCOL]); ET0 = sb("ET0", [128, NT, NCOL]); ET1 = sb("ET1", [128, NT, NCOL])
    scal = Buf("scal")
    s1_mark = len(allocs)
    wsm = sb("wsm", [128, 8, 32], BF16); wsmb = Buf("wsm")
    sl_wsm = Slot(nc, "wsm")
    pool.dma(wsm[:], w_in_d[0, :, OFF_SM:OFF_SM + 32].rearrange("(k p) c -> p k c", p=128), sl_wsm, w=[wsmb])
    asm = sb("asm", [128, NT, 32]); asmb = Buf("asm")
    for tt in range(NT):
        p_, pb = next_ps()
        for k in range(8):
            pe.op(lambda h: h.matmul(p_[:, 0:32], lhsT=hT[:, k, tt * 128:(tt + 1) * 128], rhs=wsm[:, k, :], start=(k == 0), stop=(k == 7)),
                  r=[hTb, wsmb], w=[pb])
        act.op(lambda h: h.copy(out=asm[:, tt, :], in_=p_[:, 0:32]), r=[pb], w=[asmb])
    t16 = sb("t16", [128, NT, NCOL]); t16b = Buf("t16")
    dve.op(lambda h: h.tensor_tensor(out=t16[:], in0=asm[:, :, 0:16], in1=prm16[:, 1:2, :].to_broadcast([128, NT, NCOL]), op=ALU.add), r=[asmb, prm], w=[t16b])
    act.op(lambda h: h.activation(out=t16[:], in_=t16[:], func=AF.Exp), r=[t16b], w=[t16b])
    act.op(lambda h: h.activation(out=t16[:], in_=t16[:], func=AF.Ln, bias=1.0), r=[t16b], w=[t16b])
    nA = sb("nA", [128, 1, NCOL]); nAb = Buf("nA")
    act.op(lambda h: h.activation(out=nA[:], in_=prm16[:, 0:1, :], func=AF.Exp), r=[prm], w=[nAb])
    dve.op(lambda h: h.scalar_tensor_tensor(out=LG[:], in0=t16[:], scalar=-1.0, in1=nA[:].to_broadcast([128, NT, NCOL]), op0=ALU.mult, op1=ALU.mult),
           r=[t16b, nAb], w=[scal])
    act.op(lambda h: h.activation(out=t16[:], in_=asm[:, :, 16:32], func=AF.Exp, scale=-1.0), r=[asmb, scal], w=[t16b])
    act.op(lambda h: h.activation(out=t16[:], in_=t16[:], func=AF.Ln, bias=1.0), r=[t16b], w=[t16b])
    act.op(lambda h: h.mul(out=LNB[:], in_=t16[:], mul=-1.0), r=[t16b], w=[scal])
    act.op(lambda h: h.activation(out=BETA[:], in_=LNB[:], func=AF.Exp), r=[scal], w=[scal])
    G_tok = sb("G_tok", [128, NT, NCOL]); TOTO = sb("TOTO", [128, NT, NCOL])
    for tt in range(NT):
        p_, pb = next_ps()
        for i, M in enumerate([TRI[0], TRI[1], BDONES, SEL0, SEL1]):
            pe.op(lambda h: h.matmul(p_[:, i * 16:(i + 1) * 16], lhsT=M[:], rhs=LG[:, tt, :], start=True, stop=True), r=[cst, scal], w=[pb])
        act.op(lambda h: h.copy(out=G_tok[:, tt, 0:8], in_=p_[:, 0:8]), r=[pb], w=[scal])
        act.op(lambda h: h.copy(out=G_tok[:, tt, 8:16], in_=p_[:, 24:32]), r=[pb], w=[scal])
        act.op(lambda h: h.copy(out=TOTO[:, tt, :], in_=p_[:, 32:48]), r=[pb], w=[scal])
        act.op(lambda h: h.activation(out=ET0[:, tt, :], in_=p_[:, 48:64], func=AF.Exp), r=[pb], w=[scal])
        act.op(lambda h: h.activation(out=ET1[:, tt, :], in_=p_[:, 64:80], func=AF.Exp), r=[pb], w=[scal])
    dve.op(lambda h: h.tensor_tensor(out=A_tok[:], in0=G_tok[:], in1=LNB[:], op=ALU.add), r=[scal], w=[scal])
    act.op(lambda h: h.mul(out=NG_tok[:], in_=G_tok[:], mul=-1.0), r=[scal], w=[scal])
    act.op(lambda h: h.activation(out=BG[:], in_=A_tok[:], func=AF.Exp), r=[scal], w=[scal])
    dve.op(lambda h: h.tensor_tensor(out=KD[:], in0=TOTO[:], in1=G_tok[:], op=ALU.subtract), r=[scal], w=[scal])
    act.op(lambda h: h.activation(out=KD[:], in_=KD[:], func=AF.Exp), r=[scal], w=[scal])
    act.op(lambda h: h.copy(out=LG_R[:], in_=LG[:]), r=[scal], w=[scal])
    act.op(lambda h: h.copy(out=LNB_R[:], in_=LNB[:]), r=[scal], w=[scal])
    if "LG" in dbg:
        sp.dma(dbg["LG"].rearrange("(t p) c -> p t c", p=128), LG[:], sl_dbg, r=[scal])
        sp.dma(dbg["BETA"].rearrange("(t p) c -> p t c", p=128), BETA[:], sl_dbg, r=[scal])
        sp.dma(dbg["G_tok"].rearrange("(t p) c -> p t c", p=128), G_tok[:], sl_dbg, r=[scal])

    free_to(s1_mark)
    gdn_mark = len(allocs)
    if gdn_heads > 0:
        pre = sb("pre", [128, T + 4]); preb = Buf("pre")
        pool.op(lambda h: h.memset(pre[:, 0:2], 0.0), w=[preb]); pool.op(lambda h: h.memset(pre[:, T + 2:T + 4], 0.0), w=[preb])
        acc = sb("acc", [128, T]); accb = Buf("acc")
        sqt = sb("sqt", [128, 512], BF16); sqtb = Buf("sqt")
        rn = sb("rn", [128, 512]); rnb = Buf("rn")
        HB = []
        for bs in range(2):
            HB.append(dict(qT=sb(f"qT{bs}", [128, T], BF16), qTb=Buf(f"qT{bs}"), kT=sb(f"kT{bs}", [128, T], BF16), kTb=Buf(f"kT{bs}"),
                           k_tok=sb(f"k_tok{bs}", [128, NT, 128], BF16), ktb=Buf(f"k_tok{bs}"),
                           v_tok=sb(f"v_tok{bs}", [128, NT, 128], BF16), vtb=Buf(f"v_tok{bs}")))
        vT = sb("vT", [128, T], BF16); vTb = Buf("vT")
        zs = sb("zs", [128, T], BF16); zsb = Buf("zs")
        oTa = sb("oTa", [128, T]); oTabs = [[Buf(f"oTa{t_}_{c_}") for c_ in range(2)] for t_ in range(NT)]
        oTa_all = [b_ for l_ in oTabs for b_ in l_]
        NSL = 2
        WK = {}
        ST = {}
        for d_ in range(2):
            for sl_i in range(NSL):
                W_ = {}
                for nm, shp, dt in [("D1", [128, 128], F32), ("D1T", [128, 128], F32),
                                    ("D2T", [128, 128], BF16), ("grep", [128, 128], F32),
                                    ("LU", [128, 2, 256], BF16),
                                    ("LU2", [128, 2, 256], BF16),
                                    ("attnT", [128, 128], BF16), ("qdT", [128, 128], BF16), ("khat", [128, 128], BF16), ("bv", [128, 128], BF16),
                                    ("kd", [128, 128], BF16), ("wT", [128, 128], BF16), ("u", [128, 128], F32), ("vnew", [128, 128], BF16)]:
                    W_[nm] = sb(f"{nm}_{d_}_{sl_i}", shp, dt)
                    W_[nm + "_b"] = Buf(f"{nm}_{d_}_{sl_i}")
                WK[(d_, sl_i)] = W_
            S_ = {}
            for nm, shp, dt in [("S32", [128, 128], F32), ("S16", [128, 128], BF16)]:
                S_[nm] = sb(f"{nm}_{d_}", shp, dt)
                S_[nm + "_b"] = Buf(f"{nm}_{d_}")
            ST[d_] = S_

    def gdn_projA(hh, B_):
        for which, off in (("q", OFF_Q), ("k", OFF_K), ("v", OFF_V)):
            wt, wb = load_wblk(w_in_d[0, :, off + hh * 128: off + (hh + 1) * 128])
            for tg in range(4):
                ipj = yield from acquire("proj"); p_, pb = ps[ipj], psb[ipj]
                proj_fm(wt, wb, tg, p_, pb)
                act.op(lambda h: h.copy(out=pre[:, 2 + tg * 512: 2 + (tg + 1) * 512], in_=p_[:, :]), r=[pb], w=[preb])
                release(ipj)
                yield
            blk = off // 128 + hh
            act.op(lambda h: h.activation(out=acc[:], in_=pre[:, 0:T], func=AF.Copy, scale=cw[:, blk, 0:1]), r=[preb, prm], w=[accb])
            for t_ in range(1, 5):
                dve.op(lambda h: h.scalar_tensor_tensor(out=acc[:], in0=pre[:, t_:t_ + T], scalar=cw[:, blk, t_:t_ + 1], in1=acc[:], op0=ALU.mult, op1=ALU.add),
                       r=[preb, prm, accb], w=[accb])
                yield
            if which == "v":
                act.op(lambda h: h.activation(out=vT[:], in_=acc[:], func=AF.Silu), r=[accb], w=[vTb])
                continue
            act.op(lambda h: h.activation(out=acc[:], in_=acc[:], func=AF.Silu), r=[accb], w=[accb])
            dstT, dstb = (B_["qT"], B_["qTb"]) if which == "q" else (B_["kT"], B_["kTb"])
            post = (128.0 ** -0.5) if which == "q" else 1.0
            for tg in range(4):
                sl_ = slice(tg * 512, (tg + 1) * 512)
                act.op(lambda h: h.activation(out=sqt[:], in_=acc[:, sl_], func=AF.Square), r=[accb], w=[sqtb])
                ipj = yield from acquire("proj"); p_, pb = ps[ipj], psb[ipj]
                pe.op(lambda h: h.matmul(p_[:, :], lhsT=ones_bf[:], rhs=sqt[:], start=True, stop=True), r=[cst, sqtb], w=[pb])
                yield
                act.op(lambda h: h.activation(out=rn[:], in_=p_[:, :], func=AF.Ln, bias=EPS), r=[pb], w=[rnb])
                release(ipj)
                act.op(lambda h: h.activation(out=rn[:], in_=rn[:], func=AF.Exp, scale=-0.5), r=[rnb], w=[rnb])
                dve.op(lambda h: h.scalar_tensor_tensor(out=dstT[:, sl_], in0=acc[:, sl_], scalar=post, in1=rn[:], op0=ALU.mult, op1=ALU.mult),
                       r=[accb, rnb], w=[dstb])
                yield
        for srcT, srcb, dst, dstb in ((B_["kT"], B_["kTb"], B_["k_tok"], B_["ktb"]), (vT, vTb, B_["v_tok"], B_["vtb"])):
            for g8 in range(2):
                for j in range(8):
                    tt = g8 * 8 + j
                    pe.op(lambda h: h.transpose(pst[:, j * 128:(j + 1) * 128], srcT[:, tt * 128:(tt + 1) * 128], ident_bf[:]), r=[srcb, cst], w=[pstb])
                act.op(lambda h: h.copy(out=dst[:, g8 * 8:(g8 + 1) * 8, :], in_=pst[:].rearrange("p (j c) -> p j c", j=8)), r=[pstb], w=[dstb])
                yield

    def gdn_projZ(hh):
        wt, wb = load_wblk(w_in_d[0, :, OFF_Z + hh * 128: OFF_Z + (hh + 1) * 128])
        for tg in range(4):
            ipj = yield from acquire("proj"); p_, pb = ps[ipj], psb[ipj]
            proj_fm(wt, wb, tg, p_, pb)
            act.op(lambda h: h.activation(out=zs[:, tg * 512:(tg + 1) * 512], in_=p_[:, :], func=AF.Silu), r=[pb], w=[zsb])
            release(ipj)
            yield

    def gdn_norm(hh):
        if f"oa{hh}" in dbg:
            sp.dma(dbg[f"oa{hh}"], oTa[:], sl_dbg, r=oTa_all)
        for tg in range(4):
            sl_ = slice(tg * 512, (tg + 1) * 512)
            act.op(lambda h: h.activation(out=sqt[:], in_=oTa[:, sl_], func=AF.Square), r=[b_ for t_ in range(tg * 4, tg * 4 + 4) for b_ in oTabs[t_]], w=[sqtb])
            ipj = yield from acquire("proj")
            p_, pb = ps[ipj], psb[ipj]
            pe.op(lambda h: h.matmul(p_[:, :], lhsT=ones_bf[:], rhs=sqt[:], start=True, stop=True), r=[cst, sqtb], w=[pb])
            yield
            act.op(lambda h: h.activation(out=rn[:], in_=p_[:, :], func=AF.Ln, bias=EPS, scale=1.0 / 128), r=[pb], w=[rnb])
            release(ipj)
            act.op(lambda h: h.activation(out=rn[:], in_=rn[:], func=AF.Exp, scale=-0.5), r=[rnb], w=[rnb])
            dve.op(lambda h: h.scalar_tensor_tensor(out=rn[:], in0=oTa[:, sl_], scalar=gdn_nw[:, 0:1], in1=rn[:], op0=ALU.mult, op1=ALU.mult),
                   r=[b_ for t_ in range(tg * 4, tg * 4 + 4) for b_ in oTabs[t_]] + [rnb, prm], w=[rnb])
            dve.op(lambda h: h.tensor_tensor(out=oaT[:, hh, sl_], in0=rn[:], in1=zs[:, sl_], op=ALU.mult), r=[rnb, zsb], w=[oaTb])
            yield

    all_tasks = []
    PRA, PRZ, NORM = [], [], []
    for hh in range(gdn_heads):
        B_ = HB[hh % 2]
        pra = Task(gdn_projA(hh, B_), deps=[PRA[hh - 1] if hh >= 1 else None, NORM[hh - 2] if hh >= 2 else None, PRZ[hh - 1] if hh >= 1 else None])
        PRA.append(pra)
        all_tasks.append(pra)
        prz = Task(gdn_projZ(hh), deps=[pra, NORM[hh - 1] if hh >= 1 else None])
        PRZ.append(prz)
        def gdn_prep(d_, tt, W_, hh=hh, B_=B_):
            qT, qTb, kT, kTb, k_tok, ktb, v_tok, vtb = B_["qT"], B_["qTb"], B_["kT"], B_["kTb"], B_["k_tok"], B_["ktb"], B_["v_tok"], B_["vtb"]
            col = d_ * 8 + hh
            tsl = slice(tt * 128, (tt + 1) * 128)
            act.op(lambda h: h.activation(out=W_["khat"][:], in_=k_tok[:, tt, :], func=AF.Copy, scale=BG[:, tt, col:col + 1]),
                   r=[ktb, scal], w=[W_["khat_b"]])
            act.op(lambda h: h.activation(out=W_["bv"][:], in_=v_tok[:, tt, :], func=AF.Copy, scale=BETA[:, tt, col:col + 1]),
                   r=[vtb, scal], w=[W_["bv_b"]])
            pool.op(lambda h: h.tensor_scalar(out=W_["kd"][:], in0=k_tok[:, tt, :], scalar1=KD[:, tt, col:col + 1], scalar2=None, op0=ALU.mult),
                    r=[ktb, scal], w=[W_["kd_b"]])
            yield
            ia = yield from acquire("prep")
            pa, pab = ps[ia], psb[ia]
            pe.op(lambda h: h.matmul(pa[:, :], lhsT=LG_R[:, tt, col:col + 1].to_broadcast([128, 128]), rhs=TRI4_R[d_][:], start=True, stop=False), r=[cst, scal], w=[pab])
            pe.op(lambda h: h.matmul(pa[:, :], lhsT=ident_r[:], rhs=MASKS_R[d_][:], start=False, stop=False), r=[cst], w=[pab])
            pe.op(lambda h: h.matmul(pa[:, 128:256], lhsT=LNB_R[:, tt, col:col + 1].to_broadcast([128, 128]), rhs=ident_r[:], start=False, stop=True), r=[cst, scal], w=[pab])
            yield
            act.op(lambda h: h.activation(out=W_["D1"][:], in_=pa[:, 0:128], func=AF.Exp, bias=A_tok[:, tt, col:col + 1], scale=-1.0),
                   r=[pab, scal], w=[W_["D1_b"]])
            act.op(lambda h: h.activation(out=W_["D1T"][:], in_=pa[:, 128:256], func=AF.Exp, bias=NG_tok[:, tt, col:col + 1], scale=1.0),
                   r=[pab, scal], w=[W_["D1T_b"]])
            act.op(lambda h: h.activation(out=W_["D2T"][:], in_=pa[:, 256:384], func=AF.Exp, bias=NG_tok[:, tt, col:col + 1], scale=1.0),
                   r=[pab, scal], w=[W_["D2T_b"]])
            act.op(lambda h: h.activation(out=W_["grep"][:], in_=pa[:, 384:512], func=AF.Exp), r=[pab], w=[W_["grep_b"]])
            release(ia)
            yield
            ikq = yield from acquire("prep")
            pkq, pkqb = ps[ikq], psb[ikq]
            pe.op(lambda h: h.matmul(pkq[:, 0:128], lhsT=kT[:, tsl], rhs=kT[:, tsl], start=True, stop=True), r=[kTb], w=[pkqb])
            pe.op(lambda h: h.matmul(pkq[:, 128:256], lhsT=kT[:, tsl], rhs=qT[:, tsl], start=True, stop=True), r=[kTb, qTb], w=[pkqb])
            yield
            LU, LUb, LU2, LU2b = W_["LU"], W_["LU_b"], W_["LU2"], W_["LU2_b"]
            dve.op(lambda h: h.tensor_tensor(out=LU[:, 1, 0:128], in0=pkq[:, 0:128], in1=W_["D1"][:], op=ALU.mult), r=[pkqb, W_["D1_b"]], w=[LUb])
            dve.op(lambda h: h.tensor_tensor(out=LU[:, 0, 0:128], in0=pkq[:, 0:128], in1=W_["D1T"][:], op=ALU.mult), r=[pkqb, W_["D1T_b"]], w=[LUb])
            dve.op(lambda h: h.tensor_tensor(out=LU2[:, 0, 128:256], in0=ident32[:], in1=LU[:, 0, 0:128], op=ALU.subtract), r=[cst, LUb], w=[LU2b])
            dve.op(lambda h: h.tensor_tensor(out=W_["attnT"][:], in0=pkq[:, 128:256], in1=W_["D2T"][:], op=ALU.mult), r=[pkqb, W_["D2T_b"]], w=[W_["attnT_b"]])
            dve.op(lambda h: h.tensor_tensor(out=W_["qdT"][:], in0=qT[:, tsl], in1=W_["grep"][:], op=ALU.mult), r=[qTb, W_["grep_b"]], w=[W_["qdT_b"]])
            release(ikq)
            yield
            cur, curb, nxt, nxtb = LU, LUb, LU2, LU2b
            for lev in range(6):
                ipn = yield from acquire("prep")
                pn, pnb = ps[ipn], psb[ipn]
                if lev == 0:
                    pe.op(lambda h: h.matmul(pn[:, 0:128], lhsT=cur[:, 1, 0:128], rhs=cur[:, 0, 0:128], start=True, stop=True), r=[curb], w=[pnb])
                elif lev < 5:
                    pe.op(lambda h: h.matmul(pn[:, 0:256], lhsT=cur[:, 1, 0:128], rhs=cur[:, 0, 0:256], start=True, stop=True), r=[curb], w=[pnb])
                else:
                    pe.op(lambda h: h.matmul(pn[:, 128:256], lhsT=cur[:, 1, 0:128], rhs=cur[:, 0, 128:256], start=True, stop=True), r=[curb], w=[pnb])
                if lev < 5:
                    pe.op(lambda h: h.matmul(pn[:, 256:384], lhsT=cur[:, 0, 0:128], rhs=cur[:, 1, 0:128], start=True, stop=True), r=[curb], w=[pnb])
                yield
                if lev < 5:
                    ev_eng = act
                    if ev_eng is dve:
                        dve.op(lambda h: h.tensor_copy(out=nxt[:, :, 0:128], in_=pn[:, :].rearrange("p (a b) -> p a b", a=2)[:, :, 0:128]), r=[pnb], w=[nxtb])
                    else:
                        act.op(lambda h: h.copy(out=nxt[:, :, 0:128], in_=pn[:, :].rearrange("p (a b) -> p a b", a=2)[:, :, 0:128]), r=[pnb], w=[nxtb])
                if lev > 0:
                    dve.op(lambda h: h.tensor_tensor(out=nxt[:, 0, 128:256], in0=pn[:, 128:256], in1=cur[:, 0, 128:256], op=ALU.add), r=[pnb, curb], w=[nxtb])
                cur, curb, nxt, nxtb = nxt, nxtb, cur, curb
                release(ipn)
                yield
            Wm = cur[:, 0, 128:256]; Wmb = curb
            ipw = yield from acquire("prep")
            pw, pwb = ps[ipw], psb[ipw]
            pe.op(lambda h: h.matmul(pw[:, 0:128], lhsT=W_["khat"][:], rhs=Wm, start=True, stop=True), r=[W_["khat_b"], Wmb], w=[pwb])
            pe.op(lambda h: h.matmul(pw[:, 128:256], lhsT=Wm, rhs=W_["bv"][:], start=True, stop=True), r=[W_["bv_b"], Wmb], w=[pwb])
            yield
            act.op(lambda h: h.copy(out=W_["wT"][:], in_=pw[:, 0:128]), r=[pwb], w=[W_["wT_b"]])
            act.op(lambda h: h.copy(out=W_["u"][:], in_=pw[:, 128:256]), r=[pwb], w=[W_["u_b"]])
            release(ipw)
            yield

        def gdn_scan(d_, tt, W_, S_, hh=hh):
            col = d_ * 8 + hh
            for c in ((0, 1) if d_ == 0 else (1, 0)):
                rs = slice(c * 64, (c + 1) * 64)
                ETc = ET0 if c == 0 else ET1
                ip1 = yield from acquire("scan")
                p1, p1b = ps[ip1], psb[ip1]
                pe.op(lambda h: h.matmul(p1[rs, 0:128], lhsT=W_["wT"][:, rs], rhs=S_["S16"][:], start=True, stop=True), r=[W_["wT_b"], S_["S16_b"]], w=[p1b])
                yield
                dve.op(lambda h: h.tensor_tensor(out=W_["vnew"][rs, :], in0=W_["u"][rs, :], in1=p1[rs, 0:128], op=ALU.subtract),
                       r=[W_["u_b"], p1b], w=[W_["vnew_b"]])
                release(ip1)
                yield
                ip2 = yield from acquire("scan")
                p2, p2b = ps[ip2], psb[ip2]
                pe.op(lambda h: h.matmul(p2[:, 0:64], lhsT=S_["S16"][:], rhs=W_["qdT"][:, rs], start=True, stop=False), r=[S_["S16_b"], W_["qdT_b"]], w=[p2b])
                pe.op(lambda h: h.matmul(p2[:, 0:64], lhsT=W_["vnew"][rs, :], rhs=W_["attnT"][rs, rs], start=False, stop=True),
                      r=[W_["vnew_b"], W_["attnT_b"]], w=[p2b])
                pe.op(lambda h: h.matmul(p2[:, 128:256], lhsT=W_["kd"][rs, :], rhs=W_["vnew"][rs, :], start=True, stop=True),
                      r=[W_["kd_b"], W_["vnew_b"]], w=[p2b])
                yield
                dve.op(lambda h: h.scalar_tensor_tensor(out=S_["S32"][:], in0=S_["S32"][:], scalar=ETc[:, tt, col:col + 1], in1=p2[:, 128:256],
                                                        op0=ALU.mult, op1=ALU.add), r=[S_["S32_b"], scal, p2b], w=[S_["S32_b"]])
                act.op(lambda h: h.copy(out=S_["S16"][:], in_=S_["S32"][:]), r=[S_["S32_b"]], w=[S_["S16_b"]])
                osl = slice(tt * 128 + c * 64, tt * 128 + (c + 1) * 64)
                first = (tt < NT // 2) if d_ == 0 else (tt >= NT // 2)
                if first:
                    act.op(lambda h: h.copy(out=oTa[:, osl], in_=p2[:, 0:64]), r=[p2b], w=[oTabs[tt][c]])
                else:
                    dve.op(lambda h: h.tensor_tensor(out=oTa[:, osl], in0=p2[:, 0:64], in1=oTa[:, osl], op=ALU.add), r=[p2b, oTabs[tt][c]], w=[oTabs[tt][c]])
                release(ip2)
                yield

        def gdn_init():
            for d_ in range(2):
                pool.op(lambda h: h.memset(ST[d_]["S32"][:], 0.0), w=[ST[d_]["S32_b"]])
                pool.op(lambda h: h.memset(ST[d_]["S16"][:], 0.0), w=[ST[d_]["S16_b"]])
            yield
        init_t = Task(gdn_init(), deps=[NORM[hh - 1] if hh >= 1 else None])
        all_tasks.append(init_t)
        order = {0: list(range(NT)), 1: list(range(NT - 1, -1, -1))}
        P = {0: [], 1: []}
        S = {0: [], 1: []}
        tasks = all_tasks
        for i in range(NT):
            for d_ in range(2):
                tt = order[d_][i]
                W_ = WK[(d_, i % NSL)]
                pt = Task(gdn_prep(d_, tt, W_), deps=[S[d_][i - NSL] if i >= NSL else None, PRA[hh], init_t])
                P[d_].append(pt)
                tasks.append(pt)
            for d_ in range(2):
                tt = order[d_][i]
                W_ = WK[(d_, i % NSL)]
                other = S[1 - d_][NT - 1 - i] if (i >= NT // 2 and len(S[1 - d_]) > NT - 1 - i) else None
                stt = Task(gdn_scan(d_, tt, W_, ST[d_]), deps=[P[d_][i], S[d_][i - 1] if i >= 1 else None, other])
                S[d_].append(stt)
                tasks.append(stt)
        all_tasks.append(prz)
        nt = Task(gdn_norm(hh), deps=[prz] + S[0] + S[1])
        NORM.append(nt)
        all_tasks.append(nt)
    if gdn_heads > 0:
        for hh in range(gdn_heads - 1):
            NORM[hh].deps.append(PRA[hh + 1])
        run_tasks(all_tasks, max_active=MAXACT + 2)
    if "oaT" in dbg:
        tmp = sb("dbg_tmp3", [128, T]); tb = Buf("dbg_tmp3")
        for k in range(8):
            act.op(lambda h: h.copy(out=tmp[:], in_=oaT[:, k, :]), r=[oaTb], w=[tb])
            sp.dma(dbg["oaT"][k * 128:(k + 1) * 128, :], tmp[:], sl_dbg, r=[tb])
    free_to(mix_mark)

    obT = sb("obT", [128, 8, T], BF16); obTb = Buf("obT")
    gla_mark = len(allocs)
    if gla_heads > 0:
        NSG = 2
        GW = {}
        GS_ = {}
        for d_ in range(2):
            for sl_i in range(NSG):
                W_ = {}
                for nm, shp, dt in [("gk", [128, 128], F32), ("ekd", [128, 128], F32), ("eg", [128, 128], F32), ("eng", [128, 128], F32), ("etot", [128, 1], F32),
                                    ("kdg", [128, 128], BF16), ("qgT", [128, 128], BF16), ("kgT", [128, 128], BF16), ("attnT", [128, 128], BF16)]:
                    W_[nm] = sb(f"g{nm}_{d_}_{sl_i}", shp, dt)
                    W_[nm + "_b"] = Buf(f"g{nm}_{d_}_{sl_i}")
                GW[(d_, sl_i)] = W_
            S_ = {}
            for nm, shp, dt in [("S32", [128, 256], F32), ("S16", [128, 256], BF16)]:
                S_[nm] = sb(f"g{nm}_{d_}", shp, dt)
                S_[nm + "_b"] = Buf(f"g{nm}_{d_}")
            GS_[d_] = S_

        rT1 = sb("rT1", [33, T]); rT1b = Buf("rT1")
        wr = sb("wr", [128, 8, 32], BF16); wrb = Buf("wr")
        sl_wr = Slot(nc, "wr")
        pool.dma(wr[:], w_in_d[0, :, OFF_R:OFF_R + 32].rearrange("(k p) c -> p k c", p=128), sl_wr, w=[wrb])
        pool.op(lambda h: h.memset(rT1[32:33, :], 1.0), w=[rT1b])
        for tg in range(4):
            p_, pb = next_ps()
            for k in range(8):
                pe.op(lambda h: h.matmul(p_[0:32, :], lhsT=wr[:, k, :], rhs=hT[:, k, tg * 512:(tg + 1) * 512], start=(k == 0), stop=(k == 7)),
                      r=[wrb, hTb], w=[pb])
            act.op(lambda h: h.copy(out=rT1[0:32, tg * 512:(tg + 1) * 512], in_=p_[0:32, :]), r=[pb], w=[rT1b])
        wkv = sb("wkv", [128, 8, 384], BF16); wkvb = Buf("wkv"); sl_wkv = Slot(nc, "wkv")
        qTg = sb("qTg", [128, T], BF16); qTgb = Buf("qTg")
        kTg = sb("kTg", [128, T], BF16); kTgb = Buf("kTg")
        kg_tok = sb("kg_tok", [128, NT, 128], BF16); kgtb = Buf("kg_tok")
        vg_tok = sb("vg_tok", [128, NT, 256], BF16); vgtb = Buf("vg_tok")
        gsT = sb("gsT", [128, 2, T], BF16); gsTb = Buf("gsT")
        obTa = sb("obTa", [128, 2, T]); obTabs = [Buf(f"obTa{t_}") for t_ in range(NT)]
        sqg = sb("sqg", [128, 512], BF16); sqgb = Buf("sqg")
        rng = sb("rng", [128, 512]); rngb = Buf("rng")
    for hb in range(gla_heads):
        for off, dst, dstb, scl in ((OFF_QB, qTg, qTgb, 128.0 ** -0.5), (OFF_KB, kTg, kTgb, 1.0)):
            wt, wb = load_wblk(w_in_d[0, :, off + hb * 128: off + (hb + 1) * 128])
            for tg in range(4):
                p_, pb = next_ps()
                proj_fm(wt, wb, tg, p_, pb)
                act.op(lambda h: h.mul(out=dst[:, tg * 512:(tg + 1) * 512], in_=p_[:, :], mul=scl), r=[pb], w=[dstb])
        for eb in range(2):
            wt, wb = load_wblk(w_in_d[0, :, OFF_GB + hb * 256 + eb * 128: OFF_GB + hb * 256 + (eb + 1) * 128])
            for tg in range(4):
                p_, pb = next_ps()
                proj_fm(wt, wb, tg, p_, pb)
                act.op(lambda h: h.activation(out=gsT[:, eb, tg * 512:(tg + 1) * 512], in_=p_[:, :], func=AF.Silu), r=[pb], w=[gsTb])
        pool.dma(wkv[:, :, 0:128], w_in_d[0, :, OFF_KB + hb * 128: OFF_KB + (hb + 1) * 128].rearrange("(k p) c -> p k c", p=128), sl_wkv, w=[wkvb])
        pool.dma(wkv[:, :, 128:384], w_in_d[0, :, OFF_VB + hb * 256: OFF_VB + (hb + 1) * 256].rearrange("(k p) c -> p k c", p=128), sl_wkv, w=[wkvb])
        for tt in range(NT):
            p_, pb = next_ps()
            for k in range(8):
                pe.op(lambda h: h.matmul(p_[:, 0:384], lhsT=hT[:, k, tt * 128:(tt + 1) * 128], rhs=wkv[:, k, :], start=(k == 0), stop=(k == 7)),
                      r=[hTb, wkvb], w=[pb])
            act.op(lambda h: h.copy(out=kg_tok[:, tt, :], in_=p_[:, 0:128]), r=[pb], w=[kgtb])
            act.op(lambda h: h.copy(out=vg_tok[:, tt, :], in_=p_[:, 128:384]), r=[pb], w=[vgtb])

        def gla_prep(d_, tt, W_, hb=hb):
            tsl = slice(tt * 128, (tt + 1) * 128)
            ix = yield from acquire("gprep")
            px, pxb = ps[ix], psb[ix]
            pe.op(lambda h: h.matmul(px[:, 0:128], lhsT=rT1[:, tsl], rhs=w2cat[:, d_, hb * 128:(hb + 1) * 128], start=True, stop=True),
                  r=[rT1b, prm], w=[pxb])
            yield
            act.op(lambda h: h.activation(out=W_["gk"][:], in_=px[:, 0:128], func=AF.Exp, scale=-1.0), r=[pxb], w=[W_["gk_b"]])
            release(ix)
            act.op(lambda h: h.activation(out=W_["gk"][:], in_=W_["gk"][:], func=AF.Ln, bias=1.0), r=[W_["gk_b"]], w=[W_["gk_b"]])
            yield
            ig = yield from acquire("gprep")
            pg, pgb = ps[ig], psb[ig]
            pe.op(lambda h: h.matmul(pg[:, 0:128], lhsT=DIFG[d_][:], rhs=W_["gk"][:], start=True, stop=True), r=[cst, W_["gk_b"]], w=[pgb])
            pe.op(lambda h: h.matmul(pg[:, 128:256], lhsT=W_["gk"][:], rhs=TRIG[d_][:], start=True, stop=True), r=[cst, W_["gk_b"]], w=[pgb])
            yield
            act.op(lambda h: h.activation(out=W_["eg"][:], in_=pg[:, 128:256], func=AF.Exp), r=[pgb], w=[W_["eg_b"]])
            act.op(lambda h: h.activation(out=W_["eng"][:], in_=pg[:, 128:256], func=AF.Exp, scale=-1.0), r=[pgb], w=[W_["eng_b"]])
            act.op(lambda h: h.activation(out=W_["ekd"][:], in_=pg[:, 0:128], func=AF.Exp), r=[pgb], w=[W_["ekd_b"]])
            lastc = 128 + (127 if d_ == 0 else 0)
            act.op(lambda h: h.activation(out=W_["etot"][:], in_=pg[:, lastc:lastc + 1], func=AF.Exp), r=[pgb], w=[W_["etot_b"]])
            release(ig)
            yield
            dve.op(lambda h: h.tensor_tensor(out=W_["qgT"][:], in0=qTg[:, tsl], in1=W_["eg"][:], op=ALU.mult), r=[qTgb, W_["eg_b"]], w=[W_["qgT_b"]])
            pool.op(lambda h: h.tensor_tensor(out=W_["kgT"][:], in0=kTg[:, tsl], in1=W_["eng"][:], op=ALU.mult), r=[kTgb, W_["eng_b"]], w=[W_["kgT_b"]])
            dve.op(lambda h: h.tensor_tensor(out=W_["kdg"][:], in0=kg_tok[:, tt, :], in1=W_["ekd"][:], op=ALU.mult), r=[kgtb, W_["ekd_b"]], w=[W_["kdg_b"]])
            yield
            ia = yield from acquire("gprep")
            pa_, pab_ = ps[ia], psb[ia]
            pe.op(lambda h: h.matmul(pa_[:, 0:128], lhsT=W_["kgT"][:], rhs=W_["qgT"][:], start=True, stop=True), r=[W_["kgT_b"], W_["qgT_b"]], w=[pab_])
            yield
            dve.op(lambda h: h.tensor_tensor(out=W_["attnT"][:], in0=pa_[:, 0:128], in1=MASKG[d_][:], op=ALU.mult), r=[pab_, cst], w=[W_["attnT_b"]])
            release(ia)
            yield

        def gla_scan(d_, tt, W_, S_):
            tsl = slice(tt * 128, (tt + 1) * 128)
            io = yield from acquire("gscan")
            po, pob = ps[io], psb[io]
            for eb in range(2):
                pe.op(lambda h: h.matmul(po[:, eb * 128:(eb + 1) * 128], lhsT=S_["S16"][:, eb * 128:(eb + 1) * 128], rhs=W_["qgT"][:], start=True, stop=False),
                      r=[S_["S16_b"], W_["qgT_b"]], w=[pob])
                pe.op(lambda h: h.matmul(po[:, eb * 128:(eb + 1) * 128], lhsT=vg_tok[:, tt, eb * 128:(eb + 1) * 128], rhs=W_["attnT"][:], start=False, stop=True),
                      r=[vgtb, W_["attnT_b"]], w=[pob])
            iS = yield from acquire("gscan")
            pS, pSb = ps[iS], psb[iS]
            pe.op(lambda h: h.matmul(pS[:, 0:256], lhsT=W_["kdg"][:], rhs=vg_tok[:, tt, :], start=True, stop=True), r=[W_["kdg_b"], vgtb], w=[pSb])
            yield
            dve.op(lambda h: h.scalar_tensor_tensor(out=S_["S32"][:], in0=S_["S32"][:], scalar=W_["etot"][:, 0:1], in1=pS[:, 0:256],
                                                    op0=ALU.mult, op1=ALU.add), r=[S_["S32_b"], W_["etot_b"], pSb], w=[S_["S32_b"]])
            release(iS)
            act.op(lambda h: h.copy(out=S_["S16"][:], in_=S_["S32"][:]), r=[S_["S32_b"]], w=[S_["S16_b"]])
            first = (tt < NT // 2) if d_ == 0 else (tt >= NT // 2)
            if first:
                act.op(lambda h: h.copy(out=obTa[:, :, tsl], in_=po[:, 0:256].rearrange("p (e t) -> p e t", e=2)), r=[pob], w=[obTabs[tt]])
            else:
                dve.op(lambda h: h.tensor_tensor(out=obTa[:, :, tsl], in0=po[:, 0:256].rearrange("p (e t) -> p e t", e=2), in1=obTa[:, :, tsl], op=ALU.add),
                       r=[pob, obTabs[tt]], w=[obTabs[tt]])
            release(io)
            yield

        for d_ in range(2):
            pool.op(lambda h: h.memset(GS_[d_]["S32"][:], 0.0), w=[GS_[d_]["S32_b"]])
            pool.op(lambda h: h.memset(GS_[d_]["S16"][:], 0.0), w=[GS_[d_]["S16_b"]])
        order = {0: list(range(NT)), 1: list(range(NT - 1, -1, -1))}
        P = {0: [], 1: []}
        S = {0: [], 1: []}
        tasks = []
        for i in range(NT):
            for d_ in range(2):
                pt = Task(gla_prep(d_, order[d_][i], GW[(d_, i % NSG)]), deps=[S[d_][i - NSG] if i >= NSG else None])
                P[d_].append(pt)
                tasks.append(pt)
            for d_ in range(2):
                other = S[1 - d_][NT - 1 - i] if (i >= NT // 2 and len(S[1 - d_]) > NT - 1 - i) else None
                stt = Task(gla_scan(d_, order[d_][i], GW[(d_, i % NSG)], GS_[d_]), deps=[P[d_][i], S[d_][i - 1] if i >= 1 else None, other])
                S[d_].append(stt)
                tasks.append(stt)
        run_tasks(tasks, max_active=MAXACT)
        if f"ob{hb}" in dbg:
            for eb in range(2):
                sp.dma(dbg[f"ob{hb}"][eb * 128:(eb + 1) * 128, :], obTa[:, eb, :], sl_dbg, r=obTabs)
        for tg in range(4):
            sl_ = slice(tg * 512, (tg + 1) * 512)
            p_, pb = next_ps()
            for eb in range(2):
                act.op(lambda h: h.activation(out=sqg[:], in_=obTa[:, eb, sl_], func=AF.Square), r=obTabs[tg * 4:tg * 4 + 4], w=[sqgb])
                pe.op(lambda h: h.matmul(p_[:, :], lhsT=ones_bf[:], rhs=sqg[:], start=(eb == 0), stop=(eb == 1)), r=[cst, sqgb], w=[pb])
            act.op(lambda h: h.activation(out=rng[:], in_=p_[:, :], func=AF.Ln, bias=EPS, scale=1.0 / 256), r=[pb], w=[rngb])
            act.op(lambda h: h.activation(out=rng[:], in_=rng[:], func=AF.Exp, scale=-0.5), r=[rngb], w=[rngb])
            for eb in range(2):
                dve.op(lambda h: h.scalar_tensor_tensor(out=obTa[:, eb, sl_], in0=obTa[:, eb, sl_], scalar=gla_nw[:, eb:eb + 1], in1=rng[:], op0=ALU.mult, op1=ALU.mult),
                       r=obTabs[tg * 4:tg * 4 + 4] + [rngb, prm], w=obTabs[tg * 4:tg * 4 + 4])
                dve.op(lambda h: h.tensor_tensor(out=obT[:, hb * 2 + eb, sl_], in0=obTa[:, eb, sl_], in1=gsT[:, eb, sl_], op=ALU.mult), r=obTabs[tg * 4:tg * 4 + 4] + [gsTb], w=[obTb])
    if "obT" in dbg:
        tmp = sb("dbg_tmp4", [128, T]); tb = Buf("dbg_tmp4")
        for k in range(8):
            act.op(lambda h: h.copy(out=tmp[:], in_=obT[:, k, :]), r=[obTb], w=[tb])
            sp.dma(dbg["obT"][k * 128:(k + 1) * 128, :], tmp[:], sl_dbg, r=[tb])
    free_to(gla_mark)

    if do_final:
        mT = sb("mT", [128, 8, T], BF16); mTb = Buf("mT")
        fin_mark = len(allocs)
        sga = sb("sga", [128, 512]); sgab = Buf("sga")
        sgb = sb("sgb", [128, 512]); sgbb = Buf("sgb")
        t1 = sb("t1", [128, 512]); t1b = Buf("t1")
        t2 = sb("t2", [128, 512]); t2b = Buf("t2")
        NW2 = 8
        wb2 = [sb(f"wb2_{i}", [128, 8, 128], BF16) for i in range(NW2)]; wb2b = [Buf(f"wb2_{i}") for i in range(NW2)]
        sl_w2 = [Slot(nc, f"w2_{i}") for i in range(NW2)]
        rr2 = [0]

        def load2(src_ap):
            i = rr2[0] % NW2
            rr2[0] += 1
            pool.dma(wb2[i][:], src_ap.rearrange("(k p) c -> p k c", p=128), sl_w2[i], w=[wb2b[i]])
            return wb2[i], wb2b[i]

        for m in range(8):
            msl = slice(m * 128, (m + 1) * 128)
            wg, wgb_ = load2(wpg_d[0, :, msl])
            wl, wlb_ = load2(wpl_d[0, :, msl])
            wa, wab_ = load2(w_in_d[0, :, OFF_GA + m * 128: OFF_GA + (m + 1) * 128])
            wbb, wbbb_ = load2(w_in_d[0, :, OFF_GBG + m * 128: OFF_GBG + (m + 1) * 128])
            for tg in range(4):
                sl_ = slice(tg * 512, (tg + 1) * 512)
                pga, pgab = next_ps(); proj_fm(wa, wab_, tg, pga, pgab)
                act.op(lambda h: h.activation(out=sga[:], in_=pga[:, :], func=AF.Sigmoid), r=[pgab], w=[sgab])
                pgb_, pgbb = next_ps(); proj_fm(wbb, wbbb_, tg, pgb_, pgbb)
                act.op(lambda h: h.activation(out=sgb[:], in_=pgb_[:, :], func=AF.Sigmoid), r=[pgbb], w=[sgbb])
                pya, pyab = next_ps(); proj_fm(wg, wgb_, tg, pya, pyab, rhsT=oaT, rb=oaTb)
                dve.op(lambda h: h.tensor_tensor(out=t1[:], in0=pya[:, :], in1=sga[:], op=ALU.mult), r=[pyab, sgab], w=[t1b])
                pyb, pybb = next_ps(); proj_fm(wl, wlb_, tg, pyb, pybb, rhsT=obT, rb=obTb)
                dve.op(lambda h: h.tensor_tensor(out=t2[:], in0=pyb[:, :], in1=sgb[:], op=ALU.mult), r=[pybb, sgbb], w=[t2b])
                dve.op(lambda h: h.tensor_tensor(out=mT[:, m, sl_], in0=t1[:], in1=t2[:], op=ALU.add), r=[t1b, t2b], w=[mTb])
        if "mT" in dbg:
            tmp = sb("dbg_tmp5", [128, T]); tb = Buf("dbg_tmp5")
            for k in range(8):
                act.op(lambda h: h.copy(out=tmp[:], in_=mT[:, k, :]), r=[mTb], w=[tb])
                sp.dma(dbg["mT"][k * 128:(k + 1) * 128, :], tmp[:], sl_dbg, r=[tb])
        free_to(fin_mark)
        wob = hTb; sl_wo = Slot(nc, "wo")
        for k in range(8):
            pool.dma(hT[:, k, 0:D], wout_d[0, k * 128:(k + 1) * 128, :], sl_wo, w=[wob])
        lnp = sb("lnp", [128, D]); sl_lnp = Slot(nc, "lnp"); lnpb = Buf("lnp")
        sp.dma(lnp[:], ln_post_d[0, :].partition_broadcast(128), sl_lnp, w=[lnpb])
        xr = [sb(f"xr{i}", [128, D]) for i in range(2)]; xrb = [Buf(f"xr{i}") for i in range(2)]
        sl_xr = [Slot(nc, f"xr{i}") for i in range(2)]
        ot = [sb(f"ot{i}", [128, D]) for i in range(2)]; otb = [Buf(f"ot{i}") for i in range(2)]
        st1 = sb("st1", [128, 8]); st1b = Buf("st1")
        junk2 = sb("junk2", [128, 512], BF16); junk2b = Buf("junk2")
        for tt in range(NT):
            s = tt % 2
            tsl = slice(tt * 128, (tt + 1) * 128)
            sp.dma(xr[s][:], x_d[tsl, :], sl_xr[s], w=[xrb[s]])
            pp = []
            for half in range(2):
                p_, pb = next_ps()
                for m in range(8):
                    pe.op(lambda h: h.matmul(p_[:, :], lhsT=mT[:, m, tsl], rhs=hT[:, m, half * 512:(half + 1) * 512], start=(m == 0), stop=(m == 7)),
                          r=[mTb, wob], w=[pb])
                act.op(lambda h: h.activation(out=junk2[:], in_=p_[:, :], func=AF.Square, accum_out=st1[:, half:half + 1]), r=[pb], w=[junk2b, st1b])
                pp.append((p_, pb))
            dve.op(lambda h: h.tensor_tensor(out=st1[:, 2:3], in0=st1[:, 0:1], in1=st1[:, 1:2], op=ALU.add), r=[st1b], w=[st1b])
            act.op(lambda h: h.activation(out=st1[:, 3:4], in_=st1[:, 2:3], func=AF.Ln, bias=EPS, scale=1.0 / D), r=[st1b], w=[st1b])
            act.op(lambda h: h.activation(out=st1[:, 4:5], in_=st1[:, 3:4], func=AF.Exp, scale=-0.5), r=[st1b], w=[st1b])
            for half in range(2):
                p_, pb = pp[half]
                hs = slice(half * 512, (half + 1) * 512)
                dve.op(lambda h: h.scalar_tensor_tensor(out=ot[s][:, hs], in0=p_[:, :], scalar=st1[:, 4:5], in1=lnp[:, hs], op0=ALU.mult, op1=ALU.mult),
                       r=[pb, st1b, lnpb], w=[otb[s]])
                dve.op(lambda h: h.tensor_tensor(out=ot[s][:, hs], in0=ot[s][:, hs], in1=xr[s][:, hs], op=ALU.add), r=[otb[s], xrb[s]], w=[otb[s]])
            sp.dma(out_d[tsl, :], ot[s][:], sl_out2[s], r=[otb[s]])
    else:
        zt = sb("zt", [128, D]); ztb = Buf("zt")
        pool.op(lambda h: h.memset(zt[:], 0.0), w=[ztb])
        for tt in range(NT):
            sp.dma(out_d[tt * 128:(tt + 1) * 128, :], zt[:], sl_out2[tt % 2], r=[ztb])
    for so_ in sl_out2:
        if so_.cnt:
            sp.h.wait_ge(so_.sem, so_.cnt)
    if sl_dbg.cnt:
        sp.h.wait_ge(sl_dbg.sem, sl_dbg.cnt)
    return nc


_INPUT_NAMES = ["ln_pre_w", "w_in", "conv_w", "a_log_fwd", "a_log_bwd", "dt_bias_fwd", "dt_bias_bwd", "gdn_norm_w", "w_proj_gdn",
                "gk_w2_fwd", "gk_b2_fwd", "gk_w2_bwd", "gk_b2_bwd", "gla_norm_w", "w_proj_gla", "w_out", "ln_post_w"]


def kernel(**inputs):
    x = np.ascontiguousarray(np.asarray(inputs["x"], dtype=np.float32))
    shared = {n: np.ascontiguousarray(np.asarray(inputs[n], dtype=np.float32)) for n in _INPUT_NAMES}
    nc = build_nc()
    in_maps = [dict(shared, x=x[b]) for b in range(8)]
    res = run_bass_kernel_spmd(nc, in_maps, core_ids=list(range(8)))
    return np.stack([np.asarray(r["out"], dtype=np.float32) for r in res.results], axis=0)
```

```python
import numpy as np
import concourse.bass as bass
import concourse.mybir as mybir
from concourse.bass_utils import run_bass_kernel_spmd

F32 = mybir.dt.float32
F32R = mybir.dt.float32r
BF16 = mybir.dt.bfloat16
AF = mybir.ActivationFunctionType
ALU = mybir.AluOpType

T = 2048
NT = 16
D = 1024
NIN = 9280
EPS = 1e-6
BIG = 30000.0
MAXACT = 6
OFF_Q, OFF_K, OFF_V, OFF_Z = 0, 1024, 2048, 3072
OFF_SM = 4096
OFF_QB, OFF_KB, OFF_VB, OFF_GB = 4128, 4640, 5152, 6176
OFF_R = 7200
OFF_GA, OFF_GBG = 7232, 8256


class Ev:
    __slots__ = ("sem", "val", "key")

    def __init__(self, sem, val, key):
        self.sem, self.val, self.key = sem, val, key


class Buf:
    __slots__ = ("name", "wev", "revs")

    def __init__(self, name):
        self.name, self.wev, self.revs = name, None, {}


class Slot:
    registry = None

    def __init__(self, nc, name):
        if Slot.registry is not None:
            Slot.registry.append(self)
        self.name = name
        self.sem = nc.semaphore("ds_" + name).__enter__()
        self.cnt = 0


class Eng:
    def __init__(self, nc, name, h, selfsync):
        self.name, self.h, self.selfsync = name, h, selfsync
        self.sem = nc.semaphore("es_" + name).__enter__()
        self.cnt = 0
        self.seen = {}

    def wait(self, ev):
        if ev is None:
            return
        if ev.sem is self.sem and not self.selfsync:
            return
        if self.seen.get(ev.key, 0) >= ev.val:
            return
        self.h.wait_ge(ev.sem, ev.val)
        self.seen[ev.key] = ev.val

    def _deps(self, r, w):
        for b in r:
            self.wait(b.wev)
        for b in w:
            self.wait(b.wev)
            for ev in b.revs.values():
                self.wait(ev)

    def _mark(self, ev, r, w):
        for b in r:
            b.revs[ev.key] = ev
        for b in w:
            b.wev = ev
            b.revs = {}

    def op(self, fn, r=(), w=()):
        self._deps(r, w)
        ins = fn(self.h)
        self.cnt += 1
        ins.then_inc(self.sem, 1)
        ev = Ev(self.sem, self.cnt, self.name)
        self._mark(ev, r, w)
        return ev

    def dma(self, out, in_, slot, r=(), w=(), **kw):
        self._deps(r, w)
        ins = self.h.dma_start(out=out, in_=in_, **kw)
        slot.cnt += 16
        ins.then_inc(slot.sem, 16)
        ev = Ev(slot.sem, slot.cnt, slot.name)
        self._mark(ev, r, w)
        return ev


def build_nc(debug=(), gdn_heads=8, gla_heads=4, do_final=True):
    nc = bass.Bass("TRN2", target_bir_lowering=False)
    dram_in = lambda n, s: nc.dram_tensor(n, s, F32, kind="ExternalInput").ap()
    x_d = dram_in("x", [T, D])
    ln_pre_d = dram_in("ln_pre_w", [1, D])
    w_in_d = dram_in("w_in", [1, D, NIN])
    conv_d = dram_in("conv_w", [1, 5, 3072])
    alog_f_d = dram_in("a_log_fwd", [1, 8]); alog_b_d = dram_in("a_log_bwd", [1, 8])
    dtb_f_d = dram_in("dt_bias_fwd", [1, 8]); dtb_b_d = dram_in("dt_bias_bwd", [1, 8])
    gdn_nw_d = dram_in("gdn_norm_w", [1, 128])
    wpg_d = dram_in("w_proj_gdn", [1, D, D])
    gkw_f_d = dram_in("gk_w2_fwd", [1, 16, 512]); gkb_f_d = dram_in("gk_b2_fwd", [1, 512])
    gkw_b_d = dram_in("gk_w2_bwd", [1, 16, 512]); gkb_b_d = dram_in("gk_b2_bwd", [1, 512])
    gla_nw_d = dram_in("gla_norm_w", [1, 256])
    wpl_d = dram_in("w_proj_gla", [1, D, D])
    wout_d = dram_in("w_out", [1, D, D])
    ln_post_d = dram_in("ln_post_w", [1, D])
    out_d = nc.dram_tensor("out", [T, D], F32, kind="ExternalOutput").ap()
    dbg = {}
    for name, shape in debug:
        dbg[name] = nc.dram_tensor("dbg_" + name, shape, F32, kind="ExternalOutput").ap()

    pe = Eng(nc, "pe", nc.tensor, False)
    act = Eng(nc, "act", nc.scalar, True)
    dve = Eng(nc, "dve", nc.vector, True)
    pool = Eng(nc, "pool", nc.gpsimd, True)
    sp = Eng(nc, "sp", nc.sync, False)

    allocs = []

    def sb(name, shape, dt=F32):
        cm = nc.sbuf_tensor(name, shape, dt)
        t = cm.__enter__()
        allocs.append(cm)
        return t

    all_slots = []
    Slot.registry = all_slots

    def barrier():
        engs = (pe, act, dve, pool, sp)
        for e in engs:
            for f in engs:
                if f is not e and f.cnt:
                    e.wait(Ev(f.sem, f.cnt, f.name))
            for sl in all_slots:
                if sl.cnt:
                    e.wait(Ev(sl.sem, sl.cnt, sl.name))

    def free_to(mark):
        barrier()
        while len(allocs) > mark:
            allocs.pop().__exit__(None, None, None)

    NPS = 7
    ps = [nc.psum_tensor(f"ps{i}", [128, 512], F32).__enter__() for i in range(NPS)]
    psb = [Buf(f"ps{i}") for i in range(NPS)]
    pst = nc.psum_tensor("pst", [128, 1024], BF16).__enter__()
    pstb = Buf("pst")
    ps_rr = {}
    PS_POOLS = {"any": list(range(NPS)), "prep": [0, 1, 2], "scan": [3, 4], "proj": [5, 6], "gprep": [0, 1, 2], "gscan": [3, 4, 5, 6]}

    def next_ps(pool_="any"):
        lst = PS_POOLS[pool_]
        c = ps_rr.get(pool_, 0)
        ps_rr[pool_] = c + 1
        i = lst[c % len(lst)]
        return ps[i], psb[i]

    ps_busy = [False] * NPS

    def acquire(pool_):
        while True:
            for i in PS_POOLS[pool_]:
                if not ps_busy[i]:
                    ps_busy[i] = True
                    return i
            yield

    def release(i):
        ps_busy[i] = False

    class Task:
        def __init__(self, gen, deps=()):
            self.gen, self.deps, self.done = gen, [d for d in deps if d is not None], False

    stall = [0]

    def run_tasks(tasks, max_active=6):
        pending = list(tasks)
        active = []
        while pending or active:
            for t in list(pending):
                if len(active) >= max_active:
                    break
                if all(d.done for d in t.deps):
                    pending.remove(t)
                    active.append(t)
            assert active, "task deadlock"
            before = (pe.cnt, act.cnt, dve.cnt, pool.cnt, len(active), len(pending))
            for t in list(active):
                try:
                    next(t.gen)
                except StopIteration:
                    t.done = True
                    active.remove(t)
            if before == (pe.cnt, act.cnt, dve.cnt, pool.cnt, len(active), len(pending)):
                stall[0] += 1
                assert stall[0] < 200, ("scheduler stall", [getattr(t.gen, "__name__", "?") for t in active], list(ps_busy), len(pending))
            else:
                stall[0] = 0

    cst = Buf("const")
    ident32 = sb("ident32", [128, 128]); ident_bf = sb("ident_bf", [128, 128], BF16)
    ones32 = sb("ones32", [128, 128]); ones_bf = sb("ones_bf", [128, 128], BF16)
    zeros32 = sb("zeros32", [128, 128])
    pool.op(lambda h: h.memset(ones32[:], 1.0), w=[cst])
    pool.op(lambda h: h.memset(zeros32[:], 0.0), w=[cst])
    pool.op(lambda h: h.affine_select(out=ident32[:], in_=zeros32[:], pattern=[[-1, 128]], compare_op=ALU.not_equal,
                                      fill=1.0, base=0, channel_multiplier=1), r=[cst], w=[cst])
    pool.op(lambda h: h.tensor_copy(out=ident_bf[:], in_=ident32[:]), r=[cst], w=[cst])
    pool.op(lambda h: h.tensor_copy(out=ones_bf[:], in_=ones32[:]), r=[cst], w=[cst])

    src_cache = {}

    def tri_const(name, n, inval, fillval, offval, step, cm, cmp):
        t = sb(name, [128, 128])
        pool.op(lambda h: h.memset(t[:], offval), w=[cst])
        if inval not in src_cache:
            src_cache[inval] = sb(f"src_{len(src_cache)}", [128, 128])
            pool.op(lambda h: h.memset(src_cache[inval][:], inval), w=[cst])
        src = src_cache[inval]
        for b0 in range(0, 128, n):
            pool.op(lambda h: h.affine_select(out=t[b0:b0 + n, b0:b0 + n], in_=src[b0:b0 + n, b0:b0 + n], pattern=[[step, n]],
                                              compare_op=cmp, fill=fillval, base=0, channel_multiplier=cm), r=[cst], w=[cst])
        return t

    TRI = {
        0: tri_const("tri_f", 64, 1.0, 0.0, 0.0, 1, -1, ALU.is_ge),
        1: tri_const("tri_b", 64, 1.0, 0.0, 0.0, -1, 1, ALU.is_ge),
    }
    NEG_A = {
        0: tri_const("nega_f", 64, 0.0, BIG, BIG, -1, 1, ALU.is_gt),
        1: tri_const("nega_b", 64, 0.0, BIG, BIG, 1, -1, ALU.is_gt),
    }
    NEG_B = {
        0: tri_const("negb_f", 64, 0.0, -BIG, -BIG, 1, -1, ALU.is_gt),
        1: tri_const("negb_b", 64, 0.0, -BIG, -BIG, -1, 1, ALU.is_gt),
    }
    NEG_C = {
        0: tri_const("negc_f", 64, 0.0, -BIG, -BIG, 1, -1, ALU.is_ge),
        1: tri_const("negc_b", 64, 0.0, -BIG, -BIG, -1, 1, ALU.is_ge),
    }
    BDONES = tri_const("bdones", 64, 1.0, 1.0, 0.0, 1, 1, ALU.is_ge)
    ones_r = sb("ones_r", [128, 128], F32R); ident_r = sb("ident_r", [128, 128], F32R)
    pool.op(lambda h: h.tensor_copy(out=ones_r[:], in_=ones32[:]), r=[cst], w=[cst])
    pool.op(lambda h: h.tensor_copy(out=ident_r[:], in_=ident32[:]), r=[cst], w=[cst])
    TRI4_R = {}
    for d__ in range(2):
        t4 = sb(f"tri4_r{d__}", [128, 512], F32R)
        for rep in range(4):
            pool.op(lambda h: h.tensor_copy(out=t4[:, rep * 128:(rep + 1) * 128], in_=TRI[d__][:]), r=[cst], w=[cst])
        TRI4_R[d__] = t4
    MASKS_R = {}
    for d__ in range(2):
        mk = sb(f"masks_r{d__}", [128, 512], F32R)
        pool.op(lambda h: h.tensor_copy(out=mk[:, 0:128], in_=NEG_A[d__][:]), r=[cst], w=[cst])
        pool.op(lambda h: h.tensor_copy(out=mk[:, 128:256], in_=NEG_B[d__][:]), r=[cst], w=[cst])
        pool.op(lambda h: h.tensor_copy(out=mk[:, 256:384], in_=NEG_C[d__][:]), r=[cst], w=[cst])
        pool.op(lambda h: h.tensor_copy(out=mk[:, 384:512], in_=zeros32[:]), r=[cst], w=[cst])
        MASKS_R[d__] = mk
    SEL0 = sb("sel0", [128, 128]); SEL1 = sb("sel1", [128, 128])
    pool.op(lambda h: h.memset(SEL0[:], 0.0), w=[cst]); pool.op(lambda h: h.memset(SEL1[:], 0.0), w=[cst])
    pool.op(lambda h: h.memset(SEL0[0:64, :], 1.0), w=[cst]); pool.op(lambda h: h.memset(SEL1[64:128, :], 1.0), w=[cst])
    GS = -1.0 / 16.0
    TRIG = {0: tri_const("trig_f", 128, GS, 0.0, 0.0, 1, -1, ALU.is_ge),
            1: tri_const("trig_b", 128, GS, 0.0, 0.0, -1, 1, ALU.is_ge)}
    DIFG = {0: tri_const("difg_f", 128, 0.0, GS, 0.0, 1, -1, ALU.is_ge),
            1: tri_const("difg_b", 128, 0.0, GS, 0.0, -1, 1, ALU.is_ge)}
    MASKG = {0: tri_const("maskg_f", 128, 1.0, 0.0, 0.0, 1, -1, ALU.is_ge),
             1: tri_const("maskg_b", 128, 1.0, 0.0, 0.0, -1, 1, ALU.is_ge)}

    prm = Buf("params")
    sl_prm = Slot(nc, "prm")
    lnw_T = sb("lnw_T", [128, 8])
    sp.dma(lnw_T[:], ln_pre_d[0, :].rearrange("(k p) -> p k", p=128), sl_prm, w=[prm], allow_slow_non_contiguous=True)
    cw = sb("cw", [128, 24, 5])
    for t_ in range(5):
        sp.dma(cw[:, :, t_], conv_d[0, t_, :].rearrange("(b p) -> p b", p=128), sl_prm, w=[prm], allow_slow_non_contiguous=True)
    prm16 = sb("prm16", [128, 2, 16])
    sp.dma(prm16[:, 0, 0:8], alog_f_d[0, :].partition_broadcast(128), sl_prm, w=[prm])
    sp.dma(prm16[:, 0, 8:16], alog_b_d[0, :].partition_broadcast(128), sl_prm, w=[prm])
    sp.dma(prm16[:, 1, 0:8], dtb_f_d[0, :].partition_broadcast(128), sl_prm, w=[prm])
    sp.dma(prm16[:, 1, 8:16], dtb_b_d[0, :].partition_broadcast(128), sl_prm, w=[prm])
    gdn_nw = sb("gdn_nw", [128, 1])
    sp.dma(gdn_nw[:], gdn_nw_d[0, :].rearrange("(p o) -> p o", o=1), sl_prm, w=[prm])
    gla_nw = sb("gla_nw", [128, 2])
    sp.dma(gla_nw[:], gla_nw_d[0, :].rearrange("(k p) -> p k", p=128), sl_prm, w=[prm], allow_slow_non_contiguous=True)
    w2cat = sb("w2cat", [33, 2, 512])
    pool.op(lambda h: h.memset(w2cat[0:32, :, :], 0.0), w=[prm])
    sp.dma(w2cat[0:16, 0, :], gkw_f_d[0, :, :], sl_prm, r=[prm], w=[prm])
    sp.dma(w2cat[16:32, 1, :], gkw_b_d[0, :, :], sl_prm, w=[prm])
    sp.dma(w2cat[32:33, 0, :], gkb_f_d[0:1, :], sl_prm, w=[prm])
    sp.dma(w2cat[32:33, 1, :], gkb_b_d[0:1, :], sl_prm, w=[prm])

    def dbg_out(name, src_ap, rbufs, dst=None):
        if name not in dbg:
            return
        d = dbg[name] if dst is None else dst
        sp.dma(d, src_ap, sl_dbg, r=rbufs)

    sl_dbg = Slot(nc, "dbg")
    sl_out2 = [Slot(nc, "out0"), Slot(nc, "out1")]

    hT = sb("hT", [128, 8, T], BF16); hTb = Buf("hT")
    oaT = sb("oaT", [128, 8, T], BF16); oaTb = Buf("oaT")
    base_mark = len(allocs)

    xt = [sb(f"xt{i}", [128, D]) for i in range(2)]; xtb = [Buf(f"xt{i}") for i in range(2)]
    sl_x = [Slot(nc, f"x{i}") for i in range(2)]
    junk = sb("junk", [128, D], BF16); junkb = Buf("junk")
    xn = sb("xn", [128, D], BF16); xnb = Buf("xn")
    st0 = sb("st0", [128, 4]); st0b = Buf("st0")
    for tt in range(NT):
        s = tt % 2
        sp.dma(xt[s][:], x_d[tt * 128:(tt + 1) * 128, :], sl_x[s], w=[xtb[s]])
        act.op(lambda h: h.activation(out=junk[:], in_=xt[s][:], func=AF.Square, accum_out=st0[:, 0:1]), r=[xtb[s]], w=[junkb, st0b])
        act.op(lambda h: h.activation(out=st0[:, 1:2], in_=st0[:, 0:1], func=AF.Ln, bias=EPS, scale=1.0 / D), r=[st0b], w=[st0b])
        act.op(lambda h: h.activation(out=st0[:, 2:3], in_=st0[:, 1:2], func=AF.Exp, scale=-0.5), r=[st0b], w=[st0b])
        act.op(lambda h: h.activation(out=xn[:], in_=xt[s][:], func=AF.Copy, scale=st0[:, 2:3]), r=[xtb[s], st0b], w=[xnb])
        for k in range(8):
            pe.op(lambda h: h.transpose(pst[:, k * 128:(k + 1) * 128], xn[:, k * 128:(k + 1) * 128], ident_bf[:]), r=[xnb, cst], w=[pstb])
        dve.op(lambda h: h.tensor_tensor(out=hT[:, :, tt * 128:(tt + 1) * 128], in0=pst[:].rearrange("p (k t) -> p k t", k=8),
                                         in1=lnw_T[:, :].unsqueeze(2).to_broadcast([128, 8, 128]), op=ALU.mult), r=[pstb, prm], w=[hTb])
    if "st0" in dbg:
        sp.dma(dbg["st0"], st0[:], sl_dbg, r=[st0b])
        tmpx = sb("dbg_tmpx", [128, D]); tbx = Buf("dbg_tmpx")
        act.op(lambda h: h.copy(out=tmpx[:], in_=xn[:]), r=[xnb], w=[tbx])
        sp.dma(dbg["xn"], tmpx[:], sl_dbg, r=[tbx])
    if "hT" in dbg:
        tmp = sb("dbg_tmp", [128, T])
        tb = Buf("dbg_tmp")
        for k in range(8):
            act.op(lambda h: h.copy(out=tmp[:], in_=hT[:, k, :]), r=[hTb], w=[tb])
            sp.dma(dbg["hT"][k * 128:(k + 1) * 128, :], tmp[:], sl_dbg, r=[tb])
    free_to(base_mark)

    NWS = 2
    wblk = [sb(f"wblk{i}", [128, 8, 128], BF16) for i in range(NWS)]
    wblkb = [Buf(f"wblk{i}") for i in range(NWS)]
    sl_w = [Slot(nc, f"w{i}") for i in range(NWS)]
    w_rr = [0]

    def load_wblk(src_ap):
        i = w_rr[0] % NWS
        w_rr[0] += 1
        pool.dma(wblk[i][:], src_ap.rearrange("(k p) c -> p k c", p=128), sl_w[i], w=[wblkb[i]])
        return wblk[i], wblkb[i]

    def proj_fm(wt, wb, tg, pst_, pb, rhsT=hT, rb=hTb):
        for k in range(8):
            pe.op(lambda h: h.matmul(pst_[:, :], lhsT=wt[:, k, :], rhs=rhsT[:, k, tg * 512:(tg + 1) * 512], start=(k == 0), stop=(k == 7)),
                  r=[wb, rb], w=[pb])

    mix_mark = len(allocs)

    NCOL = 16
    LG = sb("LG", [128, NT, NCOL]); LNB = sb("LNB", [128, NT, NCOL]); BETA = sb("BETA", [128, NT, NCOL])
    LG_R = sb("LG_R", [128, NT, NCOL], F32R); LNB_R = sb("LNB_R", [128, NT, NCOL], F32R)
    A_tok = sb("A_tok", [128, NT, NCOL]); NG_tok = sb("NG_tok", [128, NT, NCOL]); BG = sb("BG", [128, NT, NCOL])
    KD = sb("KD", [128, NT, NCOL]); ET0 = sb("ET0", [128, NT, NCOL]); ET1 = sb("ET1", [128, NT, NCOL])
    scal = Buf("scal")
    s1_mark = len(allocs)
    wsm = sb("wsm", [128, 8, 32], BF16); wsmb = Buf("wsm")
    sl_wsm = Slot(nc, "wsm")
    pool.dma(wsm[:], w_in_d[0, :, OFF_SM:OFF_SM + 32].rearrange("(k p) c -> p k c", p=128), sl_wsm, w=[wsmb])
    asm = sb("asm", [128, NT, 32]); asmb = Buf("asm")
    for tt in range(NT):
        p_, pb = next_ps()
        for k in range(8):
            pe.op(lambda h: h.matmul(p_[:, 0:32], lhsT=hT[:, k, tt * 128:(tt + 1) * 128], rhs=wsm[:, k, :], start=(k == 0), stop=(k == 7)),
                  r=[hTb, wsmb], w=[pb])
        act.op(lambda h: h.copy(out=asm[:, tt, :], in_=p_[:, 0:32]), r=[pb], w=[asmb])
    t16 = sb("t16", [128, NT, NCOL]); t16b = Buf("t16")
    dve.op(lambda h: h.tensor_tensor(out=t16[:], in0=asm[:, :, 0:16], in1=prm16[:, 1:2, :].to_broadcast([128, NT, NCOL]), op=ALU.add), r=[asmb, prm], w=[t16b])
    act.op(lambda h: h.activation(out=t16[:], in_=t16[:], func=AF.Exp), r=[t16b], w=[t16b])
    act.op(lambda h: h.activation(out=t16[:], in_=t16[:], func=AF.Ln, bias=1.0), r=[t16b], w=[t16b])
    nA = sb("nA", [128, 1, NCOL]); nAb = Buf("nA")
    act.op(lambda h: h.activation(out=nA[:], in_=prm16[:, 0:1, :], func=AF.Exp), r=[prm], w=[nAb])
    dve.op(lambda h: h.scalar_tensor_tensor(out=LG[:], in0=t16[:], scalar=-1.0, in1=nA[:].to_broadcast([128, NT, NCOL]), op0=ALU.mult, op1=ALU.mult),
           r=[t16b, nAb], w=[scal])
    act.op(lambda h: h.activation(out=t16[:], in_=asm[:, :, 16:32], func=AF.Exp, scale=-1.0), r=[asmb, scal], w=[t16b])
    act.op(lambda h: h.activation(out=t16[:], in_=t16[:], func=AF.Ln, bias=1.0), r=[t16b], w=[t16b])
    act.op(lambda h: h.mul(out=LNB[:], in_=t16[:], mul=-1.0), r=[t16b], w=[scal])
    act.op(lambda h: h.activation(out=BETA[:], in_=LNB[:], func=AF.Exp), r=[scal], w=[scal])
    G_tok = sb("G_tok", [128, NT, NCOL]); TOTO = sb("TOTO", [128, NT, NCOL])
    for tt in range(NT):
        p_, pb = next_ps()
        for i, M in enumerate([TRI[0], TRI[1], BDONES, SEL0, SEL1]):
            pe.op(lambda h: h.matmul(p_[:, i * 16:(i + 1) * 16], lhsT=M[:], rhs=LG[:, tt, :], start=True, stop=True), r=[cst, scal], w=[pb])
        act.op(lambda h: h.copy(out=G_tok[:, tt, 0:8], in_=p_[:, 0:8]), r=[pb], w=[scal])
        act.op(lambda h: h.copy(out=G_tok[:, tt, 8:16], in_=p_[:, 24:32]), r=[pb], w=[scal])
        act.op(lambda h: h.copy(out=TOTO[:, tt, :], in_=p_[:, 32:48]), r=[pb], w=[scal])
        act.op(lambda h: h.activation(out=ET0[:, tt, :], in_=p_[:, 48:64], func=AF.Exp), r=[pb], w=[scal])
        act.op(lambda h: h.activation(out=ET1[:, tt, :], in_=p_[:, 64:80], func=AF.Exp), r=[pb], w=[scal])
    dve.op(lambda h: h.tensor_tensor(out=A_tok[:], in0=G_tok[:], in1=LNB[:], op=ALU.add), r=[scal], w=[scal])
    act.op(lambda h: h.mul(out=NG_tok[:], in_=G_tok[:], mul=-1.0), r=[scal], w=[scal])
    act.op(lambda h: h.activation(out=BG[:], in_=A_tok[:], func=AF.Exp), r=[scal], w=[scal])
    dve.op(lambda h: h.tensor_tensor(out=KD[:], in0=TOTO[:], in1=G_tok[:], op=ALU.subtract), r=[scal], w=[scal])
    act.op(lambda h: h.activation(out=KD[:], in_=KD[:], func=AF.Exp), r=[scal], w=[scal])
    act.op(lambda h: h.copy(out=LG_R[:], in_=LG[:]), r=[scal], w=[scal])
    act.op(lambda h: h.copy(out=LNB_R[:], in_=LNB[:]), r=[scal], w=[scal])
    if "LG" in dbg:
        sp.dma(dbg["LG"].rearrange("(t p) c -> p t c", p=128), LG[:], sl_dbg, r=[scal])
        sp.dma(dbg["BETA"].rearrange("(t p) c -> p t c", p=128), BETA[:], sl_dbg, r=[scal])
        sp.dma(dbg["G_tok"].rearrange("(t p) c -> p t c", p=128), G_tok[:], sl_dbg, r=[scal])

    free_to(s1_mark)
    gdn_mark = len(allocs)
    if gdn_heads > 0:
        pre = sb("pre", [128, T + 4]); preb = Buf("pre")
        pool.op(lambda h: h.memset(pre[:, 0:2], 0.0), w=[preb]); pool.op(lambda h: h.memset(pre[:, T + 2:T + 4], 0.0), w=[preb])
        acc = sb("acc", [128, T]); accb = Buf("acc")
        sqt = sb("sqt", [128, 512], BF16); sqtb = Buf("sqt")
        rn = sb("rn", [128, 512]); rnb = Buf("rn")
        HB = []
        for bs in range(2):
            HB.append(dict(qT=sb(f"qT{bs}", [128, T], BF16), qTb=Buf(f"qT{bs}"), kT=sb(f"kT{bs}", [128, T], BF16), kTb=Buf(f"kT{bs}"),
                           k_tok=sb(f"k_tok{bs}", [128, NT, 128], BF16), ktb=Buf(f"k_tok{bs}"),
                           v_tok=sb(f"v_tok{bs}", [128, NT, 128], BF16), vtb=Buf(f"v_tok{bs}")))
        vT = sb("vT", [128, T], BF16); vTb = Buf("vT")
        zs = sb("zs", [128, T], BF16); zsb = Buf("zs")
        oTa = sb("oTa", [128, T]); oTabs = [[Buf(f"oTa{t_}_{c_}") for c_ in range(2)] for t_ in range(NT)]
        oTa_all = [b_ for l_ in oTabs for b_ in l_]
        NSL = 2
        WK = {}
        ST = {}
        for d_ in range(2):
            for sl_i in range(NSL):
                W_ = {}
                for nm, shp, dt in [("D1", [128, 128], F32), ("D12", [128, 256], F32), ("grep", [128, 128], F32),
                                    ("LU", [128, 2, 256], BF16),
                                    ("LU2", [128, 2, 256], BF16),
                                    ("attnT", [128, 128], BF16), ("qdT", [128, 128], BF16), ("khat", [128, 128], BF16), ("bv", [128, 128], BF16),
                                    ("kd", [128, 128], BF16), ("wT", [128, 128], BF16), ("u", [128, 128], F32), ("vnew", [128, 128], BF16)]:
                    W_[nm] = sb(f"{nm}_{d_}_{sl_i}", shp, dt)
                    W_[nm + "_b"] = Buf(f"{nm}_{d_}_{sl_i}")
                WK[(d_, sl_i)] = W_
            S_ = {}
            for nm, shp, dt in [("S32", [128, 128], F32), ("S16", [128, 128], BF16)]:
                S_[nm] = sb(f"{nm}_{d_}", shp, dt)
                S_[nm + "_b"] = Buf(f"{nm}_{d_}")
            ST[d_] = S_

    def gdn_projA(hh, B_):
        for which, off in (("q", OFF_Q), ("k", OFF_K), ("v", OFF_V)):
            wt, wb = load_wblk(w_in_d[0, :, off + hh * 128: off + (hh + 1) * 128])
            for tg in range(4):
                ipj = yield from acquire("proj"); p_, pb = ps[ipj], psb[ipj]
                proj_fm(wt, wb, tg, p_, pb)
                act.op(lambda h: h.copy(out=pre[:, 2 + tg * 512: 2 + (tg + 1) * 512], in_=p_[:, :]), r=[pb], w=[preb])
                release(ipj)
                yield
            blk = off // 128 + hh
            act.op(lambda h: h.activation(out=acc[:], in_=pre[:, 0:T], func=AF.Copy, scale=cw[:, blk, 0:1]), r=[preb, prm], w=[accb])
            for t_ in range(1, 5):
                dve.op(lambda h: h.scalar_tensor_tensor(out=acc[:], in0=pre[:, t_:t_ + T], scalar=cw[:, blk, t_:t_ + 1], in1=acc[:], op0=ALU.mult, op1=ALU.add),
                       r=[preb, prm, accb], w=[accb])
                yield
            if which == "v":
                act.op(lambda h: h.activation(out=vT[:], in_=acc[:], func=AF.Silu), r=[accb], w=[vTb])
                continue
            act.op(lambda h: h.activation(out=acc[:], in_=acc[:], func=AF.Silu), r=[accb], w=[accb])
            dstT, dstb = (B_["qT"], B_["qTb"]) if which == "q" else (B_["kT"], B_["kTb"])
            post = (128.0 ** -0.5) if which == "q" else 1.0
            for tg in range(4):
                sl_ = slice(tg * 512, (tg + 1) * 512)
                act.op(lambda h: h.activation(out=sqt[:], in_=acc[:, sl_], func=AF.Square), r=[accb], w=[sqtb])
                ipj = yield from acquire("proj"); p_, pb = ps[ipj], psb[ipj]
                pe.op(lambda h: h.matmul(p_[:, :], lhsT=ones_bf[:], rhs=sqt[:], start=True, stop=True), r=[cst, sqtb], w=[pb])
                yield
                act.op(lambda h: h.activation(out=rn[:], in_=p_[:, :], func=AF.Ln, bias=EPS), r=[pb], w=[rnb])
                release(ipj)
                act.op(lambda h: h.activation(out=rn[:], in_=rn[:], func=AF.Exp, scale=-0.5), r=[rnb], w=[rnb])
                dve.op(lambda h: h.scalar_tensor_tensor(out=dstT[:, sl_], in0=acc[:, sl_], scalar=post, in1=rn[:], op0=ALU.mult, op1=ALU.mult),
                       r=[accb, rnb], w=[dstb])
                yield
        for srcT, srcb, dst, dstb in ((B_["kT"], B_["kTb"], B_["k_tok"], B_["ktb"]), (vT, vTb, B_["v_tok"], B_["vtb"])):
            for g8 in range(2):
                for j in range(8):
                    tt = g8 * 8 + j
                    pe.op(lambda h: h.transpose(pst[:, j * 128:(j + 1) * 128], srcT[:, tt * 128:(tt + 1) * 128], ident_bf[:]), r=[srcb, cst], w=[pstb])
                act.op(lambda h: h.copy(out=dst[:, g8 * 8:(g8 + 1) * 8, :], in_=pst[:].rearrange("p (j c) -> p j c", j=8)), r=[pstb], w=[dstb])
                yield

    def gdn_projZ(hh):
        wt, wb = load_wblk(w_in_d[0, :, OFF_Z + hh * 128: OFF_Z + (hh + 1) * 128])
        for tg in range(4):
            ipj = yield from acquire("proj"); p_, pb = ps[ipj], psb[ipj]
            proj_fm(wt, wb, tg, p_, pb)
            act.op(lambda h: h.activation(out=zs[:, tg * 512:(tg + 1) * 512], in_=p_[:, :], func=AF.Silu), r=[pb], w=[zsb])
            release(ipj)
            yield

    def gdn_norm(hh):
        if f"oa{hh}" in dbg:
            sp.dma(dbg[f"oa{hh}"], oTa[:], sl_dbg, r=oTa_all)
        for tg in range(4):
            sl_ = slice(tg * 512, (tg + 1) * 512)
            act.op(lambda h: h.activation(out=sqt[:], in_=oTa[:, sl_], func=AF.Square), r=[b_ for t_ in range(tg * 4, tg * 4 + 4) for b_ in oTabs[t_]], w=[sqtb])
            ipj = yield from acquire("proj")
            p_, pb = ps[ipj], psb[ipj]
            pe.op(lambda h: h.matmul(p_[:, :], lhsT=ones_bf[:], rhs=sqt[:], start=True, stop=True), r=[cst, sqtb], w=[pb])
            yield
            act.op(lambda h: h.activation(out=rn[:], in_=p_[:, :], func=AF.Ln, bias=EPS, scale=1.0 / 128), r=[pb], w=[rnb])
            release(ipj)
            act.op(lambda h: h.activation(out=rn[:], in_=rn[:], func=AF.Exp, scale=-0.5), r=[rnb], w=[rnb])
            dve.op(lambda h: h.scalar_tensor_tensor(out=rn[:], in0=oTa[:, sl_], scalar=gdn_nw[:, 0:1], in1=rn[:], op0=ALU.mult, op1=ALU.mult),
                   r=[b_ for t_ in range(tg * 4, tg * 4 + 4) for b_ in oTabs[t_]] + [rnb, prm], w=[rnb])
            dve.op(lambda h: h.tensor_tensor(out=oaT[:, hh, sl_], in0=rn[:], in1=zs[:, sl_], op=ALU.mult), r=[rnb, zsb], w=[oaTb])
            yield

    all_tasks = []
    PRA, PRZ, NORM = [], [], []
    for hh in range(gdn_heads):
        B_ = HB[hh % 2]
        pra = Task(gdn_projA(hh, B_), deps=[PRA[hh - 1] if hh >= 1 else None, NORM[hh - 2] if hh >= 2 else None, PRZ[hh - 1] if hh >= 1 else None])
        PRA.append(pra)
        all_tasks.append(pra)
        prz = Task(gdn_projZ(hh), deps=[pra, NORM[hh - 1] if hh >= 1 else None])
        PRZ.append(prz)
        def gdn_prep(d_, tt, W_, hh=hh, B_=B_):
            qT, qTb, kT, kTb, k_tok, ktb, v_tok, vtb = B_["qT"], B_["qTb"], B_["kT"], B_["kTb"], B_["k_tok"], B_["ktb"], B_["v_tok"], B_["vtb"]
            col = d_ * 8 + hh
            tsl = slice(tt * 128, (tt + 1) * 128)
            act.op(lambda h: h.activation(out=W_["khat"][:], in_=k_tok[:, tt, :], func=AF.Copy, scale=BG[:, tt, col:col + 1]),
                   r=[ktb, scal], w=[W_["khat_b"]])
            act.op(lambda h: h.activation(out=W_["bv"][:], in_=v_tok[:, tt, :], func=AF.Copy, scale=BETA[:, tt, col:col + 1]),
                   r=[vtb, scal], w=[W_["bv_b"]])
            pool.op(lambda h: h.tensor_scalar(out=W_["kd"][:], in0=k_tok[:, tt, :], scalar1=KD[:, tt, col:col + 1], scalar2=None, op0=ALU.mult),
                    r=[ktb, scal], w=[W_["kd_b"]])
            yield
            ia = yield from acquire("prep")
            pa, pab = ps[ia], psb[ia]
            pe.op(lambda h: h.matmul(pa[:, :], lhsT=LG_R[:, tt, col:col + 1].to_broadcast([128, 128]), rhs=TRI4_R[d_][:], start=True, stop=False), r=[cst, scal], w=[pab])
            pe.op(lambda h: h.matmul(pa[:, :], lhsT=ident_r[:], rhs=MASKS_R[d_][:], start=False, stop=False), r=[cst], w=[pab])
            pe.op(lambda h: h.matmul(pa[:, 128:256], lhsT=LNB_R[:, tt, col:col + 1].to_broadcast([128, 128]), rhs=ident_r[:], start=False, stop=True), r=[cst, scal], w=[pab])
            yield
            act.op(lambda h: h.activation(out=W_["D1"][:], in_=pa[:, 0:128], func=AF.Exp, bias=A_tok[:, tt, col:col + 1], scale=-1.0),
                   r=[pab, scal], w=[W_["D1_b"]])
            act.op(lambda h: h.activation(out=W_["D12"][:], in_=pa[:, 128:384], func=AF.Exp, bias=NG_tok[:, tt, col:col + 1], scale=1.0),
                   r=[pab, scal], w=[W_["D12_b"]])
            act.op(lambda h: h.activation(out=W_["grep"][:], in_=pa[:, 384:512], func=AF.Exp), r=[pab], w=[W_["grep_b"]])
            release(ia)
            yield
            ikq = yield from acquire("prep")
            pkq, pkqb = ps[ikq], psb[ikq]
            pe.op(lambda h: h.matmul(pkq[:, 0:128], lhsT=kT[:, tsl], rhs=kT[:, tsl], start=True, stop=True), r=[kTb], w=[pkqb])
            pe.op(lambda h: h.matmul(pkq[:, 128:256], lhsT=kT[:, tsl], rhs=qT[:, tsl], start=True, stop=True), r=[kTb, qTb], w=[pkqb])
            yield
            LU, LUb, LU2, LU2b = W_["LU"], W_["LU_b"], W_["LU2"], W_["LU2_b"]
            dve.op(lambda h: h.tensor_tensor(out=LU[:, 1, 0:128], in0=pkq[:, 0:128], in1=W_["D1"][:], op=ALU.mult), r=[pkqb, W_["D1_b"]], w=[LUb])
            dve.op(lambda h: h.tensor_tensor(out=LU[:, 0, 0:128], in0=pkq[:, 0:128], in1=W_["D12"][:, 0:128], op=ALU.mult), r=[pkqb, W_["D12_b"]], w=[LUb])
            dve.op(lambda h: h.tensor_tensor(out=LU2[:, 0, 128:256], in0=ident32[:], in1=LU[:, 0, 0:128], op=ALU.subtract), r=[cst, LUb], w=[LU2b])
            dve.op(lambda h: h.tensor_tensor(out=W_["attnT"][:], in0=pkq[:, 128:256], in1=W_["D12"][:, 128:256], op=ALU.mult), r=[pkqb, W_["D12_b"]], w=[W_["attnT_b"]])
            dve.op(lambda h: h.tensor_tensor(out=W_["qdT"][:], in0=qT[:, tsl], in1=W_["grep"][:], op=ALU.mult), r=[qTb, W_["grep_b"]], w=[W_["qdT_b"]])
            release(ikq)
            yield
            cur, curb, nxt, nxtb = LU, LUb, LU2, LU2b
            for lev in range(6):
                ipn = yield from acquire("prep")
                pn, pnb = ps[ipn], psb[ipn]
                if lev == 0:
                    pe.op(lambda h: h.matmul(pn[:, 0:128], lhsT=cur[:, 1, 0:128], rhs=cur[:, 0, 0:128], start=True, stop=True), r=[curb], w=[pnb])
                elif lev < 5:
                    pe.op(lambda h: h.matmul(pn[:, 0:256], lhsT=cur[:, 1, 0:128], rhs=cur[:, 0, 0:256], start=True, stop=True), r=[curb], w=[pnb])
                else:
                    pe.op(lambda h: h.matmul(pn[:, 128:256], lhsT=cur[:, 1, 0:128], rhs=cur[:, 0, 128:256], start=True, stop=True), r=[curb], w=[pnb])
                if lev < 5:
                    pe.op(lambda h: h.matmul(pn[:, 256:384], lhsT=cur[:, 0, 0:128], rhs=cur[:, 1, 0:128], start=True, stop=True), r=[curb], w=[pnb])
                yield
                if lev < 5:
                    ev_eng = act
                    if ev_eng is dve:
                        dve.op(lambda h: h.tensor_copy(out=nxt[:, :, 0:128], in_=pn[:, :].rearrange("p (a b) -> p a b", a=2)[:, :, 0:128]), r=[pnb], w=[nxtb])
                    else:
                        act.op(lambda h: h.copy(out=nxt[:, :, 0:128], in_=pn[:, :].rearrange("p (a b) -> p a b", a=2)[:, :, 0:128]), r=[pnb], w=[nxtb])
                if lev > 0:
                    dve.op(lambda h: h.tensor_tensor(out=nxt[:, 0, 128:256], in0=pn[:, 128:256], in1=cur[:, 0, 128:256], op=ALU.add), r=[pnb, curb], w=[nxtb])
                cur, curb, nxt, nxtb = nxt, nxtb, cur, curb
                release(ipn)
                yield
            Wm = cur[:, 0, 128:256]; Wmb = curb
            ipw = yield from acquire("prep")
            pw, pwb = ps[ipw], psb[ipw]
            pe.op(lambda h: h.matmul(pw[:, 0:128], lhsT=W_["khat"][:], rhs=Wm, start=True, stop=True), r=[W_["khat_b"], Wmb], w=[pwb])
            pe.op(lambda h: h.matmul(pw[:, 128:256], lhsT=Wm, rhs=W_["bv"][:], start=True, stop=True), r=[W_["bv_b"], Wmb], w=[pwb])
            yield
            act.op(lambda h: h.copy(out=W_["wT"][:], in_=pw[:, 0:128]), r=[pwb], w=[W_["wT_b"]])
            act.op(lambda h: h.copy(out=W_["u"][:], in_=pw[:, 128:256]), r=[pwb], w=[W_["u_b"]])
            release(ipw)
            yield

        def gdn_scan(d_, tt, W_, S_, hh=hh):
            col = d_ * 8 + hh
            for c in ((0, 1) if d_ == 0 else (1, 0)):
                rs = slice(c * 64, (c + 1) * 64)
                ETc = ET0 if c == 0 else ET1
                ip1 = yield from acquire("scan")
                p1, p1b = ps[ip1], psb[ip1]
                pe.op(lambda h: h.matmul(p1[rs, 0:128], lhsT=W_["wT"][:, rs], rhs=S_["S16"][:], start=True, stop=True), r=[W_["wT_b"], S_["S16_b"]], w=[p1b])
                yield
                dve.op(lambda h: h.tensor_tensor(out=W_["vnew"][rs, :], in0=W_["u"][rs, :], in1=p1[rs, 0:128], op=ALU.subtract),
                       r=[W_["u_b"], p1b], w=[W_["vnew_b"]])
                release(ip1)
                yield
                ip2 = yield from acquire("scan")
                p2, p2b = ps[ip2], psb[ip2]
                pe.op(lambda h: h.matmul(p2[:, 0:64], lhsT=S_["S16"][:], rhs=W_["qdT"][:, rs], start=True, stop=False), r=[S_["S16_b"], W_["qdT_b"]], w=[p2b])
                pe.op(lambda h: h.matmul(p2[:, 0:64], lhsT=W_["vnew"][rs, :], rhs=W_["attnT"][rs, rs], start=False, stop=True),
                      r=[W_["vnew_b"], W_["attnT_b"]], w=[p2b])
                pe.op(lambda h: h.matmul(p2[:, 128:256], lhsT=W_["kd"][rs, :], rhs=W_["vnew"][rs, :], start=True, stop=True),
                      r=[W_["kd_b"], W_["vnew_b"]], w=[p2b])
                yield
                dve.op(lambda h: h.scalar_tensor_tensor(out=S_["S32"][:], in0=S_["S32"][:], scalar=ETc[:, tt, col:col + 1], in1=p2[:, 128:256],
                                                        op0=ALU.mult, op1=ALU.add), r=[S_["S32_b"], scal, p2b], w=[S_["S32_b"]])
                act.op(lambda h: h.copy(out=S_["S16"][:], in_=S_["S32"][:]), r=[S_["S32_b"]], w=[S_["S16_b"]])
                osl = slice(tt * 128 + c * 64, tt * 128 + (c + 1) * 64)
                first = (tt < NT // 2) if d_ == 0 else (tt >= NT // 2)
                if first:
                    act.op(lambda h: h.copy(out=oTa[:, osl], in_=p2[:, 0:64]), r=[p2b], w=[oTabs[tt][c]])
                else:
                    dve.op(lambda h: h.tensor_tensor(out=oTa[:, osl], in0=p2[:, 0:64], in1=oTa[:, osl], op=ALU.add), r=[p2b, oTabs[tt][c]], w=[oTabs[tt][c]])
                release(ip2)
                yield

        def gdn_init():
            for d_ in range(2):
                pool.op(lambda h: h.memset(ST[d_]["S32"][:], 0.0), w=[ST[d_]["S32_b"]])
                pool.op(lambda h: h.memset(ST[d_]["S16"][:], 0.0), w=[ST[d_]["S16_b"]])
            yield
        init_t = Task(gdn_init(), deps=[NORM[hh - 1] if hh >= 1 else None])
        all_tasks.append(init_t)
        order = {0: list(range(NT)), 1: list(range(NT - 1, -1, -1))}
        P = {0: [], 1: []}
        S = {0: [], 1: []}
        tasks = all_tasks
        for i in range(NT):
            for d_ in range(2):
                tt = order[d_][i]
                W_ = WK[(d_, i % NSL)]
                pt = Task(gdn_prep(d_, tt, W_), deps=[S[d_][i - NSL] if i >= NSL else None, PRA[hh], init_t])
                P[d_].append(pt)
                tasks.append(pt)
            for d_ in range(2):
                tt = order[d_][i]
                W_ = WK[(d_, i % NSL)]
                other = S[1 - d_][NT - 1 - i] if (i >= NT // 2 and len(S[1 - d_]) > NT - 1 - i) else None
                stt = Task(gdn_scan(d_, tt, W_, ST[d_]), deps=[P[d_][i], S[d_][i - 1] if i >= 1 else None, other])
                S[d_].append(stt)
                tasks.append(stt)
        all_tasks.append(prz)
        nt = Task(gdn_norm(hh), deps=[prz] + S[0] + S[1])
        NORM.append(nt)
        all_tasks.append(nt)
    if gdn_heads > 0:
        for hh in range(gdn_heads - 1):
            NORM[hh].deps.append(PRA[hh + 1])
        run_tasks(all_tasks, max_active=MAXACT + 2)
    if "oaT" in dbg:
        tmp = sb("dbg_tmp3", [128, T]); tb = Buf("dbg_tmp3")
        for k in range(8):
            act.op(lambda h: h.copy(out=tmp[:], in_=oaT[:, k, :]), r=[oaTb], w=[tb])
            sp.dma(dbg["oaT"][k * 128:(k + 1) * 128, :], tmp[:], sl_dbg, r=[tb])
    free_to(mix_mark)

    obT = sb("obT", [128, 8, T], BF16); obTb = Buf("obT")
    gla_mark = len(allocs)
    if gla_heads > 0:
        NSG = 2
        GW = {}
        GS_ = {}
        for d_ in range(2):
            for sl_i in range(NSG):
                W_ = {}
                for nm, shp, dt in [("gk", [128, 128], F32), ("ekd", [128, 128], F32), ("eg", [128, 128], F32), ("eng", [128, 128], F32), ("etot", [128, 1], F32),
                                    ("kdg", [128, 128], BF16), ("qgT", [128, 128], BF16), ("kgT", [128, 128], BF16), ("attnT", [128, 128], BF16)]:
                    W_[nm] = sb(f"g{nm}_{d_}_{sl_i}", shp, dt)
                    W_[nm + "_b"] = Buf(f"g{nm}_{d_}_{sl_i}")
                GW[(d_, sl_i)] = W_
            S_ = {}
            for nm, shp, dt in [("S32", [128, 256], F32), ("S16", [128, 256], BF16)]:
                S_[nm] = sb(f"g{nm}_{d_}", shp, dt)
                S_[nm + "_b"] = Buf(f"g{nm}_{d_}")
            GS_[d_] = S_

        rT1 = sb("rT1", [33, T]); rT1b = Buf("rT1")
        wr = sb("wr", [128, 8, 32], BF16); wrb = Buf("wr")
        sl_wr = Slot(nc, "wr")
        pool.dma(wr[:], w_in_d[0, :, OFF_R:OFF_R + 32].rearrange("(k p) c -> p k c", p=128), sl_wr, w=[wrb])
        pool.op(lambda h: h.memset(rT1[32:33, :], 1.0), w=[rT1b])
        for tg in range(4):
            p_, pb = next_ps()
            for k in range(8):
                pe.op(lambda h: h.matmul(p_[0:32, :], lhsT=wr[:, k, :], rhs=hT[:, k, tg * 512:(tg + 1) * 512], start=(k == 0), stop=(k == 7)),
                      r=[wrb, hTb], w=[pb])
            act.op(lambda h: h.copy(out=rT1[0:32, tg * 512:(tg + 1) * 512], in_=p_[0:32, :]), r=[pb], w=[rT1b])
        wkv = sb("wkv", [128, 8, 384], BF16); wkvb = Buf("wkv"); sl_wkv = Slot(nc, "wkv")
        qTg = sb("qTg", [128, T], BF16); qTgb = Buf("qTg")
        kTg = sb("kTg", [128, T], BF16); kTgb = Buf("kTg")
        kg_tok = sb("kg_tok", [128, NT, 128], BF16); kgtb = Buf("kg_tok")
        vg_tok = sb("vg_tok", [128, NT, 256], BF16); vgtb = Buf("vg_tok")
        gsT = sb("gsT", [128, 2, T], BF16); gsTb = Buf("gsT")
        obTa = sb("obTa", [128, 2, T]); obTabs = [Buf(f"obTa{t_}") for t_ in range(NT)]
        sqg = sb("sqg", [128, 512], BF16); sqgb = Buf("sqg")
        rng = sb("rng", [128, 512]); rngb = Buf("rng")
    for hb in range(gla_heads):
        for off, dst, dstb, scl in ((OFF_QB, qTg, qTgb, 128.0 ** -0.5), (OFF_KB, kTg, kTgb, 1.0)):
            wt, wb = load_wblk(w_in_d[0, :, off + hb * 128: off + (hb + 1) * 128])
            for tg in range(4):
                p_, pb = next_ps()
                proj_fm(wt, wb, tg, p_, pb)
                act.op(lambda h: h.mul(out=dst[:, tg * 512:(tg + 1) * 512], in_=p_[:, :], mul=scl), r=[pb], w=[dstb])
        for eb in range(2):
            wt, wb = load_wblk(w_in_d[0, :, OFF_GB + hb * 256 + eb * 128: OFF_GB + hb * 256 + (eb + 1) * 128])
            for tg in range(4):
                p_, pb = next_ps()
                proj_fm(wt, wb, tg, p_, pb)
                act.op(lambda h: h.activation(out=gsT[:, eb, tg * 512:(tg + 1) * 512], in_=p_[:, :], func=AF.Silu), r=[pb], w=[gsTb])
        pool.dma(wkv[:, :, 0:128], w_in_d[0, :, OFF_KB + hb * 128: OFF_KB + (hb + 1) * 128].rearrange("(k p) c -> p k c", p=128), sl_wkv, w=[wkvb])
        pool.dma(wkv[:, :, 128:384], w_in_d[0, :, OFF_VB + hb * 256: OFF_VB + (hb + 1) * 256].rearrange("(k p) c -> p k c", p=128), sl_wkv, w=[wkvb])
        for tt in range(NT):
            p_, pb = next_ps()
            for k in range(8):
                pe.op(lambda h: h.matmul(p_[:, 0:384], lhsT=hT[:, k, tt * 128:(tt + 1) * 128], rhs=wkv[:, k, :], start=(k == 0), stop=(k == 7)),
                      r=[hTb, wkvb], w=[pb])
            act.op(lambda h: h.copy(out=kg_tok[:, tt, :], in_=p_[:, 0:128]), r=[pb], w=[kgtb])
            act.op(lambda h: h.copy(out=vg_tok[:, tt, :], in_=p_[:, 128:384]), r=[pb], w=[vgtb])

        def gla_prep(d_, tt, W_, hb=hb):
            tsl = slice(tt * 128, (tt + 1) * 128)
            ix = yield from acquire("gprep")
            px, pxb = ps[ix], psb[ix]
            pe.op(lambda h: h.matmul(px[:, 0:128], lhsT=rT1[:, tsl], rhs=w2cat[:, d_, hb * 128:(hb + 1) * 128], start=True, stop=True),
                  r=[rT1b, prm], w=[pxb])
            yield
            act.op(lambda h: h.activation(out=W_["gk"][:], in_=px[:, 0:128], func=AF.Exp, scale=-1.0), r=[pxb], w=[W_["gk_b"]])
            release(ix)
            act.op(lambda h: h.activation(out=W_["gk"][:], in_=W_["gk"][:], func=AF.Ln, bias=1.0), r=[W_["gk_b"]], w=[W_["gk_b"]])
            yield
            ig = yield from acquire("gprep")
            pg, pgb = ps[ig], psb[ig]
            pe.op(lambda h: h.matmul(pg[:, 0:128], lhsT=DIFG[d_][:], rhs=W_["gk"][:], start=True, stop=True), r=[cst, W_["gk_b"]], w=[pgb])
            pe.op(lambda h: h.matmul(pg[:, 128:256], lhsT=W_["gk"][:], rhs=TRIG[d_][:], start=True, stop=True), r=[cst, W_["gk_b"]], w=[pgb])
            yield
            act.op(lambda h: h.activation(out=W_["eg"][:], in_=pg[:, 128:256], func=AF.Exp), r=[pgb], w=[W_["eg_b"]])
            act.op(lambda h: h.activation(out=W_["eng"][:], in_=pg[:, 128:256], func=AF.Exp, scale=-1.0), r=[pgb], w=[W_["eng_b"]])
            act.op(lambda h: h.activation(out=W_["ekd"][:], in_=pg[:, 0:128], func=AF.Exp), r=[pgb], w=[W_["ekd_b"]])
            lastc = 128 + (127 if d_ == 0 else 0)
            act.op(lambda h: h.activation(out=W_["etot"][:], in_=pg[:, lastc:lastc + 1], func=AF.Exp), r=[pgb], w=[W_["etot_b"]])
            release(ig)
            yield
            dve.op(lambda h: h.tensor_tensor(out=W_["qgT"][:], in0=qTg[:, tsl], in1=W_["eg"][:], op=ALU.mult), r=[qTgb, W_["eg_b"]], w=[W_["qgT_b"]])
            pool.op(lambda h: h.tensor_tensor(out=W_["kgT"][:], in0=kTg[:, tsl], in1=W_["eng"][:], op=ALU.mult), r=[kTgb, W_["eng_b"]], w=[W_["kgT_b"]])
            dve.op(lambda h: h.tensor_tensor(out=W_["kdg"][:], in0=kg_tok[:, tt, :], in1=W_["ekd"][:], op=ALU.mult), r=[kgtb, W_["ekd_b"]], w=[W_["kdg_b"]])
            yield
            ia = yield from acquire("gprep")
            pa_, pab_ = ps[ia], psb[ia]
            pe.op(lambda h: h.matmul(pa_[:, 0:128], lhsT=W_["kgT"][:], rhs=W_["qgT"][:], start=True, stop=True), r=[W_["kgT_b"], W_["qgT_b"]], w=[pab_])
            yield
            dve.op(lambda h: h.tensor_tensor(out=W_["attnT"][:], in0=pa_[:, 0:128], in1=MASKG[d_][:], op=ALU.mult), r=[pab_, cst], w=[W_["attnT_b"]])
            release(ia)
            yield

        def gla_scan(d_, tt, W_, S_):
            tsl = slice(tt * 128, (tt + 1) * 128)
            io = yield from acquire("gscan")
            po, pob = ps[io], psb[io]
            for eb in range(2):
                pe.op(lambda h: h.matmul(po[:, eb * 128:(eb + 1) * 128], lhsT=S_["S16"][:, eb * 128:(eb + 1) * 128], rhs=W_["qgT"][:], start=True, stop=False),
                      r=[S_["S16_b"], W_["qgT_b"]], w=[pob])
                pe.op(lambda h: h.matmul(po[:, eb * 128:(eb + 1) * 128], lhsT=vg_tok[:, tt, eb * 128:(eb + 1) * 128], rhs=W_["attnT"][:], start=False, stop=True),
                      r=[vgtb, W_["attnT_b"]], w=[pob])
            iS = yield from acquire("gscan")
            pS, pSb = ps[iS], psb[iS]
            pe.op(lambda h: h.matmul(pS[:, 0:256], lhsT=W_["kdg"][:], rhs=vg_tok[:, tt, :], start=True, stop=True), r=[W_["kdg_b"], vgtb], w=[pSb])
            yield
            dve.op(lambda h: h.scalar_tensor_tensor(out=S_["S32"][:], in0=S_["S32"][:], scalar=W_["etot"][:, 0:1], in1=pS[:, 0:256],
                                                    op0=ALU.mult, op1=ALU.add), r=[S_["S32_b"], W_["etot_b"], pSb], w=[S_["S32_b"]])
            release(iS)
            act.op(lambda h: h.copy(out=S_["S16"][:], in_=S_["S32"][:]), r=[S_["S32_b"]], w=[S_["S16_b"]])
            first = (tt < NT // 2) if d_ == 0 else (tt >= NT // 2)
            if first:
                act.op(lambda h: h.copy(out=obTa[:, :, tsl], in_=po[:, 0:256].rearrange("p (e t) -> p e t", e=2)), r=[pob], w=[obTabs[tt]])
            else:
                dve.op(lambda h: h.tensor_tensor(out=obTa[:, :, tsl], in0=po[:, 0:256].rearrange("p (e t) -> p e t", e=2), in1=obTa[:, :, tsl], op=ALU.add),
                       r=[pob, obTabs[tt]], w=[obTabs[tt]])
            release(io)
            yield

        for d_ in range(2):
            pool.op(lambda h: h.memset(GS_[d_]["S32"][:], 0.0), w=[GS_[d_]["S32_b"]])
            pool.op(lambda h: h.memset(GS_[d_]["S16"][:], 0.0), w=[GS_[d_]["S16_b"]])
        order = {0: list(range(NT)), 1: list(range(NT - 1, -1, -1))}
        P = {0: [], 1: []}
        S = {0: [], 1: []}
        tasks = []
        for i in range(NT):
            for d_ in range(2):
                pt = Task(gla_prep(d_, order[d_][i], GW[(d_, i % NSG)]), deps=[S[d_][i - NSG] if i >= NSG else None])
                P[d_].append(pt)
                tasks.append(pt)
            for d_ in range(2):
                other = S[1 - d_][NT - 1 - i] if (i >= NT // 2 and len(S[1 - d_]) > NT - 1 - i) else None
                stt = Task(gla_scan(d_, order[d_][i], GW[(d_, i % NSG)], GS_[d_]), deps=[P[d_][i], S[d_][i - 1] if i >= 1 else None, other])
                S[d_].append(stt)
                tasks.append(stt)
        run_tasks(tasks, max_active=MAXACT)
        if f"ob{hb}" in dbg:
            for eb in range(2):
                sp.dma(dbg[f"ob{hb}"][eb * 128:(eb + 1) * 128, :], obTa[:, eb, :], sl_dbg, r=obTabs)
        for tg in range(4):
            sl_ = slice(tg * 512, (tg + 1) * 512)
            p_, pb = next_ps()
            for eb in range(2):
                act.op(lambda h: h.activation(out=sqg[:], in_=obTa[:, eb, sl_], func=AF.Square), r=obTabs[tg * 4:tg * 4 + 4], w=[sqgb])
                pe.op(lambda h: h.matmul(p_[:, :], lhsT=ones_bf[:], rhs=sqg[:], start=(eb == 0), stop=(eb == 1)), r=[cst, sqgb], w=[pb])
            act.op(lambda h: h.activation(out=rng[:], in_=p_[:, :], func=AF.Ln, bias=EPS, scale=1.0 / 256), r=[pb], w=[rngb])
            act.op(lambda h: h.activation(out=rng[:], in_=rng[:], func=AF.Exp, scale=-0.5), r=[rngb], w=[rngb])
            for eb in range(2):
                dve.op(lambda h: h.scalar_tensor_tensor(out=obTa[:, eb, sl_], in0=obTa[:, eb, sl_], scalar=gla_nw[:, eb:eb + 1], in1=rng[:], op0=ALU.mult, op1=ALU.mult),
                       r=obTabs[tg * 4:tg * 4 + 4] + [rngb, prm], w=obTabs[tg * 4:tg * 4 + 4])
                dve.op(lambda h: h.tensor_tensor(out=obT[:, hb * 2 + eb, sl_], in0=obTa[:, eb, sl_], in1=gsT[:, eb, sl_], op=ALU.mult), r=obTabs[tg * 4:tg * 4 + 4] + [gsTb], w=[obTb])
    if "obT" in dbg:
        tmp = sb("dbg_tmp4", [128, T]); tb = Buf("dbg_tmp4")
        for k in range(8):
            act.op(lambda h: h.copy(out=tmp[:], in_=obT[:, k, :]), r=[obTb], w=[tb])
            sp.dma(dbg["obT"][k * 128:(k + 1) * 128, :], tmp[:], sl_dbg, r=[tb])
    free_to(gla_mark)

    if do_final:
        mT = sb("mT", [128, 8, T], BF16); mTb = Buf("mT")
        fin_mark = len(allocs)
        sga = sb("sga", [128, 512]); sgab = Buf("sga")
        sgb = sb("sgb", [128, 512]); sgbb = Buf("sgb")
        t1 = sb("t1", [128, 512]); t1b = Buf("t1")
        t2 = sb("t2", [128, 512]); t2b = Buf("t2")
        NW2 = 8
        wb2 = [sb(f"wb2_{i}", [128, 8, 128], BF16) for i in range(NW2)]; wb2b = [Buf(f"wb2_{i}") for i in range(NW2)]
        sl_w2 = [Slot(nc, f"w2_{i}") for i in range(NW2)]
        rr2 = [0]

        def load2(src_ap):
            i = rr2[0] % NW2
            rr2[0] += 1
            pool.dma(wb2[i][:], src_ap.rearrange("(k p) c -> p k c", p=128), sl_w2[i], w=[wb2b[i]])
            return wb2[i], wb2b[i]

        for m in range(8):
            msl = slice(m * 128, (m + 1) * 128)
            wg, wgb_ = load2(wpg_d[0, :, msl])
            wl, wlb_ = load2(wpl_d[0, :, msl])
            wa, wab_ = load2(w_in_d[0, :, OFF_GA + m * 128: OFF_GA + (m + 1) * 128])
            wbb, wbbb_ = load2(w_in_d[0, :, OFF_GBG + m * 128: OFF_GBG + (m + 1) * 128])
            for tg in range(4):
                sl_ = slice(tg * 512, (tg + 1) * 512)
                pga, pgab = next_ps(); proj_fm(wa, wab_, tg, pga, pgab)
                act.op(lambda h: h.activation(out=sga[:], in_=pga[:, :], func=AF.Sigmoid), r=[pgab], w=[sgab])
                pgb_, pgbb = next_ps(); proj_fm(wbb, wbbb_, tg, pgb_, pgbb)
                act.op(lambda h: h.activation(out=sgb[:], in_=pgb_[:, :], func=AF.Sigmoid), r=[pgbb], w=[sgbb])
                pya, pyab = next_ps(); proj_fm(wg, wgb_, tg, pya, pyab, rhsT=oaT, rb=oaTb)
                dve.op(lambda h: h.tensor_tensor(out=t1[:], in0=pya[:, :], in1=sga[:], op=ALU.mult), r=[pyab, sgab], w=[t1b])
                pyb, pybb = next_ps(); proj_fm(wl, wlb_, tg, pyb, pybb, rhsT=obT, rb=obTb)
                dve.op(lambda h: h.tensor_tensor(out=t2[:], in0=pyb[:, :], in1=sgb[:], op=ALU.mult), r=[pybb, sgbb], w=[t2b])
                dve.op(lambda h: h.tensor_tensor(out=mT[:, m, sl_], in0=t1[:], in1=t2[:], op=ALU.add), r=[t1b, t2b], w=[mTb])
        if "mT" in dbg:
            tmp = sb("dbg_tmp5", [128, T]); tb = Buf("dbg_tmp5")
            for k in range(8):
                act.op(lambda h: h.copy(out=tmp[:], in_=mT[:, k, :]), r=[mTb], w=[tb])
                sp.dma(dbg["mT"][k * 128:(k + 1) * 128, :], tmp[:], sl_dbg, r=[tb])
        free_to(fin_mark)
        wob = hTb; sl_wo = Slot(nc, "wo")
        for k in range(8):
            pool.dma(hT[:, k, 0:D], wout_d[0, k * 128:(k + 1) * 128, :], sl_wo, w=[wob])
        lnp = sb("lnp", [128, D]); sl_lnp = Slot(nc, "lnp"); lnpb = Buf("lnp")
        sp.dma(lnp[:], ln_post_d[0, :].partition_broadcast(128), sl_lnp, w=[lnpb])
        xr = [sb(f"xr{i}", [128, D]) for i in range(2)]; xrb = [Buf(f"xr{i}") for i in range(2)]
        sl_xr = [Slot(nc, f"xr{i}") for i in range(2)]
        ot = [sb(f"ot{i}", [128, D]) for i in range(2)]; otb = [Buf(f"ot{i}") for i in range(2)]
        st1 = sb("st1", [128, 8]); st1b = Buf("st1")
        junk2 = sb("junk2", [128, 512], BF16); junk2b = Buf("junk2")
        for tt in range(NT):
            s = tt % 2
            tsl = slice(tt * 128, (tt + 1) * 128)
            sp.dma(xr[s][:], x_d[tsl, :], sl_xr[s], w=[xrb[s]])
            pp = []
            for half in range(2):
                p_, pb = next_ps()
                for m in range(8):
                    pe.op(lambda h: h.matmul(p_[:, :], lhsT=mT[:, m, tsl], rhs=hT[:, m, half * 512:(half + 1) * 512], start=(m == 0), stop=(m == 7)),
                          r=[mTb, wob], w=[pb])
                act.op(lambda h: h.activation(out=junk2[:], in_=p_[:, :], func=AF.Square, accum_out=st1[:, half:half + 1]), r=[pb], w=[junk2b, st1b])
                pp.append((p_, pb))
            dve.op(lambda h: h.tensor_tensor(out=st1[:, 2:3], in0=st1[:, 0:1], in1=st1[:, 1:2], op=ALU.add), r=[st1b], w=[st1b])
            act.op(lambda h: h.activation(out=st1[:, 3:4], in_=st1[:, 2:3], func=AF.Ln, bias=EPS, scale=1.0 / D), r=[st1b], w=[st1b])
            act.op(lambda h: h.activation(out=st1[:, 4:5], in_=st1[:, 3:4], func=AF.Exp, scale=-0.5), r=[st1b], w=[st1b])
            for half in range(2):
                p_, pb = pp[half]
                hs = slice(half * 512, (half + 1) * 512)
                dve.op(lambda h: h.scalar_tensor_tensor(out=ot[s][:, hs], in0=p_[:, :], scalar=st1[:, 4:5], in1=lnp[:, hs], op0=ALU.mult, op1=ALU.mult),
                       r=[pb, st1b, lnpb], w=[otb[s]])
                dve.op(lambda h: h.tensor_tensor(out=ot[s][:, hs], in0=ot[s][:, hs], in1=xr[s][:, hs], op=ALU.add), r=[otb[s], xrb[s]], w=[otb[s]])
            sp.dma(out_d[tsl, :], ot[s][:], sl_out2[s], r=[otb[s]])
    else:
        zt = sb("zt", [128, D]); ztb = Buf("zt")
        pool.op(lambda h: h.memset(zt[:], 0.0), w=[ztb])
        for tt in range(NT):
            sp.dma(out_d[tt * 128:(tt + 1) * 128, :], zt[:], sl_out2[tt % 2], r=[ztb])
    for so_ in sl_out2:
        if so_.cnt:
            sp.h.wait_ge(so_.sem, so_.cnt)
    if sl_dbg.cnt:
        sp.h.wait_ge(sl_dbg.sem, sl_dbg.cnt)
    return nc


_INPUT_NAMES = ["ln_pre_w", "w_in", "conv_w", "a_log_fwd", "a_log_bwd", "dt_bias_fwd", "dt_bias_bwd", "gdn_norm_w", "w_proj_gdn",
                "gk_w2_fwd", "gk_b2_fwd", "gk_w2_bwd", "gk_b2_bwd", "gla_norm_w", "w_proj_gla", "w_out", "ln_post_w"]


def kernel(**inputs):
    x = np.ascontiguousarray(np.asarray(inputs["x"], dtype=np.float32))
    shared = {n: np.ascontiguousarray(np.asarray(inputs[n], dtype=np.float32)) for n in _INPUT_NAMES}
    nc = build_nc()
    in_maps = [dict(shared, x=x[b]) for b in range(8)]
    res = run_bass_kernel_spmd(nc, in_maps, core_ids=list(range(8)))
    return np.stack([np.asarray(r["out"], dtype=np.float32) for r in res.results], axis=0)
```

```python
import numpy as np
import concourse.bass as bass
import concourse.mybir as mybir
from concourse.bass_utils import run_bass_kernel_spmd

F32 = mybir.dt.float32
F32R = mybir.dt.float32r
BF16 = mybir.dt.bfloat16
AF = mybir.ActivationFunctionType
ALU = mybir.AluOpType

T = 2048
NT = 16
D = 1024
NIN = 9280
EPS = 1e-6
BIG = 30000.0
MAXACT = 6
OFF_Q, OFF_K, OFF_V, OFF_Z = 0, 1024, 2048, 3072
OFF_SM = 4096
OFF_QB, OFF_KB, OFF_VB, OFF_GB = 4128, 4640, 5152, 6176
OFF_R = 7200
OFF_GA, OFF_GBG = 7232, 8256


class Ev:
    __slots__ = ("sem", "val", "key")

    def __init__(self, sem, val, key):
        self.sem, self.val, self.key = sem, val, key


class Buf:
    __slots__ = ("name", "wev", "revs")

    def __init__(self, name):
        self.name, self.wev, self.revs = name, None, {}


class Slot:
    registry = None

    def __init__(self, nc, name):
        if Slot.registry is not None:
            Slot.registry.append(self)
        self.name = name
        self.sem = nc.semaphore("ds_" + name).__enter__()
        self.cnt = 0


class Eng:
    def __init__(self, nc, name, h, selfsync):
        self.name, self.h, self.selfsync = name, h, selfsync
        self.sem = nc.semaphore("es_" + name).__enter__()
        self.cnt = 0
        self.seen = {}

    def wait(self, ev):
        if ev is None:
            return
        if ev.sem is self.sem and not self.selfsync:
            return
        if self.seen.get(ev.key, 0) >= ev.val:
            return
        self.h.wait_ge(ev.sem, ev.val)
        self.seen[ev.key] = ev.val

    def _deps(self, r, w):
        for b in r:
            self.wait(b.wev)
        for b in w:
            self.wait(b.wev)
            for ev in b.revs.values():
                self.wait(ev)

    def _mark(self, ev, r, w):
        for b in r:
            b.revs[ev.key] = ev
        for b in w:
            b.wev = ev
            b.revs = {}

    def op(self, fn, r=(), w=()):
        self._deps(r, w)
        ins = fn(self.h)
        self.cnt += 1
        ins.then_inc(self.sem, 1)
        ev = Ev(self.sem, self.cnt, self.name)
        self._mark(ev, r, w)
        return ev

    def dma(self, out, in_, slot, r=(), w=(), **kw):
        self._deps(r, w)
        ins = self.h.dma_start(out=out, in_=in_, **kw)
        slot.cnt += 16
        ins.then_inc(slot.sem, 16)
        ev = Ev(slot.sem, slot.cnt, slot.name)
        self._mark(ev, r, w)
        return ev


def build_nc(debug=(), gdn_heads=8, gla_heads=4, do_final=True):
    nc = bass.Bass("TRN2", target_bir_lowering=False)
    dram_in = lambda n, s: nc.dram_tensor(n, s, F32, kind="ExternalInput").ap()
    x_d = dram_in("x", [T, D])
    ln_pre_d = dram_in("ln_pre_w", [1, D])
    w_in_d = dram_in("w_in", [1, D, NIN])
    conv_d = dram_in("conv_w", [1, 5, 3072])
    alog_f_d = dram_in("a_log_fwd", [1, 8]); alog_b_d = dram_in("a_log_bwd", [1, 8])
    dtb_f_d = dram_in("dt_bias_fwd", [1, 8]); dtb_b_d = dram_in("dt_bias_bwd", [1, 8])
    gdn_nw_d = dram_in("gdn_norm_w", [1, 128])
    wpg_d = dram_in("w_proj_gdn", [1, D, D])
    gkw_f_d = dram_in("gk_w2_fwd", [1, 16, 512]); gkb_f_d = dram_in("gk_b2_fwd", [1, 512])
    gkw_b_d = dram_in("gk_w2_bwd", [1, 16, 512]); gkb_b_d = dram_in("gk_b2_bwd", [1, 512])
    gla_nw_d = dram_in("gla_norm_w", [1, 256])
    wpl_d = dram_in("w_proj_gla", [1, D, D])
    wout_d = dram_in("w_out", [1, D, D])
    ln_post_d = dram_in("ln_post_w", [1, D])
    out_d = nc.dram_tensor("out", [T, D], F32, kind="ExternalOutput").ap()
    dbg = {}
    for name, shape in debug:
        dbg[name] = nc.dram_tensor("dbg_" + name, shape, F32, kind="ExternalOutput").ap()

    pe = Eng(nc, "pe", nc.tensor, False)
    act = Eng(nc, "act", nc.scalar, True)
    dve = Eng(nc, "dve", nc.vector, True)
    pool = Eng(nc, "pool", nc.gpsimd, True)
    sp = Eng(nc, "sp", nc.sync, False)

    allocs = []

    def sb(name, shape, dt=F32):
        cm = nc.sbuf_tensor(name, shape, dt)
        t = cm.__enter__()
        allocs.append(cm)
        return t

    all_slots = []
    Slot.registry = all_slots

    def barrier():
        engs = (pe, act, dve, pool, sp)
        for e in engs:
            for f in engs:
                if f is not e and f.cnt:
                    e.wait(Ev(f.sem, f.cnt, f.name))
            for sl in all_slots:
                if sl.cnt:
                    e.wait(Ev(sl.sem, sl.cnt, sl.name))

    def free_to(mark):
        barrier()
        while len(allocs) > mark:
            allocs.pop().__exit__(None, None, None)

    NPS = 7
    ps = [nc.psum_tensor(f"ps{i}", [128, 512], F32).__enter__() for i in range(NPS)]
    psb = [Buf(f"ps{i}") for i in range(NPS)]
    pst = nc.psum_tensor("pst", [128, 1024], BF16).__enter__()
    pstb = Buf("pst")
    ps_rr = {}
    PS_POOLS = {"any": list(range(NPS)), "prep": [0, 1, 2], "scan": [3, 4], "proj": [5, 6], "gprep": [0, 1, 2], "gscan": [3, 4, 5, 6]}

    def next_ps(pool_="any"):
        lst = PS_POOLS[pool_]
        c = ps_rr.get(pool_, 0)
        ps_rr[pool_] = c + 1
        i = lst[c % len(lst)]
        return ps[i], psb[i]

    ps_busy = [False] * NPS

    def acquire(pool_):
        while True:
            for i in PS_POOLS[pool_]:
                if not ps_busy[i]:
                    ps_busy[i] = True
                    return i
            yield

    def release(i):
        ps_busy[i] = False

    class Task:
        def __init__(self, gen, deps=()):
            self.gen, self.deps, self.done = gen, [d for d in deps if d is not None], False

    stall = [0]

    def run_tasks(tasks, max_active=6):
        pending = list(tasks)
        active = []
        while pending or active:
            for t in list(pending):
                if len(active) >= max_active:
                    break
                if all(d.done for d in t.deps):
                    pending.remove(t)
                    active.append(t)
            assert active, "task deadlock"
            before = (pe.cnt, act.cnt, dve.cnt, pool.cnt, len(active), len(pending))
            for t in list(active):
                try:
                    next(t.gen)
                except StopIteration:
                    t.done = True
                    active.remove(t)
            if before == (pe.cnt, act.cnt, dve.cnt, pool.cnt, len(active), len(pending)):
                stall[0] += 1
                assert stall[0] < 200, ("scheduler stall", [getattr(t.gen, "__name__", "?") for t in active], list(ps_busy), len(pending))
            else:
                stall[0] = 0

    cst = Buf("const")
    ident32 = sb("ident32", [128, 128]); ident_bf = sb("ident_bf", [128, 128], BF16)
    ones32 = sb("ones32", [128, 128]); ones_bf = sb("ones_bf", [128, 128], BF16)
    zeros32 = sb("zeros32", [128, 128])
    pool.op(lambda h: h.memset(ones32[:], 1.0), w=[cst])
    pool.op(lambda h: h.memset(zeros32[:], 0.0), w=[cst])
    pool.op(lambda h: h.affine_select(out=ident32[:], in_=zeros32[:], pattern=[[-1, 128]], compare_op=ALU.not_equal,
                                      fill=1.0, base=0, channel_multiplier=1), r=[cst], w=[cst])
    pool.op(lambda h: h.tensor_copy(out=ident_bf[:], in_=ident32[:]), r=[cst], w=[cst])
    pool.op(lambda h: h.tensor_copy(out=ones_bf[:], in_=ones32[:]), r=[cst], w=[cst])

    src_cache = {}

    def tri_const(name, n, inval, fillval, offval, step, cm, cmp):
        t = sb(name, [128, 128])
        pool.op(lambda h: h.memset(t[:], offval), w=[cst])
        if inval not in src_cache:
            src_cache[inval] = sb(f"src_{len(src_cache)}", [128, 128])
            pool.op(lambda h: h.memset(src_cache[inval][:], inval), w=[cst])
        src = src_cache[inval]
        for b0 in range(0, 128, n):
            pool.op(lambda h: h.affine_select(out=t[b0:b0 + n, b0:b0 + n], in_=src[b0:b0 + n, b0:b0 + n], pattern=[[step, n]],
                                              compare_op=cmp, fill=fillval, base=0, channel_multiplier=cm), r=[cst], w=[cst])
        return t

    TRI = {
        0: tri_const("tri_f", 64, 1.0, 0.0, 0.0, 1, -1, ALU.is_ge),
        1: tri_const("tri_b", 64, 1.0, 0.0, 0.0, -1, 1, ALU.is_ge),
    }
    NEG_A = {
        0: tri_const("nega_f", 64, 0.0, BIG, BIG, -1, 1, ALU.is_gt),
        1: tri_const("nega_b", 64, 0.0, BIG, BIG, 1, -1, ALU.is_gt),
    }
    NEG_B = {
        0: tri_const("negb_f", 64, 0.0, -BIG, -BIG, 1, -1, ALU.is_gt),
        1: tri_const("negb_b", 64, 0.0, -BIG, -BIG, -1, 1, ALU.is_gt),
    }
    NEG_C = {
        0: tri_const("negc_f", 64, 0.0, -BIG, -BIG, 1, -1, ALU.is_ge),
        1: tri_const("negc_b", 64, 0.0, -BIG, -BIG, -1, 1, ALU.is_ge),
    }
    BDONES = tri_const("bdones", 64, 1.0, 1.0, 0.0, 1, 1, ALU.is_ge)
    ones_r = sb("ones_r", [128, 128], F32R); ident_r = sb("ident_r", [128, 128], F32R)
    pool.op(lambda h: h.tensor_copy(out=ones_r[:], in_=ones32[:]), r=[cst], w=[cst])
    pool.op(lambda h: h.tensor_copy(out=ident_r[:], in_=ident32[:]), r=[cst], w=[cst])
    TRI4_R = {}
    for d__ in range(2):
        t4 = sb(f"tri4_r{d__}", [128, 512], F32R)
        for rep in range(4):
            pool.op(lambda h: h.tensor_copy(out=t4[:, rep * 128:(rep + 1) * 128], in_=TRI[d__][:]), r=[cst], w=[cst])
        TRI4_R[d__] = t4
    MASKS_R = {}
    for d__ in range(2):
        mk = sb(f"masks_r{d__}", [128, 512], F32R)
        pool.op(lambda h: h.tensor_copy(out=mk[:, 0:128], in_=NEG_A[d__][:]), r=[cst], w=[cst])
        pool.op(lambda h: h.tensor_copy(out=mk[:, 128:256], in_=NEG_B[d__][:]), r=[cst], w=[cst])
        pool.op(lambda h: h.tensor_copy(out=mk[:, 256:384], in_=NEG_C[d__][:]), r=[cst], w=[cst])
        pool.op(lambda h: h.tensor_copy(out=mk[:, 384:512], in_=zeros32[:]), r=[cst], w=[cst])
        MASKS_R[d__] = mk
    SEL0 = sb("sel0", [128, 128]); SEL1 = sb("sel1", [128, 128])
    pool.op(lambda h: h.memset(SEL0[:], 0.0), w=[cst]); pool.op(lambda h: h.memset(SEL1[:], 0.0), w=[cst])
    pool.op(lambda h: h.memset(SEL0[0:64, :], 1.0), w=[cst]); pool.op(lambda h: h.memset(SEL1[64:128, :], 1.0), w=[cst])
    GS = -1.0 / 16.0
    TRIG = {0: tri_const("trig_f", 128, GS, 0.0, 0.0, 1, -1, ALU.is_ge),
            1: tri_const("trig_b", 128, GS, 0.0, 0.0, -1, 1, ALU.is_ge)}
    DIFG = {0: tri_const("difg_f", 128, 0.0, GS, 0.0, 1, -1, ALU.is_ge),
            1: tri_const("difg_b", 128, 0.0, GS, 0.0, -1, 1, ALU.is_ge)}
    MASKG = {0: tri_const("maskg_f", 128, 1.0, 0.0, 0.0, 1, -1, ALU.is_ge),
             1: tri_const("maskg_b", 128, 1.0, 0.0, 0.0, -1, 1, ALU.is_ge)}

    prm = Buf("params")
    sl_prm = Slot(nc, "prm")
    lnw_T = sb("lnw_T", [128, 8])
    sp.dma(lnw_T[:], ln_pre_d[0, :].rearrange("(k p) -> p k", p=128), sl_prm, w=[prm], allow_slow_non_contiguous=True)
    cw = sb("cw", [128, 24, 5])
    for t_ in range(5):
        sp.dma(cw[:, :, t_], conv_d[0, t_, :].rearrange("(b p) -> p b", p=128), sl_prm, w=[prm], allow_slow_non_contiguous=True)
    prm16 = sb("prm16", [128, 2, 16])
    sp.dma(prm16[:, 0, 0:8], alog_f_d[0, :].partition_broadcast(128), sl_prm, w=[prm])
    sp.dma(prm16[:, 0, 8:16], alog_b_d[0, :].partition_broadcast(128), sl_prm, w=[prm])
    sp.dma(prm16[:, 1, 0:8], dtb_f_d[0, :].partition_broadcast(128), sl_prm, w=[prm])
    sp.dma(prm16[:, 1, 8:16], dtb_b_d[0, :].partition_broadcast(128), sl_prm, w=[prm])
    gdn_nw = sb("gdn_nw", [128, 1])
    sp.dma(gdn_nw[:], gdn_nw_d[0, :].rearrange("(p o) -> p o", o=1), sl_prm, w=[prm])
    gla_nw = sb("gla_nw", [128, 2])
    sp.dma(gla_nw[:], gla_nw_d[0, :].rearrange("(k p) -> p k", p=128), sl_prm, w=[prm], allow_slow_non_contiguous=True)
    w2cat = sb("w2cat", [33, 2, 512])
    pool.op(lambda h: h.memset(w2cat[0:32, :, :], 0.0), w=[prm])
    sp.dma(w2cat[0:16, 0, :], gkw_f_d[0, :, :], sl_prm, r=[prm], w=[prm])
    sp.dma(w2cat[16:32, 1, :], gkw_b_d[0, :, :], sl_prm, w=[prm])
    sp.dma(w2cat[32:33, 0, :], gkb_f_d[0:1, :], sl_prm, w=[prm])
    sp.dma(w2cat[32:33, 1, :], gkb_b_d[0:1, :], sl_prm, w=[prm])

    def dbg_out(name, src_ap, rbufs, dst=None):
        if name not in dbg:
            return
        d = dbg[name] if dst is None else dst
        sp.dma(d, src_ap, sl_dbg, r=rbufs)

    sl_dbg = Slot(nc, "dbg")
    sl_out2 = [Slot(nc, "out0"), Slot(nc, "out1")]

    hT = sb("hT", [128, 8, T], BF16); hTb = Buf("hT")
    oaT = sb("oaT", [128, 8, T], BF16); oaTb = Buf("oaT")
    base_mark = len(allocs)

    xt = [sb(f"xt{i}", [128, D]) for i in range(2)]; xtb = [Buf(f"xt{i}") for i in range(2)]
    sl_x = [Slot(nc, f"x{i}") for i in range(2)]
    junk = sb("junk", [128, D], BF16); junkb = Buf("junk")
    xn = sb("xn", [128, D], BF16); xnb = Buf("xn")
    st0 = sb("st0", [128, 4]); st0b = Buf("st0")
    for tt in range(NT):
        s = tt % 2
        sp.dma(xt[s][:], x_d[tt * 128:(tt + 1) * 128, :], sl_x[s], w=[xtb[s]])
        act.op(lambda h: h.activation(out=junk[:], in_=xt[s][:], func=AF.Square, accum_out=st0[:, 0:1]), r=[xtb[s]], w=[junkb, st0b])
        act.op(lambda h: h.activation(out=st0[:, 1:2], in_=st0[:, 0:1], func=AF.Ln, bias=EPS, scale=1.0 / D), r=[st0b], w=[st0b])
        act.op(lambda h: h.activation(out=st0[:, 2:3], in_=st0[:, 1:2], func=AF.Exp, scale=-0.5), r=[st0b], w=[st0b])
        act.op(lambda h: h.activation(out=xn[:], in_=xt[s][:], func=AF.Copy, scale=st0[:, 2:3]), r=[xtb[s], st0b], w=[xnb])
        for k in range(8):
            pe.op(lambda h: h.transpose(pst[:, k * 128:(k + 1) * 128], xn[:, k * 128:(k + 1) * 128], ident_bf[:]), r=[xnb, cst], w=[pstb])
        dve.op(lambda h: h.tensor_tensor(out=hT[:, :, tt * 128:(tt + 1) * 128], in0=pst[:].rearrange("p (k t) -> p k t", k=8),
                                         in1=lnw_T[:, :].unsqueeze(2).to_broadcast([128, 8, 128]), op=ALU.mult), r=[pstb, prm], w=[hTb])
    if "st0" in dbg:
        sp.dma(dbg["st0"], st0[:], sl_dbg, r=[st0b])
        tmpx = sb("dbg_tmpx", [128, D]); tbx = Buf("dbg_tmpx")
        act.op(lambda h: h.copy(out=tmpx[:], in_=xn[:]), r=[xnb], w=[tbx])
        sp.dma(dbg["xn"], tmpx[:], sl_dbg, r=[tbx])
    if "hT" in dbg:
        tmp = sb("dbg_tmp", [128, T])
        tb = Buf("dbg_tmp")
        for k in range(8):
            act.op(lambda h: h.copy(out=tmp[:], in_=hT[:, k, :]), r=[hTb], w=[tb])
            sp.dma(dbg["hT"][k * 128:(k + 1) * 128, :], tmp[:], sl_dbg, r=[tb])
    free_to(base_mark)

    NWS = 2
    wblk = [sb(f"wblk{i}", [128, 8, 128], BF16) for i in range(NWS)]
    wblkb = [Buf(f"wblk{i}") for i in range(NWS)]
    sl_w = [Slot(nc, f"w{i}") for i in range(NWS)]
    w_rr = [0]

    def load_wblk(src_ap):
        i = w_rr[0] % NWS
        w_rr[0] += 1
        pool.dma(wblk[i][:], src_ap.rearrange("(k p) c -> p k c", p=128), sl_w[i], w=[wblkb[i]])
        return wblk[i], wblkb[i]

    def proj_fm(wt, wb, tg, pst_, pb, rhsT=hT, rb=hTb):
        for k in range(8):
            pe.op(lambda h: h.matmul(pst_[:, :], lhsT=wt[:, k, :], rhs=rhsT[:, k, tg * 512:(tg + 1) * 512], start=(k == 0), stop=(k == 7)),
                  r=[wb, rb], w=[pb])

    mix_mark = len(allocs)

    NCOL = 16
    LG = sb("LG", [128, NT, NCOL]); LNB = sb("LNB", [128, NT, NCOL]); BETA = sb("BETA", [128, NT, NCOL])
    LG_R = sb("LG_R", [128, NT, NCOL], F32R); LNB_R = sb("LNB_R", [128, NT, NCOL], F32R)
    A_tok = sb("A_tok", [128, NT, NCOL]); NG_tok = sb("NG_tok", [128, NT, NCOL]); BG = sb("BG", [128, NT, NCOL])
    KD = sb("KD", [128, NT, NCOL]); ET0 = sb("ET0", [128, NT, NCOL]); ET1 = sb("ET1", [128, NT, NCOL])
    scal = Buf("scal")
    s1_mark = len(allocs)
    wsm = sb("wsm", [128, 8, 32], BF16); wsmb = Buf("wsm")
    sl_wsm = Slot(nc, "wsm")
    pool.dma(wsm[:], w_in_d[0, :, OFF_SM:OFF_SM + 32].rearrange("(k p) c -> p k c", p=128), sl_wsm, w=[wsmb])
    asm = sb("asm", [128, NT, 32]); asmb = Buf("asm")
    for tt in range(NT):
        p_, pb = next_ps()
        for k in range(8):
            pe.op(lambda h: h.matmul(p_[:, 0:32], lhsT=hT[:, k, tt * 128:(tt + 1) * 128], rhs=wsm[:, k, :], start=(k == 0), stop=(k == 7)),
                  r=[hTb, wsmb], w=[pb])
        act.op(lambda h: h.copy(out=asm[:, tt, :], in_=p_[:, 0:32]), r=[pb], w=[asmb])
    t16 = sb("t16", [128, NT, NCOL]); t16b = Buf("t16")
    dve.op(lambda h: h.tensor_tensor(out=t16[:], in0=asm[:, :, 0:16], in1=prm16[:, 1:2, :].to_broadcast([128, NT, NCOL]), op=ALU.add), r=[asmb, prm], w=[t16b])
    act.op(lambda h: h.activation(out=t16[:], in_=t16[:], func=AF.Exp), r=[t16b], w=[t16b])
    act.op(lambda h: h.activation(out=t16[:], in_=t16[:], func=AF.Ln, bias=1.0), r=[t16b], w=[t16b])
    nA = sb("nA", [128, 1, NCOL]); nAb = Buf("nA")
    act.op(lambda h: h.activation(out=nA[:], in_=prm16[:, 0:1, :], func=AF.Exp), r=[prm], w=[nAb])
    dve.op(lambda h: h.scalar_tensor_tensor(out=LG[:], in0=t16[:], scalar=-1.0, in1=nA[:].to_broadcast([128, NT, NCOL]), op0=ALU.mult, op1=ALU.mult),
           r=[t16b, nAb], w=[scal])
    act.op(lambda h: h.activation(out=t16[:], in_=asm[:, :, 16:32], func=AF.Exp, scale=-1.0), r=[asmb, scal], w=[t16b])
    act.op(lambda h: h.activation(out=t16[:], in_=t16[:], func=AF.Ln, bias=1.0), r=[t16b], w=[t16b])
    act.op(lambda h: h.mul(out=LNB[:], in_=t16[:], mul=-1.0), r=[t16b], w=[scal])
    act.op(lambda h: h.activation(out=BETA[:], in_=LNB[:], func=AF.Exp), r=[scal], w=[scal])
    G_tok = sb("G_tok", [128, NT, NCOL]); TOTO = sb("TOTO", [128, NT, NCOL])
    for tt in range(NT):
        p_, pb = next_ps()
        for i, M in enumerate([TRI[0], TRI[1], BDONES, SEL0, SEL1]):
            pe.op(lambda h: h.matmul(p_[:, i * 16:(i + 1) * 16], lhsT=M[:], rhs=LG[:, tt, :], start=True, stop=True), r=[cst, scal], w=[pb])
        act.op(lambda h: h.copy(out=G_tok[:, tt, 0:8], in_=p_[:, 0:8]), r=[pb], w=[scal])
        act.op(lambda h: h.copy(out=G_tok[:, tt, 8:16], in_=p_[:, 24:32]), r=[pb], w=[scal])
        act.op(lambda h: h.copy(out=TOTO[:, tt, :], in_=p_[:, 32:48]), r=[pb], w=[scal])
        act.op(lambda h: h.activation(out=ET0[:, tt, :], in_=p_[:, 48:64], func=AF.Exp), r=[pb], w=[scal])
        act.op(lambda h: h.activation(out=ET1[:, tt, :], in_=p_[:, 64:80], func=AF.Exp), r=[pb], w=[scal])
    dve.op(lambda h: h.tensor_tensor(out=A_tok[:], in0=G_tok[:], in1=LNB[:], op=ALU.add), r=[scal], w=[scal])
    act.op(lambda h: h.mul(out=NG_tok[:], in_=G_tok[:], mul=-1.0), r=[scal], w=[scal])
    act.op(lambda h: h.activation(out=BG[:], in_=A_tok[:], func=AF.Exp), r=[scal], w=[scal])
    dve.op(lambda h: h.tensor_tensor(out=KD[:], in0=TOTO[:], in1=G_tok[:], op=ALU.subtract), r=[scal], w=[scal])
    act.op(lambda h: h.activation(out=KD[:], in_=KD[:], func=AF.Exp), r=[scal], w=[scal])
    act.op(lambda h: h.copy(out=LG_R[:], in_=LG[:]), r=[scal], w=[scal])
    act.op(lambda h: h.copy(out=LNB_R[:], in_=LNB[:]), r=[scal], w=[scal])
    if "LG" in dbg:
        sp.dma(dbg["LG"].rearrange("(t p) c -> p t c", p=128), LG[:], sl_dbg, r=[scal])
        sp.dma(dbg["BETA"].rearrange("(t p) c -> p t c", p=128), BETA[:], sl_dbg, r=[scal])
        sp.dma(dbg["G_tok"].rearrange("(t p) c -> p t c", p=128), G_tok[:], sl_dbg, r=[scal])

    free_to(s1_mark)
    gdn_mark = len(allocs)
    if gdn_heads > 0:
        pre = sb("pre", [128, T + 4]); preb = Buf("pre")
        pool.op(lambda h: h.memset(pre[:, 0:2], 0.0), w=[preb]); pool.op(lambda h: h.memset(pre[:, T + 2:T + 4], 0.0), w=[preb])
        acc = sb("acc", [128, T]); accb = Buf("acc")
        sqt = sb("sqt", [128, 512], BF16); sqtb = Buf("sqt")
        rn = sb("rn", [128, 512]); rnb = Buf("rn")
        HB = []
        for bs in range(2):
            HB.append(dict(qT=sb(f"qT{bs}", [128, T], BF16), qTb=Buf(f"qT{bs}"), kT=sb(f"kT{bs}", [128, T], BF16), kTb=Buf(f"kT{bs}"),
                           k_tok=sb(f"k_tok{bs}", [128, NT, 128], BF16), ktb=Buf(f"k_tok{bs}"),
                           v_tok=sb(f"v_tok{bs}", [128, NT, 128], BF16), vtb=Buf(f"v_tok{bs}")))
        vT = sb("vT", [128, T], BF16); vTb = Buf("vT")
        zs = sb("zs", [128, T], BF16); zsb = Buf("zs")
        oTa = sb("oTa", [128, T]); oTabs = [[Buf(f"oTa{t_}_{c_}") for c_ in range(2)] for t_ in range(NT)]
        oTa_all = [b_ for l_ in oTabs for b_ in l_]
        NSL = 2
        WK = {}
        ST = {}
        for d_ in range(2):
            for sl_i in range(NSL):
                W_ = {}
                for nm, shp, dt in [("D1", [128, 128], F32), ("D12", [128, 256], F32), ("grep", [128, 128], F32),
                                    ("LU", [128, 2, 256], BF16),
                                    ("LU2", [128, 2, 256], BF16),
                                    ("attnT", [128, 128], BF16), ("qdT", [128, 128], BF16), ("khat", [128, 128], BF16), ("bv", [128, 128], BF16),
                                    ("kd", [128, 128], BF16), ("wT", [128, 128], BF16), ("u", [128, 128], F32), ("vnew", [128, 128], BF16)]:
                    W_[nm] = sb(f"{nm}_{d_}_{sl_i}", shp, dt)
                    W_[nm + "_b"] = Buf(f"{nm}_{d_}_{sl_i}")
                WK[(d_, sl_i)] = W_
            S_ = {}
            for nm, shp, dt in [("S32", [128, 128], F32), ("S16", [128, 128], BF16)]:
                S_[nm] = sb(f"{nm}_{d_}", shp, dt)
                S_[nm + "_b"] = Buf(f"{nm}_{d_}")
            ST[d_] = S_

    def gdn_projA(hh, B_):
        for which, off in (("q", OFF_Q), ("k", OFF_K), ("v", OFF_V)):
            wt, wb = load_wblk(w_in_d[0, :, off + hh * 128: off + (hh + 1) * 128])
            for tg in range(4):
                ipj = yield from acquire("proj"); p_, pb = ps[ipj], psb[ipj]
                proj_fm(wt, wb, tg, p_, pb)
                act.op(lambda h: h.copy(out=pre[:, 2 + tg * 512: 2 + (tg + 1) * 512], in_=p_[:, :]), r=[pb], w=[preb])
                release(ipj)
                yield
            blk = off // 128 + hh
            act.op(lambda h: h.activation(out=acc[:], in_=pre[:, 0:T], func=AF.Copy, scale=cw[:, blk, 0:1]), r=[preb, prm], w=[accb])
            for t_ in range(1, 5):
                dve.op(lambda h: h.scalar_tensor_tensor(out=acc[:], in0=pre[:, t_:t_ + T], scalar=cw[:, blk, t_:t_ + 1], in1=acc[:], op0=ALU.mult, op1=ALU.add),
                       r=[preb, prm, accb], w=[accb])
                yield
            if which == "v":
                act.op(lambda h: h.activation(out=vT[:], in_=acc[:], func=AF.Silu), r=[accb], w=[vTb])
                continue
            act.op(lambda h: h.activation(out=acc[:], in_=acc[:], func=AF.Silu), r=[accb], w=[accb])
            dstT, dstb = (B_["qT"], B_["qTb"]) if which == "q" else (B_["kT"], B_["kTb"])
            post = (128.0 ** -0.5) if which == "q" else 1.0
            for tg in range(4):
                sl_ = slice(tg * 512, (tg + 1) * 512)
                act.op(lambda h: h.activation(out=sqt[:], in_=acc[:, sl_], func=AF.Square), r=[accb], w=[sqtb])
                ipj = yield from acquire("proj"); p_, pb = ps[ipj], psb[ipj]
                pe.op(lambda h: h.matmul(p_[:, :], lhsT=ones_bf[:], rhs=sqt[:], start=True, stop=True), r=[cst, sqtb], w=[pb])
                yield
                act.op(lambda h: h.activation(out=rn[:], in_=p_[:, :], func=AF.Ln, bias=EPS), r=[pb], w=[rnb])
                release(ipj)
                act.op(lambda h: h.activation(out=rn[:], in_=rn[:], func=AF.Exp, scale=-0.5), r=[rnb], w=[rnb])
                dve.op(lambda h: h.scalar_tensor_tensor(out=dstT[:, sl_], in0=acc[:, sl_], scalar=post, in1=rn[:], op0=ALU.mult, op1=ALU.mult),
                       r=[accb, rnb], w=[dstb])
                yield
        for srcT, srcb, dst, dstb in ((B_["kT"], B_["kTb"], B_["k_tok"], B_["ktb"]), (vT, vTb, B_["v_tok"], B_["vtb"])):
            for g8 in range(2):
                for j in range(8):
                    tt = g8 * 8 + j
                    pe.op(lambda h: h.transpose(pst[:, j * 128:(j + 1) * 128], srcT[:, tt * 128:(tt + 1) * 128], ident_bf[:]), r=[srcb, cst], w=[pstb])
                act.op(lambda h: h.copy(out=dst[:, g8 * 8:(g8 + 1) * 8, :], in_=pst[:].rearrange("p (j c) -> p j c", j=8)), r=[pstb], w=[dstb])
                yield

    def gdn_projZ(hh):
        wt, wb = load_wblk(w_in_d[0, :, OFF_Z + hh * 128: OFF_Z + (hh + 1) * 128])
        for tg in range(4):
            ipj = yield from acquire("proj"); p_, pb = ps[ipj], psb[ipj]
            proj_fm(wt, wb, tg, p_, pb)
            act.op(lambda h: h.activation(out=zs[:, tg * 512:(tg + 1) * 512], in_=p_[:, :], func=AF.Silu), r=[pb], w=[zsb])
            release(ipj)
            yield

    def gdn_norm(hh):
        if f"oa{hh}" in dbg:
            sp.dma(dbg[f"oa{hh}"], oTa[:], sl_dbg, r=oTa_all)
        for tg in range(4):
            sl_ = slice(tg * 512, (tg + 1) * 512)
            act.op(lambda h: h.activation(out=sqt[:], in_=oTa[:, sl_], func=AF.Square), r=[b_ for t_ in range(tg * 4, tg * 4 + 4) for b_ in oTabs[t_]], w=[sqtb])
            ipj = yield from acquire("proj")
            p_, pb = ps[ipj], psb[ipj]
            pe.op(lambda h: h.matmul(p_[:, :], lhsT=ones_bf[:], rhs=sqt[:], start=True, stop=True), r=[cst, sqtb], w=[pb])
            yield
            act.op(lambda h: h.activation(out=rn[:], in_=p_[:, :], func=AF.Ln, bias=EPS, scale=1.0 / 128), r=[pb], w=[rnb])
            release(ipj)
            act.op(lambda h: h.activation(out=rn[:], in_=rn[:], func=AF.Exp, scale=-0.5), r=[rnb], w=[rnb])
            dve.op(lambda h: h.scalar_tensor_tensor(out=rn[:], in0=oTa[:, sl_], scalar=gdn_nw[:, 0:1], in1=rn[:], op0=ALU.mult, op1=ALU.mult),
                   r=[b_ for t_ in range(tg * 4, tg * 4 + 4) for b_ in oTabs[t_]] + [rnb, prm], w=[rnb])
            dve.op(lambda h: h.tensor_tensor(out=oaT[:, hh, sl_], in0=rn[:], in1=zs[:, sl_], op=ALU.mult), r=[rnb, zsb], w=[oaTb])
            yield

    all_tasks = []
    PRA, PRZ, NORM = [], [], []
    for hh in range(gdn_heads):
        B_ = HB[hh % 2]
        pra = Task(gdn_projA(hh, B_), deps=[PRA[hh - 1] if hh >= 1 else None, NORM[hh - 2] if hh >= 2 else None, PRZ[hh - 1] if hh >= 1 else None])
        PRA.append(pra)
        all_tasks.append(pra)
        prz = Task(gdn_projZ(hh), deps=[pra, NORM[hh - 1] if hh >= 1 else None])
        PRZ.append(prz)
        def gdn_prep(d_, tt, W_, hh=hh, B_=B_):
            qT, qTb, kT, kTb, k_tok, ktb, v_tok, vtb = B_["qT"], B_["qTb"], B_["kT"], B_["kTb"], B_["k_tok"], B_["ktb"], B_["v_tok"], B_["vtb"]
            col = d_ * 8 + hh
            tsl = slice(tt * 128, (tt + 1) * 128)
            act.op(lambda h: h.activation(out=W_["khat"][:], in_=k_tok[:, tt, :], func=AF.Copy, scale=BG[:, tt, col:col + 1]),
                   r=[ktb, scal], w=[W_["khat_b"]])
            act.op(lambda h: h.activation(out=W_["bv"][:], in_=v_tok[:, tt, :], func=AF.Copy, scale=BETA[:, tt, col:col + 1]),
                   r=[vtb, scal], w=[W_["bv_b"]])
            pool.op(lambda h: h.tensor_scalar(out=W_["kd"][:], in0=k_tok[:, tt, :], scalar1=KD[:, tt, col:col + 1], scalar2=None, op0=ALU.mult),
                    r=[ktb, scal], w=[W_["kd_b"]])
            yield
            ia = yield from acquire("prep")
            pa, pab = ps[ia], psb[ia]
            pe.op(lambda h: h.matmul(pa[:, :], lhsT=LG_R[:, tt, col:col + 1].to_broadcast([128, 128]), rhs=TRI4_R[d_][:], start=True, stop=False), r=[cst, scal], w=[pab])
            pe.op(lambda h: h.matmul(pa[:, :], lhsT=ident_r[:], rhs=MASKS_R[d_][:], start=False, stop=False), r=[cst], w=[pab])
            pe.op(lambda h: h.matmul(pa[:, 128:256], lhsT=LNB_R[:, tt, col:col + 1].to_broadcast([128, 128]), rhs=ident_r[:], start=False, stop=True), r=[cst, scal], w=[pab])
            yield
            act.op(lambda h: h.activation(out=W_["D1"][:], in_=pa[:, 0:128], func=AF.Exp, bias=A_tok[:, tt, col:col + 1], scale=-1.0),
                   r=[pab, scal], w=[W_["D1_b"]])
            act.op(lambda h: h.activation(out=W_["D12"][:], in_=pa[:, 128:384], func=AF.Exp, bias=NG_tok[:, tt, col:col + 1], scale=1.0),
                   r=[pab, scal], w=[W_["D12_b"]])
            act.op(lambda h: h.activation(out=W_["grep"][:], in_=pa[:, 384:512], func=AF.Exp), r=[pab], w=[W_["grep_b"]])
            release(ia)
            yield
            ikq = yield from acquire("prep")
            pkq, pkqb = ps[ikq], psb[ikq]
            pe.op(lambda h: h.matmul(pkq[:, 0:128], lhsT=kT[:, tsl], rhs=kT[:, tsl], start=True, stop=True), r=[kTb], w=[pkqb])
            pe.op(lambda h: h.matmul(pkq[:, 128:256], lhsT=kT[:, tsl], rhs=qT[:, tsl], start=True, stop=True), r=[kTb, qTb], w=[pkqb])
            yield
            LU, LUb, LU2, LU2b = W_["LU"], W_["LU_b"], W_["LU2"], W_["LU2_b"]
            dve.op(lambda h: h.tensor_tensor(out=LU[:, 1, 0:128], in0=pkq[:, 0:128], in1=W_["D1"][:], op=ALU.mult), r=[pkqb, W_["D1_b"]], w=[LUb])
            dve.op(lambda h: h.tensor_tensor(out=LU[:, 0, 0:128], in0=pkq[:, 0:128], in1=W_["D12"][:, 0:128], op=ALU.mult), r=[pkqb, W_["D12_b"]], w=[LUb])
            dve.op(lambda h: h.tensor_tensor(out=LU2[:, 0, 128:256], in0=ident32[:], in1=LU[:, 0, 0:128], op=ALU.subtract), r=[cst, LUb], w=[LU2b])
            dve.op(lambda h: h.tensor_tensor(out=W_["attnT"][:], in0=pkq[:, 128:256], in1=W_["D12"][:, 128:256], op=ALU.mult), r=[pkqb, W_["D12_b"]], w=[W_["attnT_b"]])
            dve.op(lambda h: h.tensor_tensor(out=W_["qdT"][:], in0=qT[:, tsl], in1=W_["grep"][:], op=ALU.mult), r=[qTb, W_["grep_b"]], w=[W_["qdT_b"]])
            release(ikq)
            yield
            cur, curb, nxt, nxtb = LU, LUb, LU2, LU2b
            for lev in range(6):
                ipn = yield from acquire("prep")
                pn, pnb = ps[ipn], psb[ipn]
                if lev == 0:
                    pe.op(lambda h: h.matmul(pn[:, 0:128], lhsT=cur[:, 1, 0:128], rhs=cur[:, 0, 0:128], start=True, stop=True), r=[curb], w=[pnb])
                elif lev < 5:
                    pe.op(lambda h: h.matmul(pn[:, 0:256], lhsT=cur[:, 1, 0:128], rhs=cur[:, 0, 0:256], start=True, stop=True), r=[curb], w=[pnb])
                else:
                    pe.op(lambda h: h.matmul(pn[:, 128:256], lhsT=cur[:, 1, 0:128], rhs=cur[:, 0, 128:256], start=True, stop=True), r=[curb], w=[pnb])
                if lev < 5:
                    pe.op(lambda h: h.matmul(pn[:, 256:384], lhsT=cur[:, 0, 0:128], rhs=cur[:, 1, 0:128], start=True, stop=True), r=[curb], w=[pnb])
                yield
                if lev < 5:
                    ev_eng = act
                    if ev_eng is dve:
                        dve.op(lambda h: h.tensor_copy(out=nxt[:, :, 0:128], in_=pn[:, :].rearrange("p (a b) -> p a b", a=2)[:, :, 0:128]), r=[pnb], w=[nxtb])
                    else:
                        act.op(lambda h: h.copy(out=nxt[:, :, 0:128], in_=pn[:, :].rearrange("p (a b) -> p a b", a=2)[:, :, 0:128]), r=[pnb], w=[nxtb])
                if lev > 0:
                    dve.op(lambda h: h.tensor_tensor(out=nxt[:, 0, 128:256], in0=pn[:, 128:256], in1=cur[:, 0, 128:256], op=ALU.add), r=[pnb, curb], w=[nxtb])
                cur, curb, nxt, nxtb = nxt, nxtb, cur, curb
                release(ipn)
                yield
            Wm = cur[:, 0, 128:256]; Wmb = curb
            ipw = yield from acquire("prep")
            pw, pwb = ps[ipw], psb[ipw]
            pe.op(lambda h: h.matmul(pw[:, 0:128], lhsT=W_["khat"][:], rhs=Wm, start=True, stop=True), r=[W_["khat_b"], Wmb], w=[pwb])
            pe.op(lambda h: h.matmul(pw[:, 128:256], lhsT=Wm, rhs=W_["bv"][:], start=True, stop=True), r=[W_["bv_b"], Wmb], w=[pwb])
            yield
            act.op(lambda h: h.copy(out=W_["wT"][:], in_=pw[:, 0:128]), r=[pwb], w=[W_["wT_b"]])
            act.op(lambda h: h.copy(out=W_["u"][:], in_=pw[:, 128:256]), r=[pwb], w=[W_["u_b"]])
            release(ipw)
            yield

        def gdn_scan(d_, tt, W_, S_, hh=hh):
            col = d_ * 8 + hh
            for c in ((0, 1) if d_ == 0 else (1, 0)):
                rs = slice(c * 64, (c + 1) * 64)
                ETc = ET0 if c == 0 else ET1
                ip1 = yield from acquire("scan")
                p1, p1b = ps[ip1], psb[ip1]
                pe.op(lambda h: h.matmul(p1[rs, 0:128], lhsT=W_["wT"][:, rs], rhs=S_["S16"][:], start=True, stop=True), r=[W_["wT_b"], S_["S16_b"]], w=[p1b])
                yield
                dve.op(lambda h: h.tensor_tensor(out=W_["vnew"][rs, :], in0=W_["u"][rs, :], in1=p1[rs, 0:128], op=ALU.subtract),
                       r=[W_["u_b"], p1b], w=[W_["vnew_b"]])
                release(ip1)
                yield
                ip2 = yield from acquire("scan")
                p2, p2b = ps[ip2], psb[ip2]
                pe.op(lambda h: h.matmul(p2[:, 0:64], lhsT=S_["S16"][:], rhs=W_["qdT"][:, rs], start=True, stop=False), r=[S_["S16_b"], W_["qdT_b"]], w=[p2b])
                pe.op(lambda h: h.matmul(p2[:, 0:64], lhsT=W_["vnew"][rs, :], rhs=W_["attnT"][rs, rs], start=False, stop=True),
                      r=[W_["vnew_b"], W_["attnT_b"]], w=[p2b])
                pe.op(lambda h: h.matmul(p2[:, 128:256], lhsT=W_["kd"][rs, :], rhs=W_["vnew"][rs, :], start=True, stop=True),
                      r=[W_["kd_b"], W_["vnew_b"]], w=[p2b])
                yield
                dve.op(lambda h: h.scalar_tensor_tensor(out=S_["S32"][:], in0=S_["S32"][:], scalar=ETc[:, tt, col:col + 1], in1=p2[:, 128:256],
                                                        op0=ALU.mult, op1=ALU.add), r=[S_["S32_b"], scal, p2b], w=[S_["S32_b"]])
                act.op(lambda h: h.copy(out=S_["S16"][:], in_=S_["S32"][:]), r=[S_["S32_b"]], w=[S_["S16_b"]])
                osl = slice(tt * 128 + c * 64, tt * 128 + (c + 1) * 64)
                first = (tt < NT // 2) if d_ == 0 else (tt >= NT // 2)
                if first:
                    act.op(lambda h: h.copy(out=oTa[:, osl], in_=p2[:, 0:64]), r=[p2b], w=[oTabs[tt][c]])
                else:
                    dve.op(lambda h: h.tensor_tensor(out=oTa[:, osl], in0=p2[:, 0:64], in1=oTa[:, osl], op=ALU.add), r=[p2b, oTabs[tt][c]], w=[oTabs[tt][c]])
                release(ip2)
                yield

        def gdn_init():
            for d_ in range(2):
                pool.op(lambda h: h.memset(ST[d_]["S32"][:], 0.0), w=[ST[d_]["S32_b"]])
                pool.op(lambda h: h.memset(ST[d_]["S16"][:], 0.0), w=[ST[d_]["S16_b"]])
            yield
        init_t = Task(gdn_init(), deps=[NORM[hh - 1] if hh >= 1 else None])
        all_tasks.append(init_t)
        order = {0: list(range(NT)), 1: list(range(NT - 1, -1, -1))}
        P = {0: [], 1: []}
        S = {0: [], 1: []}
        tasks = all_tasks
        for i in range(NT):
            for d_ in range(2):
                tt = order[d_][i]
                W_ = WK[(d_, i % NSL)]
                pt = Task(gdn_prep(d_, tt, W_), deps=[S[d_][i - NSL] if i >= NSL else None, PRA[hh], init_t])
                P[d_].append(pt)
                tasks.append(pt)
            for d_ in range(2):
                tt = order[d_][i]
                W_ = WK[(d_, i % NSL)]
                other = S[1 - d_][NT - 1 - i] if (i >= NT // 2 and len(S[1 - d_]) > NT - 1 - i) else None
                stt = Task(gdn_scan(d_, tt, W_, ST[d_]), deps=[P[d_][i], S[d_][i - 1] if i >= 1 else None, other])
                S[d_].append(stt)
                tasks.append(stt)
        all_tasks.append(prz)
        nt = Task(gdn_norm(hh), deps=[prz] + S[0] + S[1])
        NORM.append(nt)
        all_tasks.append(nt)
    if gdn_heads > 0:
        for hh in range(gdn_heads - 1):
            NORM[hh].deps.append(PRA[hh + 1])
        run_tasks(all_tasks, max_active=MAXACT + 2)
    if "oaT" in dbg:
        tmp = sb("dbg_tmp3", [128, T]); tb = Buf("dbg_tmp3")
        for k in range(8):
            act.op(lambda h: h.copy(out=tmp[:], in_=oaT[:, k, :]), r=[oaTb], w=[tb])
            sp.dma(dbg["oaT"][k * 128:(k + 1) * 128, :], tmp[:], sl_dbg, r=[tb])
    free_to(mix_mark)

    obT = sb("obT", [128, 8, T], BF16); obTb = Buf("obT")
    gla_mark = len(allocs)
    if gla_heads > 0:
        NSG = 2
        GW = {}
        GS_ = {}
        for d_ in range(2):
            for sl_i in range(NSG):
                W_ = {}
                for nm, shp, dt in [("gk", [128, 128], F32), ("ekg", [128, 256], F32), ("eng", [128, 128], F32), ("etot", [128, 1], F32),
                                    ("kdg", [128, 128], BF16), ("qgT", [128, 128], BF16), ("kgT", [128, 128], BF16), ("attnT", [128, 128], BF16)]:
                    W_[nm] = sb(f"g{nm}_{d_}_{sl_i}", shp, dt)
                    W_[nm + "_b"] = Buf(f"g{nm}_{d_}_{sl_i}")
                GW[(d_, sl_i)] = W_
            S_ = {}
            for nm, shp, dt in [("S32", [128, 256], F32), ("S16", [128, 256], BF16)]:
                S_[nm] = sb(f"g{nm}_{d_}", shp, dt)
                S_[nm + "_b"] = Buf(f"g{nm}_{d_}")
            GS_[d_] = S_

        rT1 = sb("rT1", [33, T]); rT1b = Buf("rT1")
        wr = sb("wr", [128, 8, 32], BF16); wrb = Buf("wr")
        sl_wr = Slot(nc, "wr")
        pool.dma(wr[:], w_in_d[0, :, OFF_R:OFF_R + 32].rearrange("(k p) c -> p k c", p=128), sl_wr, w=[wrb])
        pool.op(lambda h: h.memset(rT1[32:33, :], 1.0), w=[rT1b])
        for tg in range(4):
            p_, pb = next_ps()
            for k in range(8):
                pe.op(lambda h: h.matmul(p_[0:32, :], lhsT=wr[:, k, :], rhs=hT[:, k, tg * 512:(tg + 1) * 512], start=(k == 0), stop=(k == 7)),
                      r=[wrb, hTb], w=[pb])
            act.op(lambda h: h.copy(out=rT1[0:32, tg * 512:(tg + 1) * 512], in_=p_[0:32, :]), r=[pb], w=[rT1b])
        wkv = sb("wkv", [128, 8, 384], BF16); wkvb = Buf("wkv"); sl_wkv = Slot(nc, "wkv")
        qTg = sb("qTg", [128, T], BF16); qTgb = Buf("qTg")
        kTg = sb("kTg", [128, T], BF16); kTgb = Buf("kTg")
        kg_tok = sb("kg_tok", [128, NT, 128], BF16); kgtb = Buf("kg_tok")
        vg_tok = sb("vg_tok", [128, NT, 256], BF16); vgtb = Buf("vg_tok")
        gsT = sb("gsT", [128, 2, T], BF16); gsTb = Buf("gsT")
        obTa = sb("obTa", [128, 2, T]); obTabs = [Buf(f"obTa{t_}") for t_ in range(NT)]
        sqg = sb("sqg", [128, 512], BF16); sqgb = Buf("sqg")
        rng = sb("rng", [128, 512]); rngb = Buf("rng")
    for hb in range(gla_heads):
        for off, dst, dstb, scl in ((OFF_QB, qTg, qTgb, 128.0 ** -0.5), (OFF_KB, kTg, kTgb, 1.0)):
            wt, wb = load_wblk(w_in_d[0, :, off + hb * 128: off + (hb + 1) * 128])
            for tg in range(4):
                p_, pb = next_ps()
                proj_fm(wt, wb, tg, p_, pb)
                act.op(lambda h: h.mul(out=dst[:, tg * 512:(tg + 1) * 512], in_=p_[:, :], mul=scl), r=[pb], w=[dstb])
        for eb in range(2):
            wt, wb = load_wblk(w_in_d[0, :, OFF_GB + hb * 256 + eb * 128: OFF_GB + hb * 256 + (eb + 1) * 128])
            for tg in range(4):
                p_, pb = next_ps()
                proj_fm(wt, wb, tg, p_, pb)
                act.op(lambda h: h.activation(out=gsT[:, eb, tg * 512:(tg + 1) * 512], in_=p_[:, :], func=AF.Silu), r=[pb], w=[gsTb])
        pool.dma(wkv[:, :, 0:128], w_in_d[0, :, OFF_KB + hb * 128: OFF_KB + (hb + 1) * 128].rearrange("(k p) c -> p k c", p=128), sl_wkv, w=[wkvb])
        pool.dma(wkv[:, :, 128:384], w_in_d[0, :, OFF_VB + hb * 256: OFF_VB + (hb + 1) * 256].rearrange("(k p) c -> p k c", p=128), sl_wkv, w=[wkvb])
        for tt in range(NT):
            p_, pb = next_ps()
            for k in range(8):
                pe.op(lambda h: h.matmul(p_[:, 0:384], lhsT=hT[:, k, tt * 128:(tt + 1) * 128], rhs=wkv[:, k, :], start=(k == 0), stop=(k == 7)),
                      r=[hTb, wkvb], w=[pb])
            act.op(lambda h: h.copy(out=kg_tok[:, tt, :], in_=p_[:, 0:128]), r=[pb], w=[kgtb])
            act.op(lambda h: h.copy(out=vg_tok[:, tt, :], in_=p_[:, 128:384]), r=[pb], w=[vgtb])

        def gla_prep(d_, tt, W_, hb=hb):
            tsl = slice(tt * 128, (tt + 1) * 128)
            ix = yield from acquire("gprep")
            px, pxb = ps[ix], psb[ix]
            pe.op(lambda h: h.matmul(px[:, 0:128], lhsT=rT1[:, tsl], rhs=w2cat[:, d_, hb * 128:(hb + 1) * 128], start=True, stop=True),
                  r=[rT1b, prm], w=[pxb])
            yield
            act.op(lambda h: h.activation(out=W_["gk"][:], in_=px[:, 0:128], func=AF.Exp, scale=-1.0), r=[pxb], w=[W_["gk_b"]])
            release(ix)
            act.op(lambda h: h.activation(out=W_["gk"][:], in_=W_["gk"][:], func=AF.Ln, bias=1.0), r=[W_["gk_b"]], w=[W_["gk_b"]])
            yield
            ig = yield from acquire("gprep")
            pg, pgb = ps[ig], psb[ig]
            pe.op(lambda h: h.matmul(pg[:, 0:128], lhsT=DIFG[d_][:], rhs=W_["gk"][:], start=True, stop=True), r=[cst, W_["gk_b"]], w=[pgb])
            pe.op(lambda h: h.matmul(pg[:, 128:256], lhsT=W_["gk"][:], rhs=TRIG[d_][:], start=True, stop=True), r=[cst, W_["gk_b"]], w=[pgb])
            yield
            act.op(lambda h: h.activation(out=W_["ekg"][:], in_=pg[:, 0:256], func=AF.Exp), r=[pgb], w=[W_["ekg_b"]])
            act.op(lambda h: h.activation(out=W_["eng"][:], in_=pg[:, 128:256], func=AF.Exp, scale=-1.0), r=[pgb], w=[W_["eng_b"]])
            lastc = 128 + (127 if d_ == 0 else 0)
            act.op(lambda h: h.activation(out=W_["etot"][:], in_=pg[:, lastc:lastc + 1], func=AF.Exp), r=[pgb], w=[W_["etot_b"]])
            release(ig)
            yield
            dve.op(lambda h: h.tensor_tensor(out=W_["qgT"][:], in0=qTg[:, tsl], in1=W_["ekg"][:, 128:256], op=ALU.mult), r=[qTgb, W_["ekg_b"]], w=[W_["qgT_b"]])
            pool.op(lambda h: h.tensor_tensor(out=W_["kgT"][:], in0=kTg[:, tsl], in1=W_["eng"][:], op=ALU.mult), r=[kTgb, W_["eng_b"]], w=[W_["kgT_b"]])
            dve.op(lambda h: h.tensor_tensor(out=W_["kdg"][:], in0=kg_tok[:, tt, :], in1=W_["ekg"][:, 0:128], op=ALU.mult), r=[kgtb, W_["ekg_b"]], w=[W_["kdg_b"]])
            yield
            ia = yield from acquire("gprep")
            pa_, pab_ = ps[ia], psb[ia]
            pe.op(lambda h: h.matmul(pa_[:, 0:128], lhsT=W_["kgT"][:], rhs=W_["qgT"][:], start=True, stop=True), r=[W_["kgT_b"], W_["qgT_b"]], w=[pab_])
            yield
            dve.op(lambda h: h.tensor_tensor(out=W_["attnT"][:], in0=pa_[:, 0:128], in1=MASKG[d_][:], op=ALU.mult), r=[pab_, cst], w=[W_["attnT_b"]])
            release(ia)
            yield

        def gla_scan(d_, tt, W_, S_):
            tsl = slice(tt * 128, (tt + 1) * 128)
            io = yield from acquire("gscan")
            po, pob = ps[io], psb[io]
            for eb in range(2):
                pe.op(lambda h: h.matmul(po[:, eb * 128:(eb + 1) * 128], lhsT=S_["S16"][:, eb * 128:(eb + 1) * 128], rhs=W_["qgT"][:], start=True, stop=False),
                      r=[S_["S16_b"], W_["qgT_b"]], w=[pob])
                pe.op(lambda h: h.matmul(po[:, eb * 128:(eb + 1) * 128], lhsT=vg_tok[:, tt, eb * 128:(eb + 1) * 128], rhs=W_["attnT"][:], start=False, stop=True),
                      r=[vgtb, W_["attnT_b"]], w=[pob])
            iS = yield from acquire("gscan")
            pS, pSb = ps[iS], psb[iS]
            pe.op(lambda h: h.matmul(pS[:, 0:256], lhsT=W_["kdg"][:], rhs=vg_tok[:, tt, :], start=True, stop=True), r=[W_["kdg_b"], vgtb], w=[pSb])
            yield
            dve.op(lambda h: h.scalar_tensor_tensor(out=S_["S32"][:], in0=S_["S32"][:], scalar=W_["etot"][:, 0:1], in1=pS[:, 0:256],
                                                    op0=ALU.mult, op1=ALU.add), r=[S_["S32_b"], W_["etot_b"], pSb], w=[S_["S32_b"]])
            release(iS)
            act.op(lambda h: h.copy(out=S_["S16"][:], in_=S_["S32"][:]), r=[S_["S32_b"]], w=[S_["S16_b"]])
            first = (tt < NT // 2) if d_ == 0 else (tt >= NT // 2)
            if first:
                act.op(lambda h: h.copy(out=obTa[:, :, tsl], in_=po[:, 0:256].rearrange("p (e t) -> p e t", e=2)), r=[pob], w=[obTabs[tt]])
            else:
                dve.op(lambda h: h.tensor_tensor(out=obTa[:, :, tsl], in0=po[:, 0:256].rearrange("p (e t) -> p e t", e=2), in1=obTa[:, :, tsl], op=ALU.add),
                       r=[pob, obTabs[tt]], w=[obTabs[tt]])
            release(io)
            yield

        for d_ in range(2):
            pool.op(lambda h: h.memset(GS_[d_]["S32"][:], 0.0), w=[GS_[d_]["S32_b"]])
            pool.op(lambda h: h.memset(GS_[d_]["S16"][:], 0.0), w=[GS_[d_]["S16_b"]])
        order = {0: list(range(NT)), 1: list(range(NT - 1, -1, -1))}
        P = {0: [], 1: []}
        S = {0: [], 1: []}
        tasks = []
        for i in range(NT):
            for d_ in range(2):
                pt = Task(gla_prep(d_, order[d_][i], GW[(d_, i % NSG)]), deps=[S[d_][i - NSG] if i >= NSG else None])
                P[d_].append(pt)
                tasks.append(pt)
            for d_ in range(2):
                other = S[1 - d_][NT - 1 - i] if (i >= NT // 2 and len(S[1 - d_]) > NT - 1 - i) else None
                stt = Task(gla_scan(d_, order[d_][i], GW[(d_, i % NSG)], GS_[d_]), deps=[P[d_][i], S[d_][i - 1] if i >= 1 else None, other])
                S[d_].append(stt)
                tasks.append(stt)
        run_tasks(tasks, max_active=MAXACT)
        if f"ob{hb}" in dbg:
            for eb in range(2):
                sp.dma(dbg[f"ob{hb}"][eb * 128:(eb + 1) * 128, :], obTa[:, eb, :], sl_dbg, r=obTabs)
        for tg in range(4):
            sl_ = slice(tg * 512, (tg + 1) * 512)
            p_, pb = next_ps()
            for eb in range(2):
                act.op(lambda h: h.activation(out=sqg[:], in_=obTa[:, eb, sl_], func=AF.Square), r=obTabs[tg * 4:tg * 4 + 4], w=[sqgb])
                pe.op(lambda h: h.matmul(p_[:, :], lhsT=ones_bf[:], rhs=sqg[:], start=(eb == 0), stop=(eb == 1)), r=[cst, sqgb], w=[pb])
            act.op(lambda h: h.activation(out=rng[:], in_=p_[:, :], func=AF.Ln, bias=EPS, scale=1.0 / 256), r=[pb], w=[rngb])
            act.op(lambda h: h.activation(out=rng[:], in_=rng[:], func=AF.Exp, scale=-0.5), r=[rngb], w=[rngb])
            for eb in range(2):
                dve.op(lambda h: h.scalar_tensor_tensor(out=obTa[:, eb, sl_], in0=obTa[:, eb, sl_], scalar=gla_nw[:, eb:eb + 1], in1=rng[:], op0=ALU.mult, op1=ALU.mult),
                       r=obTabs[tg * 4:tg * 4 + 4] + [rngb, prm], w=obTabs[tg * 4:tg * 4 + 4])
                dve.op(lambda h: h.tensor_tensor(out=obT[:, hb * 2 + eb, sl_], in0=obTa[:, eb, sl_], in1=gsT[:, eb, sl_], op=ALU.mult), r=obTabs[tg * 4:tg * 4 + 4] + [gsTb], w=[obTb])
    if "obT" in dbg:
        tmp = sb("dbg_tmp4", [128, T]); tb = Buf("dbg_tmp4")
        for k in range(8):
            act.op(lambda h: h.copy(out=tmp[:], in_=obT[:, k, :]), r=[obTb], w=[tb])
            sp.dma(dbg["obT"][k * 128:(k + 1) * 128, :], tmp[:], sl_dbg, r=[tb])
    free_to(gla_mark)

    if do_final:
        mT = sb("mT", [128, 8, T], BF16); mTb = Buf("mT")
        fin_mark = len(allocs)
        sga = sb("sga", [128, 512]); sgab = Buf("sga")
        sgb = sb("sgb", [128, 512]); sgbb = Buf("sgb")
        t1 = sb("t1", [128, 512]); t1b = Buf("t1")
        t2 = sb("t2", [128, 512]); t2b = Buf("t2")
        NW2 = 8
        wb2 = [sb(f"wb2_{i}", [128, 8, 128], BF16) for i in range(NW2)]; wb2b = [Buf(f"wb2_{i}") for i in range(NW2)]
        sl_w2 = [Slot(nc, f"w2_{i}") for i in range(NW2)]
        rr2 = [0]

        def load2(src_ap):
            i = rr2[0] % NW2
            rr2[0] += 1
            pool.dma(wb2[i][:], src_ap.rearrange("(k p) c -> p k c", p=128), sl_w2[i], w=[wb2b[i]])
            return wb2[i], wb2b[i]

        for m in range(8):
            msl = slice(m * 128, (m + 1) * 128)
            wg, wgb_ = load2(wpg_d[0, :, msl])
            wl, wlb_ = load2(wpl_d[0, :, msl])
            wa, wab_ = load2(w_in_d[0, :, OFF_GA + m * 128: OFF_GA + (m + 1) * 128])
            wbb, wbbb_ = load2(w_in_d[0, :, OFF_GBG + m * 128: OFF_GBG + (m + 1) * 128])
            for tg in range(4):
                sl_ = slice(tg * 512, (tg + 1) * 512)
                pga, pgab = next_ps(); proj_fm(wa, wab_, tg, pga, pgab)
                act.op(lambda h: h.activation(out=sga[:], in_=pga[:, :], func=AF.Sigmoid), r=[pgab], w=[sgab])
                pgb_, pgbb = next_ps(); proj_fm(wbb, wbbb_, tg, pgb_, pgbb)
                act.op(lambda h: h.activation(out=sgb[:], in_=pgb_[:, :], func=AF.Sigmoid), r=[pgbb], w=[sgbb])
                pya, pyab = next_ps(); proj_fm(wg, wgb_, tg, pya, pyab, rhsT=oaT, rb=oaTb)
                dve.op(lambda h: h.tensor_tensor(out=t1[:], in0=pya[:, :], in1=sga[:], op=ALU.mult), r=[pyab, sgab], w=[t1b])
                pyb, pybb = next_ps(); proj_fm(wl, wlb_, tg, pyb, pybb, rhsT=obT, rb=obTb)
                dve.op(lambda h: h.tensor_tensor(out=t2[:], in0=pyb[:, :], in1=sgb[:], op=ALU.mult), r=[pybb, sgbb], w=[t2b])
                dve.op(lambda h: h.tensor_tensor(out=mT[:, m, sl_], in0=t1[:], in1=t2[:], op=ALU.add), r=[t1b, t2b], w=[mTb])
        if "mT" in dbg:
            tmp = sb("dbg_tmp5", [128, T]); tb = Buf("dbg_tmp5")
            for k in range(8):
                act.op(lambda h: h.copy(out=tmp[:], in_=mT[:, k, :]), r=[mTb], w=[tb])
                sp.dma(dbg["mT"][k * 128:(k + 1) * 128, :], tmp[:], sl_dbg, r=[tb])
        free_to(fin_mark)
        wob = hTb; sl_wo = Slot(nc, "wo")
        for k in range(8):
            pool.dma(hT[:, k, 0:D], wout_d[0, k * 128:(k + 1) * 128, :], sl_wo, w=[wob])
        lnp = sb("lnp", [128, D]); sl_lnp = Slot(nc, "lnp"); lnpb = Buf("lnp")
        sp.dma(lnp[:], ln_post_d[0, :].partition_broadcast(128), sl_lnp, w=[lnpb])
        xr = [sb(f"xr{i}", [128, D]) for i in range(2)]; xrb = [Buf(f"xr{i}") for i in range(2)]
        sl_xr = [Slot(nc, f"xr{i}") for i in range(2)]
        ot = [sb(f"ot{i}", [128, D]) for i in range(2)]; otb = [Buf(f"ot{i}") for i in range(2)]
        st1 = sb("st1", [128, 8]); st1b = Buf("st1")
        junk2 = sb("junk2", [128, 512], BF16); junk2b = Buf("junk2")
        for tt in range(NT):
            s = tt % 2
            tsl = slice(tt * 128, (tt + 1) * 128)
            sp.dma(xr[s][:], x_d[tsl, :], sl_xr[s], w=[xrb[s]])
            pp = []
            for half in range(2):
                p_, pb = next_ps()
                for m in range(8):
                    pe.op(lambda h: h.matmul(p_[:, :], lhsT=mT[:, m, tsl], rhs=hT[:, m, half * 512:(half + 1) * 512], start=(m == 0), stop=(m == 7)),
                          r=[mTb, wob], w=[pb])
                act.op(lambda h: h.activation(out=junk2[:], in_=p_[:, :], func=AF.Square, accum_out=st1[:, half:half + 1]), r=[pb], w=[junk2b, st1b])
                pp.append((p_, pb))
            dve.op(lambda h: h.tensor_tensor(out=st1[:, 2:3], in0=st1[:, 0:1], in1=st1[:, 1:2], op=ALU.add), r=[st1b], w=[st1b])
            act.op(lambda h: h.activation(out=st1[:, 3:4], in_=st1[:, 2:3], func=AF.Ln, bias=EPS, scale=1.0 / D), r=[st1b], w=[st1b])
            act.op(lambda h: h.activation(out=st1[:, 4:5], in_=st1[:, 3:4], func=AF.Exp, scale=-0.5), r=[st1b], w=[st1b])
            for half in range(2):
                p_, pb = pp[half]
                hs = slice(half * 512, (half + 1) * 512)
                dve.op(lambda h: h.scalar_tensor_tensor(out=ot[s][:, hs], in0=p_[:, :], scalar=st1[:, 4:5], in1=lnp[:, hs], op0=ALU.mult, op1=ALU.mult),
                       r=[pb, st1b, lnpb], w=[otb[s]])
                dve.op(lambda h: h.tensor_tensor(out=ot[s][:, hs], in0=ot[s][:, hs], in1=xr[s][:, hs], op=ALU.add), r=[otb[s], xrb[s]], w=[otb[s]])
            sp.dma(out_d[tsl, :], ot[s][:], sl_out2[s], r=[otb[s]])
    else:
        zt = sb("zt", [128, D]); ztb = Buf("zt")
        pool.op(lambda h: h.memset(zt[:], 0.0), w=[ztb])
        for tt in range(NT):
            sp.dma(out_d[tt * 128:(tt + 1) * 128, :], zt[:], sl_out2[tt % 2], r=[ztb])
    for so_ in sl_out2:
        if so_.cnt:
            sp.h.wait_ge(so_.sem, so_.cnt)
    if sl_dbg.cnt:
        sp.h.wait_ge(sl_dbg.sem, sl_dbg.cnt)
    return nc


_INPUT_NAMES = ["ln_pre_w", "w_in", "conv_w", "a_log_fwd", "a_log_bwd", "dt_bias_fwd", "dt_bias_bwd", "gdn_norm_w", "w_proj_gdn",
                "gk_w2_fwd", "gk_b2_fwd", "gk_w2_bwd", "gk_b2_bwd", "gla_norm_w", "w_proj_gla", "w_out", "ln_post_w"]


def kernel(**inputs):
    x = np.ascontiguousarray(np.asarray(inputs["x"], dtype=np.float32))
    shared = {n: np.ascontiguousarray(np.asarray(inputs[n], dtype=np.float32)) for n in _INPUT_NAMES}
    nc = build_nc()
    in_maps = [dict(shared, x=x[b]) for b in range(8)]
    res = run_bass_kernel_spmd(nc, in_maps, core_ids=list(range(8)))
    return np.stack([np.asarray(r["out"], dtype=np.float32) for r in res.results], axis=0)
```
